# Optimizing a Trainium2 kernel written in Bass

```python
import math, functools
import jax, jax.numpy as jnp
from jax import lax
import numpy as np

D_MODEL = 2048
BATCH = 4
SEQ = 4096
DEPTH = 1

RET_HEADS = 8
RET_DK = 128
RET_DV = 256
RET_CHUNK = 128
ROPE_BASE = 10000.0
NSA_HEADS = 16
NSA_GROUPS = 2
NSA_HPG = NSA_HEADS // NSA_GROUPS
NSA_DH = 128
CMP_BLOCK = 32
CMP_STRIDE = 16
CMP_HIDDEN = 256
SLC_BLOCK = 64
SLC_TOPK = 16
WINDOW = 512
Q_BLOCK = 128
FORCE_SCORE = 1e4
D_FF = 5632
CONV_W = 3
EPS = 1e-6

RET_QW = RET_HEADS * RET_DK
RET_VW = RET_HEADS * RET_DV
NSA_QW = NSA_HEADS * NSA_DH
NSA_KVW = NSA_GROUPS * NSA_DH
SPLIT_SIZES = [RET_QW, RET_QW, RET_VW, RET_VW,
               NSA_QW, 6 * NSA_KVW, 3 * NSA_HEADS,
               2 * D_MODEL]
SPLIT_OFFSETS = [int(o) for o in np.cumsum(SPLIT_SIZES)[:-1]]
IN_COLS = int(sum(SPLIT_SIZES))

kernel_name = "hybrid_retention_nsa_convffn"


def rmsnorm(x, w):
    xf = x.astype(jnp.float32)
    y = xf * lax.rsqrt(jnp.mean(xf * xf, axis=-1, keepdims=True) + EPS)
    return (y * w).astype(x.dtype)


def head_rmsnorm(x):
    xf = x.astype(jnp.float32)
    return (xf * lax.rsqrt(jnp.mean(xf * xf, axis=-1, keepdims=True) + EPS)).astype(x.dtype)


def masked_softmax(s, mask):
    s = jnp.where(mask, s.astype(jnp.float32), -jnp.inf)
    m = jnp.max(s, axis=-1, keepdims=True)
    m = jnp.where(jnp.isfinite(m), m, 0.0)
    p = jnp.exp(s - m)
    return p / jnp.maximum(jnp.sum(p, axis=-1, keepdims=True), 1e-30)


def rotary(x, pos):
    d = x.shape[-1]
    half = d // 2
    freq = ROPE_BASE ** (-jnp.arange(half, dtype=jnp.float32) / half)
    ang = pos.astype(jnp.float32)[:, None] * freq[None, :]
    cos, sin = jnp.cos(ang), jnp.sin(ang)
    x1, x2 = x[..., :half], x[..., half:]
    return jnp.concatenate([x1 * cos - x2 * sin, x1 * sin + x2 * cos], axis=-1).astype(x.dtype)


def retention_chunkwise(q, k, v):
    B, H, S, dk = q.shape
    dv = v.shape[-1]
    C = RET_CHUNK
    nC = S // C
    log_gamma = jnp.log(1.0 - 2.0 ** (-5.0 - jnp.arange(H, dtype=jnp.float32)))
    q = q.reshape(B, H, nC, C, dk)
    k = k.reshape(B, H, nC, C, dk)
    v = v.reshape(B, H, nC, C, dv)
    j = jnp.arange(C, dtype=jnp.float32)
    rel = j[:, None] - j[None, :]
    decay = jnp.where(rel >= 0, jnp.exp(log_gamma[:, None, None] * jnp.maximum(rel, 0.0)), 0.0)
    scores = jnp.einsum('bhcnd,bhcmd->bhcnm', q, k) * decay[None, :, None]
    inner = jnp.einsum('bhcnm,bhcme->bhcne', scores, v)
    k_decay = jnp.exp(log_gamma[:, None] * (C - 1 - j)[None, :])
    kv = jnp.einsum('bhcjd,hj,bhcje->bhcde', k, k_decay, v)
    chunk_decay = jnp.exp(log_gamma * C)[None, :, None, None]

    def step(state, kv_c):
        return state * chunk_decay + kv_c, state

    _, states = lax.scan(step, jnp.zeros((B, H, dk, dv), kv.dtype), jnp.moveaxis(kv, 2, 0))
    states = jnp.moveaxis(states, 0, 2)
    q_decay = jnp.exp(log_gamma[:, None] * (j + 1.0)[None, :])
    cross = jnp.einsum('bhcnd,bhcde->bhcne', q, states) * q_decay[None, :, None, :, None]
    return (inner + cross).reshape(B, H, S, dv)


def compress_blocks(x, tok, pe, w1, w2):
    blocks = x[:, :, tok] + pe
    flat = blocks.reshape(blocks.shape[0], blocks.shape[1], blocks.shape[2], CMP_BLOCK * x.shape[-1])
    return jax.nn.gelu(flat @ w1) @ w2


def native_sparse_attention(q, k_c, v_c, k_s, v_s, k_w, v_w, gates,
                            pe_k, w1_k, w2_k, pe_v, w1_v, w2_v):
    B, G, S, Dh = k_s.shape
    R = NSA_HPG
    n_cmp = (S - CMP_BLOCK) // CMP_STRIDE + 1
    n_slc = S // SLC_BLOCK
    top_n = min(SLC_TOPK, n_slc)
    cmp_start = np.arange(n_cmp) * CMP_STRIDE
    tok = cmp_start[:, None] + np.arange(CMP_BLOCK)[None, :]
    k_cmp = compress_blocks(k_c, tok, pe_k, w1_k, w2_k)
    v_cmp = compress_blocks(v_c, tok, pe_v, w1_v, w2_v)
    cmp_end = jnp.asarray(cmp_start + CMP_BLOCK - 1, jnp.int32)
    slc_start = np.arange(n_slc) * SLC_BLOCK
    overlap = jnp.asarray(((cmp_start[:, None] < slc_start[None, :] + SLC_BLOCK) &
                           (cmp_start[:, None] + CMP_BLOCK > slc_start[None, :])).astype(np.float32))
    k_blocks = k_s.reshape(B, G, n_slc, SLC_BLOCK, Dh)
    v_blocks = v_s.reshape(B, G, n_slc, SLC_BLOCK, Dh)
    pad = ((0, 0), (0, 0), (WINDOW, 0), (0, 0))
    k_wp = jnp.pad(k_w, pad)
    v_wp = jnp.pad(v_w, pad)
    qg = q.reshape(B, G, R, S, Dh)
    gg = gates.reshape(B, G, R, S, 3)
    scale = Dh ** -0.5
    blk_ids = jnp.arange(n_slc, dtype=jnp.int32)
    gather = jax.vmap(jax.vmap(lambda kb, ix: kb[ix]))

    def block(start):
        qb = lax.dynamic_slice_in_dim(qg, start, Q_BLOCK, axis=3) * scale
        t = start + jnp.arange(Q_BLOCK, dtype=jnp.int32)
        s_c = jnp.einsum('bgrqd,bgnd->bgrqn', qb, k_cmp)
        p_c = masked_softmax(s_c, cmp_end[None, :] <= t[:, None])
        o_c = jnp.einsum('bgrqn,bgnd->bgrqd', p_c.astype(v_cmp.dtype), v_cmp)
        imp = jnp.einsum('bgrqn,ns->bgqs', p_c, overlap)
        cur = t // SLC_BLOCK
        forced = (blk_ids[None, :] == 0) | (blk_ids[None, :] == cur[:, None]) | (blk_ids[None, :] == cur[:, None] - 1)
        imp = jnp.where(forced, FORCE_SCORE, imp)
        imp = jnp.where(blk_ids[None, :] > cur[:, None], -FORCE_SCORE, imp)
        _, idx = lax.top_k(imp, top_n)
        kg = gather(k_blocks, idx).reshape(B, G, Q_BLOCK, top_n * SLC_BLOCK, Dh)
        vg = gather(v_blocks, idx).reshape(B, G, Q_BLOCK, top_n * SLC_BLOCK, Dh)
        kpos_s = (idx[..., None] * SLC_BLOCK + jnp.arange(SLC_BLOCK, dtype=jnp.int32)).reshape(B, G, Q_BLOCK, top_n * SLC_BLOCK)
        s_s = jnp.einsum('bgrqd,bgqkd->bgrqk', qb, kg)
        p_s = masked_softmax(s_s, (kpos_s <= t[:, None])[:, :, None])
        o_s = jnp.einsum('bgrqk,bgqkd->bgrqd', p_s.astype(vg.dtype), vg)
        kw = lax.dynamic_slice_in_dim(k_wp, start, WINDOW + Q_BLOCK, axis=2)
        vw = lax.dynamic_slice_in_dim(v_wp, start, WINDOW + Q_BLOCK, axis=2)
        kpos_w = start - WINDOW + jnp.arange(WINDOW + Q_BLOCK, dtype=jnp.int32)
        mask_w = (kpos_w[None, :] <= t[:, None]) & (kpos_w[None, :] > t[:, None] - WINDOW) & (kpos_w[None, :] >= 0)
        s_w = jnp.einsum('bgrqd,bgkd->bgrqk', qb, kw)
        p_w = masked_softmax(s_w, mask_w)
        o_w = jnp.einsum('bgrqk,bgkd->bgrqd', p_w.astype(vw.dtype), vw)
        gb = lax.dynamic_slice_in_dim(gg, start, Q_BLOCK, axis=3)
        return gb[..., 0:1] * o_c + gb[..., 1:2] * o_s + gb[..., 2:3] * o_w

    starts = jnp.arange(S // Q_BLOCK, dtype=jnp.int32) * Q_BLOCK
    out = lax.map(block, starts)
    return out.transpose(1, 0, 4, 2, 3, 5).reshape(B, S, NSA_HEADS * Dh)


def causal_dwconv(u, w, b):
    S = u.shape[1]
    up = jnp.pad(u, ((0, 0), (CONV_W - 1, 0), (0, 0)))
    y = up[:, 0:S] * w[0]
    for tap in range(1, CONV_W):
        y = y + up[:, tap:tap + S] * w[tap]
    return y + b


def setup_inputs(seed: int = 0) -> dict:
    key = jax.random.key(seed)
    ks = jax.random.split(key, 20)
    f32 = jnp.float32

    def nrm(k, shape, fan_in):
        return jax.random.normal(k, shape, f32) * (fan_in ** -0.5)

    L = DEPTH
    return {
        "x": jax.random.normal(ks[0], (BATCH, SEQ, D_MODEL), f32),
        "norm1_w": 1.0 + 0.01 * jax.random.normal(ks[1], (L, D_MODEL), f32),
        "w_in": nrm(ks[2], (L, D_MODEL, IN_COLS), D_MODEL),
        "ret_norm_w": 1.0 + 0.01 * jax.random.normal(ks[3], (L, RET_VW), f32),
        "w_ret_up": nrm(ks[4], (L, RET_VW, D_MODEL), RET_VW),
        "cmp_pe_k": 0.02 * jax.random.normal(ks[5], (L, CMP_BLOCK, NSA_DH), f32),
        "cmp_w1_k": nrm(ks[6], (L, CMP_BLOCK * NSA_DH, CMP_HIDDEN), CMP_BLOCK * NSA_DH),
        "cmp_w2_k": nrm(ks[7], (L, CMP_HIDDEN, NSA_DH), CMP_HIDDEN),
        "cmp_pe_v": 0.02 * jax.random.normal(ks[8], (L, CMP_BLOCK, NSA_DH), f32),
        "cmp_w1_v": nrm(ks[9], (L, CMP_BLOCK * NSA_DH, CMP_HIDDEN), CMP_BLOCK * NSA_DH),
        "cmp_w2_v": nrm(ks[10], (L, CMP_HIDDEN, NSA_DH), CMP_HIDDEN),
        "w_nsa_up": nrm(ks[11], (L, NSA_QW, D_MODEL), NSA_QW),
        "w_out": nrm(ks[12], (L, D_MODEL, D_MODEL), D_MODEL),
        "norm2_w": 1.0 + 0.01 * jax.random.normal(ks[13], (L, D_MODEL), f32),
        "w_ffn_up": nrm(ks[14], (L, D_MODEL, 2 * D_FF), D_MODEL),
        "conv_w": nrm(ks[15], (L, CONV_W, 2 * D_FF), CONV_W),
        "conv_b": 0.01 * jax.random.normal(ks[16], (L, 2 * D_FF), f32),
        "w_ffn_down": nrm(ks[17], (L, D_FF, D_MODEL), D_FF),
        "final_norm_w": 1.0 + 0.01 * jax.random.normal(ks[18], (D_MODEL,), f32),
    }


def reference(x, norm1_w, w_in, ret_norm_w, w_ret_up, cmp_pe_k, cmp_w1_k, cmp_w2_k,
              cmp_pe_v, cmp_w1_v, cmp_w2_v, w_nsa_up, w_out, norm2_w, w_ffn_up,
              conv_w, conv_b, w_ffn_down, final_norm_w):
    B, S, _ = x.shape
    pos = jnp.arange(S, dtype=jnp.int32)

    def heads(t, n, d):
        return t.reshape(B, S, n, d).transpose(0, 2, 1, 3)

    h = x
    for l in range(DEPTH):
        xn = rmsnorm(h, norm1_w[l])
        proj = xn @ w_in[l]
        rq, rk, rv, rg, nq, nkv, ngate, mgate = jnp.split(proj, SPLIT_OFFSETS, axis=-1)
        rq = rotary(heads(rq, RET_HEADS, RET_DK), pos) * (RET_DK ** -0.5)
        rk = rotary(heads(rk, RET_HEADS, RET_DK), pos)
        ret = retention_chunkwise(rq, rk, heads(rv, RET_HEADS, RET_DV))
        ret = head_rmsnorm(ret).transpose(0, 2, 1, 3).reshape(B, S, RET_VW) * ret_norm_w[l]
        y_ret = (jax.nn.silu(rg) * ret) @ w_ret_up[l]
        kc, vc, ksl, vsl, kwn, vwn = [heads(t, NSA_GROUPS, NSA_DH) for t in jnp.split(nkv, 6, axis=-1)]
        br_gates = jax.nn.sigmoid(ngate).reshape(B, S, NSA_HEADS, 3).transpose(0, 2, 1, 3)
        nsa = native_sparse_attention(heads(nq, NSA_HEADS, NSA_DH), kc, vc, ksl, vsl, kwn, vwn, br_gates,
                                      cmp_pe_k[l], cmp_w1_k[l], cmp_w2_k[l],
                                      cmp_pe_v[l], cmp_w1_v[l], cmp_w2_v[l])
        y_nsa = nsa @ w_nsa_up[l]
        g_ret, g_nsa = jnp.split(jax.nn.sigmoid(mgate), 2, axis=-1)
        h = h + (g_ret * y_ret + g_nsa * y_nsa) @ w_out[l]
        xn = rmsnorm(h, norm2_w[l])
        u = causal_dwconv(xn @ w_ffn_up[l], conv_w[l], conv_b[l])
        a, b = jnp.split(u, 2, axis=-1)
        h = h + (jax.nn.silu(a) * b) @ w_ffn_down[l]
    return rmsnorm(h, final_norm_w)
```

```python
import math
import os
from contextlib import ExitStack
import numpy as np
import concourse.bass as bass
import concourse.mybir as mybir
from concourse.bass_utils import run_bass_kernel_spmd

F32 = mybir.dt.float32
BF16 = mybir.dt.bfloat16
AF = mybir.ActivationFunctionType
ALU = mybir.AluOpType
AX = mybir.AxisListType

D = 2048
LT = 32
QT0 = 15
NQ = LT - QT0
NQT = NQ * 128
IN_COLS = 13872
DFF = 5632
EPS = 1e-6
NEG = -30000.0
SCALE = 128 ** -0.5


class Tok:
    __slots__ = ("sem", "val")

    def __init__(self, sem, val):
        self.sem = sem
        self.val = val


class Res:
    __slots__ = ("w", "r", "excl")

    def __init__(self, excl=False):
        self.w = None
        self.r = []
        self.excl = excl


class DSem:
    __slots__ = ("sem", "val", "eng")

    def __init__(self, sem):
        self.sem = sem
        self.val = 0
        self.eng = None


ENGS = ("pe", "act", "dve", "pool", "sp")


class Sched:
    def __init__(self, nc, stack, tag):
        self.nc = nc
        self.q = {e: [] for e in ENGS}
        self.cnt = {e: 0 for e in ENGS}
        self.allsems = []
        self.sem = {e: self._alloc(f"{tag}_s_{e}") for e in ENGS}
        self.waited = {e: {} for e in ENGS}
        self.dsems = []
        self.tag = tag
        stack.callback(self._cleanup)

    def _alloc(self, name):
        h = self.nc.alloc_semaphore(name=name)
        self.allsems.append(h)
        return h

    def _cleanup(self):
        self.nc.clear_and_free_semaphores(self.allsems)
        self.nc.all_engine_barrier()

    def dsem(self):
        d = DSem(self._alloc(f"{self.tag}_d{len(self.dsems)}"))
        self.dsems.append(d)
        return d

    def _deps(self, eng, reads, writes):
        need = {}

        def add(t):
            k = id(t.sem)
            if k not in need or need[k].val < t.val:
                need[k] = t

        for r in reads:
            if r.w is not None:
                add(r.w)
        for w in writes:
            if w.w is not None:
                add(w.w)
            for t in w.r:
                add(t)
        waits = []
        wd = self.waited[eng]
        own = id(self.sem[eng])
        for k, t in need.items():
            if eng == "pe" and k == own:
                continue
            if wd.get(k, 0) < t.val:
                wd[k] = t.val
                waits.append((t.sem, t.val))
        return waits

    def _fin(self, tok, reads, writes):
        for r in reads:
            r.r.append(tok)
        for w in writes:
            w.w = tok
            w.r = []
        return tok

    def op(self, eng, fn, reads=(), writes=()):
        ex = [r for r in reads if r.excl]
        if ex:
            reads = [r for r in reads if not r.excl]
            writes = list(writes) + ex
        waits = self._deps(eng, reads, writes)
        self.cnt[eng] += 1
        self.q[eng].append((waits, fn, (self.sem[eng], 1)))
        return self._fin(Tok(self.sem[eng], self.cnt[eng]), reads, writes)

    def dma(self, eng, fn, ds, reads=(), writes=()):
        assert ds.eng in (None, eng), "one issuing engine per DMA semaphore"
        ds.eng = eng
        waits = self._deps(eng, reads, writes)
        ds.val += 16
        self.q[eng].append((waits, fn, (ds.sem, 16)))
        return self._fin(Tok(ds.sem, ds.val), reads, writes)

    def emit(self, block):
        finals = [(d.sem, d.val) for d in self.dsems if d.val > 0]

        def run(engobj, name, tail=False):
            for waits, fn, inc in self.q[name]:
                for s, v in waits:
                    engobj.wait_ge(s, v)
                fn(engobj).then_inc(inc[0], inc[1])
            if tail:
                for s, v in finals:
                    engobj.wait_ge(s, v)
                for e in ENGS:
                    if e != name and self.cnt[e] > 0:
                        engobj.wait_ge(self.sem[e], self.cnt[e])

        @block.sync
        def _(eng):
            run(eng, "sp", tail=True)

        @block.tensor
        def _(eng):
            run(eng, "pe")

        @block.scalar
        def _(eng):
            run(eng, "act")

        @block.vector
        def _(eng):
            run(eng, "dve")

        @block.gpsimd
        def _(eng):
            run(eng, "pool")


class Ring:
    def __init__(self, bufs, S=None, with_dsem=False, excl=False):
        self.bufs = bufs
        self.res = [Res(excl) for _ in bufs]
        self.ds = [S.dsem() for _ in bufs] if with_dsem else None
        self.i = -1

    def next(self):
        self.i = (self.i + 1) % len(self.bufs)
        if self.ds is not None:
            return self.bufs[self.i], self.res[self.i], self.ds[self.i]
        return self.bufs[self.i], self.res[self.i]


ALL_PHASES = (1, 2, 3, 4, 5)


def build_program(debug=(), phases=ALL_PHASES):
    nc = bass.Bass("TRN2", target_bir_lowering=False)

    def din(name, shape, dt=F32):
        return nc.dram_tensor(name, list(shape), dt, kind="ExternalInput").ap()

    def dscr(name, shape, dt=BF16):
        kind = "ExternalOutput" if name in debug else "Internal"
        return nc.dram_tensor(name, list(shape), dt, kind=kind).ap()

    x_loc = din("x_loc", [LT * 128, D])
    norm1_w = din("norm1_w", [1, D])
    w_in = din("w_in", [D, IN_COLS])
    ret_norm_w = din("ret_norm_w", [1, D])
    w_ret_up = din("w_ret_up", [D, D])
    cmp_peT = {"k": din("cmp_peT_k", [128, 32]), "v": din("cmp_peT_v", [128, 32])}
    cmp_w1 = {"k": din("cmp_w1_k", [4096, 256]), "v": din("cmp_w1_v", [4096, 256])}
    cmp_w2 = {"k": din("cmp_w2_k", [256, 128]), "v": din("cmp_w2_v", [256, 128])}
    w_nsa_up = din("w_nsa_up", [D, D])
    w_out = din("w_out", [D, D])
    norm2_w = din("norm2_w", [1, D])
    w_ffn_up = din("w_ffn_up", [D, 2 * DFF])
    conv_wT = din("conv_wT", [128, 88, 3])
    conv_bT = din("conv_bT", [128, 88])
    w_ffn_down = din("w_ffn_down", [DFF, D])
    final_norm_w = din("final_norm_w", [1, D])
    t_cs = din("t_cs", [128, LT, 128])
    t_csq = din("t_csq", [128, NQ, 128])
    t_decT = din("t_decT", [128, 8, 128])
    t_qdec = din("t_qdec", [128, 8, 128])
    t_kdec = din("t_kdec", [128, 8])
    t_padb = din("t_padb", [128, LT])
    t_cmpb = din("t_cmpb", [128, 2])
    t_cm = din("t_cm", [512, 128])
    t_keep = din("t_keep", [NQ, 128, 64])
    t_add = din("t_add", [NQ, 128, 64])
    t_exp = din("t_exp", [64, LT, 128])
    t_caus = din("t_caus", [128, 128])
    t_acaus = din("t_acaus", [128, 128])
    t_ident = din("t_ident", [128, 128])
    t_ov = din("t_ov", [128, 2, 65])
    t_halo = din("t_halo", [128, 1])

    out = nc.dram_tensor("out", [16 * 128, D], F32, kind="ExternalOutput").ap()

    RK = dscr("RK", [LT * 128, 1024])
    RV = dscr("RV", [LT * 128, 2048])
    RQ = dscr("RQ", [NQT, 1024])
    RG = dscr("RG", [NQT, 2048])
    NQT_ = dscr("NQT", [16, 128, NQT])
    KCT = dscr("KCT", [2, 128, LT * 128])
    VCT = dscr("VCT", [2, 128, LT * 128])
    KST = dscr("KST", [2, 128, LT * 128])
    KWT = dscr("KWT", [2, 128, LT * 128])
    VS = dscr("VS", [LT * 128, 256])
    VW = dscr("VW", [LT * 128, 256])
    NGT = dscr("NGT", [48, NQT], F32)
    MGT = dscr("MGT", [32, 128, NQT])
    RETGT = dscr("RETGT", [16, 128, NQT])
    NSAT = dscr("NSAT", [16, 128, NQT])
    M1T = dscr("M1T", [16, 128, NQT])
    MRGT = dscr("MRGT", [16, 128, NQT])
    H1 = dscr("H1", [NQT, D], F32)
    XN2T = dscr("XN2T", [16, 128, NQT])
    ACTT = dscr("ACTT", [44, 128, 2048])
    H2 = dscr("H2", [2048, D], F32)

    w_in_v = w_in.rearrange("(c p) n -> p c n", p=128)

    def phase01():
        with ExitStack() as st01:
            xnT = st01.enter_context(nc.sbuf_tensor("xnT", [128, 16, LT * 128], BF16))
            identb = st01.enter_context(nc.sbuf_tensor("identb", [128, 128], BF16))

            with ExitStack() as st:
                xin = [st.enter_context(nc.sbuf_tensor(f"p0_x{i}", [128, D], F32)) for i in range(2)]
                sq = st.enter_context(nc.sbuf_tensor("p0_sq", [128, D], BF16))
                xnb = [st.enter_context(nc.sbuf_tensor(f"p0_xn{i}", [128, D], BF16)) for i in range(2)]
                wbc = st.enter_context(nc.sbuf_tensor("p0_wbc", [128, D], F32))
                ss = [st.enter_context(nc.sbuf_tensor(f"p0_ss{i}", [128, 1], F32)) for i in range(2)]
                rs = [st.enter_context(nc.sbuf_tensor(f"p0_rs{i}", [128, 1], F32)) for i in range(2)]
                pt = [st.enter_context(nc.psum_tensor(f"p0_pt{i}", [128, D], BF16)) for i in range(2)]
                S = Sched(nc, st, "p0")
                block = st.enter_context(nc.Block())
                r_c = Res()
                d_c = S.dsem()
                S.dma("sp", lambda e: e.dma_start(out=wbc[:], in_=norm1_w.partition_broadcast(128)), d_c, writes=[r_c])
                r_id = Res()
                S.dma("pool", lambda e: e.dma_start(out=identb[:], in_=t_ident), S.dsem(), writes=[r_id])
                Rx = Ring(xin, S, True)
                Rxn = Ring(xnb)
                Rss = Ring(ss)
                Rrs = Ring(rs)
                Rpt = Ring(pt, excl=True)
                r_sq = Res()
                r_xnT = Res()
                for t in range(LT):
                    xb, rx, dx = Rx.next()
                    S.dma("sp", lambda e, xb=xb, t=t: e.dma_start(out=xb[:], in_=x_loc[t * 128:(t + 1) * 128, :]),
                          dx, writes=[rx])
                    sb, rss = Rss.next()
                    S.op("act", lambda e, xb=xb, sb=sb: e.activation(out=sq[:], in_=xb[:], func=AF.Square,
                                                                      accum_out=sb[:]),
                         reads=[rx], writes=[r_sq, rss])
                    rb, rrs = Rrs.next()
                    S.op("act", lambda e, sb=sb, rb=rb: e.activation(out=rb[:], in_=sb[:], func=AF.Sqrt,
                                                                     scale=1.0 / D, bias=EPS),
                         reads=[rss], writes=[rrs])
                    S.op("dve", lambda e, rb=rb: e.reciprocal(out=rb[:], in_=rb[:]),
                         reads=[rrs], writes=[rrs])
                    xn, rxn = Rxn.next()
                    S.op("dve", lambda e, xn=xn, xb=xb, rb=rb: e.scalar_tensor_tensor(
                        out=xn[:], in0=xb[:], scalar=rb[:], in1=wbc[:], op0=ALU.mult, op1=ALU.mult),
                        reads=[rx, rrs, r_c], writes=[rxn])
                    pb, rp = Rpt.next()
                    for c in range(16):
                        S.op("pe", lambda e, pb=pb, xn=xn, c=c: e.transpose(
                            out=pb[:, c * 128:(c + 1) * 128], in_=xn[:, c * 128:(c + 1) * 128], identity=identb[:]),
                            reads=[rxn, r_id], writes=[rp])
                    for hh in range(2):
                        eng = "act" if hh == 0 else "dve"
                        if eng == "act":
                            S.op("act", lambda e, pb=pb, t=t, hh=hh: e.activation(
                                out=xnT[:, hh * 8:(hh + 1) * 8, t * 128:(t + 1) * 128],
                                in_=pb[:, hh * 1024:(hh + 1) * 1024].rearrange("p (c q) -> p c q", c=8),
                                func=AF.Copy), reads=[rp], writes=[r_xnT])
                        else:
                            S.op("dve", lambda e, pb=pb, t=t, hh=hh: e.tensor_copy(
                                out=xnT[:, hh * 8:(hh + 1) * 8, t * 128:(t + 1) * 128],
                                in_=pb[:, hh * 1024:(hh + 1) * 1024].rearrange("p (c q) -> p c q", c=8)),
                                reads=[rp], writes=[r_xnT])
                S.emit(block)

            if 0 in phases and 1 not in phases:
                return

            with ExitStack() as st:
                wb = [st.enter_context(nc.sbuf_tensor(f"p1_w{i}", [128, 16, 512], BF16)) for i in range(2)]
                cs = st.enter_context(nc.sbuf_tensor("p1_cs", [128, LT, 128], F32))
                csq = st.enter_context(nc.sbuf_tensor("p1_csq", [128, NQ, 128], F32))
                xf = [st.enter_context(nc.sbuf_tensor(f"p1_xf{i}", [128, 512], F32)) for i in range(2)]
                tmp = [st.enter_context(nc.sbuf_tensor(f"p1_t{i}", [128, 4, 256], F32)) for i in range(2)]
                ob = [st.enter_context(nc.sbuf_tensor(f"p1_o{i}", [128, 512], BF16)) for i in range(3)]
                of = [st.enter_context(nc.sbuf_tensor(f"p1_of{i}", [48, 512], F32)) for i in range(2)]
                ps = [st.enter_context(nc.psum_tensor(f"p1_ps{i}", [128, 512], F32)) for i in range(4)]
                S = Sched(nc, st, "p1")
                block = st.enter_context(nc.Block())
                r_tab = Res()
                d_tab = S.dsem()
                S.dma("sp", lambda e: e.dma_start(out=cs[:], in_=t_cs), d_tab, writes=[r_tab])
                S.dma("sp", lambda e: e.dma_start(out=csq[:], in_=t_csq), d_tab, writes=[r_tab])
                Rw = Ring(wb, S, True)
                Rps = Ring(ps, excl=True)
                Rxf = Ring(xf)
                Rtmp = Ring(tmp)
                Rob = Ring(ob, S, True)
                Rof = Ring(of, S, True)
                r_x = Res()
                alt = [0]

                def load_w(col0, ncols):
                    w, rw, dw = Rw.next()
                    S.dma("pool", lambda e, w=w: e.dma_start(out=w[:, :, 0:ncols], in_=w_in_v[:, :, col0:col0 + ncols]),
                          dw, writes=[rw])
                    return w, rw

                def mm_tm(w, rw, t, c0, n):
                    p, rp = Rps.next()
                    for c in range(16):
                        S.op("pe", lambda e, p=p, c=c: e.matmul(
                            p[:, 0:n], lhsT=xnT[:, c, t * 128:(t + 1) * 128], rhs=w[:, c, c0:c0 + n],
                            start=(c == 0), stop=(c == 15)), reads=[rw, r_x], writes=[rp])
                    return p, rp

                def mm_fm(w, rw, tok0, ntok, c0, m):
                    p, rp = Rps.next()
                    for c in range(16):
                        S.op("pe", lambda e, p=p, c=c: e.matmul(
                            p[0:m, 0:ntok], lhsT=w[:, c, c0:c0 + m], rhs=xnT[:, c, tok0:tok0 + ntok],
                            start=(c == 0), stop=(c == 15)), reads=[rw, r_x], writes=[rp])
                    return p, rp

                def evac_store(p, rp, m, n, dst, func=None):
                    o, ro, do = Rob.next()
                    if func is not None:
                        S.op("act", lambda e: e.activation(out=o[0:m, 0:n], in_=p[0:m, 0:n], func=func),
                             reads=[rp], writes=[ro])
                    else:
                        alt[0] ^= 1
                        if alt[0]:
                            S.op("act", lambda e: e.activation(out=o[0:m, 0:n], in_=p[0:m, 0:n], func=AF.Copy),
                                 reads=[rp], writes=[ro])
                        else:
                            S.op("dve", lambda e: e.tensor_copy(out=o[0:m, 0:n], in_=p[0:m, 0:n]),
                                 reads=[rp], writes=[ro])
                    S.dma("sp", lambda e: e.dma_start(out=dst, in_=o[0:m, 0:n]), do, reads=[ro])

                def rotary(p, rp, ctab, ti, dst):
                    x_, rxf = Rxf.next()
                    S.op("act", lambda e: e.activation(out=x_[:], in_=p[:], func=AF.Copy), reads=[rp], writes=[rxf])
                    xv = x_[:].rearrange("p (h t d) -> p h t d", h=4, t=2)
                    cosb = ctab[:, ti, 0:64][:, None, :].to_broadcast([128, 4, 64])
                    sinb = ctab[:, ti, 64:128][:, None, :].to_broadcast([128, 4, 64])
                    tm_, rt = Rtmp.next()
                    o, ro, do = Rob.next()
                    ov = o[:].rearrange("p (h t d) -> p h t d", h=4, t=2)
                    x1 = xv[:, :, 0, :]
                    x2 = xv[:, :, 1, :]
                    S.op("pool", lambda e: e.tensor_tensor(out=tm_[:, :, 0:64], in0=x1, in1=cosb, op=ALU.mult),
                         reads=[rxf, r_tab], writes=[rt])
                    S.op("pool", lambda e: e.tensor_tensor(out=tm_[:, :, 64:128], in0=x2, in1=sinb, op=ALU.mult),
                         reads=[rxf, r_tab], writes=[rt])
                    S.op("dve", lambda e: e.tensor_tensor(out=tm_[:, :, 128:192], in0=x1, in1=sinb, op=ALU.mult),
                         reads=[rxf, r_tab], writes=[rt])
                    S.op("dve", lambda e: e.tensor_tensor(out=tm_[:, :, 192:256], in0=x2, in1=cosb, op=ALU.mult),
                         reads=[rxf, r_tab], writes=[rt])
                    S.op("pool", lambda e: e.tensor_tensor(out=ov[:, :, 0, :], in0=tm_[:, :, 0:64],
                                                           in1=tm_[:, :, 64:128], op=ALU.subtract),
                         reads=[rt], writes=[ro])
                    S.op("dve", lambda e: e.tensor_tensor(out=ov[:, :, 1, :], in0=tm_[:, :, 128:192],
                                                          in1=tm_[:, :, 192:256], op=ALU.add),
                         reads=[rt], writes=[ro])
                    S.dma("sp", lambda e: e.dma_start(out=dst, in_=o[:]), do, reads=[ro])

                qtiles = list(range(QT0, LT))
                qgroups = [(QT0 * 128, 128)] + [((16 + 4 * g) * 128, 512) for g in range(4)]
                agroups = [(g * 512, 512) for g in range(8)]

                for blk in range(2):
                    w, rw = load_w(1024 + blk * 512, 512)
                    for t in range(LT):
                        p, rp = mm_tm(w, rw, t, 0, 512)
                        rotary(p, rp, cs, t, RK[t * 128:(t + 1) * 128, blk * 512:(blk + 1) * 512])
                for blk in range(2):
                    w, rw = load_w(blk * 512, 512)
                    for t in qtiles:
                        p, rp = mm_tm(w, rw, t, 0, 512)
                        qi = t - QT0
                        rotary(p, rp, csq, qi, RQ[qi * 128:(qi + 1) * 128, blk * 512:(blk + 1) * 512])
                for blk in range(4):
                    w, rw = load_w(2048 + blk * 512, 512)
                    for t in range(LT):
                        p, rp = mm_tm(w, rw, t, 0, 512)
                        evac_store(p, rp, 128, 512, RV[t * 128:(t + 1) * 128, blk * 512:(blk + 1) * 512])
                w, rw = load_w(8192, 512)
                for (dst, c0) in ((KCT, 0), (VCT, 256)):
                    for g in range(2):
                        for tok0, ntok in agroups:
                            p, rp = mm_fm(w, rw, tok0, ntok, c0 + g * 128, 128)
                            evac_store(p, rp, 128, ntok, dst[g, :, tok0:tok0 + ntok])
                for (col0, dfm, dtm) in ((8704, KST, VS), (9216, KWT, VW)):
                    w, rw = load_w(col0, 512)
                    for g in range(2):
                        for tok0, ntok in agroups:
                            p, rp = mm_fm(w, rw, tok0, ntok, g * 128, 128)
                            evac_store(p, rp, 128, ntok, dfm[g, :, tok0:tok0 + ntok])
                    for t in range(LT):
                        p, rp = mm_tm(w, rw, t, 256, 256)
                        evac_store(p, rp, 128, 256, dtm[t * 128:(t + 1) * 128, :])
                for blk in range(4):
                    w, rw = load_w(6144 + blk * 512, 512)
                    for ch in range(4):
                        for tok0, ntok in qgroups:
                            p, rp = mm_fm(w, rw, tok0, ntok, ch * 128, 128)
                            q0 = tok0 - QT0 * 128
                            evac_store(p, rp, 128, ntok, NQT_[blk * 4 + ch, :, q0:q0 + ntok])
                for blk in range(4):
                    w, rw = load_w(4096 + blk * 512, 512)
                    for t in qtiles:
                        p, rp = mm_tm(w, rw, t, 0, 512)
                        qi = t - QT0
                        evac_store(p, rp, 128, 512, RG[qi * 128:(qi + 1) * 128, blk * 512:(blk + 1) * 512],
                                   func=AF.Silu)
                for blk in range(8):
                    w, rw = load_w(9776 + blk * 512, 512)
                    for ch in range(4):
                        for tok0, ntok in qgroups:
                            p, rp = mm_fm(w, rw, tok0, ntok, ch * 128, 128)
                            q0 = tok0 - QT0 * 128
                            evac_store(p, rp, 128, ntok, MGT[blk * 4 + ch, :, q0:q0 + ntok], func=AF.Sigmoid)
                w, rw = load_w(9728, 48)
                for tok0, ntok in qgroups:
                    p, rp = mm_fm(w, rw, tok0, ntok, 0, 48)
                    q0 = tok0 - QT0 * 128
                    o, ro, do = Rof.next()
                    S.op("act", lambda e, o=o, p=p, ntok=ntok: e.activation(out=o[0:48, 0:ntok], in_=p[0:48, 0:ntok],
                                                                           func=AF.Sigmoid), reads=[rp], writes=[ro])
                    S.dma("sp", lambda e, o=o, q0=q0, ntok=ntok: e.dma_start(out=NGT[:, q0:q0 + ntok],
                                                                            in_=o[0:48, 0:ntok]), do, reads=[ro])
                S.emit(block)

    if 1 in phases:
        phase01()

    def phase2():
        gam = [1.0 - 2.0 ** (-5.0 - h) for h in range(8)]
        with ExitStack() as st:
            sb = lambda name, shape, dt: st.enter_context(nc.sbuf_tensor(name, list(shape), dt))
            identb = sb("p2_id", [128, 128], BF16)
            decT = sb("p2_decT", [128, 8, 128], F32)
            qdec = sb("p2_qdec", [128, 8, 128], F32)
            kdec = sb("p2_kdec", [128, 8], F32)
            retw = sb("p2_retw", [128, D], F32)
            kc_ = [sb(f"p2_k{i}", [128, 1024], BF16) for i in range(2)]
            vc_ = [sb(f"p2_v{i}", [128, 2048], BF16) for i in range(2)]
            qc_ = [sb(f"p2_q{i}", [128, 1024], BF16) for i in range(2)]
            gc_ = [sb(f"p2_g{i}", [128, 2048], BF16) for i in range(2)]
            kd_ = [sb(f"p2_kd{i}", [128, 1024], BF16) for i in range(2)]
            kT_ = [sb(f"p2_kT{i}", [128, 1024], BF16) for i in range(2)]
            qT_ = [sb(f"p2_qT{i}", [128, 1024], BF16) for i in range(2)]
            qTd_ = [sb(f"p2_qTd{i}", [128, 1024], BF16) for i in range(2)]
            ST_ = [sb(f"p2_ST{i}", [128, 512], BF16) for i in range(2)]
            stf = sb("p2_stf", [128, 2048], F32)
            stb = [sb(f"p2_stb{i}", [128, 2048], BF16) for i in range(2)]
            junk = sb("p2_junk", [128, 256], BF16)
            ssq_ = [sb(f"p2_ssq{i}", [128, 4], F32) for i in range(2)]
            tmpf_ = [sb(f"p2_tmpf{i}", [128, 1024], F32) for i in range(2)]
            gat_ = [sb(f"p2_gat{i}", [128, 2048], BF16) for i in range(2)]
            gT_ = [sb(f"p2_gT{i}", [128, 16, 128], BF16) for i in range(2)]
            pkT = st.enter_context(nc.psum_tensor("p2_pkT", [128, 1024], BF16))
            pqT = st.enter_context(nc.psum_tensor("p2_pqT", [128, 1024], BF16))
            pS = st.enter_context(nc.psum_tensor("p2_pS", [128, 512], F32))
            pO = st.enter_context(nc.psum_tensor("p2_pO", [128, 1024], F32))
            pKV = st.enter_context(nc.psum_tensor("p2_pKV", [128, 1024], F32))
            pGT = st.enter_context(nc.psum_tensor("p2_pGT", [128, 1024], BF16))
            S = Sched(nc, st, "p2")
            block = st.enter_context(nc.Block())
            r_tab = Res()
            d_tab = S.dsem()
            r_id = Res()
            S.dma("pool", lambda e: e.dma_start(out=identb[:], in_=t_ident), S.dsem(), writes=[r_id])
            S.dma("sp", lambda e: e.dma_start(out=decT[:], in_=t_decT), d_tab, writes=[r_tab])
            S.dma("sp", lambda e: e.dma_start(out=qdec[:], in_=t_qdec), d_tab, writes=[r_tab])
            S.dma("sp", lambda e: e.dma_start(out=kdec[:], in_=t_kdec), d_tab, writes=[r_tab])
            S.dma("sp", lambda e: e.dma_start(out=retw[:], in_=ret_norm_w.partition_broadcast(128)), d_tab,
                  writes=[r_tab])
            Rk, Rv, Rq, Rg = Ring(kc_, S, True), Ring(vc_, S, True), Ring(qc_, S, True), Ring(gc_, S, True)
            Rkd, RkT, RqT, RqTd, RST = Ring(kd_), Ring(kT_), Ring(qT_), Ring(qTd_), Ring(ST_)
            Rssq, Rtmpf, Rgat = Ring(ssq_), Ring(tmpf_), Ring(gat_)
            RgT = Ring(gT_, S, True)
            r_stf = Res()
            r_stb = [Res(), Res()]
            r_junk = Res()
            r_pkT, r_pqT, r_pS, r_pO, r_pKV, r_pGT = (Res(True) for _ in range(6))
            S.op("dve", lambda e: e.memset(stf[:], 0.0), writes=[r_stf])
            S.op("pool", lambda e: e.memset(stb[0][:], 0.0), writes=[r_stb[0]])
            for c in range(LT):
                k_, rk, dk = Rk.next()
                v_, rv, dv = Rv.next()
                S.dma("sp", lambda e, k_=k_, c=c: e.dma_start(out=k_[:], in_=RK[c * 128:(c + 1) * 128, :]), dk, writes=[rk])
                S.dma("sp", lambda e, v_=v_, c=c: e.dma_start(out=v_[:], in_=RV[c * 128:(c + 1) * 128, :]), dv, writes=[rv])
                sbc, rsb = stb[c % 2], r_stb[c % 2]
                sbn, rsn = stb[(c + 1) % 2], r_stb[(c + 1) % 2]
                if c >= QT0 and os.environ.get('P2_OUT', '1') == '1':
                    qi = c - QT0
                    q_, rq, dq = Rq.next()
                    g_, rg, dg = Rg.next()
                    S.dma("sp", lambda e, q_=q_, qi=qi: e.dma_start(out=q_[:], in_=RQ[qi * 128:(qi + 1) * 128, :]), dq, writes=[rq])
                    S.dma("sp", lambda e, g_=g_, qi=qi: e.dma_start(out=g_[:], in_=RG[qi * 128:(qi + 1) * 128, :]), dg, writes=[rg])
                    for h in range(8):
                        S.op("pe", lambda e, k_=k_, h=h: e.transpose(out=pkT[:, h * 128:(h + 1) * 128],
                                                                     in_=k_[:, h * 128:(h + 1) * 128], identity=identb[:]),
                             reads=[rk, r_id], writes=[r_pkT])
                    kT, rkT = RkT.next()
                    S.op("act", lambda e, kT=kT: e.activation(out=kT[:], in_=pkT[:], func=AF.Copy), reads=[r_pkT], writes=[rkT])
                    for h in range(8):
                        S.op("pe", lambda e, q_=q_, h=h: e.transpose(out=pqT[:, h * 128:(h + 1) * 128],
                                                                     in_=q_[:, h * 128:(h + 1) * 128], identity=identb[:]),
                             reads=[rq, r_id], writes=[r_pqT])
                    qT, rqT = RqT.next()
                    qTd, rqTd = RqTd.next()
                    S.op("act", lambda e, qT=qT: e.activation(out=qT[:], in_=pqT[:], func=AF.Copy), reads=[r_pqT], writes=[rqT])
                    S.op("dve", lambda e, qTd=qTd: e.tensor_tensor(
                        out=qTd[:].rearrange("p (h n) -> p h n", h=8), in0=pqT[:].rearrange("p (h n) -> p h n", h=8),
                        in1=qdec[:], op=ALU.mult), reads=[r_pqT, r_tab], writes=[rqTd])
                    gat, rgat = Rgat.next()
                    lvl = int(os.environ.get('P2_LVL', '9'))
                    for hg in (range(2) if lvl >= 2 else []):
                        for j in range(4):
                            h = hg * 4 + j
                            S.op("pe", lambda e, kT=kT, qT=qT, h=h, j=j: e.matmul(
                                pS[:, j * 128:(j + 1) * 128], lhsT=kT[:, h * 128:(h + 1) * 128],
                                rhs=qT[:, h * 128:(h + 1) * 128], start=True, stop=True),
                                reads=[rkT, rqT], writes=[r_pS])
                        ST, rST = RST.next()
                        S.op("dve", lambda e, ST=ST, hg=hg: e.tensor_tensor(
                            out=ST[:].rearrange("p (h n) -> p h n", h=4), in0=pS[:].rearrange("p (h n) -> p h n", h=4),
                            in1=decT[:, hg * 4:(hg + 1) * 4, :], op=ALU.mult), reads=[r_pS, r_tab], writes=[rST])
                        for j in range(4):
                            h = hg * 4 + j
                            S.op("pe", lambda e, ST=ST, v_=v_, h=h, j=j: e.matmul(
                                pO[:, j * 256:(j + 1) * 256], lhsT=ST[:, j * 128:(j + 1) * 128],
                                rhs=v_[:, h * 256:(h + 1) * 256], start=True, stop=False),
                                reads=[rST, rv], writes=[r_pO])
                            S.op("pe", lambda e, qTd=qTd, sbc=sbc, h=h, j=j: e.matmul(
                                pO[:, j * 256:(j + 1) * 256], lhsT=qTd[:, h * 128:(h + 1) * 128],
                                rhs=sbc[:, h * 256:(h + 1) * 256], start=False, stop=True),
                                reads=[rqTd, rsb], writes=[r_pO])
                        if lvl < 3:
                            continue
                        ssq, rssq = Rssq.next()
                        for j in range(4):
                            S.op("act", lambda e, ssq=ssq, j=j: e.activation(
                                out=junk[:], in_=pO[:, j * 256:(j + 1) * 256], func=AF.Square, accum_out=ssq[:, j:j + 1]),
                                reads=[r_pO], writes=[r_junk, rssq])
                        S.op("act", lambda e, ssq=ssq: e.activation(out=ssq[:], in_=ssq[:], func=AF.Sqrt,
                                                                    scale=1.0 / 256, bias=EPS), reads=[rssq], writes=[rssq])
                        S.op("dve", lambda e, ssq=ssq: e.reciprocal(out=ssq[:], in_=ssq[:]), reads=[rssq], writes=[rssq])
                        tmpf, rtf = Rtmpf.next()
                        for j in range(4):
                            h = hg * 4 + j
                            S.op("dve", lambda e, tmpf=tmpf, ssq=ssq, h=h, j=j: e.scalar_tensor_tensor(
                                out=tmpf[:, j * 256:(j + 1) * 256], in0=pO[:, j * 256:(j + 1) * 256],
                                scalar=ssq[:, j:j + 1], in1=retw[:, h * 256:(h + 1) * 256], op0=ALU.mult, op1=ALU.mult),
                                reads=[r_pO, rssq, r_tab], writes=[rtf])
                        if lvl < 4:
                            continue
                        S.op("pool", lambda e, gat=gat, tmpf=tmpf, g_=g_, hg=hg: e.tensor_tensor(
                            out=gat[:, hg * 1024:(hg + 1) * 1024], in0=tmpf[:], in1=g_[:, hg * 1024:(hg + 1) * 1024],
                            op=ALU.mult), reads=[rtf, rg], writes=[rgat])
                    if lvl < 5:
                        continue
                    gT, rgT, dgT = RgT.next()
                    for hh in range(2):
                        for cc in range(8):
                            ch = hh * 8 + cc
                            S.op("pe", lambda e, gat=gat, ch=ch, cc=cc: e.transpose(
                                out=pGT[:, cc * 128:(cc + 1) * 128], in_=gat[:, ch * 128:(ch + 1) * 128],
                                identity=identb[:]), reads=[rgat, r_id], writes=[r_pGT])
                        if hh == 0:
                            S.op("act", lambda e, gT=gT: e.activation(
                                out=gT[:, 0:8, :], in_=pGT[:].rearrange("p (c q) -> p c q", c=8), func=AF.Copy),
                                reads=[r_pGT], writes=[rgT])
                        else:
                            S.op("dve", lambda e, gT=gT: e.tensor_copy(
                                out=gT[:, 8:16, :], in_=pGT[:].rearrange("p (c q) -> p c q", c=8)),
                                reads=[r_pGT], writes=[rgT])
                    S.dma("sp", lambda e, gT=gT, qi=qi: e.dma_start(
                        out=RETGT[:, :, qi * 128:(qi + 1) * 128].rearrange("c p q -> p c q"), in_=gT[:]),
                        dgT, reads=[rgT])
                if c < LT - 1 and os.environ.get('P2_STATE', '1') == '1':
                    kd, rkd = Rkd.next()
                    S.op("pool", lambda e, kd=kd, k_=k_: e.tensor_tensor(
                        out=kd[:].rearrange("p (h d) -> p h d", h=8), in0=k_[:].rearrange("p (h d) -> p h d", h=8),
                        in1=kdec[:, :, None].to_broadcast([128, 8, 128]), op=ALU.mult),
                        reads=[rk, r_tab], writes=[rkd])
                    for hg in range(2):
                        for j in range(4):
                            h = hg * 4 + j
                            S.op("pe", lambda e, kd=kd, v_=v_, h=h, j=j: e.matmul(
                                pKV[:, j * 256:(j + 1) * 256], lhsT=kd[:, h * 128:(h + 1) * 128],
                                rhs=v_[:, h * 256:(h + 1) * 256], start=True, stop=True),
                                reads=[rkd, rv], writes=[r_pKV])
                        for j in range(4):
                            h = hg * 4 + j
                            S.op("dve", lambda e, h=h, j=j: e.scalar_tensor_tensor(
                                out=stf[:, h * 256:(h + 1) * 256], in0=stf[:, h * 256:(h + 1) * 256],
                                scalar=float(gam[h] ** 128), in1=pKV[:, j * 256:(j + 1) * 256],
                                op0=ALU.mult, op1=ALU.add), reads=[r_pKV, r_stf], writes=[r_stf])
                    S.op("act", lambda e, sbn=sbn: e.activation(out=sbn[:], in_=stf[:], func=AF.Copy),
                         reads=[r_stf], writes=[rsn])
            S.emit(block)

    if 2 in phases:
        phase2()

    KCMPT = dscr("KCMPT", [2, 128, 256])
    VCMP = dscr("VCMP", [2, 256, 128])

    def phase3a():
        with ExitStack() as st:
            sb = lambda name, shape, dt: st.enter_context(nc.sbuf_tensor(name, list(shape), dt))
            xc_ = [sb(f"p3a_xc{i}", [128, LT * 128], BF16) for i in range(2)]
            w1b = sb("p3a_w1", [128, 32, 256], BF16)
            peT = sb("p3a_pe", [128, 32], BF16)
            w2b = sb("p3a_w2", [128, 2, 128], BF16)
            cb = sb("p3a_cb", [128, 2], F32)
            gel_ = [sb(f"p3a_gel{i}", [128, 2, 256], BF16) for i in range(2)]
            og_ = [sb(f"p3a_og{i}", [128, 256], BF16) for i in range(2)]
            pc = st.enter_context(nc.psum_tensor("p3a_pc", [128, 2], F32))
            ph_ = [st.enter_context(nc.psum_tensor(f"p3a_ph{i}", [128, 256], F32)) for i in range(2)]
            po_ = [st.enter_context(nc.psum_tensor(f"p3a_po{i}", [128, 256], F32)) for i in range(2)]
            S = Sched(nc, st, "p3a")
            block = st.enter_context(nc.Block())
            Rxc = Ring(xc_, S, True)
            Rgel = Ring(gel_)
            Rog = Ring(og_, S, True)
            Rph = Ring(ph_, excl=True)
            Rpo = Ring(po_, excl=True)
            r_w, r_cb, r_pc = Res(), Res(), Res(True)
            d_w = S.dsem()
            for g_ in gel_:
                S.op("dve", lambda e, g_=g_: e.memset(g_[:], 0.0), writes=[Rgel.res[gel_.index(g_)]])
            for kv in ("k", "v"):
                S.dma("pool", lambda e, kv=kv: e.dma_start(
                    out=w1b[:], in_=cmp_w1[kv].rearrange("(l d) j -> d l j", d=128)), d_w, writes=[r_w])
                S.dma("pool", lambda e, kv=kv: e.dma_start(out=peT[:], in_=cmp_peT[kv]), d_w, writes=[r_w])
                S.dma("pool", lambda e, kv=kv: e.dma_start(
                    out=w2b[:], in_=cmp_w2[kv].rearrange("(c p) d -> p c d", p=128)), d_w, writes=[r_w])
                for jc in range(2):
                    for l in range(32):
                        S.op("pe", lambda e, jc=jc, l=l: e.matmul(
                            pc[:, jc:jc + 1], lhsT=w1b[:, l, jc * 128:(jc + 1) * 128], rhs=peT[:, l:l + 1],
                            start=(l == 0), stop=(l == 31)), reads=[r_w], writes=[r_pc])
                S.op("act", lambda e: e.activation(out=cb[:], in_=pc[:], func=AF.Copy), reads=[r_pc], writes=[r_cb])
                src = KCT if kv == "k" else VCT
                for g in range(2):
                    xc, rxc, dxc = Rxc.next()
                    S.dma("sp", lambda e, xc=xc, g=g, src=src: e.dma_start(out=xc[:], in_=src[g]), dxc, writes=[rxc])
                    gel, rgel = Rgel.next()
                    for jc in range(2):
                        ph, rph = Rph.next()
                        for l in range(32):
                            S.op("pe", lambda e, ph=ph, xc=xc, jc=jc, l=l: e.matmul(
                                ph[:, 0:255], lhsT=w1b[:, l, jc * 128:(jc + 1) * 128],
                                rhs=xc[:, l:l + 16 * 254 + 1:16], start=(l == 0), stop=(l == 31)),
                                reads=[r_w, rxc], writes=[rph])
                        S.op("act", lambda e, ph=ph, gel=gel, jc=jc: e.activation(
                            out=gel[:, jc, 0:255], in_=ph[:, 0:255], func=AF.Gelu_apprx_tanh, bias=cb[:, jc:jc + 1]),
                            reads=[rph, r_cb], writes=[rgel])
                    if kv == "k":
                        po, rpo = Rpo.next()
                        for jc in range(2):
                            S.op("pe", lambda e, po=po, gel=gel, jc=jc: e.matmul(
                                po[:, :], lhsT=w2b[:, jc, :], rhs=gel[:, jc, :], start=(jc == 0), stop=(jc == 1)),
                                reads=[r_w, rgel], writes=[rpo])
                        og, rog, dog = Rog.next()
                        S.op("dve", lambda e, og=og, po=po: e.tensor_copy(out=og[:], in_=po[:]), reads=[rpo], writes=[rog])
                        S.dma("sp", lambda e, og=og, g=g: e.dma_start(out=KCMPT[g], in_=og[:]), dog, reads=[rog])
                    else:
                        for nk in range(2):
                            po, rpo = Rpo.next()
                            for jc in range(2):
                                S.op("pe", lambda e, po=po, gel=gel, jc=jc, nk=nk: e.matmul(
                                    po[:, 0:128], lhsT=gel[:, jc, nk * 128:(nk + 1) * 128], rhs=w2b[:, jc, :],
                                    start=(jc == 0), stop=(jc == 1)), reads=[r_w, rgel], writes=[rpo])
                            og, rog, dog = Rog.next()
                            S.op("dve", lambda e, og=og, po=po: e.tensor_copy(out=og[:, 0:128], in_=po[:, 0:128]),
                                 reads=[rpo], writes=[rog])
                            S.dma("sp", lambda e, og=og, g=g, nk=nk: e.dma_start(
                                out=VCMP[g, nk * 128:(nk + 1) * 128, :], in_=og[:, 0:128]), dog, reads=[rog])
            S.emit(block)

    def phase3b():
        with ExitStack() as st:
            sb = lambda name, shape, dt: st.enter_context(nc.sbuf_tensor(name, list(shape), dt))
            identb = sb("p3_idb", [128, 128], BF16)
            identf = sb("p3_idf", [128, 128], F32)
            onesb = sb("p3_ones", [128, 128], BF16)
            causb = sb("p3_caus", [128, 4, 128], BF16)
            acausb = sb("p3_acaus", [128, 4, 128], BF16)
            expt = sb("p3_expt", [64, LT, 128], BF16)
            ovb = sb("p3_ov", [128, 2, 65], BF16)
            padb = sb("p3_padb", [128, LT], F32)
            cmpb = sb("p3_cmpb", [128, 2], F32)
            kcm = sb("p3_kcm", [128, 2, 256], BF16)
            vcm = sb("p3_vcm", [128, 2, 2, 128], BF16)
            kst = sb("p3_kst", [128, 2, LT * 128], BF16)
            kwt = sb("p3_kwt", [128, 2, LT * 128], BF16)
            vs = sb("p3_vs", [128, LT, 256], BF16)
            vw = sb("p3_vw", [128, LT, 256], BF16)
            qT_ = [sb(f"p3_qT{i}", [128, 16, 128], BF16) for i in range(2)]
            gbc_ = [sb(f"p3_gbc{i}", [128, 24, 128], F32) for i in range(2)]
            keep_ = [sb(f"p3_keep{i}", [128, 64], F32) for i in range(2)]
            addt_ = [sb(f"p3_add{i}", [128, 64], F32) for i in range(2)]
            cm_ = [sb(f"p3_cm{i}", [128, 2, 128], F32) for i in range(2)]
            Ec_ = [sb(f"p3_Ec{i}", [128, 1024], BF16) for i in range(2)]
            E_ = [sb(f"p3_E{i}", [128, 1024], BF16) for i in range(3)]
            Sm_ = [sb(f"p3_Sm{i}", [128, 1024], F32) for i in range(2)]
            fac_ = [sb(f"p3_fac{i}", [128, 1024], F32) for i in range(2)]
            tmp_ = [sb(f"p3_tmp{i}", [128, 1024], F32) for i in range(2)]
            acc_ = [sb(f"p3_acc{i}", [128, 1024], F32) for i in range(2)]
            accb_ = [sb(f"p3_accb{i}", [128, 1024], BF16) for i in range(2)]
            rs8 = sb("p3_rs8", [128, 8], F32)
            imp = sb("p3_imp", [128, 64], F32)
            imp2 = sb("p3_imp2", [128, 64], F32)
            m8 = sb("p3_m8", [128, 8], F32)
            selb = sb("p3_selb", [128, 64], F32)
            selbT_ = [sb(f"p3_selbT{i}", [64, 4, 128], BF16) for i in range(2)]
            pS_ = [st.enter_context(nc.psum_tensor(f"p3_pS{i}", [128, 1024], F32)) for i in range(2)]
            pO = st.enter_context(nc.psum_tensor("p3_pO", [128, 1024], F32))
            pSum = st.enter_context(nc.psum_tensor("p3_pSum", [128, 1024], F32))
            S = Sched(nc, st, "p3")
            block = st.enter_context(nc.Block())
            r_c = Res()
            r_k = Res()
            d_cp, d_cs = S.dsem(), S.dsem()
            for dst, src in ((identb[:], t_ident), (expt[:], t_exp), (ovb[:], t_ov)):
                S.dma("pool", lambda e, dst=dst, src=src: e.dma_start(out=dst, in_=src), d_cp, writes=[r_c])
            for r4 in range(4):
                S.dma("pool", lambda e, r4=r4: e.dma_start(out=causb[:, r4, :], in_=t_caus), d_cp, writes=[r_c])
                S.dma("pool", lambda e, r4=r4: e.dma_start(out=acausb[:, r4, :], in_=t_acaus), d_cp, writes=[r_c])
            S.op("pool", lambda e: e.memset(onesb[:], 1.0), reads=[], writes=[r_c])
            for dst, src in ((identf[:], t_ident), (padb[:], t_padb), (cmpb[:], t_cmpb),
                             (kcm[:], KCMPT.rearrange("g d n -> d g n")),
                             (vcm[:], VCMP.rearrange("g (k p) d -> p g k d", p=128)),
                             (kst[:], KST.rearrange("g d t -> d g t")), (kwt[:], KWT.rearrange("g d t -> d g t")),
                             (vs[:], VS.rearrange("(t p) c -> p t c", p=128)),
                             (vw[:], VW.rearrange("(t p) c -> p t c", p=128))):
                S.dma("sp", lambda e, dst=dst, src=src: e.dma_start(out=dst, in_=src), d_cs, writes=[r_k])
            RqT, Rgbc = Ring(qT_, S, True), Ring(gbc_, S, True)
            Rkeep, Radd, Rcm = Ring(keep_, S, True), Ring(addt_, S, True), Ring(cm_, S, True)
            REc, RE, RSm, Rfac, Rtmp, Racc = Ring(Ec_), Ring(E_), Ring(Sm_), Ring(fac_), Ring(tmp_), Ring(acc_)
            Raccb = Ring(accb_, S, True)
            RselbT = Ring(selbT_)
            RpS = Ring(pS_, excl=True)
            r_pO, r_pSum = Res(True), Res(True)
            r_small = Res()
            d_dbg = S.dsem()
            dbg_t = {}
            if "DBGSEL" in debug:
                dbg_t = {"DBGSEL": dscr("DBGSEL", [NQ, 2, 128, 64], F32), "DBGIMP": dscr("DBGIMP", [NQ, 2, 128, 64], F32),
                         "DBGM8": dscr("DBGM8", [NQ, 2, 128, 8], F32)}

            def finalize(gbc, rgbc, br, acc, racc, first):
                fac, rfac = Rfac.next()
                S.op("dve", lambda e: e.tensor_scalar_max(out=fac[:], in0=pSum[:], scalar1=1e-30),
                     reads=[r_pSum], writes=[rfac])
                S.op("dve", lambda e: e.reciprocal(out=fac[:], in_=fac[:]), reads=[rfac], writes=[rfac])
                S.op("pool", lambda e: e.tensor_tensor(
                    out=fac[:].rearrange("p (h q) -> p h q", h=8), in0=fac[:].rearrange("p (h q) -> p h q", h=8),
                    in1=gbc[:, br::3, :], op=ALU.mult), reads=[rfac, rgbc], writes=[rfac])
                if first:
                    S.op("dve", lambda e: e.tensor_tensor(out=acc[:], in0=pO[:], in1=fac[:], op=ALU.mult),
                         reads=[r_pO, rfac], writes=[racc])
                else:
                    tmp, rtmp = Rtmp.next()
                    S.op("dve", lambda e: e.tensor_tensor(out=tmp[:], in0=pO[:], in1=fac[:], op=ALU.mult),
                         reads=[r_pO, rfac], writes=[rtmp])
                    S.op("pool", lambda e: e.tensor_tensor(out=acc[:], in0=acc[:], in1=tmp[:], op=ALU.add),
                         reads=[rtmp, racc], writes=[racc])

            def attend(kt_list, ksrc, vsrc, g, qTg, rq, selbT, rselbT, i):
                n = len(kt_list)
                for idx, kt in enumerate(kt_list):
                    pSx, rpS = RpS.next()
                    for hf in range(2):
                        extra = []
                        if selbT is not None:
                            extra.append((expt[:, kt, :], selbT[:].rearrange("s r q -> s (r q)"), [r_c, rselbT]))
                        if kt == i:
                            extra.append((identb[:], causb[:].rearrange("k r q -> k (r q)"), [r_c]))
                        if selbT is None and kt == i - 4:
                            extra.append((identb[:], acausb[:].rearrange("k r q -> k (r q)"), [r_c]))
                        S.op("pe", lambda e, pSx=pSx, hf=hf, kt=kt, ne=len(extra): e.matmul(
                            pSx[:, hf * 512:(hf + 1) * 512], lhsT=ksrc[:, g, kt * 128:(kt + 1) * 128],
                            rhs=qTg[:, hf * 512:(hf + 1) * 512], start=True, stop=(ne == 0)),
                            reads=[r_k, rq], writes=[rpS])
                        for xi, (l_, r_, deps) in enumerate(extra):
                            S.op("pe", lambda e, pSx=pSx, hf=hf, l_=l_, r_=r_, last=(xi == len(extra) - 1): e.matmul(
                                pSx[:, hf * 512:(hf + 1) * 512], lhsT=l_, rhs=r_, start=False, stop=last),
                                reads=deps, writes=[rpS])
                    E, rE = RE.next()
                    S.op("act", lambda e, E=E, pSx=pSx, kt=kt: e.activation(
                        out=E[:], in_=pSx[:], func=AF.Exp, scale=SCALE, bias=padb[:, kt:kt + 1]),
                        reads=[rpS, r_k], writes=[rE])
                    for hf in range(2):
                        S.op("pe", lambda e, E=E, hf=hf, kt=kt, idx=idx: e.matmul(
                            pO[:, hf * 512:(hf + 1) * 512], lhsT=vsrc[:, kt, g * 128:(g + 1) * 128],
                            rhs=E[:, hf * 512:(hf + 1) * 512], start=(idx == 0), stop=(idx == n - 1)),
                            reads=[r_k, rE], writes=[r_pO])
                        S.op("pe", lambda e, E=E, hf=hf, idx=idx: e.matmul(
                            pSum[:, hf * 512:(hf + 1) * 512], lhsT=onesb[:],
                            rhs=E[:, hf * 512:(hf + 1) * 512], start=(idx == 0), stop=(idx == n - 1)),
                            reads=[r_c, rE], writes=[r_pSum])

            for i in range(QT0, LT):
                qi = i - QT0
                qT, rq, dq = RqT.next()
                S.dma("sp", lambda e, qT=qT, qi=qi: e.dma_start(
                    out=qT[:], in_=NQT_[:, :, qi * 128:(qi + 1) * 128].rearrange("h p q -> p h q")), dq, writes=[rq])
                keep, rkeep, dkeep = Rkeep.next()
                addt, radd, dadd = Radd.next()
                cm, rcm, dcm = Rcm.next()
                S.dma("sp", lambda e, keep=keep, qi=qi: e.dma_start(out=keep[:], in_=t_keep[qi]), dkeep, writes=[rkeep])
                S.dma("sp", lambda e, addt=addt, qi=qi: e.dma_start(out=addt[:], in_=t_add[qi]), dadd, writes=[radd])
                for ck in range(2):
                    r0 = 128 * ck - 8 * i + 250
                    S.dma("sp", lambda e, cm=cm, ck=ck, r0=r0: e.dma_start(out=cm[:, ck, :], in_=t_cm[r0:r0 + 128, :]),
                          dcm, writes=[rcm])
                def do_group(i, qi, g, qT, rq, keep, rkeep, addt, radd, cm, rcm):
                    gbc, rgbc, dgbc = Rgbc.next()
                    S.dma("sp", lambda e, gbc=gbc, g=g, qi=qi: e.dma_start(
                        out=gbc[:], in_=NGT[g * 24:(g + 1) * 24, qi * 128:(qi + 1) * 128].unsqueeze(0).to_broadcast(
                            [128, 24, 128])), dgbc, writes=[rgbc])
                    qTg = qT[:, g * 8:(g + 1) * 8, :].rearrange("p h q -> p (h q)")
                    acc, racc = Racc.next()
                    Ecs = []
                    for ck in range(2):
                        pSx, rpS = RpS.next()
                        for hf in range(2):
                            S.op("pe", lambda e, pSx=pSx, hf=hf, ck=ck: e.matmul(
                                pSx[:, hf * 512:(hf + 1) * 512], lhsT=kcm[:, g, ck * 128:(ck + 1) * 128],
                                rhs=qTg[:, hf * 512:(hf + 1) * 512], start=True, stop=True),
                                reads=[r_k, rq], writes=[rpS])
                        Sm, rSm = RSm.next()
                        S.op("dve", lambda e, Sm=Sm, pSx=pSx, ck=ck: e.tensor_tensor(
                            out=Sm[:].rearrange("p (h q) -> p h q", h=8), in0=pSx[:].rearrange("p (h q) -> p h q", h=8),
                            in1=cm[:, ck, :][:, None, :].to_broadcast([128, 8, 128]), op=ALU.add),
                            reads=[rpS, rcm], writes=[rSm])
                        Ec, rEc = REc.next()
                        S.op("act", lambda e, Ec=Ec, Sm=Sm, ck=ck: e.activation(
                            out=Ec[:], in_=Sm[:], func=AF.Exp, scale=SCALE, bias=cmpb[:, ck:ck + 1]),
                            reads=[rSm, r_k], writes=[rEc])
                        Ecs.append((Ec, rEc))
                        for hf in range(2):
                            S.op("pe", lambda e, Ec=Ec, hf=hf, ck=ck: e.matmul(
                                pO[:, hf * 512:(hf + 1) * 512], lhsT=vcm[:, g, ck, :],
                                rhs=Ec[:, hf * 512:(hf + 1) * 512], start=(ck == 0), stop=(ck == 1)),
                                reads=[r_k, rEc], writes=[r_pO])
                            S.op("pe", lambda e, Ec=Ec, hf=hf, ck=ck: e.matmul(
                                pSum[:, hf * 512:(hf + 1) * 512], lhsT=onesb[:],
                                rhs=Ec[:, hf * 512:(hf + 1) * 512], start=(ck == 0), stop=(ck == 1)),
                                reads=[r_c, rEc], writes=[r_pSum])
                    pU, rpU = RpS.next()
                    for h in range(8):
                        for ck in range(2):
                            Ec, rEc = Ecs[ck]
                            o0 = (h // 4) * 512 + (h % 4) * 65
                            S.op("pe", lambda e, pU=pU, Ec=Ec, h=h, ck=ck, o0=o0: e.matmul(
                                pU[:, o0:o0 + 65], lhsT=Ec[:, h * 128:(h + 1) * 128], rhs=ovb[:, ck, :],
                                start=(ck == 0), stop=(ck == 1)), reads=[r_c, rEc], writes=[rpU])
                    for hh in range(2):
                        S.op("dve", lambda e, pU=pU, hh=hh: e.tensor_scalar_max(
                            out=rs8[:, hh * 4:(hh + 1) * 4],
                            in0=pU[:, hh * 512:hh * 512 + 260].rearrange("p (h s) -> p h s", s=65)[:, :, 64],
                            scalar1=1e-30), reads=[rpU], writes=[r_small])
                    S.op("dve", lambda e: e.reciprocal(out=rs8[:], in_=rs8[:]), reads=[r_small], writes=[r_small])
                    for h in range(8):
                        o0 = (h // 4) * 512 + (h % 4) * 65
                        if h == 0:
                            S.op("dve", lambda e, pU=pU, o0=o0: e.tensor_scalar(
                                out=imp[:], in0=pU[:, o0:o0 + 64], scalar1=rs8[:, 0:1], scalar2=None, op0=ALU.mult),
                                reads=[rpU, r_small], writes=[r_small])
                        else:
                            S.op("dve", lambda e, pU=pU, o0=o0, h=h: e.scalar_tensor_tensor(
                                out=imp[:], in0=pU[:, o0:o0 + 64], scalar=rs8[:, h:h + 1], in1=imp[:],
                                op0=ALU.mult, op1=ALU.add), reads=[rpU, r_small], writes=[r_small])
                    S.op("dve", lambda e, keep=keep: e.tensor_tensor(out=imp[:], in0=imp[:], in1=keep[:], op=ALU.mult),
                         reads=[r_small, rkeep], writes=[r_small])
                    S.op("dve", lambda e, addt=addt: e.tensor_tensor(out=imp[:], in0=imp[:], in1=addt[:], op=ALU.add),
                         reads=[r_small, radd], writes=[r_small])
                    S.op("dve", lambda e: e.max(out=m8[:], in_=imp[:]), reads=[r_small], writes=[r_small])
                    S.op("dve", lambda e: e.match_replace(out=imp2[:], in_to_replace=m8[:], in_values=imp[:],
                                                          imm_value=-1e30), reads=[r_small], writes=[r_small])
                    S.op("dve", lambda e: e.max(out=m8[:], in_=imp2[:]), reads=[r_small], writes=[r_small])
                    S.op("dve", lambda e: e.tensor_scalar(out=selb[:], in0=imp[:], scalar1=m8[:, 7:8], scalar2=NEG,
                                                          op0=ALU.is_lt, op1=ALU.mult), reads=[r_small], writes=[r_small])
                    if "DBGSEL" in debug:
                        for nm_, src_ in (("DBGSEL", selb), ("DBGIMP", imp), ("DBGM8", m8)):
                            S.dma("sp", lambda e, nm_=nm_, src_=src_, qi=qi, g=g: e.dma_start(
                                out=dbg_t[nm_][qi, g], in_=src_[:]), d_dbg, reads=[r_small])
                    pT, rpT = RpS.next()
                    S.op("pe", lambda e, pT=pT: e.transpose(out=pT[0:64, 0:128], in_=selb[:, 0:64], identity=identf[:]),
                         reads=[r_small, r_k], writes=[rpT])
                    selbT, rselbT = RselbT.next()
                    S.op("act", lambda e, pT=pT, selbT=selbT: e.activation(
                        out=selbT[:], in_=pT[0:64, 0:128][:, None, :].to_broadcast([64, 4, 128]), func=AF.Copy),
                        reads=[rpT], writes=[rselbT])
                    finalize(gbc, rgbc, 0, acc, racc, True)
                    attend(list(range(0, i + 1)), kst, vs, g, qTg, rq, selbT, rselbT, i)
                    finalize(gbc, rgbc, 1, acc, racc, False)
                    attend(list(range(i - 4, i + 1)), kwt, vw, g, qTg, rq, None, None, i)
                    finalize(gbc, rgbc, 2, acc, racc, False)
                    accb, raccb, daccb = Raccb.next()
                    S.op("act", lambda e, accb=accb, acc=acc: e.activation(out=accb[:], in_=acc[:], func=AF.Copy),
                         reads=[racc], writes=[raccb])
                    S.dma("sp", lambda e, accb=accb, g=g, qi=qi: e.dma_start(
                        out=NSAT[g * 8:(g + 1) * 8, :, qi * 128:(qi + 1) * 128].rearrange("h p q -> p h q"),
                        in_=accb[:].rearrange("p (h q) -> p h q", h=8)), daccb, reads=[raccb])

                for g in range(2):
                    do_group(i, qi, g, qT, rq, keep, rkeep, addt, radd, cm, rcm)
            S.emit(block)

    if 3 in phases:
        phase3a()
        phase3b()


    QGROUPS = [(0, 128)] + [(128 + 512 * g, 512) for g in range(4)]

    def upproj(tag, srcT, w, gate0, addsrc, dst):
        with ExitStack() as st:
            sb = lambda name, shape, dt: st.enter_context(nc.sbuf_tensor(name, list(shape), dt))
            src = sb(f"{tag}_src", [128, 16, NQT], BF16)
            wb = [sb(f"{tag}_w{i}", [128, 16, 512], BF16) for i in range(2)]
            mg_ = [sb(f"{tag}_mg{i}", [128, NQT], BF16) for i in range(2)]
            m1_ = [sb(f"{tag}_m1{i}", [128, NQT], BF16) for i in range(2)]
            tf_ = [sb(f"{tag}_tf{i}", [128, 512], F32) for i in range(2)]
            o_ = [sb(f"{tag}_o{i}", [128, NQT], BF16) for i in range(2)]
            ps = [st.enter_context(nc.psum_tensor(f"{tag}_ps{i}", [128, 512], F32)) for i in range(4)]
            S = Sched(nc, st, tag)
            block = st.enter_context(nc.Block())
            r_src = Res()
            d_src = S.dsem()
            S.dma("sp", lambda e: e.dma_start(out=src[:], in_=srcT.rearrange("c p q -> p c q")), d_src, writes=[r_src])
            Rw, Rmg, Rm1, Ro = Ring(wb, S, True), Ring(mg_, S, True), Ring(m1_, S, True), Ring(o_, S, True)
            Rtf = Ring(tf_)
            Rps = Ring(ps, excl=True)
            wv = w.rearrange("(c p) n -> p c n", p=128)

            def chunk(wt, rw, ch, dc):
                mg, rmg, dmg = Rmg.next()
                S.dma("sp", lambda e: e.dma_start(out=mg[:], in_=MGT[gate0 + dc]), dmg, writes=[rmg])
                if addsrc is not None:
                    m1, rm1, dm1 = Rm1.next()
                    S.dma("sp", lambda e: e.dma_start(out=m1[:], in_=addsrc[dc]), dm1, writes=[rm1])
                o, ro, do = Ro.next()
                for (q0, nt) in QGROUPS:
                    p, rp = Rps.next()
                    for c in range(16):
                        S.op("pe", lambda e, p=p, c=c, q0=q0, nt=nt: e.matmul(
                            p[:, 0:nt], lhsT=wt[:, c, ch * 128:(ch + 1) * 128], rhs=src[:, c, q0:q0 + nt],
                            start=(c == 0), stop=(c == 15)), reads=[rw, r_src], writes=[rp])
                    if addsrc is None:
                        S.op("dve", lambda e, p=p, q0=q0, nt=nt: e.tensor_tensor(
                            out=o[:, q0:q0 + nt], in0=p[:, 0:nt], in1=mg[:, q0:q0 + nt], op=ALU.mult),
                            reads=[rp, rmg], writes=[ro])
                    else:
                        tf, rtf = Rtf.next()
                        S.op("dve", lambda e, p=p, q0=q0, nt=nt, tf=tf: e.tensor_tensor(
                            out=tf[:, 0:nt], in0=p[:, 0:nt], in1=mg[:, q0:q0 + nt], op=ALU.mult),
                            reads=[rp, rmg], writes=[rtf])
                        S.op("pool", lambda e, q0=q0, nt=nt, tf=tf: e.tensor_tensor(
                            out=o[:, q0:q0 + nt], in0=tf[:, 0:nt], in1=m1[:, q0:q0 + nt], op=ALU.add),
                            reads=[rtf, rm1], writes=[ro])
                S.dma("sp", lambda e: e.dma_start(out=dst[dc], in_=o[:]), do, reads=[ro])

            def wblock(blk):
                wt, rw, dw = Rw.next()
                S.dma("pool", lambda e: e.dma_start(out=wt[:], in_=wv[:, :, blk * 512:(blk + 1) * 512]), dw, writes=[rw])
                for ch in range(4):
                    chunk(wt, rw, ch, blk * 4 + ch)

            for blk in range(4):
                wblock(blk)
            S.emit(block)

    def phase4b():
        with ExitStack() as st:
            sb = lambda name, shape, dt: st.enter_context(nc.sbuf_tensor(name, list(shape), dt))
            identb = sb("p4_id", [128, 128], BF16)
            wo = sb("p4_wo", [128, 16, D], BF16)
            w2bc = sb("p4_w2", [128, D], F32)
            mT_ = [sb(f"p4_mT{i}", [128, 16, 128], BF16) for i in range(2)]
            x_ = [sb(f"p4_x{i}", [128, D], F32) for i in range(2)]
            h_ = [sb(f"p4_h{i}", [128, D], F32) for i in range(2)]
            xn_ = [sb(f"p4_xn{i}", [128, D], BF16) for i in range(2)]
            xT_ = [sb(f"p4_xT{i}", [128, 16, 128], BF16) for i in range(2)]
            junk = sb("p4_junk", [128, D], BF16)
            ss_ = [sb(f"p4_ss{i}", [128, 1], F32) for i in range(2)]
            ps = [st.enter_context(nc.psum_tensor(f"p4_ps{i}", [128, 512], F32)) for i in range(4)]
            pt = [st.enter_context(nc.psum_tensor(f"p4_pt{i}", [128, 1024], BF16)) for i in range(2)]
            S = Sched(nc, st, "p4b")
            block = st.enter_context(nc.Block())
            r_id, r_wo, r_w2, r_junk = Res(), Res(), Res(), Res()
            d_p, d_s = S.dsem(), S.dsem()
            S.dma("pool", lambda e: e.dma_start(out=identb[:], in_=t_ident), d_p, writes=[r_id])
            wov = w_out.rearrange("(c p) n -> p c n", p=128)
            for blk in range(4):
                S.dma("pool", lambda e, blk=blk: e.dma_start(out=wo[:, :, blk * 512:(blk + 1) * 512],
                                                           in_=wov[:, :, blk * 512:(blk + 1) * 512]), d_p, writes=[r_wo])
            S.dma("sp", lambda e: e.dma_start(out=w2bc[:], in_=norm2_w.partition_broadcast(128)), d_s, writes=[r_w2])
            RmT, Rx, Rh, RxT = Ring(mT_, S, True), Ring(x_, S, True), Ring(h_, S, True), Ring(xT_, S, True)
            Rxn, Rss = Ring(xn_), Ring(ss_)
            Rps = Ring(ps, excl=True)
            r_pt = [Res(True), Res(True)]

            def tile(qi):
                mT, rmT, dmT = RmT.next()
                S.dma("sp", lambda e: e.dma_start(out=mT[:], in_=MRGT[:, :, qi * 128:(qi + 1) * 128].rearrange("c p q -> p c q")),
                      dmT, writes=[rmT])
                xt, rx, dx = Rx.next()
                S.dma("sp", lambda e: e.dma_start(out=xt[:], in_=x_loc[(QT0 + qi) * 128:(QT0 + qi + 1) * 128, :]), dx, writes=[rx])
                h, rh, dh = Rh.next()
                for cb in range(4):
                    p, rp = Rps.next()
                    for c in range(16):
                        S.op("pe", lambda e, p=p, c=c, cb=cb: e.matmul(
                            p[:], lhsT=mT[:, c, :], rhs=wo[:, c, cb * 512:(cb + 1) * 512],
                            start=(c == 0), stop=(c == 15)), reads=[rmT, r_wo], writes=[rp])
                    S.op("dve", lambda e, p=p, cb=cb: e.tensor_tensor(
                        out=h[:, cb * 512:(cb + 1) * 512], in0=p[:], in1=xt[:, cb * 512:(cb + 1) * 512], op=ALU.add),
                        reads=[rp, rx], writes=[rh])
                S.dma("sp", lambda e: e.dma_start(out=H1[qi * 128:(qi + 1) * 128, :], in_=h[:]), dh, reads=[rh])
                ss, rss = Rss.next()
                S.op("act", lambda e: e.activation(out=junk[:], in_=h[:], func=AF.Square, accum_out=ss[:]),
                     reads=[rh], writes=[r_junk, rss])
                S.op("act", lambda e: e.activation(out=ss[:], in_=ss[:], func=AF.Sqrt, scale=1.0 / D, bias=EPS),
                     reads=[rss], writes=[rss])
                S.op("dve", lambda e: e.reciprocal(out=ss[:], in_=ss[:]), reads=[rss], writes=[rss])
                xn, rxn = Rxn.next()
                S.op("dve", lambda e: e.scalar_tensor_tensor(out=xn[:], in0=h[:], scalar=ss[:], in1=w2bc[:],
                                                             op0=ALU.mult, op1=ALU.mult),
                     reads=[rh, rss, r_w2], writes=[rxn])
                xT, rxT, dxT = RxT.next()
                for hh in range(2):
                    for cc in range(8):
                        c = hh * 8 + cc
                        S.op("pe", lambda e, c=c, cc=cc, hh=hh: e.transpose(
                            out=pt[hh][:, cc * 128:(cc + 1) * 128], in_=xn[:, c * 128:(c + 1) * 128], identity=identb[:]),
                            reads=[rxn, r_id], writes=[r_pt[hh]])
                    if hh == 0:
                        S.op("act", lambda e, hh=hh: e.activation(
                            out=xT[:, 0:8, :], in_=pt[0][:].rearrange("p (c q) -> p c q", c=8), func=AF.Copy),
                            reads=[r_pt[0]], writes=[rxT])
                    else:
                        S.op("dve", lambda e, hh=hh: e.tensor_copy(
                            out=xT[:, 8:16, :], in_=pt[1][:].rearrange("p (c q) -> p c q", c=8)),
                            reads=[r_pt[1]], writes=[rxT])
                S.dma("sp", lambda e: e.dma_start(
                    out=XN2T[:, :, qi * 128:(qi + 1) * 128].rearrange("c p q -> p c q"), in_=xT[:]), dxT, reads=[rxT])

            for qi in range(NQ):
                tile(qi)
            S.emit(block)

    if 4 in phases:
        upproj("p4a", RETGT, w_ret_up, 0, None, M1T)
        upproj("p4n", NSAT, w_nsa_up, 16, M1T, MRGT)
        phase4b()

    def phase5a():
        with ExitStack() as st:
            sb = lambda name, shape, dt: st.enter_context(nc.sbuf_tensor(name, list(shape), dt))
            xs = sb("p5_xs", [128, 16, NQT], BF16)
            wa_ = [sb(f"p5_wa{i}", [128, 16, 512], BF16) for i in range(2)]
            wb_ = [sb(f"p5_wb{i}", [128, 16, 512], BF16) for i in range(2)]
            cw = sb("p5_cw", [128, 88, 3], F32)
            cbias = sb("p5_cb", [128, 88], F32)
            halo = sb("p5_halo", [128, 1], F32)
            u_ = [sb(f"p5_u{i}", [128, 2050], F32) for i in range(4)]
            y_ = [sb(f"p5_y{i}", [128, 2048], F32) for i in range(3)]
            o_ = [sb(f"p5_o{i}", [128, 2048], BF16) for i in range(2)]
            ps = [st.enter_context(nc.psum_tensor(f"p5_ps{i}", [128, 512], F32)) for i in range(6)]
            ph = [st.enter_context(nc.psum_tensor(f"p5_ph{i}", [128, 2], F32)) for i in range(2)]
            S = Sched(nc, st, "p5a")
            block = st.enter_context(nc.Block())
            r_c = Res()
            d_c = S.dsem()
            S.dma("sp", lambda e: e.dma_start(out=xs[:], in_=XN2T.rearrange("c p q -> p c q")), d_c, writes=[r_c])
            S.dma("sp", lambda e: e.dma_start(out=cw[:], in_=conv_wT), d_c, writes=[r_c])
            S.dma("sp", lambda e: e.dma_start(out=cbias[:], in_=conv_bT), d_c, writes=[r_c])
            S.dma("sp", lambda e: e.dma_start(out=halo[:], in_=t_halo), d_c, writes=[r_c])
            Rwa, Rwb = Ring(wa_, S, True), Ring(wb_, S, True)
            Ru, Ry = Ring(u_), Ring(y_)
            Ro = Ring(o_, S, True)
            Rps, Rph = Ring(ps, excl=True), Ring(ph, excl=True)
            wv = w_ffn_up.rearrange("(c p) n -> p c n", p=128)

            def half(wt, rw, ch, cidx, ceng):
                u, ru = Ru.next()
                p, rp = Rph.next()
                for c in range(16):
                    S.op("pe", lambda e, c=c: e.matmul(p[:, 0:2], lhsT=wt[:, c, ch * 128:(ch + 1) * 128],
                                                       rhs=xs[:, c, 126:128], start=(c == 0), stop=(c == 15)),
                         reads=[rw, r_c], writes=[rp])
                S.op("act", lambda e: e.activation(out=u[:, 0:2], in_=p[:, 0:2], func=AF.Copy, scale=halo[:]),
                     reads=[rp, r_c], writes=[ru])
                for g in range(4):
                    pp, rpp = Rps.next()
                    for c in range(16):
                        S.op("pe", lambda e, c=c, pp=pp, g=g: e.matmul(
                            pp[:], lhsT=wt[:, c, ch * 128:(ch + 1) * 128],
                            rhs=xs[:, c, 128 + g * 512:128 + (g + 1) * 512], start=(c == 0), stop=(c == 15)),
                            reads=[rw, r_c], writes=[rpp])
                    S.op("act", lambda e, pp=pp, g=g: e.activation(out=u[:, 2 + g * 512:2 + (g + 1) * 512], in_=pp[:],
                                                                   func=AF.Copy), reads=[rpp], writes=[ru])
                y, ry = Ry.next()
                S.op(ceng, lambda e: e.tensor_scalar(out=y[:], in0=u[:, 2:2050], scalar1=cw[:, cidx, 2:3],
                                                     scalar2=cbias[:, cidx:cidx + 1], op0=ALU.mult, op1=ALU.add),
                     reads=[ru, r_c], writes=[ry])
                S.op(ceng, lambda e: e.scalar_tensor_tensor(out=y[:], in0=u[:, 1:2049], scalar=cw[:, cidx, 1:2], in1=y[:],
                                                            op0=ALU.mult, op1=ALU.add), reads=[ru, r_c, ry], writes=[ry])
                S.op(ceng, lambda e: e.scalar_tensor_tensor(out=y[:], in0=u[:, 0:2048], scalar=cw[:, cidx, 0:1], in1=y[:],
                                                            op0=ALU.mult, op1=ALU.add), reads=[ru, r_c, ry], writes=[ry])
                return y, ry

            def chunk(wa, rwa, wb, rwb, ch, j):
                ya, rya = half(wa, rwa, ch, j, "dve")
                yb, ryb = half(wb, rwb, ch, 44 + j, "dve")
                S.op("act", lambda e: e.activation(out=ya[:], in_=ya[:], func=AF.Silu), reads=[rya], writes=[rya])
                o, ro, do = Ro.next()
                S.op("pool", lambda e: e.tensor_tensor(out=o[:], in0=ya[:], in1=yb[:], op=ALU.mult),
                     reads=[rya, ryb], writes=[ro])
                S.dma("sp", lambda e: e.dma_start(out=ACTT[j], in_=o[:]), do, reads=[ro])

            def wblock(jb):
                wa, rwa, dwa = Rwa.next()
                wb, rwb, dwb = Rwb.next()
                S.dma("pool", lambda e: e.dma_start(out=wa[:], in_=wv[:, :, jb * 512:(jb + 1) * 512]), dwa, writes=[rwa])
                S.dma("pool", lambda e: e.dma_start(out=wb[:], in_=wv[:, :, DFF + jb * 512:DFF + (jb + 1) * 512]), dwb,
                      writes=[rwb])
                for ch in range(4):
                    chunk(wa, rwa, wb, rwb, ch, jb * 4 + ch)

            for jb in range(11):
                wblock(jb)
            S.emit(block)

    def phase5b():
        with ExitStack() as st:
            sb = lambda name, shape, dt: st.enter_context(nc.sbuf_tensor(name, list(shape), dt))
            wd_ = [sb(f"p5b_w{i}", [128, 44, 512], BF16) for i in range(2)]
            a_ = [sb(f"p5b_a{i}", [128, 44, 128], BF16) for i in range(3)]
            h_ = [sb(f"p5b_h{i}", [128, 512], F32) for i in range(3)]
            ps = [st.enter_context(nc.psum_tensor(f"p5b_ps{i}", [128, 512], F32)) for i in range(4)]
            S = Sched(nc, st, "p5b")
            block = st.enter_context(nc.Block())
            Rw, Ra, Rh = Ring(wd_, S, True), Ring(a_, S, True), Ring(h_, S, True)
            r_hs = [Res() for _ in h_]
            d_hs = [S.dsem() for _ in h_]
            Rps = Ring(ps, excl=True)
            wv = w_ffn_down.rearrange("(j p) n -> p j n", p=128)

            def tile(wt, rw, cb, t):
                a, ra, da = Ra.next()
                S.dma("sp", lambda e: e.dma_start(out=a[:], in_=ACTT[:, :, t * 128:(t + 1) * 128].rearrange("j p q -> p j q")),
                      da, writes=[ra])
                h, rh, dh = Rh.next()
                k = Rh.i
                S.dma("sp", lambda e: e.dma_start(out=h[:], in_=H1[(t + 1) * 128:(t + 2) * 128, cb * 512:(cb + 1) * 512]),
                      dh, writes=[rh])
                p, rp = Rps.next()
                for j in range(44):
                    S.op("pe", lambda e, j=j: e.matmul(p[:], lhsT=a[:, j, :], rhs=wt[:, j, :], start=(j == 0), stop=(j == 43)),
                         reads=[ra, rw], writes=[rp])
                S.op("dve", lambda e: e.tensor_tensor(out=h[:], in0=p[:], in1=h[:], op=ALU.add), reads=[rp, rh], writes=[rh])
                S.dma("sp", lambda e: e.dma_start(out=H2[t * 128:(t + 1) * 128, cb * 512:(cb + 1) * 512], in_=h[:]),
                      d_hs[k], reads=[rh], writes=[r_hs[k]])

            def wblock(cb):
                wt, rw, dw = Rw.next()
                for q4 in range(4):
                    S.dma("pool", lambda e, q4=q4: e.dma_start(out=wt[:, q4 * 11:(q4 + 1) * 11, :],
                                                             in_=wv[:, q4 * 11:(q4 + 1) * 11, cb * 512:(cb + 1) * 512]),
                          dw, writes=[rw])
                for t in range(16):
                    tile(wt, rw, cb, t)

            for cb in range(4):
                wblock(cb)
            S.emit(block)

    def phase5c():
        with ExitStack() as st:
            sb = lambda name, shape, dt: st.enter_context(nc.sbuf_tensor(name, list(shape), dt))
            wf = sb("p5c_wf", [128, D], F32)
            h_ = [sb(f"p5c_h{i}", [128, D], F32) for i in range(2)]
            o_ = [sb(f"p5c_o{i}", [128, D], F32) for i in range(2)]
            junk = sb("p5c_junk", [128, D], BF16)
            ss_ = [sb(f"p5c_ss{i}", [128, 1], F32) for i in range(2)]
            S = Sched(nc, st, "p5c")
            block = st.enter_context(nc.Block())
            r_w, r_junk = Res(), Res()
            S.dma("sp", lambda e: e.dma_start(out=wf[:], in_=final_norm_w.partition_broadcast(128)), S.dsem(), writes=[r_w])
            Rh, Ro, Rss = Ring(h_, S, True), Ring(o_, S, True), Ring(ss_)

            def tile(t):
                h, rh, dh = Rh.next()
                S.dma("sp", lambda e: e.dma_start(out=h[:], in_=H2[t * 128:(t + 1) * 128, :]), dh, writes=[rh])
                ss, rss = Rss.next()
                S.op("act", lambda e: e.activation(out=junk[:], in_=h[:], func=AF.Square, accum_out=ss[:]),
                     reads=[rh], writes=[r_junk, rss])
                S.op("act", lambda e: e.activation(out=ss[:], in_=ss[:], func=AF.Sqrt, scale=1.0 / D, bias=EPS),
                     reads=[rss], writes=[rss])
                S.op("dve", lambda e: e.reciprocal(out=ss[:], in_=ss[:]), reads=[rss], writes=[rss])
                o, ro, do = Ro.next()
                S.op("dve", lambda e: e.scalar_tensor_tensor(out=o[:], in0=h[:], scalar=ss[:], in1=wf[:],
                                                             op0=ALU.mult, op1=ALU.mult), reads=[rh, rss, r_w], writes=[ro])
                S.dma("sp", lambda e: e.dma_start(out=out[t * 128:(t + 1) * 128, :], in_=o[:]), do, reads=[ro])

            for t in range(16):
                tile(t)
            S.emit(block)

    if 5 in phases:
        phase5a()
        phase5b()
        phase5c()

    return nc


def _tables(s):
    pad = 2048 if s == 0 else 0
    tl = np.arange(LT * 128)
    act = np.maximum(tl - pad, 0).astype(np.float64)
    is_pad = tl < pad
    half = 64
    freq = 10000.0 ** (-np.arange(half, dtype=np.float64) / half)
    ang = act[:, None] * freq[None, :]
    cs = np.concatenate([np.cos(ang), np.sin(ang)], axis=1)
    t_cs = cs.reshape(LT, 128, 128).transpose(1, 0, 2).astype(np.float32)
    t_csq = (cs[QT0 * 128:] * (128 ** -0.5)).reshape(NQ, 128, 128).transpose(1, 0, 2).astype(np.float32)
    gam = 1.0 - 2.0 ** (-5.0 - np.arange(8, dtype=np.float64))
    j = np.arange(128, dtype=np.float64)
    rel = j[None, :] - j[:, None]
    decT = np.where(rel[:, None, :] >= 0, gam[None, :, None] ** np.maximum(rel[:, None, :], 0), 0.0)
    qdec = np.broadcast_to((gam[:, None] ** (j[None, :] + 1.0))[None], (128, 8, 128))
    kdec = gam[None, :] ** (127.0 - j[:, None])
    padb = np.where(is_pad, NEG, 0.0).reshape(LT, 128).T
    n = np.arange(256)
    cmp_invalid = (n * 16 < pad) | (n >= 255)
    cmpb = np.where(cmp_invalid, NEG, 0.0).reshape(2, 128).T
    r = np.arange(512)[:, None]
    q = np.arange(128)[None, :]
    cm = np.where(16 * (r - 250) + 31 <= q, 0.0, NEG)
    keep = np.ones((NQ, 128, 64))
    add = np.zeros((NQ, 128, 64))
    blk = np.arange(64)[None, :]
    b0 = pad // 64
    for qi in range(NQ):
        t = (QT0 + qi) * 128 + np.arange(128)
        cur = (t // 64)[:, None]
        forced = (blk == b0) | (blk == cur) | (blk == cur - 1)
        neg = (blk > cur) | (blk < b0)
        keep[qi] = np.where(forced | neg, 0.0, 1.0)
        add[qi] = np.where(neg, -1e4, np.where(forced, 1e4, 0.0))
    sidx = np.arange(64)[:, None, None]
    kt = np.arange(LT)[None, :, None]
    kk = np.arange(128)[None, None, :]
    texp = (sidx == 2 * kt + (kk >= 64)).astype(np.float32)
    k_ = np.arange(128)[:, None]
    caus = np.where(k_ > q, NEG, 0.0)
    acaus = np.where(k_ <= q, NEG, 0.0)
    nn = np.arange(256)[:, None]
    ss = np.arange(64)[None, :]
    ov = ((nn * 16 < ss * 64 + 64) & (nn * 16 + 32 > ss * 64)).astype(np.float64)
    ov = np.concatenate([ov, np.ones((256, 1))], axis=1)
    ov[255] = 0.0
    t_ov = ov.reshape(2, 128, 65).transpose(1, 0, 2)
    f = lambda a: np.ascontiguousarray(a, dtype=np.float32)
    return {
        "t_cs": f(t_cs), "t_csq": f(t_csq), "t_decT": f(decT), "t_qdec": f(qdec), "t_kdec": f(kdec),
        "t_padb": f(padb), "t_cmpb": f(cmpb), "t_cm": f(cm), "t_keep": f(keep), "t_add": f(add),
        "t_exp": f(texp), "t_caus": f(caus), "t_acaus": f(acaus), "t_ident": f(np.eye(128)),
        "t_ov": f(t_ov), "t_halo": f(np.full((128, 1), float(s))),
    }


def make_in_maps(inputs):
    g = lambda k: np.asarray(inputs[k], dtype=np.float32)
    x = g("x")
    shared = {
        "norm1_w": g("norm1_w")[0][None], "w_in": g("w_in")[0], "ret_norm_w": g("ret_norm_w")[0][None],
        "w_ret_up": g("w_ret_up")[0],
        "cmp_peT_k": np.ascontiguousarray(g("cmp_pe_k")[0].T), "cmp_peT_v": np.ascontiguousarray(g("cmp_pe_v")[0].T),
        "cmp_w1_k": g("cmp_w1_k")[0], "cmp_w1_v": g("cmp_w1_v")[0],
        "cmp_w2_k": g("cmp_w2_k")[0], "cmp_w2_v": g("cmp_w2_v")[0],
        "w_nsa_up": g("w_nsa_up")[0], "w_out": g("w_out")[0], "norm2_w": g("norm2_w")[0][None],
        "w_ffn_up": g("w_ffn_up")[0],
        "conv_wT": np.ascontiguousarray(g("conv_w")[0].reshape(3, 88, 128).transpose(2, 1, 0)),
        "conv_bT": np.ascontiguousarray(g("conv_b")[0].reshape(88, 128).T),
        "w_ffn_down": g("w_ffn_down")[0], "final_norm_w": g("final_norm_w")[None],
    }
    tabs = [_tables(0), _tables(1)]
    maps = []
    for c in range(8):
        b, s = c // 2, c % 2
        if s == 1:
            xl = np.ascontiguousarray(x[b])
        else:
            xl = np.concatenate([np.zeros((2048, D), np.float32), x[b, :2048]], axis=0)
        m = dict(shared)
        m.update(tabs[s])
        m["x_loc"] = xl
        maps.append(m)
    return maps


_NC = None


def kernel(**inputs):
    global _NC
    if _NC is None:
        _NC = build_program()
    maps = make_in_maps(inputs)
    res = run_bass_kernel_spmd(_NC, maps, core_ids=list(range(8)))
    outp = np.zeros((4, 4096, D), np.float32)
    for c in range(8):
        b, s = c // 2, c % 2
        outp[b, s * 2048:(s + 1) * 2048] = res.results[c]["out"]
    return outp
```

```python
import math
import os
from contextlib import ExitStack
import numpy as np
import concourse.bass as bass
import concourse.mybir as mybir
from concourse.bass_utils import run_bass_kernel_spmd

F32 = mybir.dt.float32
BF16 = mybir.dt.bfloat16
AF = mybir.ActivationFunctionType
ALU = mybir.AluOpType
AX = mybir.AxisListType

D = 2048
LT = 32
QT0 = 15
NQ = LT - QT0
NQT = NQ * 128
IN_COLS = 13872
DFF = 5632
EPS = 1e-6
NEG = -30000.0
SCALE = 128 ** -0.5


class Tok:
    __slots__ = ("sem", "val")

    def __init__(self, sem, val):
        self.sem = sem
        self.val = val


class Res:
    __slots__ = ("w", "r", "excl")

    def __init__(self, excl=False):
        self.w = None
        self.r = []
        self.excl = excl


class DSem:
    __slots__ = ("sem", "val", "eng")

    def __init__(self, sem):
        self.sem = sem
        self.val = 0
        self.eng = None


ENGS = ("pe", "act", "dve", "pool", "sp")


class Sched:
    def __init__(self, nc, stack, tag):
        self.nc = nc
        self.q = {e: [] for e in ENGS}
        self.cnt = {e: 0 for e in ENGS}
        self.allsems = []
        self.sem = {e: self._alloc(f"{tag}_s_{e}") for e in ENGS}
        self.waited = {e: {} for e in ENGS}
        self.dsems = []
        self.tag = tag
        stack.callback(self._cleanup)

    def _alloc(self, name):
        h = self.nc.alloc_semaphore(name=name)
        self.allsems.append(h)
        return h

    def _cleanup(self):
        self.nc.clear_and_free_semaphores(self.allsems)
        self.nc.all_engine_barrier()

    def dsem(self):
        d = DSem(self._alloc(f"{self.tag}_d{len(self.dsems)}"))
        self.dsems.append(d)
        return d

    def _deps(self, eng, reads, writes):
        need = {}

        def add(t):
            k = id(t.sem)
            if k not in need or need[k].val < t.val:
                need[k] = t

        for r in reads:
            if r.w is not None:
                add(r.w)
        for w in writes:
            if w.w is not None:
                add(w.w)
            for t in w.r:
                add(t)
        waits = []
        wd = self.waited[eng]
        own = id(self.sem[eng])
        for k, t in need.items():
            if eng == "pe" and k == own:
                continue
            if wd.get(k, 0) < t.val:
                wd[k] = t.val
                waits.append((t.sem, t.val))
        return waits

    def _fin(self, tok, reads, writes):
        for r in reads:
            r.r.append(tok)
        for w in writes:
            w.w = tok
            w.r = []
        return tok

    def op(self, eng, fn, reads=(), writes=()):
        ex = [r for r in reads if r.excl]
        if ex:
            reads = [r for r in reads if not r.excl]
            writes = list(writes) + ex
        waits = self._deps(eng, reads, writes)
        self.cnt[eng] += 1
        self.q[eng].append((waits, fn, (self.sem[eng], 1)))
        return self._fin(Tok(self.sem[eng], self.cnt[eng]), reads, writes)

    def dma(self, eng, fn, ds, reads=(), writes=()):
        assert ds.eng in (None, eng), "one issuing engine per DMA semaphore"
        ds.eng = eng
        waits = self._deps(eng, reads, writes)
        ds.val += 16
        self.q[eng].append((waits, fn, (ds.sem, 16)))
        return self._fin(Tok(ds.sem, ds.val), reads, writes)

    def emit(self, block):
        finals = [(d.sem, d.val) for d in self.dsems if d.val > 0]

        def run(engobj, name, tail=False):
            for waits, fn, inc in self.q[name]:
                for s, v in waits:
                    engobj.wait_ge(s, v)
                fn(engobj).then_inc(inc[0], inc[1])
            if tail:
                for s, v in finals:
                    engobj.wait_ge(s, v)
                for e in ENGS:
                    if e != name and self.cnt[e] > 0:
                        engobj.wait_ge(self.sem[e], self.cnt[e])

        @block.sync
        def _(eng):
            run(eng, "sp", tail=True)

        @block.tensor
        def _(eng):
            run(eng, "pe")

        @block.scalar
        def _(eng):
            run(eng, "act")

        @block.vector
        def _(eng):
            run(eng, "dve")

        @block.gpsimd
        def _(eng):
            run(eng, "pool")


class Ring:
    def __init__(self, bufs, S=None, with_dsem=False, excl=False):
        self.bufs = bufs
        self.res = [Res(excl) for _ in bufs]
        self.ds = [S.dsem() for _ in bufs] if with_dsem else None
        self.i = -1

    def next(self):
        self.i = (self.i + 1) % len(self.bufs)
        if self.ds is not None:
            return self.bufs[self.i], self.res[self.i], self.ds[self.i]
        return self.bufs[self.i], self.res[self.i]


ALL_PHASES = (1, 2, 3, 4, 5)


def build_program(debug=(), phases=ALL_PHASES):
    nc = bass.Bass("TRN2", target_bir_lowering=False)

    def din(name, shape, dt=F32):
        return nc.dram_tensor(name, list(shape), dt, kind="ExternalInput").ap()

    def dscr(name, shape, dt=BF16):
        kind = "ExternalOutput" if name in debug else "Internal"
        return nc.dram_tensor(name, list(shape), dt, kind=kind).ap()

    x_loc = din("x_loc", [LT * 128, D])
    norm1_w = din("norm1_w", [1, D])
    w_in = din("w_in", [D, IN_COLS])
    ret_norm_w = din("ret_norm_w", [1, D])
    w_ret_up = din("w_ret_up", [D, D])
    cmp_peT = {"k": din("cmp_peT_k", [128, 32]), "v": din("cmp_peT_v", [128, 32])}
    cmp_w1 = {"k": din("cmp_w1_k", [4096, 256]), "v": din("cmp_w1_v", [4096, 256])}
    cmp_w2 = {"k": din("cmp_w2_k", [256, 128]), "v": din("cmp_w2_v", [256, 128])}
    w_nsa_up = din("w_nsa_up", [D, D])
    w_out = din("w_out", [D, D])
    norm2_w = din("norm2_w", [1, D])
    w_ffn_up = din("w_ffn_up", [D, 2 * DFF])
    conv_wT = din("conv_wT", [128, 88, 3])
    conv_bT = din("conv_bT", [128, 88])
    w_ffn_down = din("w_ffn_down", [DFF, D])
    final_norm_w = din("final_norm_w", [1, D])
    t_cs = din("t_cs", [128, LT, 128])
    t_csq = din("t_csq", [128, NQ, 128])
    t_decT = din("t_decT", [128, 8, 128])
    t_qdec = din("t_qdec", [128, 8, 128])
    t_kdec = din("t_kdec", [128, 8])
    t_padb = din("t_padb", [128, LT])
    t_cmpb = din("t_cmpb", [128, 2])
    t_cm = din("t_cm", [512, 128])
    t_keep = din("t_keep", [NQ, 128, 64])
    t_add = din("t_add", [NQ, 128, 64])
    t_exp = din("t_exp", [64, LT, 128])
    t_caus = din("t_caus", [128, 128])
    t_acaus = din("t_acaus", [128, 128])
    t_ident = din("t_ident", [128, 128])
    t_ov = din("t_ov", [128, 2, 65])
    t_halo = din("t_halo", [128, 1])

    out = nc.dram_tensor("out", [16 * 128, D], F32, kind="ExternalOutput").ap()

    RK = dscr("RK", [LT * 128, 1024])
    RV = dscr("RV", [LT * 128, 2048])
    RQ = dscr("RQ", [NQT, 1024])
    RG = dscr("RG", [NQT, 2048])
    NQT_ = dscr("NQT", [16, 128, NQT])
    KCT = dscr("KCT", [2, 128, LT * 128])
    VCT = dscr("VCT", [2, 128, LT * 128])
    KST = dscr("KST", [2, 128, LT * 128])
    KWT = dscr("KWT", [2, 128, LT * 128])
    VS = dscr("VS", [LT * 128, 256])
    VW = dscr("VW", [LT * 128, 256])
    NGT = dscr("NGT", [48, NQT], F32)
    MGT = dscr("MGT", [32, 128, NQT])
    RETGT = dscr("RETGT", [16, 128, NQT])
    NSAT = dscr("NSAT", [16, 128, NQT])
    M1T = dscr("M1T", [16, 128, NQT])
    MRGT = dscr("MRGT", [16, 128, NQT])
    H1 = dscr("H1", [NQT, D], F32)
    XN2T = dscr("XN2T", [16, 128, NQT])
    ACTT = dscr("ACTT", [44, 128, 2048])
    H2 = dscr("H2", [2048, D], F32)

    w_in_v = w_in.rearrange("(c p) n -> p c n", p=128)

    def phase01():
        with ExitStack() as st01:
            xnT = st01.enter_context(nc.sbuf_tensor("xnT", [128, 16, LT * 128], BF16))
            identb = st01.enter_context(nc.sbuf_tensor("identb", [128, 128], BF16))

            with ExitStack() as st:
                xin = [st.enter_context(nc.sbuf_tensor(f"p0_x{i}", [128, D], F32)) for i in range(2)]
                sq = st.enter_context(nc.sbuf_tensor("p0_sq", [128, D], BF16))
                xnb = [st.enter_context(nc.sbuf_tensor(f"p0_xn{i}", [128, D], BF16)) for i in range(2)]
                wbc = st.enter_context(nc.sbuf_tensor("p0_wbc", [128, D], F32))
                ss = [st.enter_context(nc.sbuf_tensor(f"p0_ss{i}", [128, 1], F32)) for i in range(2)]
                rs = [st.enter_context(nc.sbuf_tensor(f"p0_rs{i}", [128, 1], F32)) for i in range(2)]
                pt = [st.enter_context(nc.psum_tensor(f"p0_pt{i}", [128, D], BF16)) for i in range(2)]
                S = Sched(nc, st, "p0")
                block = st.enter_context(nc.Block())
                r_c = Res()
                d_c = S.dsem()
                S.dma("sp", lambda e: e.dma_start(out=wbc[:], in_=norm1_w.partition_broadcast(128)), d_c, writes=[r_c])
                r_id = Res()
                S.dma("pool", lambda e: e.dma_start(out=identb[:], in_=t_ident), S.dsem(), writes=[r_id])
                Rx = Ring(xin, S, True)
                Rxn = Ring(xnb)
                Rss = Ring(ss)
                Rrs = Ring(rs)
                Rpt = Ring(pt, excl=True)
                r_sq = Res()
                r_xnT = Res()
                for t in range(LT):
                    xb, rx, dx = Rx.next()
                    S.dma("sp", lambda e, xb=xb, t=t: e.dma_start(out=xb[:], in_=x_loc[t * 128:(t + 1) * 128, :]),
                          dx, writes=[rx])
                    sb, rss = Rss.next()
                    S.op("act", lambda e, xb=xb, sb=sb: e.activation(out=sq[:], in_=xb[:], func=AF.Square,
                                                                      accum_out=sb[:]),
                         reads=[rx], writes=[r_sq, rss])
                    rb, rrs = Rrs.next()
                    S.op("act", lambda e, sb=sb, rb=rb: e.activation(out=rb[:], in_=sb[:], func=AF.Sqrt,
                                                                     scale=1.0 / D, bias=EPS),
                         reads=[rss], writes=[rrs])
                    S.op("dve", lambda e, rb=rb: e.reciprocal(out=rb[:], in_=rb[:]),
                         reads=[rrs], writes=[rrs])
                    xn, rxn = Rxn.next()
                    S.op("dve", lambda e, xn=xn, xb=xb, rb=rb: e.scalar_tensor_tensor(
                        out=xn[:], in0=xb[:], scalar=rb[:], in1=wbc[:], op0=ALU.mult, op1=ALU.mult),
                        reads=[rx, rrs, r_c], writes=[rxn])
                    pb, rp = Rpt.next()
                    for c in range(16):
                        S.op("pe", lambda e, pb=pb, xn=xn, c=c: e.transpose(
                            out=pb[:, c * 128:(c + 1) * 128], in_=xn[:, c * 128:(c + 1) * 128], identity=identb[:]),
                            reads=[rxn, r_id], writes=[rp])
                    for hh in range(2):
                        eng = "act" if hh == 0 else "dve"
                        if eng == "act":
                            S.op("act", lambda e, pb=pb, t=t, hh=hh: e.activation(
                                out=xnT[:, hh * 8:(hh + 1) * 8, t * 128:(t + 1) * 128],
                                in_=pb[:, hh * 1024:(hh + 1) * 1024].rearrange("p (c q) -> p c q", c=8),
                                func=AF.Copy), reads=[rp], writes=[r_xnT])
                        else:
                            S.op("dve", lambda e, pb=pb, t=t, hh=hh: e.tensor_copy(
                                out=xnT[:, hh * 8:(hh + 1) * 8, t * 128:(t + 1) * 128],
                                in_=pb[:, hh * 1024:(hh + 1) * 1024].rearrange("p (c q) -> p c q", c=8)),
                                reads=[rp], writes=[r_xnT])
                S.emit(block)

            if 0 in phases and 1 not in phases:
                return

            with ExitStack() as st:
                wb = [st.enter_context(nc.sbuf_tensor(f"p1_w{i}", [128, 16, 512], BF16)) for i in range(2)]
                cs = st.enter_context(nc.sbuf_tensor("p1_cs", [128, LT, 128], F32))
                csq = st.enter_context(nc.sbuf_tensor("p1_csq", [128, NQ, 128], F32))
                xf = [st.enter_context(nc.sbuf_tensor(f"p1_xf{i}", [128, 512], F32)) for i in range(2)]
                tmp = [st.enter_context(nc.sbuf_tensor(f"p1_t{i}", [128, 4, 256], F32)) for i in range(2)]
                ob = [st.enter_context(nc.sbuf_tensor(f"p1_o{i}", [128, 512], BF16)) for i in range(3)]
                of = [st.enter_context(nc.sbuf_tensor(f"p1_of{i}", [48, 512], F32)) for i in range(2)]
                ps = [st.enter_context(nc.psum_tensor(f"p1_ps{i}", [128, 512], F32)) for i in range(4)]
                S = Sched(nc, st, "p1")
                block = st.enter_context(nc.Block())
                r_tab = Res()
                d_tab = S.dsem()
                S.dma("sp", lambda e: e.dma_start(out=cs[:], in_=t_cs), d_tab, writes=[r_tab])
                S.dma("sp", lambda e: e.dma_start(out=csq[:], in_=t_csq), d_tab, writes=[r_tab])
                Rw = Ring(wb, S, True)
                Rps = Ring(ps, excl=True)
                Rxf = Ring(xf)
                Rtmp = Ring(tmp)
                Rob = Ring(ob, S, True)
                Rof = Ring(of, S, True)
                r_x = Res()
                alt = [0]

                def load_w(col0, ncols):
                    w, rw, dw = Rw.next()
                    S.dma("pool", lambda e, w=w: e.dma_start(out=w[:, :, 0:ncols], in_=w_in_v[:, :, col0:col0 + ncols]),
                          dw, writes=[rw])
                    return w, rw

                def mm_tm(w, rw, t, c0, n):
                    p, rp = Rps.next()
                    for c in range(16):
                        S.op("pe", lambda e, p=p, c=c: e.matmul(
                            p[:, 0:n], lhsT=xnT[:, c, t * 128:(t + 1) * 128], rhs=w[:, c, c0:c0 + n],
                            start=(c == 0), stop=(c == 15)), reads=[rw, r_x], writes=[rp])
                    return p, rp

                def mm_fm(w, rw, tok0, ntok, c0, m):
                    p, rp = Rps.next()
                    for c in range(16):
                        S.op("pe", lambda e, p=p, c=c: e.matmul(
                            p[0:m, 0:ntok], lhsT=w[:, c, c0:c0 + m], rhs=xnT[:, c, tok0:tok0 + ntok],
                            start=(c == 0), stop=(c == 15)), reads=[rw, r_x], writes=[rp])
                    return p, rp

                def evac_store(p, rp, m, n, dst, func=None):
                    o, ro, do = Rob.next()
                    if func is not None:
                        S.op("act", lambda e: e.activation(out=o[0:m, 0:n], in_=p[0:m, 0:n], func=func),
                             reads=[rp], writes=[ro])
                    else:
                        alt[0] ^= 1
                        if alt[0]:
                            S.op("act", lambda e: e.activation(out=o[0:m, 0:n], in_=p[0:m, 0:n], func=AF.Copy),
                                 reads=[rp], writes=[ro])
                        else:
                            S.op("dve", lambda e: e.tensor_copy(out=o[0:m, 0:n], in_=p[0:m, 0:n]),
                                 reads=[rp], writes=[ro])
                    S.dma("sp", lambda e: e.dma_start(out=dst, in_=o[0:m, 0:n]), do, reads=[ro])

                def rotary(p, rp, ctab, ti, dst):
                    x_, rxf = Rxf.next()
                    S.op("act", lambda e: e.activation(out=x_[:], in_=p[:], func=AF.Copy), reads=[rp], writes=[rxf])
                    xv = x_[:].rearrange("p (h t d) -> p h t d", h=4, t=2)
                    cosb = ctab[:, ti, 0:64][:, None, :].to_broadcast([128, 4, 64])
                    sinb = ctab[:, ti, 64:128][:, None, :].to_broadcast([128, 4, 64])
                    tm_, rt = Rtmp.next()
                    o, ro, do = Rob.next()
                    ov = o[:].rearrange("p (h t d) -> p h t d", h=4, t=2)
                    x1 = xv[:, :, 0, :]
                    x2 = xv[:, :, 1, :]
                    S.op("pool", lambda e: e.tensor_tensor(out=tm_[:, :, 0:64], in0=x1, in1=cosb, op=ALU.mult),
                         reads=[rxf, r_tab], writes=[rt])
                    S.op("pool", lambda e: e.tensor_tensor(out=tm_[:, :, 64:128], in0=x2, in1=sinb, op=ALU.mult),
                         reads=[rxf, r_tab], writes=[rt])
                    S.op("dve", lambda e: e.tensor_tensor(out=tm_[:, :, 128:192], in0=x1, in1=sinb, op=ALU.mult),
                         reads=[rxf, r_tab], writes=[rt])
                    S.op("dve", lambda e: e.tensor_tensor(out=tm_[:, :, 192:256], in0=x2, in1=cosb, op=ALU.mult),
                         reads=[rxf, r_tab], writes=[rt])
                    S.op("pool", lambda e: e.tensor_tensor(out=ov[:, :, 0, :], in0=tm_[:, :, 0:64],
                                                           in1=tm_[:, :, 64:128], op=ALU.subtract),
                         reads=[rt], writes=[ro])
                    S.op("dve", lambda e: e.tensor_tensor(out=ov[:, :, 1, :], in0=tm_[:, :, 128:192],
                                                          in1=tm_[:, :, 192:256], op=ALU.add),
                         reads=[rt], writes=[ro])
                    S.dma("sp", lambda e: e.dma_start(out=dst, in_=o[:]), do, reads=[ro])

                qtiles = list(range(QT0, LT))
                qgroups = [(QT0 * 128, 128)] + [((16 + 4 * g) * 128, 512) for g in range(4)]
                agroups = [(g * 512, 512) for g in range(8)]

                for blk in range(2):
                    w, rw = load_w(1024 + blk * 512, 512)
                    for t in range(LT):
                        p, rp = mm_tm(w, rw, t, 0, 512)
                        rotary(p, rp, cs, t, RK[t * 128:(t + 1) * 128, blk * 512:(blk + 1) * 512])
                for blk in range(2):
                    w, rw = load_w(blk * 512, 512)
                    for t in qtiles:
                        p, rp = mm_tm(w, rw, t, 0, 512)
                        qi = t - QT0
                        rotary(p, rp, csq, qi, RQ[qi * 128:(qi + 1) * 128, blk * 512:(blk + 1) * 512])
                for blk in range(4):
                    w, rw = load_w(2048 + blk * 512, 512)
                    for t in range(LT):
                        p, rp = mm_tm(w, rw, t, 0, 512)
                        evac_store(p, rp, 128, 512, RV[t * 128:(t + 1) * 128, blk * 512:(blk + 1) * 512])
                w, rw = load_w(8192, 512)
                for (dst, c0) in ((KCT, 0), (VCT, 256)):
                    for g in range(2):
                        for tok0, ntok in agroups:
                            p, rp = mm_fm(w, rw, tok0, ntok, c0 + g * 128, 128)
                            evac_store(p, rp, 128, ntok, dst[g, :, tok0:tok0 + ntok])
                for (col0, dfm, dtm) in ((8704, KST, VS), (9216, KWT, VW)):
                    w, rw = load_w(col0, 512)
                    for g in range(2):
                        for tok0, ntok in agroups:
                            p, rp = mm_fm(w, rw, tok0, ntok, g * 128, 128)
                            evac_store(p, rp, 128, ntok, dfm[g, :, tok0:tok0 + ntok])
                    for t in range(LT):
                        p, rp = mm_tm(w, rw, t, 256, 256)
                        evac_store(p, rp, 128, 256, dtm[t * 128:(t + 1) * 128, :])
                for blk in range(4):
                    w, rw = load_w(6144 + blk * 512, 512)
                    for ch in range(4):
                        for tok0, ntok in qgroups:
                            p, rp = mm_fm(w, rw, tok0, ntok, ch * 128, 128)
                            q0 = tok0 - QT0 * 128
                            evac_store(p, rp, 128, ntok, NQT_[blk * 4 + ch, :, q0:q0 + ntok])
                for blk in range(4):
                    w, rw = load_w(4096 + blk * 512, 512)
                    for t in qtiles:
                        p, rp = mm_tm(w, rw, t, 0, 512)
                        qi = t - QT0
                        evac_store(p, rp, 128, 512, RG[qi * 128:(qi + 1) * 128, blk * 512:(blk + 1) * 512],
                                   func=AF.Silu)
                for blk in range(8):
                    w, rw = load_w(9776 + blk * 512, 512)
                    for ch in range(4):
                        for tok0, ntok in qgroups:
                            p, rp = mm_fm(w, rw, tok0, ntok, ch * 128, 128)
                            q0 = tok0 - QT0 * 128
                            evac_store(p, rp, 128, ntok, MGT[blk * 4 + ch, :, q0:q0 + ntok], func=AF.Sigmoid)
                w, rw = load_w(9728, 48)
                for tok0, ntok in qgroups:
                    p, rp = mm_fm(w, rw, tok0, ntok, 0, 48)
                    q0 = tok0 - QT0 * 128
                    o, ro, do = Rof.next()
                    S.op("act", lambda e, o=o, p=p, ntok=ntok: e.activation(out=o[0:48, 0:ntok], in_=p[0:48, 0:ntok],
                                                                           func=AF.Sigmoid), reads=[rp], writes=[ro])
                    S.dma("sp", lambda e, o=o, q0=q0, ntok=ntok: e.dma_start(out=NGT[:, q0:q0 + ntok],
                                                                            in_=o[0:48, 0:ntok]), do, reads=[ro])
                S.emit(block)

    if 1 in phases:
        phase01()

    def phase2():
        gam = [1.0 - 2.0 ** (-5.0 - h) for h in range(8)]
        with ExitStack() as st:
            sb = lambda name, shape, dt: st.enter_context(nc.sbuf_tensor(name, list(shape), dt))
            identb = sb("p2_id", [128, 128], BF16)
            decT = sb("p2_decT", [128, 8, 128], F32)
            qdec = sb("p2_qdec", [128, 8, 128], F32)
            kdec = sb("p2_kdec", [128, 8], F32)
            retw = sb("p2_retw", [128, D], F32)
            kc_ = [sb(f"p2_k{i}", [128, 1024], BF16) for i in range(2)]
            vc_ = [sb(f"p2_v{i}", [128, 2048], BF16) for i in range(2)]
            qc_ = [sb(f"p2_q{i}", [128, 1024], BF16) for i in range(2)]
            gc_ = [sb(f"p2_g{i}", [128, 2048], BF16) for i in range(2)]
            kd_ = [sb(f"p2_kd{i}", [128, 1024], BF16) for i in range(2)]
            kT_ = [sb(f"p2_kT{i}", [128, 1024], BF16) for i in range(2)]
            qT_ = [sb(f"p2_qT{i}", [128, 1024], BF16) for i in range(2)]
            qTd_ = [sb(f"p2_qTd{i}", [128, 1024], BF16) for i in range(2)]
            ST_ = [sb(f"p2_ST{i}", [128, 512], BF16) for i in range(2)]
            stf = sb("p2_stf", [128, 2048], F32)
            stb = [sb(f"p2_stb{i}", [128, 2048], BF16) for i in range(2)]
            junk = sb("p2_junk", [128, 256], BF16)
            ssq_ = [sb(f"p2_ssq{i}", [128, 4], F32) for i in range(2)]
            tmpf_ = [sb(f"p2_tmpf{i}", [128, 1024], F32) for i in range(2)]
            gat_ = [sb(f"p2_gat{i}", [128, 2048], BF16) for i in range(2)]
            gT_ = [sb(f"p2_gT{i}", [128, 16, 128], BF16) for i in range(2)]
            pkT = st.enter_context(nc.psum_tensor("p2_pkT", [128, 1024], BF16))
            pqT = st.enter_context(nc.psum_tensor("p2_pqT", [128, 1024], BF16))
            pS = st.enter_context(nc.psum_tensor("p2_pS", [128, 512], F32))
            pO = st.enter_context(nc.psum_tensor("p2_pO", [128, 1024], F32))
            pKV = st.enter_context(nc.psum_tensor("p2_pKV", [128, 1024], F32))
            pGT = st.enter_context(nc.psum_tensor("p2_pGT", [128, 1024], BF16))
            S = Sched(nc, st, "p2")
            block = st.enter_context(nc.Block())
            r_tab = Res()
            d_tab = S.dsem()
            r_id = Res()
            S.dma("pool", lambda e: e.dma_start(out=identb[:], in_=t_ident), S.dsem(), writes=[r_id])
            S.dma("sp", lambda e: e.dma_start(out=decT[:], in_=t_decT), d_tab, writes=[r_tab])
            S.dma("sp", lambda e: e.dma_start(out=qdec[:], in_=t_qdec), d_tab, writes=[r_tab])
            S.dma("sp", lambda e: e.dma_start(out=kdec[:], in_=t_kdec), d_tab, writes=[r_tab])
            S.dma("sp", lambda e: e.dma_start(out=retw[:], in_=ret_norm_w.partition_broadcast(128)), d_tab,
                  writes=[r_tab])
            Rk, Rv, Rq, Rg = Ring(kc_, S, True), Ring(vc_, S, True), Ring(qc_, S, True), Ring(gc_, S, True)
            Rkd, RkT, RqT, RqTd, RST = Ring(kd_), Ring(kT_), Ring(qT_), Ring(qTd_), Ring(ST_)
            Rssq, Rtmpf, Rgat = Ring(ssq_), Ring(tmpf_), Ring(gat_)
            RgT = Ring(gT_, S, True)
            r_stf = Res()
            r_stb = [Res(), Res()]
            r_junk = Res()
            r_pkT, r_pqT, r_pS, r_pO, r_pKV, r_pGT = (Res(True) for _ in range(6))
            S.op("dve", lambda e: e.memset(stf[:], 0.0), writes=[r_stf])
            S.op("pool", lambda e: e.memset(stb[0][:], 0.0), writes=[r_stb[0]])
            for c in range(LT):
                k_, rk, dk = Rk.next()
                v_, rv, dv = Rv.next()
                S.dma("sp", lambda e, k_=k_, c=c: e.dma_start(out=k_[:], in_=RK[c * 128:(c + 1) * 128, :]), dk, writes=[rk])
                S.dma("sp", lambda e, v_=v_, c=c: e.dma_start(out=v_[:], in_=RV[c * 128:(c + 1) * 128, :]), dv, writes=[rv])
                sbc, rsb = stb[c % 2], r_stb[c % 2]
                sbn, rsn = stb[(c + 1) % 2], r_stb[(c + 1) % 2]
                if c >= QT0 and os.environ.get('P2_OUT', '1') == '1':
                    qi = c - QT0
                    q_, rq, dq = Rq.next()
                    g_, rg, dg = Rg.next()
                    S.dma("sp", lambda e, q_=q_, qi=qi: e.dma_start(out=q_[:], in_=RQ[qi * 128:(qi + 1) * 128, :]), dq, writes=[rq])
                    S.dma("sp", lambda e, g_=g_, qi=qi: e.dma_start(out=g_[:], in_=RG[qi * 128:(qi + 1) * 128, :]), dg, writes=[rg])
                    for h in range(8):
                        S.op("pe", lambda e, k_=k_, h=h: e.transpose(out=pkT[:, h * 128:(h + 1) * 128],
                                                                     in_=k_[:, h * 128:(h + 1) * 128], identity=identb[:]),
                             reads=[rk, r_id], writes=[r_pkT])
                    kT, rkT = RkT.next()
                    S.op("act", lambda e, kT=kT: e.activation(out=kT[:], in_=pkT[:], func=AF.Copy), reads=[r_pkT], writes=[rkT])
                    for h in range(8):
                        S.op("pe", lambda e, q_=q_, h=h: e.transpose(out=pqT[:, h * 128:(h + 1) * 128],
                                                                     in_=q_[:, h * 128:(h + 1) * 128], identity=identb[:]),
                             reads=[rq, r_id], writes=[r_pqT])
                    qT, rqT = RqT.next()
                    qTd, rqTd = RqTd.next()
                    S.op("act", lambda e, qT=qT: e.activation(out=qT[:], in_=pqT[:], func=AF.Copy), reads=[r_pqT], writes=[rqT])
                    S.op("dve", lambda e, qTd=qTd: e.tensor_tensor(
                        out=qTd[:].rearrange("p (h n) -> p h n", h=8), in0=pqT[:].rearrange("p (h n) -> p h n", h=8),
                        in1=qdec[:], op=ALU.mult), reads=[r_pqT, r_tab], writes=[rqTd])
                    gat, rgat = Rgat.next()
                    lvl = int(os.environ.get('P2_LVL', '9'))
                    for hg in (range(2) if lvl >= 2 else []):
                        for j in range(4):
                            h = hg * 4 + j
                            S.op("pe", lambda e, kT=kT, qT=qT, h=h, j=j: e.matmul(
                                pS[:, j * 128:(j + 1) * 128], lhsT=kT[:, h * 128:(h + 1) * 128],
                                rhs=qT[:, h * 128:(h + 1) * 128], start=True, stop=True),
                                reads=[rkT, rqT], writes=[r_pS])
                        ST, rST = RST.next()
                        S.op("dve", lambda e, ST=ST, hg=hg: e.tensor_tensor(
                            out=ST[:].rearrange("p (h n) -> p h n", h=4), in0=pS[:].rearrange("p (h n) -> p h n", h=4),
                            in1=decT[:, hg * 4:(hg + 1) * 4, :], op=ALU.mult), reads=[r_pS, r_tab], writes=[rST])
                        for j in range(4):
                            h = hg * 4 + j
                            S.op("pe", lambda e, ST=ST, v_=v_, h=h, j=j: e.matmul(
                                pO[:, j * 256:(j + 1) * 256], lhsT=ST[:, j * 128:(j + 1) * 128],
                                rhs=v_[:, h * 256:(h + 1) * 256], start=True, stop=False),
                                reads=[rST, rv], writes=[r_pO])
                            S.op("pe", lambda e, qTd=qTd, sbc=sbc, h=h, j=j: e.matmul(
                                pO[:, j * 256:(j + 1) * 256], lhsT=qTd[:, h * 128:(h + 1) * 128],
                                rhs=sbc[:, h * 256:(h + 1) * 256], start=False, stop=True),
                                reads=[rqTd, rsb], writes=[r_pO])
                        if lvl < 3:
                            continue
                        ssq, rssq = Rssq.next()
                        for j in range(4):
                            S.op("act", lambda e, ssq=ssq, j=j: e.activation(
                                out=junk[:], in_=pO[:, j * 256:(j + 1) * 256], func=AF.Square, accum_out=ssq[:, j:j + 1]),
                                reads=[r_pO], writes=[r_junk, rssq])
                        S.op("act", lambda e, ssq=ssq: e.activation(out=ssq[:], in_=ssq[:], func=AF.Sqrt,
                                                                    scale=1.0 / 256, bias=EPS), reads=[rssq], writes=[rssq])
                        S.op("dve", lambda e, ssq=ssq: e.reciprocal(out=ssq[:], in_=ssq[:]), reads=[rssq], writes=[rssq])
                        tmpf, rtf = Rtmpf.next()
                        for j in range(4):
                            h = hg * 4 + j
                            S.op("dve", lambda e, tmpf=tmpf, ssq=ssq, h=h, j=j: e.scalar_tensor_tensor(
                                out=tmpf[:, j * 256:(j + 1) * 256], in0=pO[:, j * 256:(j + 1) * 256],
                                scalar=ssq[:, j:j + 1], in1=retw[:, h * 256:(h + 1) * 256], op0=ALU.mult, op1=ALU.mult),
                                reads=[r_pO, rssq, r_tab], writes=[rtf])
                        if lvl < 4:
                            continue
                        S.op("pool", lambda e, gat=gat, tmpf=tmpf, g_=g_, hg=hg: e.tensor_tensor(
                            out=gat[:, hg * 1024:(hg + 1) * 1024], in0=tmpf[:], in1=g_[:, hg * 1024:(hg + 1) * 1024],
                            op=ALU.mult), reads=[rtf, rg], writes=[rgat])
                    if lvl < 5:
                        continue
                    gT, rgT, dgT = RgT.next()
                    for hh in range(2):
                        for cc in range(8):
                            ch = hh * 8 + cc
                            S.op("pe", lambda e, gat=gat, ch=ch, cc=cc: e.transpose(
                                out=pGT[:, cc * 128:(cc + 1) * 128], in_=gat[:, ch * 128:(ch + 1) * 128],
                                identity=identb[:]), reads=[rgat, r_id], writes=[r_pGT])
                        if hh == 0:
                            S.op("act", lambda e, gT=gT: e.activation(
                                out=gT[:, 0:8, :], in_=pGT[:].rearrange("p (c q) -> p c q", c=8), func=AF.Copy),
                                reads=[r_pGT], writes=[rgT])
                        else:
                            S.op("dve", lambda e, gT=gT: e.tensor_copy(
                                out=gT[:, 8:16, :], in_=pGT[:].rearrange("p (c q) -> p c q", c=8)),
                                reads=[r_pGT], writes=[rgT])
                    S.dma("sp", lambda e, gT=gT, qi=qi: e.dma_start(
                        out=RETGT[:, :, qi * 128:(qi + 1) * 128].rearrange("c p q -> p c q"), in_=gT[:]),
                        dgT, reads=[rgT])
                if c < LT - 1 and os.environ.get('P2_STATE', '1') == '1':
                    kd, rkd = Rkd.next()
                    S.op("pool", lambda e, kd=kd, k_=k_: e.tensor_tensor(
                        out=kd[:].rearrange("p (h d) -> p h d", h=8), in0=k_[:].rearrange("p (h d) -> p h d", h=8),
                        in1=kdec[:, :, None].to_broadcast([128, 8, 128]), op=ALU.mult),
                        reads=[rk, r_tab], writes=[rkd])
                    for hg in range(2):
                        for j in range(4):
                            h = hg * 4 + j
                            S.op("pe", lambda e, kd=kd, v_=v_, h=h, j=j: e.matmul(
                                pKV[:, j * 256:(j + 1) * 256], lhsT=kd[:, h * 128:(h + 1) * 128],
                                rhs=v_[:, h * 256:(h + 1) * 256], start=True, stop=True),
                                reads=[rkd, rv], writes=[r_pKV])
                        for j in range(4):
                            h = hg * 4 + j
                            S.op("dve", lambda e, h=h, j=j: e.scalar_tensor_tensor(
                                out=stf[:, h * 256:(h + 1) * 256], in0=stf[:, h * 256:(h + 1) * 256],
                                scalar=float(gam[h] ** 128), in1=pKV[:, j * 256:(j + 1) * 256],
                                op0=ALU.mult, op1=ALU.add), reads=[r_pKV, r_stf], writes=[r_stf])
                    S.op("act", lambda e, sbn=sbn: e.activation(out=sbn[:], in_=stf[:], func=AF.Copy),
                         reads=[r_stf], writes=[rsn])
            S.emit(block)

    if 2 in phases:
        phase2()

    KCMPT = dscr("KCMPT", [2, 128, 256])
    VCMP = dscr("VCMP", [2, 256, 128])

    def phase3a():
        with ExitStack() as st:
            sb = lambda name, shape, dt: st.enter_context(nc.sbuf_tensor(name, list(shape), dt))
            xc_ = [sb(f"p3a_xc{i}", [128, LT * 128], BF16) for i in range(2)]
            w1b = sb("p3a_w1", [128, 32, 256], BF16)
            peT = sb("p3a_pe", [128, 32], BF16)
            w2b = sb("p3a_w2", [128, 2, 128], BF16)
            cb = sb("p3a_cb", [128, 2], F32)
            gel_ = [sb(f"p3a_gel{i}", [128, 2, 256], BF16) for i in range(2)]
            og_ = [sb(f"p3a_og{i}", [128, 256], BF16) for i in range(2)]
            pc = st.enter_context(nc.psum_tensor("p3a_pc", [128, 2], F32))
            ph_ = [st.enter_context(nc.psum_tensor(f"p3a_ph{i}", [128, 256], F32)) for i in range(2)]
            po_ = [st.enter_context(nc.psum_tensor(f"p3a_po{i}", [128, 256], F32)) for i in range(2)]
            S = Sched(nc, st, "p3a")
            block = st.enter_context(nc.Block())
            Rxc = Ring(xc_, S, True)
            Rgel = Ring(gel_)
            Rog = Ring(og_, S, True)
            Rph = Ring(ph_, excl=True)
            Rpo = Ring(po_, excl=True)
            r_w, r_cb, r_pc = Res(), Res(), Res(True)
            d_w = S.dsem()
            for g_ in gel_:
                S.op("dve", lambda e, g_=g_: e.memset(g_[:], 0.0), writes=[Rgel.res[gel_.index(g_)]])
            for kv in ("k", "v"):
                S.dma("pool", lambda e, kv=kv: e.dma_start(
                    out=w1b[:], in_=cmp_w1[kv].rearrange("(l d) j -> d l j", d=128)), d_w, writes=[r_w])
                S.dma("pool", lambda e, kv=kv: e.dma_start(out=peT[:], in_=cmp_peT[kv]), d_w, writes=[r_w])
                S.dma("pool", lambda e, kv=kv: e.dma_start(
                    out=w2b[:], in_=cmp_w2[kv].rearrange("(c p) d -> p c d", p=128)), d_w, writes=[r_w])
                for jc in range(2):
                    for l in range(32):
                        S.op("pe", lambda e, jc=jc, l=l: e.matmul(
                            pc[:, jc:jc + 1], lhsT=w1b[:, l, jc * 128:(jc + 1) * 128], rhs=peT[:, l:l + 1],
                            start=(l == 0), stop=(l == 31)), reads=[r_w], writes=[r_pc])
                S.op("act", lambda e: e.activation(out=cb[:], in_=pc[:], func=AF.Copy), reads=[r_pc], writes=[r_cb])
                src = KCT if kv == "k" else VCT
                for g in range(2):
                    xc, rxc, dxc = Rxc.next()
                    S.dma("sp", lambda e, xc=xc, g=g, src=src: e.dma_start(out=xc[:], in_=src[g]), dxc, writes=[rxc])
                    gel, rgel = Rgel.next()
                    for jc in range(2):
                        ph, rph = Rph.next()
                        for l in range(32):
                            S.op("pe", lambda e, ph=ph, xc=xc, jc=jc, l=l: e.matmul(
                                ph[:, 0:255], lhsT=w1b[:, l, jc * 128:(jc + 1) * 128],
                                rhs=xc[:, l:l + 16 * 254 + 1:16], start=(l == 0), stop=(l == 31)),
                                reads=[r_w, rxc], writes=[rph])
                        S.op("act", lambda e, ph=ph, gel=gel, jc=jc: e.activation(
                            out=gel[:, jc, 0:255], in_=ph[:, 0:255], func=AF.Gelu_apprx_tanh, bias=cb[:, jc:jc + 1]),
                            reads=[rph, r_cb], writes=[rgel])
                    if kv == "k":
                        po, rpo = Rpo.next()
                        for jc in range(2):
                            S.op("pe", lambda e, po=po, gel=gel, jc=jc: e.matmul(
                                po[:, :], lhsT=w2b[:, jc, :], rhs=gel[:, jc, :], start=(jc == 0), stop=(jc == 1)),
                                reads=[r_w, rgel], writes=[rpo])
                        og, rog, dog = Rog.next()
                        S.op("dve", lambda e, og=og, po=po: e.tensor_copy(out=og[:], in_=po[:]), reads=[rpo], writes=[rog])
                        S.dma("sp", lambda e, og=og, g=g: e.dma_start(out=KCMPT[g], in_=og[:]), dog, reads=[rog])
                    else:
                        for nk in range(2):
                            po, rpo = Rpo.next()
                            for jc in range(2):
                                S.op("pe", lambda e, po=po, gel=gel, jc=jc, nk=nk: e.matmul(
                                    po[:, 0:128], lhsT=gel[:, jc, nk * 128:(nk + 1) * 128], rhs=w2b[:, jc, :],
                                    start=(jc == 0), stop=(jc == 1)), reads=[r_w, rgel], writes=[rpo])
                            og, rog, dog = Rog.next()
                            S.op("dve", lambda e, og=og, po=po: e.tensor_copy(out=og[:, 0:128], in_=po[:, 0:128]),
                                 reads=[rpo], writes=[rog])
                            S.dma("sp", lambda e, og=og, g=g, nk=nk: e.dma_start(
                                out=VCMP[g, nk * 128:(nk + 1) * 128, :], in_=og[:, 0:128]), dog, reads=[rog])
            S.emit(block)

    def phase3b(sfx=""):
        with ExitStack() as st:
            sb = lambda name, shape, dt: st.enter_context(nc.sbuf_tensor(name + sfx, list(shape), dt))
            identb = sb("p3_idb", [128, 128], BF16)
            identf = sb("p3_idf", [128, 128], F32)
            onesb = sb("p3_ones", [128, 128], BF16)
            causb = sb("p3_caus", [128, 4, 128], BF16)
            acausb = sb("p3_acaus", [128, 4, 128], BF16)
            expt = sb("p3_expt", [64, LT, 128], BF16)
            ovb = sb("p3_ov", [128, 2, 65], BF16)
            padb = sb("p3_padb", [128, LT], F32)
            cmpb = sb("p3_cmpb", [128, 2], F32)
            kcm = sb("p3_kcm", [128, 2, 256], BF16)
            vcm = sb("p3_vcm", [128, 2, 2, 128], BF16)
            kst = sb("p3_kst", [128, 2, LT * 128], BF16)
            kwt = sb("p3_kwt", [128, 2, LT * 128], BF16)
            vs = sb("p3_vs", [128, LT, 256], BF16)
            vw = sb("p3_vw", [128, LT, 256], BF16)
            qT_ = [sb(f"p3_qT{i}", [128, 16, 128], BF16) for i in range(2)]
            gbc_ = [sb(f"p3_gbc{i}", [128, 24, 128], F32) for i in range(2)]
            keep_ = [sb(f"p3_keep{i}", [128, 64], F32) for i in range(2)]
            addt_ = [sb(f"p3_add{i}", [128, 64], F32) for i in range(2)]
            cm_ = [sb(f"p3_cm{i}", [128, 2, 128], F32) for i in range(2)]
            Ec_ = [sb(f"p3_Ec{i}", [128, 1024], BF16) for i in range(2)]
            E_ = [sb(f"p3_E{i}", [128, 1024], BF16) for i in range(3)]
            Sm_ = [sb(f"p3_Sm{i}", [128, 1024], F32) for i in range(2)]
            fac_ = [sb(f"p3_fac{i}", [128, 1024], F32) for i in range(2)]
            tmp_ = [sb(f"p3_tmp{i}", [128, 1024], F32) for i in range(2)]
            acc_ = [sb(f"p3_acc{i}", [128, 1024], F32) for i in range(2)]
            accb_ = [sb(f"p3_accb{i}", [128, 1024], BF16) for i in range(2)]
            rs8 = sb("p3_rs8", [128, 8], F32)
            imp = sb("p3_imp", [128, 64], F32)
            imp2 = sb("p3_imp2", [128, 64], F32)
            m8 = sb("p3_m8", [128, 8], F32)
            selb = sb("p3_selb", [128, 64], F32)
            selbT_ = [sb(f"p3_selbT{i}", [64, 4, 128], BF16) for i in range(2)]
            pS_ = [st.enter_context(nc.psum_tensor(f"p3_pS{i}" + sfx, [128, 1024], F32)) for i in range(2)]
            pO = st.enter_context(nc.psum_tensor("p3_pO" + sfx, [128, 1024], F32))
            pSum = st.enter_context(nc.psum_tensor("p3_pSum" + sfx, [128, 1024], F32))
            S = Sched(nc, st, "p3" + sfx)
            block = st.enter_context(nc.Block())
            r_c = Res()
            r_k = Res()
            d_cp, d_cs = S.dsem(), S.dsem()
            for dst, src in ((identb[:], t_ident), (expt[:], t_exp), (ovb[:], t_ov)):
                S.dma("pool", lambda e, dst=dst, src=src: e.dma_start(out=dst, in_=src), d_cp, writes=[r_c])
            for r4 in range(4):
                S.dma("pool", lambda e, r4=r4: e.dma_start(out=causb[:, r4, :], in_=t_caus), d_cp, writes=[r_c])
                S.dma("pool", lambda e, r4=r4: e.dma_start(out=acausb[:, r4, :], in_=t_acaus), d_cp, writes=[r_c])
            S.op("pool", lambda e: e.memset(onesb[:], 1.0), reads=[], writes=[r_c])
            for dst, src in ((identf[:], t_ident), (padb[:], t_padb), (cmpb[:], t_cmpb),
                             (kcm[:], KCMPT.rearrange("g d n -> d g n")),
                             (vcm[:], VCMP.rearrange("g (k p) d -> p g k d", p=128)),
                             (kst[:], KST.rearrange("g d t -> d g t")), (kwt[:], KWT.rearrange("g d t -> d g t")),
                             (vs[:], VS.rearrange("(t p) c -> p t c", p=128)),
                             (vw[:], VW.rearrange("(t p) c -> p t c", p=128))):
                S.dma("sp", lambda e, dst=dst, src=src: e.dma_start(out=dst, in_=src), d_cs, writes=[r_k])
            RqT, Rgbc = Ring(qT_, S, True), Ring(gbc_, S, True)
            Rkeep, Radd, Rcm = Ring(keep_, S, True), Ring(addt_, S, True), Ring(cm_, S, True)
            REc, RE, RSm, Rfac, Rtmp, Racc = Ring(Ec_), Ring(E_), Ring(Sm_), Ring(fac_), Ring(tmp_), Ring(acc_)
            Raccb = Ring(accb_, S, True)
            RselbT = Ring(selbT_)
            RpS = Ring(pS_, excl=True)
            r_pO, r_pSum = Res(True), Res(True)
            r_small = Res()
            d_dbg = S.dsem()
            dbg_t = {}
            if "DBGSEL" in debug:
                dbg_t = {"DBGSEL": dscr("DBGSEL", [NQ, 2, 128, 64], F32), "DBGIMP": dscr("DBGIMP", [NQ, 2, 128, 64], F32),
                         "DBGM8": dscr("DBGM8", [NQ, 2, 128, 8], F32)}

            def finalize(gbc, rgbc, br, acc, racc, first):
                fac, rfac = Rfac.next()
                S.op("dve", lambda e: e.tensor_scalar_max(out=fac[:], in0=pSum[:], scalar1=1e-30),
                     reads=[r_pSum], writes=[rfac])
                S.op("dve", lambda e: e.reciprocal(out=fac[:], in_=fac[:]), reads=[rfac], writes=[rfac])
                S.op("pool", lambda e: e.tensor_tensor(
                    out=fac[:].rearrange("p (h q) -> p h q", h=8), in0=fac[:].rearrange("p (h q) -> p h q", h=8),
                    in1=gbc[:, br::3, :], op=ALU.mult), reads=[rfac, rgbc], writes=[rfac])
                if first:
                    S.op("dve", lambda e: e.tensor_tensor(out=acc[:], in0=pO[:], in1=fac[:], op=ALU.mult),
                         reads=[r_pO, rfac], writes=[racc])
                else:
                    tmp, rtmp = Rtmp.next()
                    S.op("dve", lambda e: e.tensor_tensor(out=tmp[:], in0=pO[:], in1=fac[:], op=ALU.mult),
                         reads=[r_pO, rfac], writes=[rtmp])
                    S.op("pool", lambda e: e.tensor_tensor(out=acc[:], in0=acc[:], in1=tmp[:], op=ALU.add),
                         reads=[rtmp, racc], writes=[racc])

            def attend(kt_list, ksrc, vsrc, g, qTg, rq, selbT, rselbT, i):
                n = len(kt_list)

                def scores(kt):
                    pSx, rpS = RpS.next()
                    for hf in range(2):
                        extra = []
                        if selbT is not None:
                            extra.append((expt[:, kt, :], selbT[:].rearrange("s r q -> s (r q)"), [r_c, rselbT]))
                        if kt == i:
                            extra.append((identb[:], causb[:].rearrange("k r q -> k (r q)"), [r_c]))
                        if selbT is None and kt == i - 4:
                            extra.append((identb[:], acausb[:].rearrange("k r q -> k (r q)"), [r_c]))
                        S.op("pe", lambda e, hf=hf, ne=len(extra): e.matmul(
                            pSx[:, hf * 512:(hf + 1) * 512], lhsT=ksrc[:, g, kt * 128:(kt + 1) * 128],
                            rhs=qTg[:, hf * 512:(hf + 1) * 512], start=True, stop=(ne == 0)),
                            reads=[r_k, rq], writes=[rpS])
                        for xi, (l_, r_, deps) in enumerate(extra):
                            S.op("pe", lambda e, hf=hf, l_=l_, r_=r_, last=(xi == len(extra) - 1): e.matmul(
                                pSx[:, hf * 512:(hf + 1) * 512], lhsT=l_, rhs=r_, start=False, stop=last),
                                reads=deps, writes=[rpS])
                    E, rE = RE.next()
                    S.op("act", lambda e: e.activation(
                        out=E[:], in_=pSx[:], func=AF.Exp, scale=SCALE, bias=padb[:, kt:kt + 1]),
                        reads=[rpS, r_k], writes=[rE])
                    return E, rE

                def values(idx, kt, E, rE):
                    for hf in range(2):
                        S.op("pe", lambda e, hf=hf: e.matmul(
                            pO[:, hf * 512:(hf + 1) * 512], lhsT=vsrc[:, kt, g * 128:(g + 1) * 128],
                            rhs=E[:, hf * 512:(hf + 1) * 512], start=(idx == 0), stop=(idx == n - 1)),
                            reads=[r_k, rE], writes=[r_pO])
                        S.op("pe", lambda e, hf=hf: e.matmul(
                            pSum[:, hf * 512:(hf + 1) * 512], lhsT=onesb[:],
                            rhs=E[:, hf * 512:(hf + 1) * 512], start=(idx == 0), stop=(idx == n - 1)),
                            reads=[r_c, rE], writes=[r_pSum])

                pend = None
                for idx, kt in enumerate(kt_list):
                    cur = (idx, kt) + scores(kt)
                    if pend is not None:
                        values(*pend)
                    pend = cur
                values(*pend)

            for i in range(QT0, LT):
                qi = i - QT0
                qT, rq, dq = RqT.next()
                S.dma("sp", lambda e, qT=qT, qi=qi: e.dma_start(
                    out=qT[:], in_=NQT_[:, :, qi * 128:(qi + 1) * 128].rearrange("h p q -> p h q")), dq, writes=[rq])
                keep, rkeep, dkeep = Rkeep.next()
                addt, radd, dadd = Radd.next()
                cm, rcm, dcm = Rcm.next()
                S.dma("sp", lambda e, keep=keep, qi=qi: e.dma_start(out=keep[:], in_=t_keep[qi]), dkeep, writes=[rkeep])
                S.dma("sp", lambda e, addt=addt, qi=qi: e.dma_start(out=addt[:], in_=t_add[qi]), dadd, writes=[radd])
                for ck in range(2):
                    r0 = 128 * ck - 8 * i + 250
                    S.dma("sp", lambda e, cm=cm, ck=ck, r0=r0: e.dma_start(out=cm[:, ck, :], in_=t_cm[r0:r0 + 128, :]),
                          dcm, writes=[rcm])
                def do_group(i, qi, g, qT, rq, keep, rkeep, addt, radd, cm, rcm):
                    gbc, rgbc, dgbc = Rgbc.next()
                    S.dma("sp", lambda e, gbc=gbc, g=g, qi=qi: e.dma_start(
                        out=gbc[:], in_=NGT[g * 24:(g + 1) * 24, qi * 128:(qi + 1) * 128].unsqueeze(0).to_broadcast(
                            [128, 24, 128])), dgbc, writes=[rgbc])
                    qTg = qT[:, g * 8:(g + 1) * 8, :].rearrange("p h q -> p (h q)")
                    acc, racc = Racc.next()
                    Ecs = []
                    for ck in range(2):
                        pSx, rpS = RpS.next()
                        for hf in range(2):
                            S.op("pe", lambda e, pSx=pSx, hf=hf, ck=ck: e.matmul(
                                pSx[:, hf * 512:(hf + 1) * 512], lhsT=kcm[:, g, ck * 128:(ck + 1) * 128],
                                rhs=qTg[:, hf * 512:(hf + 1) * 512], start=True, stop=True),
                                reads=[r_k, rq], writes=[rpS])
                        Sm, rSm = RSm.next()
                        S.op("dve", lambda e, Sm=Sm, pSx=pSx, ck=ck: e.tensor_tensor(
                            out=Sm[:].rearrange("p (h q) -> p h q", h=8), in0=pSx[:].rearrange("p (h q) -> p h q", h=8),
                            in1=cm[:, ck, :][:, None, :].to_broadcast([128, 8, 128]), op=ALU.add),
                            reads=[rpS, rcm], writes=[rSm])
                        Ec, rEc = REc.next()
                        S.op("act", lambda e, Ec=Ec, Sm=Sm, ck=ck: e.activation(
                            out=Ec[:], in_=Sm[:], func=AF.Exp, scale=SCALE, bias=cmpb[:, ck:ck + 1]),
                            reads=[rSm, r_k], writes=[rEc])
                        Ecs.append((Ec, rEc))
                        for hf in range(2):
                            S.op("pe", lambda e, Ec=Ec, hf=hf, ck=ck: e.matmul(
                                pO[:, hf * 512:(hf + 1) * 512], lhsT=vcm[:, g, ck, :],
                                rhs=Ec[:, hf * 512:(hf + 1) * 512], start=(ck == 0), stop=(ck == 1)),
                                reads=[r_k, rEc], writes=[r_pO])
                            S.op("pe", lambda e, Ec=Ec, hf=hf, ck=ck: e.matmul(
                                pSum[:, hf * 512:(hf + 1) * 512], lhsT=onesb[:],
                                rhs=Ec[:, hf * 512:(hf + 1) * 512], start=(ck == 0), stop=(ck == 1)),
                                reads=[r_c, rEc], writes=[r_pSum])
                    pU, rpU = RpS.next()
                    for h in range(8):
                        for ck in range(2):
                            Ec, rEc = Ecs[ck]
                            o0 = (h // 4) * 512 + (h % 4) * 65
                            S.op("pe", lambda e, pU=pU, Ec=Ec, h=h, ck=ck, o0=o0: e.matmul(
                                pU[:, o0:o0 + 65], lhsT=Ec[:, h * 128:(h + 1) * 128], rhs=ovb[:, ck, :],
                                start=(ck == 0), stop=(ck == 1)), reads=[r_c, rEc], writes=[rpU])
                    for hh in range(2):
                        S.op("dve", lambda e, pU=pU, hh=hh: e.tensor_scalar_max(
                            out=rs8[:, hh * 4:(hh + 1) * 4],
                            in0=pU[:, hh * 512:hh * 512 + 260].rearrange("p (h s) -> p h s", s=65)[:, :, 64],
                            scalar1=1e-30), reads=[rpU], writes=[r_small])
                    S.op("dve", lambda e: e.reciprocal(out=rs8[:], in_=rs8[:]), reads=[r_small], writes=[r_small])
                    for h in range(8):
                        o0 = (h // 4) * 512 + (h % 4) * 65
                        if h == 0:
                            S.op("dve", lambda e, pU=pU, o0=o0: e.tensor_scalar(
                                out=imp[:], in0=pU[:, o0:o0 + 64], scalar1=rs8[:, 0:1], scalar2=None, op0=ALU.mult),
                                reads=[rpU, r_small], writes=[r_small])
                        else:
                            S.op("dve", lambda e, pU=pU, o0=o0, h=h: e.scalar_tensor_tensor(
                                out=imp[:], in0=pU[:, o0:o0 + 64], scalar=rs8[:, h:h + 1], in1=imp[:],
                                op0=ALU.mult, op1=ALU.add), reads=[rpU, r_small], writes=[r_small])
                    S.op("dve", lambda e, keep=keep: e.tensor_tensor(out=imp[:], in0=imp[:], in1=keep[:], op=ALU.mult),
                         reads=[r_small, rkeep], writes=[r_small])
                    S.op("dve", lambda e, addt=addt: e.tensor_tensor(out=imp[:], in0=imp[:], in1=addt[:], op=ALU.add),
                         reads=[r_small, radd], writes=[r_small])
                    S.op("dve", lambda e: e.max(out=m8[:], in_=imp[:]), reads=[r_small], writes=[r_small])
                    S.op("dve", lambda e: e.match_replace(out=imp2[:], in_to_replace=m8[:], in_values=imp[:],
                                                          imm_value=-1e30), reads=[r_small], writes=[r_small])
                    S.op("dve", lambda e: e.max(out=m8[:], in_=imp2[:]), reads=[r_small], writes=[r_small])
                    S.op("dve", lambda e: e.tensor_scalar(out=selb[:], in0=imp[:], scalar1=m8[:, 7:8], scalar2=NEG,
                                                          op0=ALU.is_lt, op1=ALU.mult), reads=[r_small], writes=[r_small])
                    if "DBGSEL" in debug:
                        for nm_, src_ in (("DBGSEL", selb), ("DBGIMP", imp), ("DBGM8", m8)):
                            S.dma("sp", lambda e, nm_=nm_, src_=src_, qi=qi, g=g: e.dma_start(
                                out=dbg_t[nm_][qi, g], in_=src_[:]), d_dbg, reads=[r_small])
                    pT, rpT = RpS.next()
                    S.op("pe", lambda e, pT=pT: e.transpose(out=pT[0:64, 0:128], in_=selb[:, 0:64], identity=identf[:]),
                         reads=[r_small, r_k], writes=[rpT])
                    selbT, rselbT = RselbT.next()
                    S.op("act", lambda e, pT=pT, selbT=selbT: e.activation(
                        out=selbT[:], in_=pT[0:64, 0:128][:, None, :].to_broadcast([64, 4, 128]), func=AF.Copy),
                        reads=[rpT], writes=[rselbT])
                    finalize(gbc, rgbc, 0, acc, racc, True)
                    attend(list(range(0, i + 1)), kst, vs, g, qTg, rq, selbT, rselbT, i)
                    finalize(gbc, rgbc, 1, acc, racc, False)
                    attend(list(range(i - 4, i + 1)), kwt, vw, g, qTg, rq, None, None, i)
                    finalize(gbc, rgbc, 2, acc, racc, False)
                    accb, raccb, daccb = Raccb.next()
                    S.op("act", lambda e, accb=accb, acc=acc: e.activation(out=accb[:], in_=acc[:], func=AF.Copy),
                         reads=[racc], writes=[raccb])
                    S.dma("sp", lambda e, accb=accb, g=g, qi=qi: e.dma_start(
                        out=NSAT[g * 8:(g + 1) * 8, :, qi * 128:(qi + 1) * 128].rearrange("h p q -> p h q"),
                        in_=accb[:].rearrange("p (h q) -> p h q", h=8)), daccb, reads=[raccb])

                for g in range(2):
                    do_group(i, qi, g, qT, rq, keep, rkeep, addt, radd, cm, rcm)
            S.emit(block)

    if 3 in phases:
        phase3a()
        phase3b()
    if 33 in phases:
        phase3b("x")


    QGROUPS = [(0, 128)] + [(128 + 512 * g, 512) for g in range(4)]

    def upproj(tag, srcT, w, gate0, addsrc, dst):
        with ExitStack() as st:
            sb = lambda name, shape, dt: st.enter_context(nc.sbuf_tensor(name, list(shape), dt))
            src = sb(f"{tag}_src", [128, 16, NQT], BF16)
            wb = [sb(f"{tag}_w{i}", [128, 16, 512], BF16) for i in range(2)]
            mg_ = [sb(f"{tag}_mg{i}", [128, NQT], BF16) for i in range(2)]
            m1_ = [sb(f"{tag}_m1{i}", [128, NQT], BF16) for i in range(2)]
            tf_ = [sb(f"{tag}_tf{i}", [128, 512], F32) for i in range(2)]
            o_ = [sb(f"{tag}_o{i}", [128, NQT], BF16) for i in range(2)]
            ps = [st.enter_context(nc.psum_tensor(f"{tag}_ps{i}", [128, 512], F32)) for i in range(4)]
            S = Sched(nc, st, tag)
            block = st.enter_context(nc.Block())
            r_src = Res()
            d_src = S.dsem()
            S.dma("sp", lambda e: e.dma_start(out=src[:], in_=srcT.rearrange("c p q -> p c q")), d_src, writes=[r_src])
            Rw, Rmg, Rm1, Ro = Ring(wb, S, True), Ring(mg_, S, True), Ring(m1_, S, True), Ring(o_, S, True)
            Rtf = Ring(tf_)
            Rps = Ring(ps, excl=True)
            wv = w.rearrange("(c p) n -> p c n", p=128)

            def chunk(wt, rw, ch, dc):
                mg, rmg, dmg = Rmg.next()
                S.dma("sp", lambda e: e.dma_start(out=mg[:], in_=MGT[gate0 + dc]), dmg, writes=[rmg])
                if addsrc is not None:
                    m1, rm1, dm1 = Rm1.next()
                    S.dma("sp", lambda e: e.dma_start(out=m1[:], in_=addsrc[dc]), dm1, writes=[rm1])
                o, ro, do = Ro.next()
                for (q0, nt) in QGROUPS:
                    p, rp = Rps.next()
                    for c in (range(16) if not os.environ.get("SKIPMM") else range(1)):
                        S.op("pe", lambda e, p=p, c=c, q0=q0, nt=nt: e.matmul(
                            p[:, 0:nt], lhsT=wt[:, c, ch * 128:(ch + 1) * 128], rhs=src[:, c, q0:q0 + nt],
                            start=(c == 0), stop=(c == 15)), reads=[rw, r_src], writes=[rp])
                    if addsrc is None:
                        S.op("dve", lambda e, p=p, q0=q0, nt=nt: e.tensor_tensor(
                            out=o[:, q0:q0 + nt], in0=p[:, 0:nt], in1=mg[:, q0:q0 + nt], op=ALU.mult),
                            reads=[rp, rmg], writes=[ro])
                    else:
                        tf, rtf = Rtf.next()
                        S.op("dve", lambda e, p=p, q0=q0, nt=nt, tf=tf: e.tensor_tensor(
                            out=tf[:, 0:nt], in0=p[:, 0:nt], in1=mg[:, q0:q0 + nt], op=ALU.mult),
                            reads=[rp, rmg], writes=[rtf])
                        S.op("pool", lambda e, q0=q0, nt=nt, tf=tf: e.tensor_tensor(
                            out=o[:, q0:q0 + nt], in0=tf[:, 0:nt], in1=m1[:, q0:q0 + nt], op=ALU.add),
                            reads=[rtf, rm1], writes=[ro])
                S.dma("sp", lambda e: e.dma_start(out=dst[dc], in_=o[:]), do, reads=[ro])

            def wblock(blk):
                wt, rw, dw = Rw.next()
                S.dma("pool", lambda e: e.dma_start(out=wt[:], in_=wv[:, :, blk * 512:(blk + 1) * 512]), dw, writes=[rw])
                for ch in range(4):
                    chunk(wt, rw, ch, blk * 4 + ch)

            for blk in range(4):
                wblock(blk)
            S.emit(block)

    def phase4b():
        with ExitStack() as st:
            sb = lambda name, shape, dt: st.enter_context(nc.sbuf_tensor(name, list(shape), dt))
            identb = sb("p4_id", [128, 128], BF16)
            wo = sb("p4_wo", [128, 16, D], BF16)
            w2bc = sb("p4_w2", [128, D], F32)
            mT_ = [sb(f"p4_mT{i}", [128, 16, 128], BF16) for i in range(2)]
            x_ = [sb(f"p4_x{i}", [128, D], F32) for i in range(2)]
            h_ = [sb(f"p4_h{i}", [128, D], F32) for i in range(2)]
            xn_ = [sb(f"p4_xn{i}", [128, D], BF16) for i in range(2)]
            xT_ = [sb(f"p4_xT{i}", [128, 16, 128], BF16) for i in range(2)]
            junk = sb("p4_junk", [128, D], BF16)
            ss_ = [sb(f"p4_ss{i}", [128, 1], F32) for i in range(2)]
            ps = [st.enter_context(nc.psum_tensor(f"p4_ps{i}", [128, 512], F32)) for i in range(4)]
            pt = [st.enter_context(nc.psum_tensor(f"p4_pt{i}", [128, 1024], BF16)) for i in range(2)]
            S = Sched(nc, st, "p4b")
            block = st.enter_context(nc.Block())
            r_id, r_wo, r_w2, r_junk = Res(), Res(), Res(), Res()
            d_p, d_s = S.dsem(), S.dsem()
            S.dma("pool", lambda e: e.dma_start(out=identb[:], in_=t_ident), d_p, writes=[r_id])
            wov = w_out.rearrange("(c p) n -> p c n", p=128)
            for blk in range(4):
                S.dma("pool", lambda e, blk=blk: e.dma_start(out=wo[:, :, blk * 512:(blk + 1) * 512],
                                                           in_=wov[:, :, blk * 512:(blk + 1) * 512]), d_p, writes=[r_wo])
            S.dma("sp", lambda e: e.dma_start(out=w2bc[:], in_=norm2_w.partition_broadcast(128)), d_s, writes=[r_w2])
            RmT, Rx, Rh, RxT = Ring(mT_, S, True), Ring(x_, S, True), Ring(h_, S, True), Ring(xT_, S, True)
            Rxn, Rss = Ring(xn_), Ring(ss_)
            Rps = Ring(ps, excl=True)
            r_pt = [Res(True), Res(True)]

            def tile(qi):
                mT, rmT, dmT = RmT.next()
                S.dma("sp", lambda e: e.dma_start(out=mT[:], in_=MRGT[:, :, qi * 128:(qi + 1) * 128].rearrange("c p q -> p c q")),
                      dmT, writes=[rmT])
                xt, rx, dx = Rx.next()
                S.dma("sp", lambda e: e.dma_start(out=xt[:], in_=x_loc[(QT0 + qi) * 128:(QT0 + qi + 1) * 128, :]), dx, writes=[rx])
                h, rh, dh = Rh.next()
                for cb in range(4):
                    p, rp = Rps.next()
                    for c in (range(16) if not os.environ.get("SKIPMM") else range(1)):
                        S.op("pe", lambda e, p=p, c=c, cb=cb: e.matmul(
                            p[:], lhsT=mT[:, c, :], rhs=wo[:, c, cb * 512:(cb + 1) * 512],
                            start=(c == 0), stop=(c == 15)), reads=[rmT, r_wo], writes=[rp])
                    S.op("dve", lambda e, p=p, cb=cb: e.tensor_tensor(
                        out=h[:, cb * 512:(cb + 1) * 512], in0=p[:], in1=xt[:, cb * 512:(cb + 1) * 512], op=ALU.add),
                        reads=[rp, rx], writes=[rh])
                S.dma("sp", lambda e: e.dma_start(out=H1[qi * 128:(qi + 1) * 128, :], in_=h[:]), dh, reads=[rh])
                ss, rss = Rss.next()
                S.op("act", lambda e: e.activation(out=junk[:], in_=h[:], func=AF.Square, accum_out=ss[:]),
                     reads=[rh], writes=[r_junk, rss])
                S.op("act", lambda e: e.activation(out=ss[:], in_=ss[:], func=AF.Sqrt, scale=1.0 / D, bias=EPS),
                     reads=[rss], writes=[rss])
                S.op("dve", lambda e: e.reciprocal(out=ss[:], in_=ss[:]), reads=[rss], writes=[rss])
                xn, rxn = Rxn.next()
                S.op("dve", lambda e: e.scalar_tensor_tensor(out=xn[:], in0=h[:], scalar=ss[:], in1=w2bc[:],
                                                             op0=ALU.mult, op1=ALU.mult),
                     reads=[rh, rss, r_w2], writes=[rxn])
                xT, rxT, dxT = RxT.next()
                for hh in range(2):
                    for cc in range(8):
                        c = hh * 8 + cc
                        S.op("pe", lambda e, c=c, cc=cc, hh=hh: e.transpose(
                            out=pt[hh][:, cc * 128:(cc + 1) * 128], in_=xn[:, c * 128:(c + 1) * 128], identity=identb[:]),
                            reads=[rxn, r_id], writes=[r_pt[hh]])
                    if hh == 0:
                        S.op("act", lambda e, hh=hh: e.activation(
                            out=xT[:, 0:8, :], in_=pt[0][:].rearrange("p (c q) -> p c q", c=8), func=AF.Copy),
                            reads=[r_pt[0]], writes=[rxT])
                    else:
                        S.op("dve", lambda e, hh=hh: e.tensor_copy(
                            out=xT[:, 8:16, :], in_=pt[1][:].rearrange("p (c q) -> p c q", c=8)),
                            reads=[r_pt[1]], writes=[rxT])
                S.dma("sp", lambda e: e.dma_start(
                    out=XN2T[:, :, qi * 128:(qi + 1) * 128].rearrange("c p q -> p c q"), in_=xT[:]), dxT, reads=[rxT])

            for qi in range(NQ):
                tile(qi)
            S.emit(block)

    if 4 in phases:
        upproj("p4a", RETGT, w_ret_up, 0, None, M1T)
        upproj("p4n", NSAT, w_nsa_up, 16, M1T, MRGT)
        phase4b()

    def phase5a():
        with ExitStack() as st:
            sb = lambda name, shape, dt: st.enter_context(nc.sbuf_tensor(name, list(shape), dt))
            xs = sb("p5_xs", [128, 16, NQT], BF16)
            wa_ = [sb(f"p5_wa{i}", [128, 16, 512], BF16) for i in range(2)]
            wb_ = [sb(f"p5_wb{i}", [128, 16, 512], BF16) for i in range(2)]
            cw = sb("p5_cw", [128, 88, 3], F32)
            cbias = sb("p5_cb", [128, 88], F32)
            halo = sb("p5_halo", [128, 1], F32)
            u_ = [sb(f"p5_u{i}", [128, 2050], F32) for i in range(4)]
            y_ = [sb(f"p5_y{i}", [128, 2048], F32) for i in range(3)]
            o_ = [sb(f"p5_o{i}", [128, 2048], BF16) for i in range(2)]
            ps = [st.enter_context(nc.psum_tensor(f"p5_ps{i}", [128, 512], F32)) for i in range(6)]
            ph = [st.enter_context(nc.psum_tensor(f"p5_ph{i}", [128, 2], F32)) for i in range(2)]
            S = Sched(nc, st, "p5a")
            block = st.enter_context(nc.Block())
            r_c = Res()
            d_c = S.dsem()
            S.dma("sp", lambda e: e.dma_start(out=xs[:], in_=XN2T.rearrange("c p q -> p c q")), d_c, writes=[r_c])
            S.dma("sp", lambda e: e.dma_start(out=cw[:], in_=conv_wT), d_c, writes=[r_c])
            S.dma("sp", lambda e: e.dma_start(out=cbias[:], in_=conv_bT), d_c, writes=[r_c])
            S.dma("sp", lambda e: e.dma_start(out=halo[:], in_=t_halo), d_c, writes=[r_c])
            Rwa, Rwb = Ring(wa_, S, True), Ring(wb_, S, True)
            Ru, Ry = Ring(u_), Ring(y_)
            Ro = Ring(o_, S, True)
            Rps, Rph = Ring(ps, excl=True), Ring(ph, excl=True)
            wv = w_ffn_up.rearrange("(c p) n -> p c n", p=128)

            def half(wt, rw, ch, cidx, ceng):
                u, ru = Ru.next()
                p, rp = Rph.next()
                for c in range(16):
                    S.op("pe", lambda e, c=c: e.matmul(p[:, 0:2], lhsT=wt[:, c, ch * 128:(ch + 1) * 128],
                                                       rhs=xs[:, c, 126:128], start=(c == 0), stop=(c == 15)),
                         reads=[rw, r_c], writes=[rp])
                S.op("act", lambda e: e.activation(out=u[:, 0:2], in_=p[:, 0:2], func=AF.Copy, scale=halo[:]),
                     reads=[rp, r_c], writes=[ru])
                for g in range(4):
                    pp, rpp = Rps.next()
                    for c in range(16):
                        S.op("pe", lambda e, c=c, pp=pp, g=g: e.matmul(
                            pp[:], lhsT=wt[:, c, ch * 128:(ch + 1) * 128],
                            rhs=xs[:, c, 128 + g * 512:128 + (g + 1) * 512], start=(c == 0), stop=(c == 15)),
                            reads=[rw, r_c], writes=[rpp])
                    S.op("act", lambda e, pp=pp, g=g: e.activation(out=u[:, 2 + g * 512:2 + (g + 1) * 512], in_=pp[:],
                                                                   func=AF.Copy), reads=[rpp], writes=[ru])
                y, ry = Ry.next()
                S.op(ceng, lambda e: e.tensor_scalar(out=y[:], in0=u[:, 2:2050], scalar1=cw[:, cidx, 2:3],
                                                     scalar2=cbias[:, cidx:cidx + 1], op0=ALU.mult, op1=ALU.add),
                     reads=[ru, r_c], writes=[ry])
                S.op(ceng, lambda e: e.scalar_tensor_tensor(out=y[:], in0=u[:, 1:2049], scalar=cw[:, cidx, 1:2], in1=y[:],
                                                            op0=ALU.mult, op1=ALU.add), reads=[ru, r_c, ry], writes=[ry])
                S.op(ceng, lambda e: e.scalar_tensor_tensor(out=y[:], in0=u[:, 0:2048], scalar=cw[:, cidx, 0:1], in1=y[:],
                                                            op0=ALU.mult, op1=ALU.add), reads=[ru, r_c, ry], writes=[ry])
                return y, ry

            def chunk(wa, rwa, wb, rwb, ch, j):
                ya, rya = half(wa, rwa, ch, j, "dve")
                yb, ryb = half(wb, rwb, ch, 44 + j, "dve")
                S.op("act", lambda e: e.activation(out=ya[:], in_=ya[:], func=AF.Silu), reads=[rya], writes=[rya])
                o, ro, do = Ro.next()
                S.op("pool", lambda e: e.tensor_tensor(out=o[:], in0=ya[:], in1=yb[:], op=ALU.mult),
                     reads=[rya, ryb], writes=[ro])
                S.dma("sp", lambda e: e.dma_start(out=ACTT[j], in_=o[:]), do, reads=[ro])

            def wblock(jb):
                wa, rwa, dwa = Rwa.next()
                wb, rwb, dwb = Rwb.next()
                S.dma("pool", lambda e: e.dma_start(out=wa[:], in_=wv[:, :, jb * 512:(jb + 1) * 512]), dwa, writes=[rwa])
                S.dma("pool", lambda e: e.dma_start(out=wb[:], in_=wv[:, :, DFF + jb * 512:DFF + (jb + 1) * 512]), dwb,
                      writes=[rwb])
                for ch in range(4):
                    chunk(wa, rwa, wb, rwb, ch, jb * 4 + ch)

            for jb in range(11):
                wblock(jb)
            S.emit(block)

    def phase5b():
        with ExitStack() as st:
            sb = lambda name, shape, dt: st.enter_context(nc.sbuf_tensor(name, list(shape), dt))
            wd_ = [sb(f"p5b_w{i}", [128, 44, 512], BF16) for i in range(2)]
            a_ = [sb(f"p5b_a{i}", [128, 44, 128], BF16) for i in range(3)]
            h_ = [sb(f"p5b_h{i}", [128, 512], F32) for i in range(3)]
            ps = [st.enter_context(nc.psum_tensor(f"p5b_ps{i}", [128, 512], F32)) for i in range(4)]
            S = Sched(nc, st, "p5b")
            block = st.enter_context(nc.Block())
            Rw, Ra, Rh = Ring(wd_, S, True), Ring(a_, S, True), Ring(h_, S, True)
            r_hs = [Res() for _ in h_]
            d_hs = [S.dsem() for _ in h_]
            Rps = Ring(ps, excl=True)
            wv = w_ffn_down.rearrange("(j p) n -> p j n", p=128)

            def tile(wt, rw, cb, t):
                a, ra, da = Ra.next()
                S.dma("sp", lambda e: e.dma_start(out=a[:], in_=ACTT[:, :, t * 128:(t + 1) * 128].rearrange("j p q -> p j q")),
                      da, writes=[ra])
                h, rh, dh = Rh.next()
                k = Rh.i
                S.dma("sp", lambda e: e.dma_start(out=h[:], in_=H1[(t + 1) * 128:(t + 2) * 128, cb * 512:(cb + 1) * 512]),
                      dh, writes=[rh])
                p, rp = Rps.next()
                for j in (range(44) if not os.environ.get("SKIPMM") else range(1)):
                    S.op("pe", lambda e, j=j: e.matmul(p[:], lhsT=a[:, j, :], rhs=wt[:, j, :], start=(j == 0), stop=(j == 43)),
                         reads=[ra, rw], writes=[rp])
                S.op("dve", lambda e: e.tensor_tensor(out=h[:], in0=p[:], in1=h[:], op=ALU.add), reads=[rp, rh], writes=[rh])
                S.dma("sp", lambda e: e.dma_start(out=H2[t * 128:(t + 1) * 128, cb * 512:(cb + 1) * 512], in_=h[:]),
                      d_hs[k], reads=[rh], writes=[r_hs[k]])

            def wblock(cb):
                wt, rw, dw = Rw.next()
                for q4 in range(4):
                    S.dma("pool", lambda e, q4=q4: e.dma_start(out=wt[:, q4 * 11:(q4 + 1) * 11, :],
                                                             in_=wv[:, q4 * 11:(q4 + 1) * 11, cb * 512:(cb + 1) * 512]),
                          dw, writes=[rw])
                for t in range(16):
                    tile(wt, rw, cb, t)

            for cb in range(4):
                wblock(cb)
            S.emit(block)

    def phase5c():
        with ExitStack() as st:
            sb = lambda name, shape, dt: st.enter_context(nc.sbuf_tensor(name, list(shape), dt))
            wf = sb("p5c_wf", [128, D], F32)
            h_ = [sb(f"p5c_h{i}", [128, D], F32) for i in range(2)]
            o_ = [sb(f"p5c_o{i}", [128, D], F32) for i in range(2)]
            junk = sb("p5c_junk", [128, D], BF16)
            ss_ = [sb(f"p5c_ss{i}", [128, 1], F32) for i in range(2)]
            S = Sched(nc, st, "p5c")
            block = st.enter_context(nc.Block())
            r_w, r_junk = Res(), Res()
            S.dma("sp", lambda e: e.dma_start(out=wf[:], in_=final_norm_w.partition_broadcast(128)), S.dsem(), writes=[r_w])
            Rh, Ro, Rss = Ring(h_, S, True), Ring(o_, S, True), Ring(ss_)

            def tile(t):
                h, rh, dh = Rh.next()
                S.dma("sp", lambda e: e.dma_start(out=h[:], in_=H2[t * 128:(t + 1) * 128, :]), dh, writes=[rh])
                ss, rss = Rss.next()
                S.op("act", lambda e: e.activation(out=junk[:], in_=h[:], func=AF.Square, accum_out=ss[:]),
                     reads=[rh], writes=[r_junk, rss])
                S.op("act", lambda e: e.activation(out=ss[:], in_=ss[:], func=AF.Sqrt, scale=1.0 / D, bias=EPS),
                     reads=[rss], writes=[rss])
                S.op("dve", lambda e: e.reciprocal(out=ss[:], in_=ss[:]), reads=[rss], writes=[rss])
                o, ro, do = Ro.next()
                S.op("dve", lambda e: e.scalar_tensor_tensor(out=o[:], in0=h[:], scalar=ss[:], in1=wf[:],
                                                             op0=ALU.mult, op1=ALU.mult), reads=[rh, rss, r_w], writes=[ro])
                S.dma("sp", lambda e: e.dma_start(out=out[t * 128:(t + 1) * 128, :], in_=o[:]), do, reads=[ro])

            for t in range(16):
                tile(t)
            S.emit(block)

    if 5 in phases:
        phase5a()
        phase5b()
        phase5c()

    return nc


def _tables(s):
    pad = 2048 if s == 0 else 0
    tl = np.arange(LT * 128)
    act = np.maximum(tl - pad, 0).astype(np.float64)
    is_pad = tl < pad
    half = 64
    freq = 10000.0 ** (-np.arange(half, dtype=np.float64) / half)
    ang = act[:, None] * freq[None, :]
    cs = np.concatenate([np.cos(ang), np.sin(ang)], axis=1)
    t_cs = cs.reshape(LT, 128, 128).transpose(1, 0, 2).astype(np.float32)
    t_csq = (cs[QT0 * 128:] * (128 ** -0.5)).reshape(NQ, 128, 128).transpose(1, 0, 2).astype(np.float32)
    gam = 1.0 - 2.0 ** (-5.0 - np.arange(8, dtype=np.float64))
    j = np.arange(128, dtype=np.float64)
    rel = j[None, :] - j[:, None]
    decT = np.where(rel[:, None, :] >= 0, gam[None, :, None] ** np.maximum(rel[:, None, :], 0), 0.0)
    qdec = np.broadcast_to((gam[:, None] ** (j[None, :] + 1.0))[None], (128, 8, 128))
    kdec = gam[None, :] ** (127.0 - j[:, None])
    padb = np.where(is_pad, NEG, 0.0).reshape(LT, 128).T
    n = np.arange(256)
    cmp_invalid = (n * 16 < pad) | (n >= 255)
    cmpb = np.where(cmp_invalid, NEG, 0.0).reshape(2, 128).T
    r = np.arange(512)[:, None]
    q = np.arange(128)[None, :]
    cm = np.where(16 * (r - 250) + 31 <= q, 0.0, NEG)
    keep = np.ones((NQ, 128, 64))
    add = np.zeros((NQ, 128, 64))
    blk = np.arange(64)[None, :]
    b0 = pad // 64
    for qi in range(NQ):
        t = (QT0 + qi) * 128 + np.arange(128)
        cur = (t // 64)[:, None]
        forced = (blk == b0) | (blk == cur) | (blk == cur - 1)
        neg = (blk > cur) | (blk < b0)
        keep[qi] = np.where(forced | neg, 0.0, 1.0)
        add[qi] = np.where(neg, -1e4, np.where(forced, 1e4, 0.0))
    sidx = np.arange(64)[:, None, None]
    kt = np.arange(LT)[None, :, None]
    kk = np.arange(128)[None, None, :]
    texp = (sidx == 2 * kt + (kk >= 64)).astype(np.float32)
    k_ = np.arange(128)[:, None]
    caus = np.where(k_ > q, NEG, 0.0)
    acaus = np.where(k_ <= q, NEG, 0.0)
    nn = np.arange(256)[:, None]
    ss = np.arange(64)[None, :]
    ov = ((nn * 16 < ss * 64 + 64) & (nn * 16 + 32 > ss * 64)).astype(np.float64)
    ov = np.concatenate([ov, np.ones((256, 1))], axis=1)
    ov[255] = 0.0
    t_ov = ov.reshape(2, 128, 65).transpose(1, 0, 2)
    f = lambda a: np.ascontiguousarray(a, dtype=np.float32)
    return {
        "t_cs": f(t_cs), "t_csq": f(t_csq), "t_decT": f(decT), "t_qdec": f(qdec), "t_kdec": f(kdec),
        "t_padb": f(padb), "t_cmpb": f(cmpb), "t_cm": f(cm), "t_keep": f(keep), "t_add": f(add),
        "t_exp": f(texp), "t_caus": f(caus), "t_acaus": f(acaus), "t_ident": f(np.eye(128)),
        "t_ov": f(t_ov), "t_halo": f(np.full((128, 1), float(s))),
    }


def make_in_maps(inputs):
    g = lambda k: np.asarray(inputs[k], dtype=np.float32)
    x = g("x")
    shared = {
        "norm1_w": g("norm1_w")[0][None], "w_in": g("w_in")[0], "ret_norm_w": g("ret_norm_w")[0][None],
        "w_ret_up": g("w_ret_up")[0],
        "cmp_peT_k": np.ascontiguousarray(g("cmp_pe_k")[0].T), "cmp_peT_v": np.ascontiguousarray(g("cmp_pe_v")[0].T),
        "cmp_w1_k": g("cmp_w1_k")[0], "cmp_w1_v": g("cmp_w1_v")[0],
        "cmp_w2_k": g("cmp_w2_k")[0], "cmp_w2_v": g("cmp_w2_v")[0],
        "w_nsa_up": g("w_nsa_up")[0], "w_out": g("w_out")[0], "norm2_w": g("norm2_w")[0][None],
        "w_ffn_up": g("w_ffn_up")[0],
        "conv_wT": np.ascontiguousarray(g("conv_w")[0].reshape(3, 88, 128).transpose(2, 1, 0)),
        "conv_bT": np.ascontiguousarray(g("conv_b")[0].reshape(88, 128).T),
        "w_ffn_down": g("w_ffn_down")[0], "final_norm_w": g("final_norm_w")[None],
    }
    tabs = [_tables(0), _tables(1)]
    maps = []
    for c in range(8):
        b, s = c // 2, c % 2
        if s == 1:
            xl = np.ascontiguousarray(x[b])
        else:
            xl = np.concatenate([np.zeros((2048, D), np.float32), x[b, :2048]], axis=0)
        m = dict(shared)
        m.update(tabs[s])
        m["x_loc"] = xl
        maps.append(m)
    return maps


_NC = None


def kernel(**inputs):
    global _NC
    if _NC is None:
        _NC = build_program()
    maps = make_in_maps(inputs)
    res = run_bass_kernel_spmd(_NC, maps, core_ids=list(range(8)))
    outp = np.zeros((4, 4096, D), np.float32)
    for c in range(8):
        b, s = c // 2, c % 2
        outp[b, s * 2048:(s + 1) * 2048] = res.results[c]["out"]
    return outp
```

```python
import math
import os
from contextlib import ExitStack
import numpy as np
import concourse.bass as bass
import concourse.mybir as mybir
from concourse.bass_utils import run_bass_kernel_spmd

F32 = mybir.dt.float32
BF16 = mybir.dt.bfloat16
AF = mybir.ActivationFunctionType
ALU = mybir.AluOpType
AX = mybir.AxisListType

D = 2048
LT = 32
QT0 = 15
NQ = LT - QT0
NQT = NQ * 128
IN_COLS = 13872
DFF = 5632
EPS = 1e-6
NEG = -30000.0
SCALE = 128 ** -0.5


class Tok:
    __slots__ = ("sem", "val")

    def __init__(self, sem, val):
        self.sem = sem
        self.val = val


class Res:
    __slots__ = ("w", "r", "excl")

    def __init__(self, excl=False):
        self.w = None
        self.r = []
        self.excl = excl


class DSem:
    __slots__ = ("sem", "val", "eng")

    def __init__(self, sem):
        self.sem = sem
        self.val = 0
        self.eng = None


ENGS = ("pe", "act", "dve", "pool", "sp")


class Sched:
    def __init__(self, nc, stack, tag):
        self.nc = nc
        self.q = {e: [] for e in ENGS}
        self.cnt = {e: 0 for e in ENGS}
        self.allsems = []
        self.sem = {e: self._alloc(f"{tag}_s_{e}") for e in ENGS}
        self.waited = {e: {} for e in ENGS}
        self.dsems = []
        self.tag = tag
        stack.callback(self._cleanup)

    def _alloc(self, name):
        h = self.nc.alloc_semaphore(name=name)
        self.allsems.append(h)
        return h

    def _cleanup(self):
        self.nc.clear_and_free_semaphores(self.allsems)
        self.nc.all_engine_barrier()

    def dsem(self):
        d = DSem(self._alloc(f"{self.tag}_d{len(self.dsems)}"))
        self.dsems.append(d)
        return d

    def _deps(self, eng, reads, writes):
        need = {}

        def add(t):
            k = id(t.sem)
            if k not in need or need[k].val < t.val:
                need[k] = t

        for r in reads:
            if r.w is not None:
                add(r.w)
        for w in writes:
            if w.w is not None:
                add(w.w)
            for t in w.r:
                add(t)
        waits = []
        wd = self.waited[eng]
        own = id(self.sem[eng])
        for k, t in need.items():
            if eng == "pe" and k == own:
                continue
            if wd.get(k, 0) < t.val:
                wd[k] = t.val
                waits.append((t.sem, t.val))
        return waits

    def _fin(self, tok, reads, writes):
        for r in reads:
            r.r.append(tok)
        for w in writes:
            w.w = tok
            w.r = []
        return tok

    def op(self, eng, fn, reads=(), writes=()):
        ex = [r for r in reads if r.excl]
        if ex:
            reads = [r for r in reads if not r.excl]
            writes = list(writes) + ex
        waits = self._deps(eng, reads, writes)
        self.cnt[eng] += 1
        self.q[eng].append((waits, fn, (self.sem[eng], 1)))
        return self._fin(Tok(self.sem[eng], self.cnt[eng]), reads, writes)

    def dma(self, eng, fn, ds, reads=(), writes=()):
        assert ds.eng in (None, eng), "one issuing engine per DMA semaphore"
        ds.eng = eng
        waits = self._deps(eng, reads, writes)
        ds.val += 16
        self.q[eng].append((waits, fn, (ds.sem, 16)))
        return self._fin(Tok(ds.sem, ds.val), reads, writes)

    def emit(self, block):
        finals = [(d.sem, d.val) for d in self.dsems if d.val > 0]

        def run(engobj, name, tail=False):
            for waits, fn, inc in self.q[name]:
                for s, v in waits:
                    engobj.wait_ge(s, v)
                fn(engobj).then_inc(inc[0], inc[1])
            if tail:
                for s, v in finals:
                    engobj.wait_ge(s, v)
                for e in ENGS:
                    if e != name and self.cnt[e] > 0:
                        engobj.wait_ge(self.sem[e], self.cnt[e])

        @block.sync
        def _(eng):
            run(eng, "sp", tail=True)

        @block.tensor
        def _(eng):
            run(eng, "pe")

        @block.scalar
        def _(eng):
            run(eng, "act")

        @block.vector
        def _(eng):
            run(eng, "dve")

        @block.gpsimd
        def _(eng):
            run(eng, "pool")


class Ring:
    def __init__(self, bufs, S=None, with_dsem=False, excl=False):
        self.bufs = bufs
        self.res = [Res(excl) for _ in bufs]
        self.ds = [S.dsem() for _ in bufs] if with_dsem else None
        self.i = -1

    def next(self):
        self.i = (self.i + 1) % len(self.bufs)
        if self.ds is not None:
            return self.bufs[self.i], self.res[self.i], self.ds[self.i]
        return self.bufs[self.i], self.res[self.i]


ALL_PHASES = (1, 2, 3, 4, 5)


def build_program(debug=(), phases=ALL_PHASES):
    nc = bass.Bass("TRN2", target_bir_lowering=False)

    def din(name, shape, dt=F32):
        return nc.dram_tensor(name, list(shape), dt, kind="ExternalInput").ap()

    def dscr(name, shape, dt=BF16):
        kind = "ExternalOutput" if name in debug else "Internal"
        return nc.dram_tensor(name, list(shape), dt, kind=kind).ap()

    x_loc = din("x_loc", [LT * 128, D])
    norm1_w = din("norm1_w", [1, D])
    w_in = din("w_in", [D, IN_COLS])
    ret_norm_w = din("ret_norm_w", [1, D])
    w_ret_up = din("w_ret_up", [D, D])
    cmp_peT = {"k": din("cmp_peT_k", [128, 32]), "v": din("cmp_peT_v", [128, 32])}
    cmp_w1 = {"k": din("cmp_w1_k", [4096, 256]), "v": din("cmp_w1_v", [4096, 256])}
    cmp_w2 = {"k": din("cmp_w2_k", [256, 128]), "v": din("cmp_w2_v", [256, 128])}
    w_nsa_up = din("w_nsa_up", [D, D])
    w_out = din("w_out", [D, D])
    norm2_w = din("norm2_w", [1, D])
    w_ffn_up = din("w_ffn_up", [D, 2 * DFF])
    conv_wT = din("conv_wT", [128, 88, 3])
    conv_bT = din("conv_bT", [128, 88])
    w_ffn_down = din("w_ffn_down", [DFF, D])
    final_norm_w = din("final_norm_w", [1, D])
    t_cs = din("t_cs", [128, LT, 128])
    t_csq = din("t_csq", [128, NQ, 128])
    t_decT = din("t_decT", [128, 8, 128])
    t_qdec = din("t_qdec", [128, 8, 128])
    t_kdec = din("t_kdec", [128, 8])
    t_padb = din("t_padb", [128, LT])
    t_cmpb = din("t_cmpb", [128, 2])
    t_cm = din("t_cm", [512, 128])
    t_keep = din("t_keep", [NQ, 128, 64])
    t_add = din("t_add", [NQ, 128, 64])
    t_exp = din("t_exp", [64, LT, 128])
    t_caus = din("t_caus", [128, 128])
    t_acaus = din("t_acaus", [128, 128])
    t_ident = din("t_ident", [128, 128])
    t_ov = din("t_ov", [128, 2, 65])
    t_halo = din("t_halo", [128, 1])

    out = nc.dram_tensor("out", [16 * 128, D], F32, kind="ExternalOutput").ap()

    RK = dscr("RK", [LT * 128, 1024])
    RV = dscr("RV", [LT * 128, 2048])
    RQ = dscr("RQ", [NQT, 1024])
    RG = dscr("RG", [NQT, 2048])
    NQT_ = dscr("NQT", [16, 128, NQT])
    KCT = dscr("KCT", [2, 128, LT * 128])
    VCT = dscr("VCT", [2, 128, LT * 128])
    KST = dscr("KST", [2, 128, LT * 128])
    KWT = dscr("KWT", [2, 128, LT * 128])
    VS = dscr("VS", [LT * 128, 256])
    VW = dscr("VW", [LT * 128, 256])
    NGT = dscr("NGT", [48, NQT], F32)
    MGT = dscr("MGT", [32, 128, NQT])
    RETGT = dscr("RETGT", [16, 128, NQT])
    NSAT = dscr("NSAT", [16, 128, NQT])
    M1T = dscr("M1T", [16, 128, NQT])
    MRGT = dscr("MRGT", [16, 128, NQT])
    H1 = dscr("H1", [NQT, D], F32)
    XN2T = dscr("XN2T", [16, 128, NQT])
    ACTT = dscr("ACTT", [16, 128, 44, 128])
    H2 = dscr("H2", [2048, D], F32)

    w_in_v = w_in.rearrange("(c p) n -> p c n", p=128)

    def phase01():
        with ExitStack() as st01:
            xnT = st01.enter_context(nc.sbuf_tensor("xnT", [128, 16, LT * 128], BF16))
            identb = st01.enter_context(nc.sbuf_tensor("identb", [128, 128], BF16))

            with ExitStack() as st:
                xin = [st.enter_context(nc.sbuf_tensor(f"p0_x{i}", [128, D], F32)) for i in range(2)]
                sq = st.enter_context(nc.sbuf_tensor("p0_sq", [128, D], BF16))
                xnb = [st.enter_context(nc.sbuf_tensor(f"p0_xn{i}", [128, D], BF16)) for i in range(2)]
                wbc = st.enter_context(nc.sbuf_tensor("p0_wbc", [128, D], F32))
                ss = [st.enter_context(nc.sbuf_tensor(f"p0_ss{i}", [128, 1], F32)) for i in range(2)]
                rs = [st.enter_context(nc.sbuf_tensor(f"p0_rs{i}", [128, 1], F32)) for i in range(2)]
                pt = [st.enter_context(nc.psum_tensor(f"p0_pt{i}", [128, D], BF16)) for i in range(2)]
                S = Sched(nc, st, "p0")
                block = st.enter_context(nc.Block())
                r_c = Res()
                d_c = S.dsem()
                S.dma("sp", lambda e: e.dma_start(out=wbc[:], in_=norm1_w.partition_broadcast(128)), d_c, writes=[r_c])
                r_id = Res()
                S.dma("pool", lambda e: e.dma_start(out=identb[:], in_=t_ident), S.dsem(), writes=[r_id])
                Rx = Ring(xin, S, True)
                Rxn = Ring(xnb)
                Rss = Ring(ss)
                Rrs = Ring(rs)
                Rpt = Ring(pt, excl=True)
                r_sq = Res()
                r_xnT = Res()
                for t in range(LT):
                    xb, rx, dx = Rx.next()
                    S.dma("sp", lambda e, xb=xb, t=t: e.dma_start(out=xb[:], in_=x_loc[t * 128:(t + 1) * 128, :]),
                          dx, writes=[rx])
                    sb, rss = Rss.next()
                    S.op("act", lambda e, xb=xb, sb=sb: e.activation(out=sq[:], in_=xb[:], func=AF.Square,
                                                                      accum_out=sb[:]),
                         reads=[rx], writes=[r_sq, rss])
                    rb, rrs = Rrs.next()
                    S.op("act", lambda e, sb=sb, rb=rb: e.activation(out=rb[:], in_=sb[:], func=AF.Sqrt,
                                                                     scale=1.0 / D, bias=EPS),
                         reads=[rss], writes=[rrs])
                    S.op("dve", lambda e, rb=rb: e.reciprocal(out=rb[:], in_=rb[:]),
                         reads=[rrs], writes=[rrs])
                    xn, rxn = Rxn.next()
                    S.op("dve", lambda e, xn=xn, xb=xb, rb=rb: e.scalar_tensor_tensor(
                        out=xn[:], in0=xb[:], scalar=rb[:], in1=wbc[:], op0=ALU.mult, op1=ALU.mult),
                        reads=[rx, rrs, r_c], writes=[rxn])
                    pb, rp = Rpt.next()
                    for c in range(16):
                        S.op("pe", lambda e, pb=pb, xn=xn, c=c: e.transpose(
                            out=pb[:, c * 128:(c + 1) * 128], in_=xn[:, c * 128:(c + 1) * 128], identity=identb[:]),
                            reads=[rxn, r_id], writes=[rp])
                    for hh in range(2):
                        eng = "act" if hh == 0 else "dve"
                        if eng == "act":
                            S.op("act", lambda e, pb=pb, t=t, hh=hh: e.activation(
                                out=xnT[:, hh * 8:(hh + 1) * 8, t * 128:(t + 1) * 128],
                                in_=pb[:, hh * 1024:(hh + 1) * 1024].rearrange("p (c q) -> p c q", c=8),
                                func=AF.Copy), reads=[rp], writes=[r_xnT])
                        else:
                            S.op("dve", lambda e, pb=pb, t=t, hh=hh: e.tensor_copy(
                                out=xnT[:, hh * 8:(hh + 1) * 8, t * 128:(t + 1) * 128],
                                in_=pb[:, hh * 1024:(hh + 1) * 1024].rearrange("p (c q) -> p c q", c=8)),
                                reads=[rp], writes=[r_xnT])
                S.emit(block)

            if 0 in phases and 1 not in phases:
                return

            with ExitStack() as st:
                wb = [st.enter_context(nc.sbuf_tensor(f"p1_w{i}", [128, 16, 512], BF16)) for i in range(2)]
                cs = st.enter_context(nc.sbuf_tensor("p1_cs", [128, LT, 128], F32))
                csq = st.enter_context(nc.sbuf_tensor("p1_csq", [128, NQ, 128], F32))
                xf = [st.enter_context(nc.sbuf_tensor(f"p1_xf{i}", [128, 512], F32)) for i in range(2)]
                tmp = [st.enter_context(nc.sbuf_tensor(f"p1_t{i}", [128, 4, 256], F32)) for i in range(2)]
                ob = [st.enter_context(nc.sbuf_tensor(f"p1_o{i}", [128, 512], BF16)) for i in range(3)]
                of = [st.enter_context(nc.sbuf_tensor(f"p1_of{i}", [48, 512], F32)) for i in range(2)]
                ps = [st.enter_context(nc.psum_tensor(f"p1_ps{i}", [128, 512], F32)) for i in range(4)]
                S = Sched(nc, st, "p1")
                block = st.enter_context(nc.Block())
                r_tab = Res()
                d_tab = S.dsem()
                S.dma("sp", lambda e: e.dma_start(out=cs[:], in_=t_cs), d_tab, writes=[r_tab])
                S.dma("sp", lambda e: e.dma_start(out=csq[:], in_=t_csq), d_tab, writes=[r_tab])
                Rw = Ring(wb, S, True)
                Rps = Ring(ps, excl=True)
                Rxf = Ring(xf)
                Rtmp = Ring(tmp)
                Rob = Ring(ob, S, True)
                Rof = Ring(of, S, True)
                r_x = Res()
                alt = [0]

                def load_w(col0, ncols):
                    w, rw, dw = Rw.next()
                    S.dma("pool", lambda e, w=w: e.dma_start(out=w[:, :, 0:ncols], in_=w_in_v[:, :, col0:col0 + ncols]),
                          dw, writes=[rw])
                    return w, rw

                def mm_tm(w, rw, t, c0, n):
                    p, rp = Rps.next()
                    for c in range(16):
                        S.op("pe", lambda e, p=p, c=c: e.matmul(
                            p[:, 0:n], lhsT=xnT[:, c, t * 128:(t + 1) * 128], rhs=w[:, c, c0:c0 + n],
                            start=(c == 0), stop=(c == 15)), reads=[rw, r_x], writes=[rp])
                    return p, rp

                def mm_fm(w, rw, tok0, ntok, c0, m):
                    p, rp = Rps.next()
                    for c in range(16):
                        S.op("pe", lambda e, p=p, c=c: e.matmul(
                            p[0:m, 0:ntok], lhsT=w[:, c, c0:c0 + m], rhs=xnT[:, c, tok0:tok0 + ntok],
                            start=(c == 0), stop=(c == 15)), reads=[rw, r_x], writes=[rp])
                    return p, rp

                def evac_store(p, rp, m, n, dst, func=None):
                    o, ro, do = Rob.next()
                    if func is not None:
                        S.op("act", lambda e: e.activation(out=o[0:m, 0:n], in_=p[0:m, 0:n], func=func),
                             reads=[rp], writes=[ro])
                    else:
                        alt[0] ^= 1
                        if alt[0]:
                            S.op("act", lambda e: e.activation(out=o[0:m, 0:n], in_=p[0:m, 0:n], func=AF.Copy),
                                 reads=[rp], writes=[ro])
                        else:
                            S.op("dve", lambda e: e.tensor_copy(out=o[0:m, 0:n], in_=p[0:m, 0:n]),
                                 reads=[rp], writes=[ro])
                    S.dma("sp", lambda e: e.dma_start(out=dst, in_=o[0:m, 0:n]), do, reads=[ro])

                def rotary(p, rp, ctab, ti, dst):
                    x_, rxf = Rxf.next()
                    S.op("act", lambda e: e.activation(out=x_[:], in_=p[:], func=AF.Copy), reads=[rp], writes=[rxf])
                    xv = x_[:].rearrange("p (h t d) -> p h t d", h=4, t=2)
                    cosb = ctab[:, ti, 0:64][:, None, :].to_broadcast([128, 4, 64])
                    sinb = ctab[:, ti, 64:128][:, None, :].to_broadcast([128, 4, 64])
                    tm_, rt = Rtmp.next()
                    o, ro, do = Rob.next()
                    ov = o[:].rearrange("p (h t d) -> p h t d", h=4, t=2)
                    x1 = xv[:, :, 0, :]
                    x2 = xv[:, :, 1, :]
                    S.op("pool", lambda e: e.tensor_tensor(out=tm_[:, :, 0:64], in0=x1, in1=cosb, op=ALU.mult),
                         reads=[rxf, r_tab], writes=[rt])
                    S.op("pool", lambda e: e.tensor_tensor(out=tm_[:, :, 64:128], in0=x2, in1=sinb, op=ALU.mult),
                         reads=[rxf, r_tab], writes=[rt])
                    S.op("dve", lambda e: e.tensor_tensor(out=tm_[:, :, 128:192], in0=x1, in1=sinb, op=ALU.mult),
                         reads=[rxf, r_tab], writes=[rt])
                    S.op("dve", lambda e: e.tensor_tensor(out=tm_[:, :, 192:256], in0=x2, in1=cosb, op=ALU.mult),
                         reads=[rxf, r_tab], writes=[rt])
                    S.op("pool", lambda e: e.tensor_tensor(out=ov[:, :, 0, :], in0=tm_[:, :, 0:64],
                                                           in1=tm_[:, :, 64:128], op=ALU.subtract),
                         reads=[rt], writes=[ro])
                    S.op("dve", lambda e: e.tensor_tensor(out=ov[:, :, 1, :], in0=tm_[:, :, 128:192],
                                                          in1=tm_[:, :, 192:256], op=ALU.add),
                         reads=[rt], writes=[ro])
                    S.dma("sp", lambda e: e.dma_start(out=dst, in_=o[:]), do, reads=[ro])

                qtiles = list(range(QT0, LT))
                qgroups = [(QT0 * 128, 128)] + [((16 + 4 * g) * 128, 512) for g in range(4)]
                agroups = [(g * 512, 512) for g in range(8)]

                for blk in range(2):
                    w, rw = load_w(1024 + blk * 512, 512)
                    for t in range(LT):
                        p, rp = mm_tm(w, rw, t, 0, 512)
                        rotary(p, rp, cs, t, RK[t * 128:(t + 1) * 128, blk * 512:(blk + 1) * 512])
                for blk in range(2):
                    w, rw = load_w(blk * 512, 512)
                    for t in qtiles:
                        p, rp = mm_tm(w, rw, t, 0, 512)
                        qi = t - QT0
                        rotary(p, rp, csq, qi, RQ[qi * 128:(qi + 1) * 128, blk * 512:(blk + 1) * 512])
                for blk in range(4):
                    w, rw = load_w(2048 + blk * 512, 512)
                    for t in range(LT):
                        p, rp = mm_tm(w, rw, t, 0, 512)
                        evac_store(p, rp, 128, 512, RV[t * 128:(t + 1) * 128, blk * 512:(blk + 1) * 512])
                w, rw = load_w(8192, 512)
                for (dst, c0) in ((KCT, 0), (VCT, 256)):
                    for g in range(2):
                        for tok0, ntok in agroups:
                            p, rp = mm_fm(w, rw, tok0, ntok, c0 + g * 128, 128)
                            evac_store(p, rp, 128, ntok, dst[g, :, tok0:tok0 + ntok])
                for (col0, dfm, dtm) in ((8704, KST, VS), (9216, KWT, VW)):
                    w, rw = load_w(col0, 512)
                    for g in range(2):
                        for tok0, ntok in agroups:
                            p, rp = mm_fm(w, rw, tok0, ntok, g * 128, 128)
                            evac_store(p, rp, 128, ntok, dfm[g, :, tok0:tok0 + ntok])
                    for t in range(LT):
                        p, rp = mm_tm(w, rw, t, 256, 256)
                        evac_store(p, rp, 128, 256, dtm[t * 128:(t + 1) * 128, :])
                for blk in range(4):
                    w, rw = load_w(6144 + blk * 512, 512)
                    for ch in range(4):
                        for tok0, ntok in qgroups:
                            p, rp = mm_fm(w, rw, tok0, ntok, ch * 128, 128)
                            q0 = tok0 - QT0 * 128
                            evac_store(p, rp, 128, ntok, NQT_[blk * 4 + ch, :, q0:q0 + ntok])
                for blk in range(4):
                    w, rw = load_w(4096 + blk * 512, 512)
                    for t in qtiles:
                        p, rp = mm_tm(w, rw, t, 0, 512)
                        qi = t - QT0
                        evac_store(p, rp, 128, 512, RG[qi * 128:(qi + 1) * 128, blk * 512:(blk + 1) * 512],
                                   func=AF.Silu)
                for blk in range(8):
                    w, rw = load_w(9776 + blk * 512, 512)
                    for ch in range(4):
                        for tok0, ntok in qgroups:
                            p, rp = mm_fm(w, rw, tok0, ntok, ch * 128, 128)
                            q0 = tok0 - QT0 * 128
                            evac_store(p, rp, 128, ntok, MGT[blk * 4 + ch, :, q0:q0 + ntok], func=AF.Sigmoid)
                w, rw = load_w(9728, 48)
                for tok0, ntok in qgroups:
                    p, rp = mm_fm(w, rw, tok0, ntok, 0, 48)
                    q0 = tok0 - QT0 * 128
                    o, ro, do = Rof.next()
                    S.op("act", lambda e, o=o, p=p, ntok=ntok: e.activation(out=o[0:48, 0:ntok], in_=p[0:48, 0:ntok],
                                                                           func=AF.Sigmoid), reads=[rp], writes=[ro])
                    S.dma("sp", lambda e, o=o, q0=q0, ntok=ntok: e.dma_start(out=NGT[:, q0:q0 + ntok],
                                                                            in_=o[0:48, 0:ntok]), do, reads=[ro])
                S.emit(block)

    if 1 in phases:
        phase01()

    def phase2():
        gam = [1.0 - 2.0 ** (-5.0 - h) for h in range(8)]
        with ExitStack() as st:
            sb = lambda name, shape, dt: st.enter_context(nc.sbuf_tensor(name, list(shape), dt))
            identb = sb("p2_id", [128, 128], BF16)
            decT = sb("p2_decT", [128, 8, 128], F32)
            qdec = sb("p2_qdec", [128, 8, 128], F32)
            kdec = sb("p2_kdec", [128, 8], F32)
            retw = sb("p2_retw", [128, D], F32)
            kc_ = [sb(f"p2_k{i}", [128, 1024], BF16) for i in range(2)]
            vc_ = [sb(f"p2_v{i}", [128, 2048], BF16) for i in range(2)]
            qc_ = [sb(f"p2_q{i}", [128, 1024], BF16) for i in range(2)]
            gc_ = [sb(f"p2_g{i}", [128, 2048], BF16) for i in range(2)]
            kd_ = [sb(f"p2_kd{i}", [128, 1024], BF16) for i in range(2)]
            kT_ = [sb(f"p2_kT{i}", [128, 1024], BF16) for i in range(2)]
            qT_ = [sb(f"p2_qT{i}", [128, 1024], BF16) for i in range(2)]
            qTd_ = [sb(f"p2_qTd{i}", [128, 1024], BF16) for i in range(2)]
            ST_ = [sb(f"p2_ST{i}", [128, 512], BF16) for i in range(2)]
            stf = sb("p2_stf", [128, 2048], F32)
            stb = [sb(f"p2_stb{i}", [128, 2048], BF16) for i in range(2)]
            junk = sb("p2_junk", [128, 256], BF16)
            ssq_ = [sb(f"p2_ssq{i}", [128, 4], F32) for i in range(2)]
            tmpf_ = [sb(f"p2_tmpf{i}", [128, 1024], F32) for i in range(2)]
            gat_ = [sb(f"p2_gat{i}", [128, 2048], BF16) for i in range(2)]
            gT_ = [sb(f"p2_gT{i}", [128, 16, 128], BF16) for i in range(2)]
            pkT = st.enter_context(nc.psum_tensor("p2_pkT", [128, 1024], BF16))
            pqT = st.enter_context(nc.psum_tensor("p2_pqT", [128, 1024], BF16))
            pS = st.enter_context(nc.psum_tensor("p2_pS", [128, 512], F32))
            pO = st.enter_context(nc.psum_tensor("p2_pO", [128, 1024], F32))
            pKV = st.enter_context(nc.psum_tensor("p2_pKV", [128, 1024], F32))
            pGT = st.enter_context(nc.psum_tensor("p2_pGT", [128, 1024], BF16))
            S = Sched(nc, st, "p2")
            block = st.enter_context(nc.Block())
            r_tab = Res()
            d_tab = S.dsem()
            r_id = Res()
            S.dma("pool", lambda e: e.dma_start(out=identb[:], in_=t_ident), S.dsem(), writes=[r_id])
            S.dma("sp", lambda e: e.dma_start(out=decT[:], in_=t_decT), d_tab, writes=[r_tab])
            S.dma("sp", lambda e: e.dma_start(out=qdec[:], in_=t_qdec), d_tab, writes=[r_tab])
            S.dma("sp", lambda e: e.dma_start(out=kdec[:], in_=t_kdec), d_tab, writes=[r_tab])
            S.dma("sp", lambda e: e.dma_start(out=retw[:], in_=ret_norm_w.partition_broadcast(128)), d_tab,
                  writes=[r_tab])
            Rk, Rv, Rq, Rg = Ring(kc_, S, True), Ring(vc_, S, True), Ring(qc_, S, True), Ring(gc_, S, True)
            Rkd, RkT, RqT, RqTd, RST = Ring(kd_), Ring(kT_), Ring(qT_), Ring(qTd_), Ring(ST_)
            Rssq, Rtmpf, Rgat = Ring(ssq_), Ring(tmpf_), Ring(gat_)
            RgT = Ring(gT_, S, True)
            r_stf = Res()
            r_stb = [Res(), Res()]
            r_junk = Res()
            r_pkT, r_pqT, r_pS, r_pO, r_pKV, r_pGT = (Res(True) for _ in range(6))
            S.op("dve", lambda e: e.memset(stf[:], 0.0), writes=[r_stf])
            S.op("pool", lambda e: e.memset(stb[0][:], 0.0), writes=[r_stb[0]])
            for c in range(LT):
                k_, rk, dk = Rk.next()
                v_, rv, dv = Rv.next()
                S.dma("sp", lambda e, k_=k_, c=c: e.dma_start(out=k_[:], in_=RK[c * 128:(c + 1) * 128, :]), dk, writes=[rk])
                S.dma("sp", lambda e, v_=v_, c=c: e.dma_start(out=v_[:], in_=RV[c * 128:(c + 1) * 128, :]), dv, writes=[rv])
                sbc, rsb = stb[c % 2], r_stb[c % 2]
                sbn, rsn = stb[(c + 1) % 2], r_stb[(c + 1) % 2]
                if c >= QT0 and os.environ.get('P2_OUT', '1') == '1':
                    qi = c - QT0
                    q_, rq, dq = Rq.next()
                    g_, rg, dg = Rg.next()
                    S.dma("sp", lambda e, q_=q_, qi=qi: e.dma_start(out=q_[:], in_=RQ[qi * 128:(qi + 1) * 128, :]), dq, writes=[rq])
                    S.dma("sp", lambda e, g_=g_, qi=qi: e.dma_start(out=g_[:], in_=RG[qi * 128:(qi + 1) * 128, :]), dg, writes=[rg])
                    for h in range(8):
                        S.op("pe", lambda e, k_=k_, h=h: e.transpose(out=pkT[:, h * 128:(h + 1) * 128],
                                                                     in_=k_[:, h * 128:(h + 1) * 128], identity=identb[:]),
                             reads=[rk, r_id], writes=[r_pkT])
                    kT, rkT = RkT.next()
                    S.op("act", lambda e, kT=kT: e.activation(out=kT[:], in_=pkT[:], func=AF.Copy), reads=[r_pkT], writes=[rkT])
                    for h in range(8):
                        S.op("pe", lambda e, q_=q_, h=h: e.transpose(out=pqT[:, h * 128:(h + 1) * 128],
                                                                     in_=q_[:, h * 128:(h + 1) * 128], identity=identb[:]),
                             reads=[rq, r_id], writes=[r_pqT])
                    qT, rqT = RqT.next()
                    qTd, rqTd = RqTd.next()
                    S.op("act", lambda e, qT=qT: e.activation(out=qT[:], in_=pqT[:], func=AF.Copy), reads=[r_pqT], writes=[rqT])
                    S.op("dve", lambda e, qTd=qTd: e.tensor_tensor(
                        out=qTd[:].rearrange("p (h n) -> p h n", h=8), in0=pqT[:].rearrange("p (h n) -> p h n", h=8),
                        in1=qdec[:], op=ALU.mult), reads=[r_pqT, r_tab], writes=[rqTd])
                    gat, rgat = Rgat.next()
                    lvl = int(os.environ.get('P2_LVL', '9'))
                    for hg in (range(2) if lvl >= 2 else []):
                        for j in range(4):
                            h = hg * 4 + j
                            S.op("pe", lambda e, kT=kT, qT=qT, h=h, j=j: e.matmul(
                                pS[:, j * 128:(j + 1) * 128], lhsT=kT[:, h * 128:(h + 1) * 128],
                                rhs=qT[:, h * 128:(h + 1) * 128], start=True, stop=True),
                                reads=[rkT, rqT], writes=[r_pS])
                        ST, rST = RST.next()
                        S.op("dve", lambda e, ST=ST, hg=hg: e.tensor_tensor(
                            out=ST[:].rearrange("p (h n) -> p h n", h=4), in0=pS[:].rearrange("p (h n) -> p h n", h=4),
                            in1=decT[:, hg * 4:(hg + 1) * 4, :], op=ALU.mult), reads=[r_pS, r_tab], writes=[rST])
                        for j in range(4):
                            h = hg * 4 + j
                            S.op("pe", lambda e, ST=ST, v_=v_, h=h, j=j: e.matmul(
                                pO[:, j * 256:(j + 1) * 256], lhsT=ST[:, j * 128:(j + 1) * 128],
                                rhs=v_[:, h * 256:(h + 1) * 256], start=True, stop=False),
                                reads=[rST, rv], writes=[r_pO])
                            S.op("pe", lambda e, qTd=qTd, sbc=sbc, h=h, j=j: e.matmul(
                                pO[:, j * 256:(j + 1) * 256], lhsT=qTd[:, h * 128:(h + 1) * 128],
                                rhs=sbc[:, h * 256:(h + 1) * 256], start=False, stop=True),
                                reads=[rqTd, rsb], writes=[r_pO])
                        if lvl < 3:
                            continue
                        ssq, rssq = Rssq.next()
                        for j in range(4):
                            S.op("act", lambda e, ssq=ssq, j=j: e.activation(
                                out=junk[:], in_=pO[:, j * 256:(j + 1) * 256], func=AF.Square, accum_out=ssq[:, j:j + 1]),
                                reads=[r_pO], writes=[r_junk, rssq])
                        S.op("act", lambda e, ssq=ssq: e.activation(out=ssq[:], in_=ssq[:], func=AF.Sqrt,
                                                                    scale=1.0 / 256, bias=EPS), reads=[rssq], writes=[rssq])
                        S.op("dve", lambda e, ssq=ssq: e.reciprocal(out=ssq[:], in_=ssq[:]), reads=[rssq], writes=[rssq])
                        tmpf, rtf = Rtmpf.next()
                        for j in range(4):
                            h = hg * 4 + j
                            S.op("dve", lambda e, tmpf=tmpf, ssq=ssq, h=h, j=j: e.scalar_tensor_tensor(
                                out=tmpf[:, j * 256:(j + 1) * 256], in0=pO[:, j * 256:(j + 1) * 256],
                                scalar=ssq[:, j:j + 1], in1=retw[:, h * 256:(h + 1) * 256], op0=ALU.mult, op1=ALU.mult),
                                reads=[r_pO, rssq, r_tab], writes=[rtf])
                        if lvl < 4:
                            continue
                        S.op("pool", lambda e, gat=gat, tmpf=tmpf, g_=g_, hg=hg: e.tensor_tensor(
                            out=gat[:, hg * 1024:(hg + 1) * 1024], in0=tmpf[:], in1=g_[:, hg * 1024:(hg + 1) * 1024],
                            op=ALU.mult), reads=[rtf, rg], writes=[rgat])
                    if lvl < 5:
                        continue
                    gT, rgT, dgT = RgT.next()
                    for hh in range(2):
                        for cc in range(8):
                            ch = hh * 8 + cc
                            S.op("pe", lambda e, gat=gat, ch=ch, cc=cc: e.transpose(
                                out=pGT[:, cc * 128:(cc + 1) * 128], in_=gat[:, ch * 128:(ch + 1) * 128],
                                identity=identb[:]), reads=[rgat, r_id], writes=[r_pGT])
                        if hh == 0:
                            S.op("act", lambda e, gT=gT: e.activation(
                                out=gT[:, 0:8, :], in_=pGT[:].rearrange("p (c q) -> p c q", c=8), func=AF.Copy),
                                reads=[r_pGT], writes=[rgT])
                        else:
                            S.op("dve", lambda e, gT=gT: e.tensor_copy(
                                out=gT[:, 8:16, :], in_=pGT[:].rearrange("p (c q) -> p c q", c=8)),
                                reads=[r_pGT], writes=[rgT])
                    S.dma("sp", lambda e, gT=gT, qi=qi: e.dma_start(
                        out=RETGT[:, :, qi * 128:(qi + 1) * 128].rearrange("c p q -> p c q"), in_=gT[:]),
                        dgT, reads=[rgT])
                if c < LT - 1 and os.environ.get('P2_STATE', '1') == '1':
                    kd, rkd = Rkd.next()
                    S.op("pool", lambda e, kd=kd, k_=k_: e.tensor_tensor(
                        out=kd[:].rearrange("p (h d) -> p h d", h=8), in0=k_[:].rearrange("p (h d) -> p h d", h=8),
                        in1=kdec[:, :, None].to_broadcast([128, 8, 128]), op=ALU.mult),
                        reads=[rk, r_tab], writes=[rkd])
                    for hg in range(2):
                        for j in range(4):
                            h = hg * 4 + j
                            S.op("pe", lambda e, kd=kd, v_=v_, h=h, j=j: e.matmul(
                                pKV[:, j * 256:(j + 1) * 256], lhsT=kd[:, h * 128:(h + 1) * 128],
                                rhs=v_[:, h * 256:(h + 1) * 256], start=True, stop=True),
                                reads=[rkd, rv], writes=[r_pKV])
                        for j in range(4):
                            h = hg * 4 + j
                            S.op("dve", lambda e, h=h, j=j: e.scalar_tensor_tensor(
                                out=stf[:, h * 256:(h + 1) * 256], in0=stf[:, h * 256:(h + 1) * 256],
                                scalar=float(gam[h] ** 128), in1=pKV[:, j * 256:(j + 1) * 256],
                                op0=ALU.mult, op1=ALU.add), reads=[r_pKV, r_stf], writes=[r_stf])
                    S.op("act", lambda e, sbn=sbn: e.activation(out=sbn[:], in_=stf[:], func=AF.Copy),
                         reads=[r_stf], writes=[rsn])
            S.emit(block)

    if 2 in phases:
        phase2()

    KCMPT = dscr("KCMPT", [2, 128, 256])
    VCMP = dscr("VCMP", [2, 256, 128])

    def phase3a():
        with ExitStack() as st:
            sb = lambda name, shape, dt: st.enter_context(nc.sbuf_tensor(name, list(shape), dt))
            xc_ = [sb(f"p3a_xc{i}", [128, LT * 128], BF16) for i in range(2)]
            w1b = sb("p3a_w1", [128, 32, 256], BF16)
            peT = sb("p3a_pe", [128, 32], BF16)
            w2b = sb("p3a_w2", [128, 2, 128], BF16)
            cb = sb("p3a_cb", [128, 2], F32)
            gel_ = [sb(f"p3a_gel{i}", [128, 2, 256], BF16) for i in range(2)]
            og_ = [sb(f"p3a_og{i}", [128, 256], BF16) for i in range(2)]
            pc = st.enter_context(nc.psum_tensor("p3a_pc", [128, 2], F32))
            ph_ = [st.enter_context(nc.psum_tensor(f"p3a_ph{i}", [128, 256], F32)) for i in range(2)]
            po_ = [st.enter_context(nc.psum_tensor(f"p3a_po{i}", [128, 256], F32)) for i in range(2)]
            S = Sched(nc, st, "p3a")
            block = st.enter_context(nc.Block())
            Rxc = Ring(xc_, S, True)
            Rgel = Ring(gel_)
            Rog = Ring(og_, S, True)
            Rph = Ring(ph_, excl=True)
            Rpo = Ring(po_, excl=True)
            r_w, r_cb, r_pc = Res(), Res(), Res(True)
            d_w = S.dsem()
            for g_ in gel_:
                S.op("dve", lambda e, g_=g_: e.memset(g_[:], 0.0), writes=[Rgel.res[gel_.index(g_)]])
            for kv in ("k", "v"):
                S.dma("pool", lambda e, kv=kv: e.dma_start(
                    out=w1b[:], in_=cmp_w1[kv].rearrange("(l d) j -> d l j", d=128)), d_w, writes=[r_w])
                S.dma("pool", lambda e, kv=kv: e.dma_start(out=peT[:], in_=cmp_peT[kv]), d_w, writes=[r_w])
                S.dma("pool", lambda e, kv=kv: e.dma_start(
                    out=w2b[:], in_=cmp_w2[kv].rearrange("(c p) d -> p c d", p=128)), d_w, writes=[r_w])
                for jc in range(2):
                    for l in range(32):
                        S.op("pe", lambda e, jc=jc, l=l: e.matmul(
                            pc[:, jc:jc + 1], lhsT=w1b[:, l, jc * 128:(jc + 1) * 128], rhs=peT[:, l:l + 1],
                            start=(l == 0), stop=(l == 31)), reads=[r_w], writes=[r_pc])
                S.op("act", lambda e: e.activation(out=cb[:], in_=pc[:], func=AF.Copy), reads=[r_pc], writes=[r_cb])
                src = KCT if kv == "k" else VCT
                for g in range(2):
                    xc, rxc, dxc = Rxc.next()
                    S.dma("sp", lambda e, xc=xc, g=g, src=src: e.dma_start(out=xc[:], in_=src[g]), dxc, writes=[rxc])
                    gel, rgel = Rgel.next()
                    for jc in range(2):
                        ph, rph = Rph.next()
                        for l in range(32):
                            S.op("pe", lambda e, ph=ph, xc=xc, jc=jc, l=l: e.matmul(
                                ph[:, 0:255], lhsT=w1b[:, l, jc * 128:(jc + 1) * 128],
                                rhs=xc[:, l:l + 16 * 254 + 1:16], start=(l == 0), stop=(l == 31)),
                                reads=[r_w, rxc], writes=[rph])
                        S.op("act", lambda e, ph=ph, gel=gel, jc=jc: e.activation(
                            out=gel[:, jc, 0:255], in_=ph[:, 0:255], func=AF.Gelu_apprx_tanh, bias=cb[:, jc:jc + 1]),
                            reads=[rph, r_cb], writes=[rgel])
                    if kv == "k":
                        po, rpo = Rpo.next()
                        for jc in range(2):
                            S.op("pe", lambda e, po=po, gel=gel, jc=jc: e.matmul(
                                po[:, :], lhsT=w2b[:, jc, :], rhs=gel[:, jc, :], start=(jc == 0), stop=(jc == 1)),
                                reads=[r_w, rgel], writes=[rpo])
                        og, rog, dog = Rog.next()
                        S.op("dve", lambda e, og=og, po=po: e.tensor_copy(out=og[:], in_=po[:]), reads=[rpo], writes=[rog])
                        S.dma("sp", lambda e, og=og, g=g: e.dma_start(out=KCMPT[g], in_=og[:]), dog, reads=[rog])
                    else:
                        for nk in range(2):
                            po, rpo = Rpo.next()
                            for jc in range(2):
                                S.op("pe", lambda e, po=po, gel=gel, jc=jc, nk=nk: e.matmul(
                                    po[:, 0:128], lhsT=gel[:, jc, nk * 128:(nk + 1) * 128], rhs=w2b[:, jc, :],
                                    start=(jc == 0), stop=(jc == 1)), reads=[r_w, rgel], writes=[rpo])
                            og, rog, dog = Rog.next()
                            S.op("dve", lambda e, og=og, po=po: e.tensor_copy(out=og[:, 0:128], in_=po[:, 0:128]),
                                 reads=[rpo], writes=[rog])
                            S.dma("sp", lambda e, og=og, g=g, nk=nk: e.dma_start(
                                out=VCMP[g, nk * 128:(nk + 1) * 128, :], in_=og[:, 0:128]), dog, reads=[rog])
            S.emit(block)

    def phase3b(sfx=""):
        with ExitStack() as st:
            sb = lambda name, shape, dt: st.enter_context(nc.sbuf_tensor(name + sfx, list(shape), dt))
            identb = sb("p3_idb", [128, 128], BF16)
            identf = sb("p3_idf", [128, 128], F32)
            onesb = sb("p3_ones", [128, 128], BF16)
            causb = sb("p3_caus", [128, 4, 128], BF16)
            acausb = sb("p3_acaus", [128, 4, 128], BF16)
            expt = sb("p3_expt", [64, LT, 128], BF16)
            ovb = sb("p3_ov", [128, 2, 65], BF16)
            padb = sb("p3_padb", [128, LT], F32)
            cmpb = sb("p3_cmpb", [128, 2], F32)
            kcm = sb("p3_kcm", [128, 2, 256], BF16)
            vcm = sb("p3_vcm", [128, 2, 2, 128], BF16)
            kst = sb("p3_kst", [128, 2, LT * 128], BF16)
            kwt = sb("p3_kwt", [128, 2, LT * 128], BF16)
            vs = sb("p3_vs", [128, LT, 256], BF16)
            vw = sb("p3_vw", [128, LT, 256], BF16)
            qT_ = [sb(f"p3_qT{i}", [128, 16, 128], BF16) for i in range(2)]
            gbc_ = [sb(f"p3_gbc{i}", [128, 24, 128], F32) for i in range(2)]
            keep_ = [sb(f"p3_keep{i}", [128, 64], F32) for i in range(2)]
            addt_ = [sb(f"p3_add{i}", [128, 64], F32) for i in range(2)]
            cm_ = [sb(f"p3_cm{i}", [128, 2, 128], F32) for i in range(2)]
            Ec_ = [sb(f"p3_Ec{i}", [128, 1024], BF16) for i in range(2)]
            E_ = [sb(f"p3_E{i}", [128, 1024], BF16) for i in range(3)]
            Sm_ = [sb(f"p3_Sm{i}", [128, 1024], F32) for i in range(2)]
            fac_ = [sb(f"p3_fac{i}", [128, 1024], F32) for i in range(2)]
            tmp_ = [sb(f"p3_tmp{i}", [128, 1024], F32) for i in range(2)]
            acc_ = [sb(f"p3_acc{i}", [128, 1024], F32) for i in range(2)]
            accb_ = [sb(f"p3_accb{i}", [128, 1024], BF16) for i in range(2)]
            rs8 = sb("p3_rs8", [128, 8], F32)
            imp = sb("p3_imp", [128, 64], F32)
            imp2 = sb("p3_imp2", [128, 64], F32)
            m8 = sb("p3_m8", [128, 8], F32)
            selb = sb("p3_selb", [128, 64], F32)
            selbT_ = [sb(f"p3_selbT{i}", [64, 4, 128], BF16) for i in range(2)]
            pS_ = [st.enter_context(nc.psum_tensor(f"p3_pS{i}" + sfx, [128, 1024], F32)) for i in range(2)]
            pO = st.enter_context(nc.psum_tensor("p3_pO" + sfx, [128, 1024], F32))
            pSum = st.enter_context(nc.psum_tensor("p3_pSum" + sfx, [128, 1024], F32))
            S = Sched(nc, st, "p3" + sfx)
            block = st.enter_context(nc.Block())
            r_c = Res()
            r_k = Res()
            d_cp, d_cs = S.dsem(), S.dsem()
            for dst, src in ((identb[:], t_ident), (expt[:], t_exp), (ovb[:], t_ov)):
                S.dma("pool", lambda e, dst=dst, src=src: e.dma_start(out=dst, in_=src), d_cp, writes=[r_c])
            for r4 in range(4):
                S.dma("pool", lambda e, r4=r4: e.dma_start(out=causb[:, r4, :], in_=t_caus), d_cp, writes=[r_c])
                S.dma("pool", lambda e, r4=r4: e.dma_start(out=acausb[:, r4, :], in_=t_acaus), d_cp, writes=[r_c])
            S.op("pool", lambda e: e.memset(onesb[:], 1.0), reads=[], writes=[r_c])
            for dst, src in ((identf[:], t_ident), (padb[:], t_padb), (cmpb[:], t_cmpb),
                             (kcm[:], KCMPT.rearrange("g d n -> d g n")),
                             (vcm[:], VCMP.rearrange("g (k p) d -> p g k d", p=128)),
                             (kst[:], KST.rearrange("g d t -> d g t")), (kwt[:], KWT.rearrange("g d t -> d g t")),
                             (vs[:], VS.rearrange("(t p) c -> p t c", p=128)),
                             (vw[:], VW.rearrange("(t p) c -> p t c", p=128))):
                S.dma("sp", lambda e, dst=dst, src=src: e.dma_start(out=dst, in_=src), d_cs, writes=[r_k])
            RqT, Rgbc = Ring(qT_, S, True), Ring(gbc_, S, True)
            Rkeep, Radd, Rcm = Ring(keep_, S, True), Ring(addt_, S, True), Ring(cm_, S, True)
            REc, RE, RSm, Rfac, Rtmp, Racc = Ring(Ec_), Ring(E_), Ring(Sm_), Ring(fac_), Ring(tmp_), Ring(acc_)
            Raccb = Ring(accb_, S, True)
            RselbT = Ring(selbT_)
            RpS = Ring(pS_, excl=True)
            r_pO, r_pSum = Res(True), Res(True)
            r_small = Res()
            d_dbg = S.dsem()
            dbg_t = {}
            if "DBGSEL" in debug:
                dbg_t = {"DBGSEL": dscr("DBGSEL", [NQ, 2, 128, 64], F32), "DBGIMP": dscr("DBGIMP", [NQ, 2, 128, 64], F32),
                         "DBGM8": dscr("DBGM8", [NQ, 2, 128, 8], F32)}

            def finalize(gbc, rgbc, br, acc, racc, first):
                fac, rfac = Rfac.next()
                S.op("dve", lambda e: e.tensor_scalar_max(out=fac[:], in0=pSum[:], scalar1=1e-30),
                     reads=[r_pSum], writes=[rfac])
                S.op("dve", lambda e: e.reciprocal(out=fac[:], in_=fac[:]), reads=[rfac], writes=[rfac])
                S.op("pool", lambda e: e.tensor_tensor(
                    out=fac[:].rearrange("p (h q) -> p h q", h=8), in0=fac[:].rearrange("p (h q) -> p h q", h=8),
                    in1=gbc[:, br::3, :], op=ALU.mult), reads=[rfac, rgbc], writes=[rfac])
                if first:
                    S.op("dve", lambda e: e.tensor_tensor(out=acc[:], in0=pO[:], in1=fac[:], op=ALU.mult),
                         reads=[r_pO, rfac], writes=[racc])
                else:
                    tmp, rtmp = Rtmp.next()
                    S.op("dve", lambda e: e.tensor_tensor(out=tmp[:], in0=pO[:], in1=fac[:], op=ALU.mult),
                         reads=[r_pO, rfac], writes=[rtmp])
                    S.op("pool", lambda e: e.tensor_tensor(out=acc[:], in0=acc[:], in1=tmp[:], op=ALU.add),
                         reads=[rtmp, racc], writes=[racc])

            def attend(kt_list, ksrc, vsrc, g, qTg, rq, selbT, rselbT, i):
                n = len(kt_list)

                def scores(kt):
                    pSx, rpS = RpS.next()
                    for hf in range(2):
                        extra = []
                        if selbT is not None:
                            extra.append((expt[:, kt, :], selbT[:].rearrange("s r q -> s (r q)"), [r_c, rselbT]))
                        if kt == i:
                            extra.append((identb[:], causb[:].rearrange("k r q -> k (r q)"), [r_c]))
                        if selbT is None and kt == i - 4:
                            extra.append((identb[:], acausb[:].rearrange("k r q -> k (r q)"), [r_c]))
                        S.op("pe", lambda e, hf=hf, ne=len(extra): e.matmul(
                            pSx[:, hf * 512:(hf + 1) * 512], lhsT=ksrc[:, g, kt * 128:(kt + 1) * 128],
                            rhs=qTg[:, hf * 512:(hf + 1) * 512], start=True, stop=(ne == 0)),
                            reads=[r_k, rq], writes=[rpS])
                        for xi, (l_, r_, deps) in enumerate(extra):
                            S.op("pe", lambda e, hf=hf, l_=l_, r_=r_, last=(xi == len(extra) - 1): e.matmul(
                                pSx[:, hf * 512:(hf + 1) * 512], lhsT=l_, rhs=r_, start=False, stop=last),
                                reads=deps, writes=[rpS])
                    E, rE = RE.next()
                    S.op("act", lambda e: e.activation(
                        out=E[:], in_=pSx[:], func=AF.Exp, scale=SCALE, bias=padb[:, kt:kt + 1]),
                        reads=[rpS, r_k], writes=[rE])
                    return E, rE

                def values(idx, kt, E, rE):
                    for hf in range(2):
                        S.op("pe", lambda e, hf=hf: e.matmul(
                            pO[:, hf * 512:(hf + 1) * 512], lhsT=vsrc[:, kt, g * 128:(g + 1) * 128],
                            rhs=E[:, hf * 512:(hf + 1) * 512], start=(idx == 0), stop=(idx == n - 1)),
                            reads=[r_k, rE], writes=[r_pO])
                        S.op("pe", lambda e, hf=hf: e.matmul(
                            pSum[:, hf * 512:(hf + 1) * 512], lhsT=onesb[:],
                            rhs=E[:, hf * 512:(hf + 1) * 512], start=(idx == 0), stop=(idx == n - 1)),
                            reads=[r_c, rE], writes=[r_pSum])

                pend = None
                for idx, kt in enumerate(kt_list):
                    cur = (idx, kt) + scores(kt)
                    if pend is not None:
                        values(*pend)
                    pend = cur
                values(*pend)

            for i in range(QT0, LT):
                qi = i - QT0
                qT, rq, dq = RqT.next()
                S.dma("sp", lambda e, qT=qT, qi=qi: e.dma_start(
                    out=qT[:], in_=NQT_[:, :, qi * 128:(qi + 1) * 128].rearrange("h p q -> p h q")), dq, writes=[rq])
                keep, rkeep, dkeep = Rkeep.next()
                addt, radd, dadd = Radd.next()
                cm, rcm, dcm = Rcm.next()
                S.dma("sp", lambda e, keep=keep, qi=qi: e.dma_start(out=keep[:], in_=t_keep[qi]), dkeep, writes=[rkeep])
                S.dma("sp", lambda e, addt=addt, qi=qi: e.dma_start(out=addt[:], in_=t_add[qi]), dadd, writes=[radd])
                for ck in range(2):
                    r0 = 128 * ck - 8 * i + 250
                    S.dma("sp", lambda e, cm=cm, ck=ck, r0=r0: e.dma_start(out=cm[:, ck, :], in_=t_cm[r0:r0 + 128, :]),
                          dcm, writes=[rcm])
                def do_group(i, qi, g, qT, rq, keep, rkeep, addt, radd, cm, rcm):
                    gbc, rgbc, dgbc = Rgbc.next()
                    S.dma("sp", lambda e, gbc=gbc, g=g, qi=qi: e.dma_start(
                        out=gbc[:], in_=NGT[g * 24:(g + 1) * 24, qi * 128:(qi + 1) * 128].unsqueeze(0).to_broadcast(
                            [128, 24, 128])), dgbc, writes=[rgbc])
                    qTg = qT[:, g * 8:(g + 1) * 8, :].rearrange("p h q -> p (h q)")
                    acc, racc = Racc.next()
                    Ecs = []
                    for ck in range(2):
                        pSx, rpS = RpS.next()
                        for hf in range(2):
                            S.op("pe", lambda e, pSx=pSx, hf=hf, ck=ck: e.matmul(
                                pSx[:, hf * 512:(hf + 1) * 512], lhsT=kcm[:, g, ck * 128:(ck + 1) * 128],
                                rhs=qTg[:, hf * 512:(hf + 1) * 512], start=True, stop=True),
                                reads=[r_k, rq], writes=[rpS])
                        Sm, rSm = RSm.next()
                        S.op("dve", lambda e, Sm=Sm, pSx=pSx, ck=ck: e.tensor_tensor(
                            out=Sm[:].rearrange("p (h q) -> p h q", h=8), in0=pSx[:].rearrange("p (h q) -> p h q", h=8),
                            in1=cm[:, ck, :][:, None, :].to_broadcast([128, 8, 128]), op=ALU.add),
                            reads=[rpS, rcm], writes=[rSm])
                        Ec, rEc = REc.next()
                        S.op("act", lambda e, Ec=Ec, Sm=Sm, ck=ck: e.activation(
                            out=Ec[:], in_=Sm[:], func=AF.Exp, scale=SCALE, bias=cmpb[:, ck:ck + 1]),
                            reads=[rSm, r_k], writes=[rEc])
                        Ecs.append((Ec, rEc))
                        for hf in range(2):
                            S.op("pe", lambda e, Ec=Ec, hf=hf, ck=ck: e.matmul(
                                pO[:, hf * 512:(hf + 1) * 512], lhsT=vcm[:, g, ck, :],
                                rhs=Ec[:, hf * 512:(hf + 1) * 512], start=(ck == 0), stop=(ck == 1)),
                                reads=[r_k, rEc], writes=[r_pO])
                            S.op("pe", lambda e, Ec=Ec, hf=hf, ck=ck: e.matmul(
                                pSum[:, hf * 512:(hf + 1) * 512], lhsT=onesb[:],
                                rhs=Ec[:, hf * 512:(hf + 1) * 512], start=(ck == 0), stop=(ck == 1)),
                                reads=[r_c, rEc], writes=[r_pSum])
                    pU, rpU = RpS.next()
                    for h in range(8):
                        for ck in range(2):
                            Ec, rEc = Ecs[ck]
                            o0 = (h // 4) * 512 + (h % 4) * 65
                            S.op("pe", lambda e, pU=pU, Ec=Ec, h=h, ck=ck, o0=o0: e.matmul(
                                pU[:, o0:o0 + 65], lhsT=Ec[:, h * 128:(h + 1) * 128], rhs=ovb[:, ck, :],
                                start=(ck == 0), stop=(ck == 1)), reads=[r_c, rEc], writes=[rpU])
                    finalize(gbc, rgbc, 0, acc, racc, True)
                    for hh in range(2):
                        S.op("dve", lambda e, pU=pU, hh=hh: e.tensor_scalar_max(
                            out=rs8[:, hh * 4:(hh + 1) * 4],
                            in0=pU[:, hh * 512:hh * 512 + 260].rearrange("p (h s) -> p h s", s=65)[:, :, 64],
                            scalar1=1e-30), reads=[rpU], writes=[r_small])
                    S.op("dve", lambda e: e.reciprocal(out=rs8[:], in_=rs8[:]), reads=[r_small], writes=[r_small])
                    for h in range(8):
                        o0 = (h // 4) * 512 + (h % 4) * 65
                        if h == 0:
                            S.op("dve", lambda e, pU=pU, o0=o0: e.tensor_scalar(
                                out=imp[:], in0=pU[:, o0:o0 + 64], scalar1=rs8[:, 0:1], scalar2=None, op0=ALU.mult),
                                reads=[rpU, r_small], writes=[r_small])
                        else:
                            S.op("dve", lambda e, pU=pU, o0=o0, h=h: e.scalar_tensor_tensor(
                                out=imp[:], in0=pU[:, o0:o0 + 64], scalar=rs8[:, h:h + 1], in1=imp[:],
                                op0=ALU.mult, op1=ALU.add), reads=[rpU, r_small], writes=[r_small])
                    S.op("dve", lambda e, keep=keep: e.tensor_tensor(out=imp[:], in0=imp[:], in1=keep[:], op=ALU.mult),
                         reads=[r_small, rkeep], writes=[r_small])
                    S.op("dve", lambda e, addt=addt: e.tensor_tensor(out=imp[:], in0=imp[:], in1=addt[:], op=ALU.add),
                         reads=[r_small, radd], writes=[r_small])
                    S.op("dve", lambda e: e.max(out=m8[:], in_=imp[:]), reads=[r_small], writes=[r_small])
                    S.op("dve", lambda e: e.match_replace(out=imp2[:], in_to_replace=m8[:], in_values=imp[:],
                                                          imm_value=-1e30), reads=[r_small], writes=[r_small])
                    S.op("dve", lambda e: e.max(out=m8[:], in_=imp2[:]), reads=[r_small], writes=[r_small])
                    S.op("dve", lambda e: e.tensor_scalar(out=selb[:], in0=imp[:], scalar1=m8[:, 7:8], scalar2=NEG,
                                                          op0=ALU.is_lt, op1=ALU.mult), reads=[r_small], writes=[r_small])
                    if "DBGSEL" in debug:
                        for nm_, src_ in (("DBGSEL", selb), ("DBGIMP", imp), ("DBGM8", m8)):
                            S.dma("sp", lambda e, nm_=nm_, src_=src_, qi=qi, g=g: e.dma_start(
                                out=dbg_t[nm_][qi, g], in_=src_[:]), d_dbg, reads=[r_small])
                    attend(list(range(i - 4, i + 1)), kwt, vw, g, qTg, rq, None, None, i)
                    finalize(gbc, rgbc, 2, acc, racc, False)
                    pT, rpT = RpS.next()
                    S.op("pe", lambda e, pT=pT: e.transpose(out=pT[0:64, 0:128], in_=selb[:, 0:64], identity=identf[:]),
                         reads=[r_small, r_k], writes=[rpT])
                    selbT, rselbT = RselbT.next()
                    S.op("act", lambda e, pT=pT, selbT=selbT: e.activation(
                        out=selbT[:], in_=pT[0:64, 0:128][:, None, :].to_broadcast([64, 4, 128]), func=AF.Copy),
                        reads=[rpT], writes=[rselbT])
                    attend(list(range(0, i + 1)), kst, vs, g, qTg, rq, selbT, rselbT, i)
                    finalize(gbc, rgbc, 1, acc, racc, False)
                    accb, raccb, daccb = Raccb.next()
                    S.op("act", lambda e, accb=accb, acc=acc: e.activation(out=accb[:], in_=acc[:], func=AF.Copy),
                         reads=[racc], writes=[raccb])
                    S.dma("sp", lambda e, accb=accb, g=g, qi=qi: e.dma_start(
                        out=NSAT[g * 8:(g + 1) * 8, :, qi * 128:(qi + 1) * 128].rearrange("h p q -> p h q"),
                        in_=accb[:].rearrange("p (h q) -> p h q", h=8)), daccb, reads=[raccb])

                for g in range(2):
                    do_group(i, qi, g, qT, rq, keep, rkeep, addt, radd, cm, rcm)
            S.emit(block)

    if 3 in phases:
        phase3a()
        phase3b()
    if 33 in phases:
        phase3b("x")


    QGROUPS = [(0, 128)] + [(128 + 512 * g, 512) for g in range(4)]

    def upproj(tag, srcT, w, gate0, addsrc, dst):
        with ExitStack() as st:
            sb = lambda name, shape, dt: st.enter_context(nc.sbuf_tensor(name, list(shape), dt))
            src = sb(f"{tag}_src", [128, 16, NQT], BF16)
            wb = [sb(f"{tag}_w{i}", [128, 16, 512], BF16) for i in range(2)]
            mg_ = [sb(f"{tag}_mg{i}", [128, NQT], BF16) for i in range(2)]
            m1_ = [sb(f"{tag}_m1{i}", [128, NQT], BF16) for i in range(2)]
            tf_ = [sb(f"{tag}_tf{i}", [128, 512], F32) for i in range(2)]
            o_ = [sb(f"{tag}_o{i}", [128, NQT], BF16) for i in range(2)]
            ps = [st.enter_context(nc.psum_tensor(f"{tag}_ps{i}", [128, 512], F32)) for i in range(4)]
            S = Sched(nc, st, tag)
            block = st.enter_context(nc.Block())
            r_src = Res()
            d_src = S.dsem()
            S.dma("sp", lambda e: e.dma_start(out=src[:], in_=srcT.rearrange("c p q -> p c q")), d_src, writes=[r_src])
            Rw, Rmg, Rm1, Ro = Ring(wb, S, True), Ring(mg_, S, True), Ring(m1_, S, True), Ring(o_, S, True)
            Rtf = Ring(tf_)
            Rps = Ring(ps, excl=True)
            wv = w.rearrange("(c p) n -> p c n", p=128)

            def chunk(wt, rw, ch, dc):
                mg, rmg, dmg = Rmg.next()
                S.dma("sp", lambda e: e.dma_start(out=mg[:], in_=MGT[gate0 + dc]), dmg, writes=[rmg])
                if addsrc is not None:
                    m1, rm1, dm1 = Rm1.next()
                    S.dma("sp", lambda e: e.dma_start(out=m1[:], in_=addsrc[dc]), dm1, writes=[rm1])
                o, ro, do = Ro.next()
                for (q0, nt) in QGROUPS:
                    p, rp = Rps.next()
                    for c in (range(16) if not os.environ.get("SKIPMM") else range(1)):
                        S.op("pe", lambda e, p=p, c=c, q0=q0, nt=nt: e.matmul(
                            p[:, 0:nt], lhsT=wt[:, c, ch * 128:(ch + 1) * 128], rhs=src[:, c, q0:q0 + nt],
                            start=(c == 0), stop=(c == 15)), reads=[rw, r_src], writes=[rp])
                    if addsrc is None:
                        S.op("dve", lambda e, p=p, q0=q0, nt=nt: e.tensor_tensor(
                            out=o[:, q0:q0 + nt], in0=p[:, 0:nt], in1=mg[:, q0:q0 + nt], op=ALU.mult),
                            reads=[rp, rmg], writes=[ro])
                    else:
                        tf, rtf = Rtf.next()
                        S.op("dve", lambda e, p=p, q0=q0, nt=nt, tf=tf: e.tensor_tensor(
                            out=tf[:, 0:nt], in0=p[:, 0:nt], in1=mg[:, q0:q0 + nt], op=ALU.mult),
                            reads=[rp, rmg], writes=[rtf])
                        S.op("pool", lambda e, q0=q0, nt=nt, tf=tf: e.tensor_tensor(
                            out=o[:, q0:q0 + nt], in0=tf[:, 0:nt], in1=m1[:, q0:q0 + nt], op=ALU.add),
                            reads=[rtf, rm1], writes=[ro])
                S.dma("sp", lambda e: e.dma_start(out=dst[dc], in_=o[:]), do, reads=[ro])

            def wblock(blk):
                wt, rw, dw = Rw.next()
                S.dma("pool", lambda e: e.dma_start(out=wt[:], in_=wv[:, :, blk * 512:(blk + 1) * 512]), dw, writes=[rw])
                for ch in range(4):
                    chunk(wt, rw, ch, blk * 4 + ch)

            for blk in range(4):
                wblock(blk)
            S.emit(block)

    def phase4b():
        with ExitStack() as st:
            sb = lambda name, shape, dt: st.enter_context(nc.sbuf_tensor(name, list(shape), dt))
            identb = sb("p4_id", [128, 128], BF16)
            wo = sb("p4_wo", [128, 16, D], BF16)
            w2bc = sb("p4_w2", [128, D], F32)
            mT_ = [sb(f"p4_mT{i}", [128, 16, 128], BF16) for i in range(2)]
            x_ = [sb(f"p4_x{i}", [128, D], F32) for i in range(2)]
            h_ = [sb(f"p4_h{i}", [128, D], F32) for i in range(2)]
            xn_ = [sb(f"p4_xn{i}", [128, D], BF16) for i in range(2)]
            xT_ = [sb(f"p4_xT{i}", [128, 16, 128], BF16) for i in range(2)]
            junk = sb("p4_junk", [128, D], BF16)
            ss_ = [sb(f"p4_ss{i}", [128, 1], F32) for i in range(2)]
            ps = [st.enter_context(nc.psum_tensor(f"p4_ps{i}", [128, 512], F32)) for i in range(4)]
            pt = [st.enter_context(nc.psum_tensor(f"p4_pt{i}", [128, 1024], BF16)) for i in range(2)]
            S = Sched(nc, st, "p4b")
            block = st.enter_context(nc.Block())
            r_id, r_wo, r_w2, r_junk = Res(), Res(), Res(), Res()
            d_p, d_s = S.dsem(), S.dsem()
            S.dma("pool", lambda e: e.dma_start(out=identb[:], in_=t_ident), d_p, writes=[r_id])
            wov = w_out.rearrange("(c p) n -> p c n", p=128)
            for blk in range(4):
                S.dma("pool", lambda e, blk=blk: e.dma_start(out=wo[:, :, blk * 512:(blk + 1) * 512],
                                                           in_=wov[:, :, blk * 512:(blk + 1) * 512]), d_p, writes=[r_wo])
            S.dma("sp", lambda e: e.dma_start(out=w2bc[:], in_=norm2_w.partition_broadcast(128)), d_s, writes=[r_w2])
            RmT, Rx, Rh, RxT = Ring(mT_, S, True), Ring(x_, S, True), Ring(h_, S, True), Ring(xT_, S, True)
            Rxn, Rss = Ring(xn_), Ring(ss_)
            Rps = Ring(ps, excl=True)
            r_pt = [Res(True), Res(True)]

            def tile(qi):
                mT, rmT, dmT = RmT.next()
                S.dma("sp", lambda e: e.dma_start(out=mT[:], in_=MRGT[:, :, qi * 128:(qi + 1) * 128].rearrange("c p q -> p c q")),
                      dmT, writes=[rmT])
                xt, rx, dx = Rx.next()
                S.dma("sp", lambda e: e.dma_start(out=xt[:], in_=x_loc[(QT0 + qi) * 128:(QT0 + qi + 1) * 128, :]), dx, writes=[rx])
                h, rh, dh = Rh.next()
                for cb in range(4):
                    p, rp = Rps.next()
                    for c in (range(16) if not os.environ.get("SKIPMM") else range(1)):
                        S.op("pe", lambda e, p=p, c=c, cb=cb: e.matmul(
                            p[:], lhsT=mT[:, c, :], rhs=wo[:, c, cb * 512:(cb + 1) * 512],
                            start=(c == 0), stop=(c == 15)), reads=[rmT, r_wo], writes=[rp])
                    S.op("dve", lambda e, p=p, cb=cb: e.tensor_tensor(
                        out=h[:, cb * 512:(cb + 1) * 512], in0=p[:], in1=xt[:, cb * 512:(cb + 1) * 512], op=ALU.add),
                        reads=[rp, rx], writes=[rh])
                S.dma("sp", lambda e: e.dma_start(out=H1[qi * 128:(qi + 1) * 128, :], in_=h[:]), dh, reads=[rh])
                ss, rss = Rss.next()
                S.op("act", lambda e: e.activation(out=junk[:], in_=h[:], func=AF.Square, accum_out=ss[:]),
                     reads=[rh], writes=[r_junk, rss])
                S.op("act", lambda e: e.activation(out=ss[:], in_=ss[:], func=AF.Sqrt, scale=1.0 / D, bias=EPS),
                     reads=[rss], writes=[rss])
                S.op("dve", lambda e: e.reciprocal(out=ss[:], in_=ss[:]), reads=[rss], writes=[rss])
                xn, rxn = Rxn.next()
                S.op("dve", lambda e: e.scalar_tensor_tensor(out=xn[:], in0=h[:], scalar=ss[:], in1=w2bc[:],
                                                             op0=ALU.mult, op1=ALU.mult),
                     reads=[rh, rss, r_w2], writes=[rxn])
                xT, rxT, dxT = RxT.next()
                for hh in range(2):
                    for cc in range(8):
                        c = hh * 8 + cc
                        S.op("pe", lambda e, c=c, cc=cc, hh=hh: e.transpose(
                            out=pt[hh][:, cc * 128:(cc + 1) * 128], in_=xn[:, c * 128:(c + 1) * 128], identity=identb[:]),
                            reads=[rxn, r_id], writes=[r_pt[hh]])
                    if hh == 0:
                        S.op("act", lambda e, hh=hh: e.activation(
                            out=xT[:, 0:8, :], in_=pt[0][:].rearrange("p (c q) -> p c q", c=8), func=AF.Copy),
                            reads=[r_pt[0]], writes=[rxT])
                    else:
                        S.op("dve", lambda e, hh=hh: e.tensor_copy(
                            out=xT[:, 8:16, :], in_=pt[1][:].rearrange("p (c q) -> p c q", c=8)),
                            reads=[r_pt[1]], writes=[rxT])
                S.dma("sp", lambda e: e.dma_start(
                    out=XN2T[:, :, qi * 128:(qi + 1) * 128].rearrange("c p q -> p c q"), in_=xT[:]), dxT, reads=[rxT])

            for qi in range(NQ):
                tile(qi)
            S.emit(block)

    if 4 in phases:
        upproj("p4a", RETGT, w_ret_up, 0, None, M1T)
        upproj("p4n", NSAT, w_nsa_up, 16, M1T, MRGT)
        phase4b()

    def phase5a():
        with ExitStack() as st:
            sb = lambda name, shape, dt: st.enter_context(nc.sbuf_tensor(name, list(shape), dt))
            xs = sb("p5_xs", [128, 16, NQT], BF16)
            wa_ = [sb(f"p5_wa{i}", [128, 16, 512], BF16) for i in range(2)]
            wb_ = [sb(f"p5_wb{i}", [128, 16, 512], BF16) for i in range(2)]
            cw = sb("p5_cw", [128, 88, 3], F32)
            cbias = sb("p5_cb", [128, 88], F32)
            halo = sb("p5_halo", [128, 1], F32)
            u_ = [sb(f"p5_u{i}", [128, 2050], F32) for i in range(4)]
            y_ = [sb(f"p5_y{i}", [128, 2048], F32) for i in range(3)]
            o_ = [sb(f"p5_o{i}", [128, 2048], BF16) for i in range(2)]
            ps = [st.enter_context(nc.psum_tensor(f"p5_ps{i}", [128, 512], F32)) for i in range(6)]
            ph = [st.enter_context(nc.psum_tensor(f"p5_ph{i}", [128, 2], F32)) for i in range(2)]
            S = Sched(nc, st, "p5a")
            block = st.enter_context(nc.Block())
            r_c = Res()
            d_c = S.dsem()
            S.dma("sp", lambda e: e.dma_start(out=xs[:], in_=XN2T.rearrange("c p q -> p c q")), d_c, writes=[r_c])
            S.dma("sp", lambda e: e.dma_start(out=cw[:], in_=conv_wT), d_c, writes=[r_c])
            S.dma("sp", lambda e: e.dma_start(out=cbias[:], in_=conv_bT), d_c, writes=[r_c])
            S.dma("sp", lambda e: e.dma_start(out=halo[:], in_=t_halo), d_c, writes=[r_c])
            Rwa, Rwb = Ring(wa_, S, True), Ring(wb_, S, True)
            Ru, Ry = Ring(u_), Ring(y_)
            Ro = Ring(o_, S, True)
            Rps, Rph = Ring(ps, excl=True), Ring(ph, excl=True)
            wv = w_ffn_up.rearrange("(c p) n -> p c n", p=128)

            def half(wt, rw, ch, cidx, ceng):
                u, ru = Ru.next()
                p, rp = Rph.next()
                for c in range(16):
                    S.op("pe", lambda e, c=c: e.matmul(p[:, 0:2], lhsT=wt[:, c, ch * 128:(ch + 1) * 128],
                                                       rhs=xs[:, c, 126:128], start=(c == 0), stop=(c == 15)),
                         reads=[rw, r_c], writes=[rp])
                S.op("act", lambda e: e.activation(out=u[:, 0:2], in_=p[:, 0:2], func=AF.Copy, scale=halo[:]),
                     reads=[rp, r_c], writes=[ru])
                for g in range(4):
                    pp, rpp = Rps.next()
                    for c in range(16):
                        S.op("pe", lambda e, c=c, pp=pp, g=g: e.matmul(
                            pp[:], lhsT=wt[:, c, ch * 128:(ch + 1) * 128],
                            rhs=xs[:, c, 128 + g * 512:128 + (g + 1) * 512], start=(c == 0), stop=(c == 15)),
                            reads=[rw, r_c], writes=[rpp])
                    S.op("act", lambda e, pp=pp, g=g: e.activation(out=u[:, 2 + g * 512:2 + (g + 1) * 512], in_=pp[:],
                                                                   func=AF.Copy), reads=[rpp], writes=[ru])
                y, ry = Ry.next()
                S.op(ceng, lambda e: e.tensor_scalar(out=y[:], in0=u[:, 2:2050], scalar1=cw[:, cidx, 2:3],
                                                     scalar2=cbias[:, cidx:cidx + 1], op0=ALU.mult, op1=ALU.add),
                     reads=[ru, r_c], writes=[ry])
                S.op(ceng, lambda e: e.scalar_tensor_tensor(out=y[:], in0=u[:, 1:2049], scalar=cw[:, cidx, 1:2], in1=y[:],
                                                            op0=ALU.mult, op1=ALU.add), reads=[ru, r_c, ry], writes=[ry])
                S.op(ceng, lambda e: e.scalar_tensor_tensor(out=y[:], in0=u[:, 0:2048], scalar=cw[:, cidx, 0:1], in1=y[:],
                                                            op0=ALU.mult, op1=ALU.add), reads=[ru, r_c, ry], writes=[ry])
                return y, ry

            def chunk(wa, rwa, wb, rwb, ch, j):
                ya, rya = half(wa, rwa, ch, j, "dve")
                yb, ryb = half(wb, rwb, ch, 44 + j, "dve")
                S.op("act", lambda e: e.activation(out=ya[:], in_=ya[:], func=AF.Silu), reads=[rya], writes=[rya])
                o, ro, do = Ro.next()
                S.op("pool", lambda e: e.tensor_tensor(out=o[:], in0=ya[:], in1=yb[:], op=ALU.mult),
                     reads=[rya, ryb], writes=[ro])
                S.dma("sp", lambda e: e.dma_start(out=ACTT[:, :, j, :].rearrange("t p q -> p t q"),
                                                  in_=o[:].rearrange("p (t q) -> p t q", q=128)), do, reads=[ro])

            def wblock(jb):
                wa, rwa, dwa = Rwa.next()
                wb, rwb, dwb = Rwb.next()
                S.dma("pool", lambda e: e.dma_start(out=wa[:], in_=wv[:, :, jb * 512:(jb + 1) * 512]), dwa, writes=[rwa])
                S.dma("pool", lambda e: e.dma_start(out=wb[:], in_=wv[:, :, DFF + jb * 512:DFF + (jb + 1) * 512]), dwb,
                      writes=[rwb])
                for ch in range(4):
                    chunk(wa, rwa, wb, rwb, ch, jb * 4 + ch)

            for jb in range(11):
                wblock(jb)
            S.emit(block)

    def phase5b():
        with ExitStack() as st:
            sb = lambda name, shape, dt: st.enter_context(nc.sbuf_tensor(name, list(shape), dt))
            wd_ = [sb(f"p5b_w{i}", [128, 44, 512], BF16) for i in range(2)]
            a_ = [sb(f"p5b_a{i}", [128, 44, 128], BF16) for i in range(3)]
            h_ = [sb(f"p5b_h{i}", [128, 512], F32) for i in range(3)]
            ps = [st.enter_context(nc.psum_tensor(f"p5b_ps{i}", [128, 512], F32)) for i in range(4)]
            S = Sched(nc, st, "p5b")
            block = st.enter_context(nc.Block())
            Rw, Ra, Rh = Ring(wd_, S, True), Ring(a_, S, True), Ring(h_, S, True)
            r_hs = [Res() for _ in h_]
            d_hs = [S.dsem() for _ in h_]
            Rps = Ring(ps, excl=True)
            wv = w_ffn_down.rearrange("(j p) n -> p j n", p=128)

            def tile(wt, rw, cb, t):
                a, ra, da = Ra.next()
                S.dma("sp", lambda e: e.dma_start(out=a[:], in_=ACTT[t]), da, writes=[ra])
                h, rh, dh = Rh.next()
                k = Rh.i
                S.dma("sp", lambda e: e.dma_start(out=h[:], in_=H1[(t + 1) * 128:(t + 2) * 128, cb * 512:(cb + 1) * 512]),
                      dh, writes=[rh])
                p, rp = Rps.next()
                for j in (range(44) if not os.environ.get("SKIPMM") else range(1)):
                    S.op("pe", lambda e, j=j: e.matmul(p[:], lhsT=a[:, j, :], rhs=wt[:, j, :], start=(j == 0), stop=(j == 43)),
                         reads=[ra, rw], writes=[rp])
                S.op("dve", lambda e: e.tensor_tensor(out=h[:], in0=p[:], in1=h[:], op=ALU.add), reads=[rp, rh], writes=[rh])
                S.dma("sp", lambda e: e.dma_start(out=H2[t * 128:(t + 1) * 128, cb * 512:(cb + 1) * 512], in_=h[:]),
                      d_hs[k], reads=[rh], writes=[r_hs[k]])

            def wblock(cb):
                wt, rw, dw = Rw.next()
                for q4 in range(4):
                    S.dma("pool", lambda e, q4=q4: e.dma_start(out=wt[:, q4 * 11:(q4 + 1) * 11, :],
                                                             in_=wv[:, q4 * 11:(q4 + 1) * 11, cb * 512:(cb + 1) * 512]),
                          dw, writes=[rw])
                for t in range(16):
                    tile(wt, rw, cb, t)

            for cb in range(4):
                wblock(cb)
            S.emit(block)

    def phase5c():
        with ExitStack() as st:
            sb = lambda name, shape, dt: st.enter_context(nc.sbuf_tensor(name, list(shape), dt))
            wf = sb("p5c_wf", [128, D], F32)
            h_ = [sb(f"p5c_h{i}", [128, D], F32) for i in range(2)]
            o_ = [sb(f"p5c_o{i}", [128, D], F32) for i in range(2)]
            junk = sb("p5c_junk", [128, D], BF16)
            ss_ = [sb(f"p5c_ss{i}", [128, 1], F32) for i in range(2)]
            S = Sched(nc, st, "p5c")
            block = st.enter_context(nc.Block())
            r_w, r_junk = Res(), Res()
            S.dma("sp", lambda e: e.dma_start(out=wf[:], in_=final_norm_w.partition_broadcast(128)), S.dsem(), writes=[r_w])
            Rh, Ro, Rss = Ring(h_, S, True), Ring(o_, S, True), Ring(ss_)

            def tile(t):
                h, rh, dh = Rh.next()
                S.dma("sp", lambda e: e.dma_start(out=h[:], in_=H2[t * 128:(t + 1) * 128, :]), dh, writes=[rh])
                ss, rss = Rss.next()
                S.op("act", lambda e: e.activation(out=junk[:], in_=h[:], func=AF.Square, accum_out=ss[:]),
                     reads=[rh], writes=[r_junk, rss])
                S.op("act", lambda e: e.activation(out=ss[:], in_=ss[:], func=AF.Sqrt, scale=1.0 / D, bias=EPS),
                     reads=[rss], writes=[rss])
                S.op("dve", lambda e: e.reciprocal(out=ss[:], in_=ss[:]), reads=[rss], writes=[rss])
                o, ro, do = Ro.next()
                S.op("dve", lambda e: e.scalar_tensor_tensor(out=o[:], in0=h[:], scalar=ss[:], in1=wf[:],
                                                             op0=ALU.mult, op1=ALU.mult), reads=[rh, rss, r_w], writes=[ro])
                S.dma("sp", lambda e: e.dma_start(out=out[t * 128:(t + 1) * 128, :], in_=o[:]), do, reads=[ro])

            for t in range(16):
                tile(t)
            S.emit(block)

    if 5 in phases:
        phase5a()
        phase5b()
        phase5c()

    return nc


def _tables(s):
    pad = 2048 if s == 0 else 0
    tl = np.arange(LT * 128)
    act = np.maximum(tl - pad, 0).astype(np.float64)
    is_pad = tl < pad
    half = 64
    freq = 10000.0 ** (-np.arange(half, dtype=np.float64) / half)
    ang = act[:, None] * freq[None, :]
    cs = np.concatenate([np.cos(ang), np.sin(ang)], axis=1)
    t_cs = cs.reshape(LT, 128, 128).transpose(1, 0, 2).astype(np.float32)
    t_csq = (cs[QT0 * 128:] * (128 ** -0.5)).reshape(NQ, 128, 128).transpose(1, 0, 2).astype(np.float32)
    gam = 1.0 - 2.0 ** (-5.0 - np.arange(8, dtype=np.float64))
    j = np.arange(128, dtype=np.float64)
    rel = j[None, :] - j[:, None]
    decT = np.where(rel[:, None, :] >= 0, gam[None, :, None] ** np.maximum(rel[:, None, :], 0), 0.0)
    qdec = np.broadcast_to((gam[:, None] ** (j[None, :] + 1.0))[None], (128, 8, 128))
    kdec = gam[None, :] ** (127.0 - j[:, None])
    padb = np.where(is_pad, NEG, 0.0).reshape(LT, 128).T
    n = np.arange(256)
    cmp_invalid = (n * 16 < pad) | (n >= 255)
    cmpb = np.where(cmp_invalid, NEG, 0.0).reshape(2, 128).T
    r = np.arange(512)[:, None]
    q = np.arange(128)[None, :]
    cm = np.where(16 * (r - 250) + 31 <= q, 0.0, NEG)
    keep = np.ones((NQ, 128, 64))
    add = np.zeros((NQ, 128, 64))
    blk = np.arange(64)[None, :]
    b0 = pad // 64
    for qi in range(NQ):
        t = (QT0 + qi) * 128 + np.arange(128)
        cur = (t // 64)[:, None]
        forced = (blk == b0) | (blk == cur) | (blk == cur - 1)
        neg = (blk > cur) | (blk < b0)
        keep[qi] = np.where(forced | neg, 0.0, 1.0)
        add[qi] = np.where(neg, -1e4, np.where(forced, 1e4, 0.0))
    sidx = np.arange(64)[:, None, None]
    kt = np.arange(LT)[None, :, None]
    kk = np.arange(128)[None, None, :]
    texp = (sidx == 2 * kt + (kk >= 64)).astype(np.float32)
    k_ = np.arange(128)[:, None]
    caus = np.where(k_ > q, NEG, 0.0)
    acaus = np.where(k_ <= q, NEG, 0.0)
    nn = np.arange(256)[:, None]
    ss = np.arange(64)[None, :]
    ov = ((nn * 16 < ss * 64 + 64) & (nn * 16 + 32 > ss * 64)).astype(np.float64)
    ov = np.concatenate([ov, np.ones((256, 1))], axis=1)
    ov[255] = 0.0
    t_ov = ov.reshape(2, 128, 65).transpose(1, 0, 2)
    f = lambda a: np.ascontiguousarray(a, dtype=np.float32)
    return {
        "t_cs": f(t_cs), "t_csq": f(t_csq), "t_decT": f(decT), "t_qdec": f(qdec), "t_kdec": f(kdec),
        "t_padb": f(padb), "t_cmpb": f(cmpb), "t_cm": f(cm), "t_keep": f(keep), "t_add": f(add),
        "t_exp": f(texp), "t_caus": f(caus), "t_acaus": f(acaus), "t_ident": f(np.eye(128)),
        "t_ov": f(t_ov), "t_halo": f(np.full((128, 1), float(s))),
    }


def make_in_maps(inputs):
    g = lambda k: np.asarray(inputs[k], dtype=np.float32)
    x = g("x")
    shared = {
        "norm1_w": g("norm1_w")[0][None], "w_in": g("w_in")[0], "ret_norm_w": g("ret_norm_w")[0][None],
        "w_ret_up": g("w_ret_up")[0],
        "cmp_peT_k": np.ascontiguousarray(g("cmp_pe_k")[0].T), "cmp_peT_v": np.ascontiguousarray(g("cmp_pe_v")[0].T),
        "cmp_w1_k": g("cmp_w1_k")[0], "cmp_w1_v": g("cmp_w1_v")[0],
        "cmp_w2_k": g("cmp_w2_k")[0], "cmp_w2_v": g("cmp_w2_v")[0],
        "w_nsa_up": g("w_nsa_up")[0], "w_out": g("w_out")[0], "norm2_w": g("norm2_w")[0][None],
        "w_ffn_up": g("w_ffn_up")[0],
        "conv_wT": np.ascontiguousarray(g("conv_w")[0].reshape(3, 88, 128).transpose(2, 1, 0)),
        "conv_bT": np.ascontiguousarray(g("conv_b")[0].reshape(88, 128).T),
        "w_ffn_down": g("w_ffn_down")[0], "final_norm_w": g("final_norm_w")[None],
    }
    tabs = [_tables(0), _tables(1)]
    maps = []
    for c in range(8):
        b, s = c // 2, c % 2
        if s == 1:
            xl = np.ascontiguousarray(x[b])
        else:
            xl = np.concatenate([np.zeros((2048, D), np.float32), x[b, :2048]], axis=0)
        m = dict(shared)
        m.update(tabs[s])
        m["x_loc"] = xl
        maps.append(m)
    return maps


_NC = None


def kernel(**inputs):
    global _NC
    if _NC is None:
        _NC = build_program()
    maps = make_in_maps(inputs)
    res = run_bass_kernel_spmd(_NC, maps, core_ids=list(range(8)))
    outp = np.zeros((4, 4096, D), np.float32)
    for c in range(8):
        b, s = c // 2, c % 2
        outp[b, s * 2048:(s + 1) * 2048] = res.results[c]["out"]
    return outp
```

```python
import math
import os
from contextlib import ExitStack
import numpy as np
import concourse.bass as bass
import concourse.mybir as mybir
from concourse.bass_utils import run_bass_kernel_spmd

F32 = mybir.dt.float32
BF16 = mybir.dt.bfloat16
AF = mybir.ActivationFunctionType
ALU = mybir.AluOpType
AX = mybir.AxisListType

D = 2048
LT = 32
QT0 = 15
NQ = LT - QT0
NQT = NQ * 128
IN_COLS = 13872
DFF = 5632
EPS = 1e-6
NEG = -30000.0
SCALE = 128 ** -0.5


class Tok:
    __slots__ = ("sem", "val")

    def __init__(self, sem, val):
        self.sem = sem
        self.val = val


class Res:
    __slots__ = ("w", "r", "excl")

    def __init__(self, excl=False):
        self.w = None
        self.r = []
        self.excl = excl


class DSem:
    __slots__ = ("sem", "val", "eng")

    def __init__(self, sem):
        self.sem = sem
        self.val = 0
        self.eng = None


ENGS = ("pe", "act", "dve", "pool", "sp")


class Sched:
    def __init__(self, nc, stack, tag):
        self.nc = nc
        self.q = {e: [] for e in ENGS}
        self.cnt = {e: 0 for e in ENGS}
        self.allsems = []
        self.sem = {e: self._alloc(f"{tag}_s_{e}") for e in ENGS}
        self.waited = {e: {} for e in ENGS}
        self.dsems = []
        self.tag = tag
        stack.callback(self._cleanup)

    def _alloc(self, name):
        h = self.nc.alloc_semaphore(name=name)
        self.allsems.append(h)
        return h

    def _cleanup(self):
        self.nc.clear_and_free_semaphores(self.allsems)
        self.nc.all_engine_barrier()

    def dsem(self):
        d = DSem(self._alloc(f"{self.tag}_d{len(self.dsems)}"))
        self.dsems.append(d)
        return d

    def _deps(self, eng, reads, writes):
        need = {}

        def add(t):
            k = id(t.sem)
            if k not in need or need[k].val < t.val:
                need[k] = t

        for r in reads:
            if r.w is not None:
                add(r.w)
        for w in writes:
            if w.w is not None:
                add(w.w)
            for t in w.r:
                add(t)
        waits = []
        wd = self.waited[eng]
        own = id(self.sem[eng])
        for k, t in need.items():
            if eng == "pe" and k == own:
                continue
            if wd.get(k, 0) < t.val:
                wd[k] = t.val
                waits.append((t.sem, t.val))
        return waits

    def _fin(self, tok, reads, writes):
        for r in reads:
            r.r.append(tok)
        for w in writes:
            w.w = tok
            w.r = []
        return tok

    def op(self, eng, fn, reads=(), writes=()):
        ex = [r for r in reads if r.excl]
        if ex:
            reads = [r for r in reads if not r.excl]
            writes = list(writes) + ex
        waits = self._deps(eng, reads, writes)
        self.cnt[eng] += 1
        self.q[eng].append((waits, fn, (self.sem[eng], 1)))
        return self._fin(Tok(self.sem[eng], self.cnt[eng]), reads, writes)

    def dma(self, eng, fn, ds, reads=(), writes=()):
        assert ds.eng in (None, eng), "one issuing engine per DMA semaphore"
        ds.eng = eng
        waits = self._deps(eng, reads, writes)
        ds.val += 16
        self.q[eng].append((waits, fn, (ds.sem, 16)))
        return self._fin(Tok(ds.sem, ds.val), reads, writes)

    def emit(self, block):
        finals = [(d.sem, d.val) for d in self.dsems if d.val > 0]

        def run(engobj, name, tail=False):
            for waits, fn, inc in self.q[name]:
                for s, v in waits:
                    engobj.wait_ge(s, v)
                fn(engobj).then_inc(inc[0], inc[1])
            if tail:
                for s, v in finals:
                    engobj.wait_ge(s, v)
                for e in ENGS:
                    if e != name and self.cnt[e] > 0:
                        engobj.wait_ge(self.sem[e], self.cnt[e])

        @block.sync
        def _(eng):
            run(eng, "sp", tail=True)

        @block.tensor
        def _(eng):
            run(eng, "pe")

        @block.scalar
        def _(eng):
            run(eng, "act")

        @block.vector
        def _(eng):
            run(eng, "dve")

        @block.gpsimd
        def _(eng):
            run(eng, "pool")


class Ring:
    def __init__(self, bufs, S=None, with_dsem=False, excl=False):
        self.bufs = bufs
        self.res = [Res(excl) for _ in bufs]
        self.ds = [S.dsem() for _ in bufs] if with_dsem else None
        self.i = -1

    def next(self):
        self.i = (self.i + 1) % len(self.bufs)
        if self.ds is not None:
            return self.bufs[self.i], self.res[self.i], self.ds[self.i]
        return self.bufs[self.i], self.res[self.i]


ALL_PHASES = (1, 2, 3, 4, 5)


def build_program(debug=(), phases=ALL_PHASES):
    nc = bass.Bass("TRN2", target_bir_lowering=False)

    def din(name, shape, dt=F32):
        return nc.dram_tensor(name, list(shape), dt, kind="ExternalInput").ap()

    def dscr(name, shape, dt=BF16):
        kind = "ExternalOutput" if name in debug else "Internal"
        return nc.dram_tensor(name, list(shape), dt, kind=kind).ap()

    x_loc = din("x_loc", [LT * 128, D])
    norm1_w = din("norm1_w", [1, D])
    w_in = din("w_in", [D, IN_COLS])
    ret_norm_w = din("ret_norm_w", [1, D])
    w_ret_up = din("w_ret_up", [D, D])
    cmp_peT = {"k": din("cmp_peT_k", [128, 32]), "v": din("cmp_peT_v", [128, 32])}
    cmp_w1 = {"k": din("cmp_w1_k", [4096, 256]), "v": din("cmp_w1_v", [4096, 256])}
    cmp_w2 = {"k": din("cmp_w2_k", [256, 128]), "v": din("cmp_w2_v", [256, 128])}
    w_nsa_up = din("w_nsa_up", [D, D])
    w_out = din("w_out", [D, D])
    norm2_w = din("norm2_w", [1, D])
    w_ffn_up = din("w_ffn_up", [D, 2 * DFF])
    conv_wT = din("conv_wT", [128, 88, 3])
    conv_bT = din("conv_bT", [128, 88])
    w_ffn_down = din("w_ffn_down", [DFF, D])
    final_norm_w = din("final_norm_w", [1, D])
    t_cs = din("t_cs", [128, LT, 128])
    t_csq = din("t_csq", [128, NQ, 128])
    t_decT = din("t_decT", [128, 8, 128])
    t_qdec = din("t_qdec", [128, 8, 128])
    t_kdec = din("t_kdec", [128, 8])
    t_padb = din("t_padb", [128, LT])
    t_cmpb = din("t_cmpb", [128, 2])
    t_cm = din("t_cm", [512, 128])
    t_keep = din("t_keep", [NQ, 128, 64])
    t_add = din("t_add", [NQ, 128, 64])
    t_exp = din("t_exp", [64, LT, 128])
    t_caus = din("t_caus", [128, 128])
    t_acaus = din("t_acaus", [128, 128])
    t_ident = din("t_ident", [128, 128])
    t_ov = din("t_ov", [128, 2, 65])
    t_halo = din("t_halo", [128, 1])

    out = nc.dram_tensor("out", [16 * 128, D], F32, kind="ExternalOutput").ap()

    RK = dscr("RK", [LT * 128, 1024])
    RV = dscr("RV", [LT * 128, 2048])
    RQ = dscr("RQ", [NQT, 1024])
    RG = dscr("RG", [NQT, 2048])
    NQT_ = dscr("NQT", [16, 128, NQT])
    KCT = dscr("KCT", [2, 128, LT * 128])
    VCT = dscr("VCT", [2, 128, LT * 128])
    KST = dscr("KST", [2, 128, LT * 128])
    KWT = dscr("KWT", [2, 128, LT * 128])
    VS = dscr("VS", [LT * 128, 256])
    VW = dscr("VW", [LT * 128, 256])
    NGT = dscr("NGT", [48, NQT], F32)
    MGT = dscr("MGT", [32, 128, NQT])
    RETGT = dscr("RETGT", [16, 128, NQT])
    NSAT = dscr("NSAT", [16, 128, NQT])
    M1T = dscr("M1T", [16, 128, NQT])
    MRGT = dscr("MRGT", [16, 128, NQT])
    H1 = dscr("H1", [NQT, D], F32)
    XN2T = dscr("XN2T", [16, 128, NQT])
    ACTT = dscr("ACTT", [16, 128, 44, 128])
    H2 = dscr("H2", [2048, D], F32)

    w_in_v = w_in.rearrange("(c p) n -> p c n", p=128)

    def phase01():
        with ExitStack() as st01:
            xnT = st01.enter_context(nc.sbuf_tensor("xnT", [128, 16, LT * 128], BF16))
            identb = st01.enter_context(nc.sbuf_tensor("identb", [128, 128], BF16))

            with ExitStack() as st:
                xin = [st.enter_context(nc.sbuf_tensor(f"p0_x{i}", [128, D], F32)) for i in range(2)]
                sq = st.enter_context(nc.sbuf_tensor("p0_sq", [128, D], BF16))
                xnb = [st.enter_context(nc.sbuf_tensor(f"p0_xn{i}", [128, D], BF16)) for i in range(2)]
                wbc = st.enter_context(nc.sbuf_tensor("p0_wbc", [128, D], F32))
                ss = [st.enter_context(nc.sbuf_tensor(f"p0_ss{i}", [128, 1], F32)) for i in range(2)]
                rs = [st.enter_context(nc.sbuf_tensor(f"p0_rs{i}", [128, 1], F32)) for i in range(2)]
                pt = [st.enter_context(nc.psum_tensor(f"p0_pt{i}", [128, D], BF16)) for i in range(2)]
                S = Sched(nc, st, "p0")
                block = st.enter_context(nc.Block())
                r_c = Res()
                d_c = S.dsem()
                S.dma("sp", lambda e: e.dma_start(out=wbc[:], in_=norm1_w.partition_broadcast(128)), d_c, writes=[r_c])
                r_id = Res()
                S.dma("pool", lambda e: e.dma_start(out=identb[:], in_=t_ident), S.dsem(), writes=[r_id])
                Rx = Ring(xin, S, True)
                Rxn = Ring(xnb)
                Rss = Ring(ss)
                Rrs = Ring(rs)
                Rpt = Ring(pt, excl=True)
                r_sq = Res()
                r_xnT = Res()
                for t in range(LT):
                    xb, rx, dx = Rx.next()
                    S.dma("sp", lambda e, xb=xb, t=t: e.dma_start(out=xb[:], in_=x_loc[t * 128:(t + 1) * 128, :]),
                          dx, writes=[rx])
                    sb, rss = Rss.next()
                    S.op("act", lambda e, xb=xb, sb=sb: e.activation(out=sq[:], in_=xb[:], func=AF.Square,
                                                                      accum_out=sb[:]),
                         reads=[rx], writes=[r_sq, rss])
                    rb, rrs = Rrs.next()
                    S.op("act", lambda e, sb=sb, rb=rb: e.activation(out=rb[:], in_=sb[:], func=AF.Sqrt,
                                                                     scale=1.0 / D, bias=EPS),
                         reads=[rss], writes=[rrs])
                    S.op("dve", lambda e, rb=rb: e.reciprocal(out=rb[:], in_=rb[:]),
                         reads=[rrs], writes=[rrs])
                    xn, rxn = Rxn.next()
                    S.op("dve", lambda e, xn=xn, xb=xb, rb=rb: e.scalar_tensor_tensor(
                        out=xn[:], in0=xb[:], scalar=rb[:], in1=wbc[:], op0=ALU.mult, op1=ALU.mult),
                        reads=[rx, rrs, r_c], writes=[rxn])
                    pb, rp = Rpt.next()
                    for c in range(16):
                        S.op("pe", lambda e, pb=pb, xn=xn, c=c: e.transpose(
                            out=pb[:, c * 128:(c + 1) * 128], in_=xn[:, c * 128:(c + 1) * 128], identity=identb[:]),
                            reads=[rxn, r_id], writes=[rp])
                    for hh in range(2):
                        eng = "act" if hh == 0 else "dve"
                        if eng == "act":
                            S.op("act", lambda e, pb=pb, t=t, hh=hh: e.activation(
                                out=xnT[:, hh * 8:(hh + 1) * 8, t * 128:(t + 1) * 128],
                                in_=pb[:, hh * 1024:(hh + 1) * 1024].rearrange("p (c q) -> p c q", c=8),
                                func=AF.Copy), reads=[rp], writes=[r_xnT])
                        else:
                            S.op("dve", lambda e, pb=pb, t=t, hh=hh: e.tensor_copy(
                                out=xnT[:, hh * 8:(hh + 1) * 8, t * 128:(t + 1) * 128],
                                in_=pb[:, hh * 1024:(hh + 1) * 1024].rearrange("p (c q) -> p c q", c=8)),
                                reads=[rp], writes=[r_xnT])
                S.emit(block)

            if 0 in phases and 1 not in phases:
                return

            with ExitStack() as st:
                wb = [st.enter_context(nc.sbuf_tensor(f"p1_w{i}", [128, 16, 512], BF16)) for i in range(2)]
                cs = st.enter_context(nc.sbuf_tensor("p1_cs", [128, LT, 128], F32))
                csq = st.enter_context(nc.sbuf_tensor("p1_csq", [128, NQ, 128], F32))
                xf = [st.enter_context(nc.sbuf_tensor(f"p1_xf{i}", [128, 512], F32)) for i in range(2)]
                tmp = [st.enter_context(nc.sbuf_tensor(f"p1_t{i}", [128, 4, 256], F32)) for i in range(2)]
                ob = [st.enter_context(nc.sbuf_tensor(f"p1_o{i}", [128, 512], BF16)) for i in range(3)]
                of = [st.enter_context(nc.sbuf_tensor(f"p1_of{i}", [48, 512], F32)) for i in range(2)]
                ps = [st.enter_context(nc.psum_tensor(f"p1_ps{i}", [128, 512], F32)) for i in range(4)]
                S = Sched(nc, st, "p1")
                block = st.enter_context(nc.Block())
                r_tab = Res()
                d_tab = S.dsem()
                S.dma("sp", lambda e: e.dma_start(out=cs[:], in_=t_cs), d_tab, writes=[r_tab])
                S.dma("sp", lambda e: e.dma_start(out=csq[:], in_=t_csq), d_tab, writes=[r_tab])
                Rw = Ring(wb, S, True)
                Rps = Ring(ps, excl=True)
                Rxf = Ring(xf)
                Rtmp = Ring(tmp)
                Rob = Ring(ob, S, True)
                Rof = Ring(of, S, True)
                r_x = Res()
                alt = [0]

                def load_w(col0, ncols):
                    w, rw, dw = Rw.next()
                    S.dma("pool", lambda e, w=w: e.dma_start(out=w[:, :, 0:ncols], in_=w_in_v[:, :, col0:col0 + ncols]),
                          dw, writes=[rw])
                    return w, rw

                def mm_tm(w, rw, t, c0, n):
                    p, rp = Rps.next()
                    for c in range(16):
                        S.op("pe", lambda e, p=p, c=c: e.matmul(
                            p[:, 0:n], lhsT=xnT[:, c, t * 128:(t + 1) * 128], rhs=w[:, c, c0:c0 + n],
                            start=(c == 0), stop=(c == 15)), reads=[rw, r_x], writes=[rp])
                    return p, rp

                def mm_fm(w, rw, tok0, ntok, c0, m):
                    p, rp = Rps.next()
                    for c in range(16):
                        S.op("pe", lambda e, p=p, c=c: e.matmul(
                            p[0:m, 0:ntok], lhsT=w[:, c, c0:c0 + m], rhs=xnT[:, c, tok0:tok0 + ntok],
                            start=(c == 0), stop=(c == 15)), reads=[rw, r_x], writes=[rp])
                    return p, rp

                def evac_store(p, rp, m, n, dst, func=None):
                    o, ro, do = Rob.next()
                    if func is not None:
                        S.op("act", lambda e: e.activation(out=o[0:m, 0:n], in_=p[0:m, 0:n], func=func),
                             reads=[rp], writes=[ro])
                    else:
                        alt[0] ^= 1
                        if alt[0]:
                            S.op("act", lambda e: e.activation(out=o[0:m, 0:n], in_=p[0:m, 0:n], func=AF.Copy),
                                 reads=[rp], writes=[ro])
                        else:
                            S.op("dve", lambda e: e.tensor_copy(out=o[0:m, 0:n], in_=p[0:m, 0:n]),
                                 reads=[rp], writes=[ro])
                    S.dma("sp", lambda e: e.dma_start(out=dst, in_=o[0:m, 0:n]), do, reads=[ro])

                def rotary(p, rp, ctab, ti, dst):
                    x_, rxf = Rxf.next()
                    S.op("act", lambda e: e.activation(out=x_[:], in_=p[:], func=AF.Copy), reads=[rp], writes=[rxf])
                    xv = x_[:].rearrange("p (h t d) -> p h t d", h=4, t=2)
                    cosb = ctab[:, ti, 0:64][:, None, :].to_broadcast([128, 4, 64])
                    sinb = ctab[:, ti, 64:128][:, None, :].to_broadcast([128, 4, 64])
                    tm_, rt = Rtmp.next()
                    o, ro, do = Rob.next()
                    ov = o[:].rearrange("p (h t d) -> p h t d", h=4, t=2)
                    x1 = xv[:, :, 0, :]
                    x2 = xv[:, :, 1, :]
                    S.op("pool", lambda e: e.tensor_tensor(out=tm_[:, :, 0:64], in0=x1, in1=cosb, op=ALU.mult),
                         reads=[rxf, r_tab], writes=[rt])
                    S.op("pool", lambda e: e.tensor_tensor(out=tm_[:, :, 64:128], in0=x2, in1=sinb, op=ALU.mult),
                         reads=[rxf, r_tab], writes=[rt])
                    S.op("dve", lambda e: e.tensor_tensor(out=tm_[:, :, 128:192], in0=x1, in1=sinb, op=ALU.mult),
                         reads=[rxf, r_tab], writes=[rt])
                    S.op("dve", lambda e: e.tensor_tensor(out=tm_[:, :, 192:256], in0=x2, in1=cosb, op=ALU.mult),
                         reads=[rxf, r_tab], writes=[rt])
                    S.op("pool", lambda e: e.tensor_tensor(out=ov[:, :, 0, :], in0=tm_[:, :, 0:64],
                                                           in1=tm_[:, :, 64:128], op=ALU.subtract),
                         reads=[rt], writes=[ro])
                    S.op("dve", lambda e: e.tensor_tensor(out=ov[:, :, 1, :], in0=tm_[:, :, 128:192],
                                                          in1=tm_[:, :, 192:256], op=ALU.add),
                         reads=[rt], writes=[ro])
                    S.dma("sp", lambda e: e.dma_start(out=dst, in_=o[:]), do, reads=[ro])

                qtiles = list(range(QT0, LT))
                qgroups = [(QT0 * 128, 128)] + [((16 + 4 * g) * 128, 512) for g in range(4)]
                agroups = [(g * 512, 512) for g in range(8)]

                for blk in range(2):
                    w, rw = load_w(1024 + blk * 512, 512)
                    for t in range(LT):
                        p, rp = mm_tm(w, rw, t, 0, 512)
                        rotary(p, rp, cs, t, RK[t * 128:(t + 1) * 128, blk * 512:(blk + 1) * 512])
                for blk in range(2):
                    w, rw = load_w(blk * 512, 512)
                    for t in qtiles:
                        p, rp = mm_tm(w, rw, t, 0, 512)
                        qi = t - QT0
                        rotary(p, rp, csq, qi, RQ[qi * 128:(qi + 1) * 128, blk * 512:(blk + 1) * 512])
                for blk in range(4):
                    w, rw = load_w(2048 + blk * 512, 512)
                    for t in range(LT):
                        p, rp = mm_tm(w, rw, t, 0, 512)
                        evac_store(p, rp, 128, 512, RV[t * 128:(t + 1) * 128, blk * 512:(blk + 1) * 512])
                w, rw = load_w(8192, 512)
                for (dst, c0) in ((KCT, 0), (VCT, 256)):
                    for g in range(2):
                        for tok0, ntok in agroups:
                            p, rp = mm_fm(w, rw, tok0, ntok, c0 + g * 128, 128)
                            evac_store(p, rp, 128, ntok, dst[g, :, tok0:tok0 + ntok])
                for (col0, dfm, dtm) in ((8704, KST, VS), (9216, KWT, VW)):
                    w, rw = load_w(col0, 512)
                    for g in range(2):
                        for tok0, ntok in agroups:
                            p, rp = mm_fm(w, rw, tok0, ntok, g * 128, 128)
                            evac_store(p, rp, 128, ntok, dfm[g, :, tok0:tok0 + ntok])
                    for t in range(LT):
                        p, rp = mm_tm(w, rw, t, 256, 256)
                        evac_store(p, rp, 128, 256, dtm[t * 128:(t + 1) * 128, :])
                for blk in range(4):
                    w, rw = load_w(6144 + blk * 512, 512)
                    for ch in range(4):
                        for tok0, ntok in qgroups:
                            p, rp = mm_fm(w, rw, tok0, ntok, ch * 128, 128)
                            q0 = tok0 - QT0 * 128
                            evac_store(p, rp, 128, ntok, NQT_[blk * 4 + ch, :, q0:q0 + ntok])
                for blk in range(4):
                    w, rw = load_w(4096 + blk * 512, 512)
                    for t in qtiles:
                        p, rp = mm_tm(w, rw, t, 0, 512)
                        qi = t - QT0
                        evac_store(p, rp, 128, 512, RG[qi * 128:(qi + 1) * 128, blk * 512:(blk + 1) * 512],
                                   func=AF.Silu)
                for blk in range(8):
                    w, rw = load_w(9776 + blk * 512, 512)
                    for ch in range(4):
                        for tok0, ntok in qgroups:
                            p, rp = mm_fm(w, rw, tok0, ntok, ch * 128, 128)
                            q0 = tok0 - QT0 * 128
                            evac_store(p, rp, 128, ntok, MGT[blk * 4 + ch, :, q0:q0 + ntok], func=AF.Sigmoid)
                w, rw = load_w(9728, 48)
                for tok0, ntok in qgroups:
                    p, rp = mm_fm(w, rw, tok0, ntok, 0, 48)
                    q0 = tok0 - QT0 * 128
                    o, ro, do = Rof.next()
                    S.op("act", lambda e, o=o, p=p, ntok=ntok: e.activation(out=o[0:48, 0:ntok], in_=p[0:48, 0:ntok],
                                                                           func=AF.Sigmoid), reads=[rp], writes=[ro])
                    S.dma("sp", lambda e, o=o, q0=q0, ntok=ntok: e.dma_start(out=NGT[:, q0:q0 + ntok],
                                                                            in_=o[0:48, 0:ntok]), do, reads=[ro])
                S.emit(block)

    if 1 in phases:
        phase01()

    def phase2():
        gam = [1.0 - 2.0 ** (-5.0 - h) for h in range(8)]
        with ExitStack() as st:
            sb = lambda name, shape, dt: st.enter_context(nc.sbuf_tensor(name, list(shape), dt))
            identb = sb("p2_id", [128, 128], BF16)
            decT = sb("p2_decT", [128, 8, 128], F32)
            qdec = sb("p2_qdec", [128, 8, 128], F32)
            kdec = sb("p2_kdec", [128, 8], F32)
            retw = sb("p2_retw", [128, D], F32)
            kc_ = [sb(f"p2_k{i}", [128, 1024], BF16) for i in range(2)]
            vc_ = [sb(f"p2_v{i}", [128, 2048], BF16) for i in range(2)]
            qc_ = [sb(f"p2_q{i}", [128, 1024], BF16) for i in range(2)]
            gc_ = [sb(f"p2_g{i}", [128, 2048], BF16) for i in range(2)]
            kd_ = [sb(f"p2_kd{i}", [128, 1024], BF16) for i in range(2)]
            kT_ = [sb(f"p2_kT{i}", [128, 1024], BF16) for i in range(2)]
            qT_ = [sb(f"p2_qT{i}", [128, 1024], BF16) for i in range(2)]
            qTd_ = [sb(f"p2_qTd{i}", [128, 1024], BF16) for i in range(2)]
            ST_ = [sb(f"p2_ST{i}", [128, 512], BF16) for i in range(2)]
            stf = sb("p2_stf", [128, 2048], F32)
            stb = [sb(f"p2_stb{i}", [128, 2048], BF16) for i in range(2)]
            junk = sb("p2_junk", [128, 256], BF16)
            ssq_ = [sb(f"p2_ssq{i}", [128, 4], F32) for i in range(2)]
            tmpf_ = [sb(f"p2_tmpf{i}", [128, 1024], F32) for i in range(2)]
            gat_ = [sb(f"p2_gat{i}", [128, 2048], BF16) for i in range(2)]
            gT_ = [sb(f"p2_gT{i}", [128, 16, 128], BF16) for i in range(2)]
            pkT = st.enter_context(nc.psum_tensor("p2_pkT", [128, 1024], BF16))
            pqT = st.enter_context(nc.psum_tensor("p2_pqT", [128, 1024], BF16))
            pS = st.enter_context(nc.psum_tensor("p2_pS", [128, 512], F32))
            pO = st.enter_context(nc.psum_tensor("p2_pO", [128, 1024], F32))
            pKV = st.enter_context(nc.psum_tensor("p2_pKV", [128, 1024], F32))
            pGT = st.enter_context(nc.psum_tensor("p2_pGT", [128, 1024], BF16))
            S = Sched(nc, st, "p2")
            block = st.enter_context(nc.Block())
            r_tab = Res()
            d_tab = S.dsem()
            r_id = Res()
            S.dma("pool", lambda e: e.dma_start(out=identb[:], in_=t_ident), S.dsem(), writes=[r_id])
            S.dma("sp", lambda e: e.dma_start(out=decT[:], in_=t_decT), d_tab, writes=[r_tab])
            S.dma("sp", lambda e: e.dma_start(out=qdec[:], in_=t_qdec), d_tab, writes=[r_tab])
            S.dma("sp", lambda e: e.dma_start(out=kdec[:], in_=t_kdec), d_tab, writes=[r_tab])
            S.dma("sp", lambda e: e.dma_start(out=retw[:], in_=ret_norm_w.partition_broadcast(128)), d_tab,
                  writes=[r_tab])
            Rk, Rv, Rq, Rg = Ring(kc_, S, True), Ring(vc_, S, True), Ring(qc_, S, True), Ring(gc_, S, True)
            Rkd, RkT, RqT, RqTd, RST = Ring(kd_), Ring(kT_), Ring(qT_), Ring(qTd_), Ring(ST_)
            Rssq, Rtmpf, Rgat = Ring(ssq_), Ring(tmpf_), Ring(gat_)
            RgT = Ring(gT_, S, True)
            r_stf = Res()
            r_stb = [Res(), Res()]
            r_junk = Res()
            r_pkT, r_pqT, r_pS, r_pO, r_pKV, r_pGT = (Res(True) for _ in range(6))
            S.op("dve", lambda e: e.memset(stf[:], 0.0), writes=[r_stf])
            S.op("pool", lambda e: e.memset(stb[0][:], 0.0), writes=[r_stb[0]])
            for c in range(LT):
                k_, rk, dk = Rk.next()
                v_, rv, dv = Rv.next()
                S.dma("sp", lambda e, k_=k_, c=c: e.dma_start(out=k_[:], in_=RK[c * 128:(c + 1) * 128, :]), dk, writes=[rk])
                S.dma("sp", lambda e, v_=v_, c=c: e.dma_start(out=v_[:], in_=RV[c * 128:(c + 1) * 128, :]), dv, writes=[rv])
                sbc, rsb = stb[c % 2], r_stb[c % 2]
                sbn, rsn = stb[(c + 1) % 2], r_stb[(c + 1) % 2]
                if c >= QT0 and os.environ.get('P2_OUT', '1') == '1':
                    qi = c - QT0
                    q_, rq, dq = Rq.next()
                    g_, rg, dg = Rg.next()
                    S.dma("sp", lambda e, q_=q_, qi=qi: e.dma_start(out=q_[:], in_=RQ[qi * 128:(qi + 1) * 128, :]), dq, writes=[rq])
                    S.dma("sp", lambda e, g_=g_, qi=qi: e.dma_start(out=g_[:], in_=RG[qi * 128:(qi + 1) * 128, :]), dg, writes=[rg])
                    for h in range(8):
                        S.op("pe", lambda e, k_=k_, h=h: e.transpose(out=pkT[:, h * 128:(h + 1) * 128],
                                                                     in_=k_[:, h * 128:(h + 1) * 128], identity=identb[:]),
                             reads=[rk, r_id], writes=[r_pkT])
                    kT, rkT = RkT.next()
                    S.op("act", lambda e, kT=kT: e.activation(out=kT[:], in_=pkT[:], func=AF.Copy), reads=[r_pkT], writes=[rkT])
                    for h in range(8):
                        S.op("pe", lambda e, q_=q_, h=h: e.transpose(out=pqT[:, h * 128:(h + 1) * 128],
                                                                     in_=q_[:, h * 128:(h + 1) * 128], identity=identb[:]),
                             reads=[rq, r_id], writes=[r_pqT])
                    qT, rqT = RqT.next()
                    qTd, rqTd = RqTd.next()
                    S.op("act", lambda e, qT=qT: e.activation(out=qT[:], in_=pqT[:], func=AF.Copy), reads=[r_pqT], writes=[rqT])
                    S.op("dve", lambda e, qTd=qTd: e.tensor_tensor(
                        out=qTd[:].rearrange("p (h n) -> p h n", h=8), in0=pqT[:].rearrange("p (h n) -> p h n", h=8),
                        in1=qdec[:], op=ALU.mult), reads=[r_pqT, r_tab], writes=[rqTd])
                    gat, rgat = Rgat.next()
                    lvl = int(os.environ.get('P2_LVL', '9'))
                    for hg in (range(2) if lvl >= 2 else []):
                        for j in range(4):
                            h = hg * 4 + j
                            S.op("pe", lambda e, kT=kT, qT=qT, h=h, j=j: e.matmul(
                                pS[:, j * 128:(j + 1) * 128], lhsT=kT[:, h * 128:(h + 1) * 128],
                                rhs=qT[:, h * 128:(h + 1) * 128], start=True, stop=True),
                                reads=[rkT, rqT], writes=[r_pS])
                        ST, rST = RST.next()
                        S.op("dve", lambda e, ST=ST, hg=hg: e.tensor_tensor(
                            out=ST[:].rearrange("p (h n) -> p h n", h=4), in0=pS[:].rearrange("p (h n) -> p h n", h=4),
                            in1=decT[:, hg * 4:(hg + 1) * 4, :], op=ALU.mult), reads=[r_pS, r_tab], writes=[rST])
                        for j in range(4):
                            h = hg * 4 + j
                            S.op("pe", lambda e, ST=ST, v_=v_, h=h, j=j: e.matmul(
                                pO[:, j * 256:(j + 1) * 256], lhsT=ST[:, j * 128:(j + 1) * 128],
                                rhs=v_[:, h * 256:(h + 1) * 256], start=True, stop=False),
                                reads=[rST, rv], writes=[r_pO])
                            S.op("pe", lambda e, qTd=qTd, sbc=sbc, h=h, j=j: e.matmul(
                                pO[:, j * 256:(j + 1) * 256], lhsT=qTd[:, h * 128:(h + 1) * 128],
                                rhs=sbc[:, h * 256:(h + 1) * 256], start=False, stop=True),
                                reads=[rqTd, rsb], writes=[r_pO])
                        if lvl < 3:
                            continue
                        ssq, rssq = Rssq.next()
                        for j in range(4):
                            S.op("act", lambda e, ssq=ssq, j=j: e.activation(
                                out=junk[:], in_=pO[:, j * 256:(j + 1) * 256], func=AF.Square, accum_out=ssq[:, j:j + 1]),
                                reads=[r_pO], writes=[r_junk, rssq])
                        S.op("act", lambda e, ssq=ssq: e.activation(out=ssq[:], in_=ssq[:], func=AF.Sqrt,
                                                                    scale=1.0 / 256, bias=EPS), reads=[rssq], writes=[rssq])
                        S.op("dve", lambda e, ssq=ssq: e.reciprocal(out=ssq[:], in_=ssq[:]), reads=[rssq], writes=[rssq])
                        tmpf, rtf = Rtmpf.next()
                        for j in range(4):
                            h = hg * 4 + j
                            S.op("dve", lambda e, tmpf=tmpf, ssq=ssq, h=h, j=j: e.scalar_tensor_tensor(
                                out=tmpf[:, j * 256:(j + 1) * 256], in0=pO[:, j * 256:(j + 1) * 256],
                                scalar=ssq[:, j:j + 1], in1=retw[:, h * 256:(h + 1) * 256], op0=ALU.mult, op1=ALU.mult),
                                reads=[r_pO, rssq, r_tab], writes=[rtf])
                        if lvl < 4:
                            continue
                        S.op("pool", lambda e, gat=gat, tmpf=tmpf, g_=g_, hg=hg: e.tensor_tensor(
                            out=gat[:, hg * 1024:(hg + 1) * 1024], in0=tmpf[:], in1=g_[:, hg * 1024:(hg + 1) * 1024],
                            op=ALU.mult), reads=[rtf, rg], writes=[rgat])
                    if lvl < 5:
                        continue
                    gT, rgT, dgT = RgT.next()
                    for hh in range(2):
                        for cc in range(8):
                            ch = hh * 8 + cc
                            S.op("pe", lambda e, gat=gat, ch=ch, cc=cc: e.transpose(
                                out=pGT[:, cc * 128:(cc + 1) * 128], in_=gat[:, ch * 128:(ch + 1) * 128],
                                identity=identb[:]), reads=[rgat, r_id], writes=[r_pGT])
                        if hh == 0:
                            S.op("act", lambda e, gT=gT: e.activation(
                                out=gT[:, 0:8, :], in_=pGT[:].rearrange("p (c q) -> p c q", c=8), func=AF.Copy),
                                reads=[r_pGT], writes=[rgT])
                        else:
                            S.op("dve", lambda e, gT=gT: e.tensor_copy(
                                out=gT[:, 8:16, :], in_=pGT[:].rearrange("p (c q) -> p c q", c=8)),
                                reads=[r_pGT], writes=[rgT])
                    S.dma("sp", lambda e, gT=gT, qi=qi: e.dma_start(
                        out=RETGT[:, :, qi * 128:(qi + 1) * 128].rearrange("c p q -> p c q"), in_=gT[:]),
                        dgT, reads=[rgT])
                if c < LT - 1 and os.environ.get('P2_STATE', '1') == '1':
                    kd, rkd = Rkd.next()
                    S.op("pool", lambda e, kd=kd, k_=k_: e.tensor_tensor(
                        out=kd[:].rearrange("p (h d) -> p h d", h=8), in0=k_[:].rearrange("p (h d) -> p h d", h=8),
                        in1=kdec[:, :, None].to_broadcast([128, 8, 128]), op=ALU.mult),
                        reads=[rk, r_tab], writes=[rkd])
                    for hg in range(2):
                        for j in range(4):
                            h = hg * 4 + j
                            S.op("pe", lambda e, kd=kd, v_=v_, h=h, j=j: e.matmul(
                                pKV[:, j * 256:(j + 1) * 256], lhsT=kd[:, h * 128:(h + 1) * 128],
                                rhs=v_[:, h * 256:(h + 1) * 256], start=True, stop=True),
                                reads=[rkd, rv], writes=[r_pKV])
                        for j in range(4):
                            h = hg * 4 + j
                            S.op("dve", lambda e, h=h, j=j: e.scalar_tensor_tensor(
                                out=stf[:, h * 256:(h + 1) * 256], in0=stf[:, h * 256:(h + 1) * 256],
                                scalar=float(gam[h] ** 128), in1=pKV[:, j * 256:(j + 1) * 256],
                                op0=ALU.mult, op1=ALU.add), reads=[r_pKV, r_stf], writes=[r_stf])
                    S.op("act", lambda e, sbn=sbn: e.activation(out=sbn[:], in_=stf[:], func=AF.Copy),
                         reads=[r_stf], writes=[rsn])
            S.emit(block)

    if 2 in phases:
        phase2()

    KCMPT = dscr("KCMPT", [2, 128, 256])
    VCMP = dscr("VCMP", [2, 256, 128])

    def phase3a():
        with ExitStack() as st:
            sb = lambda name, shape, dt: st.enter_context(nc.sbuf_tensor(name, list(shape), dt))
            xc_ = [sb(f"p3a_xc{i}", [128, LT * 128], BF16) for i in range(2)]
            w1b = sb("p3a_w1", [128, 32, 256], BF16)
            peT = sb("p3a_pe", [128, 32], BF16)
            w2b = sb("p3a_w2", [128, 2, 128], BF16)
            cb = sb("p3a_cb", [128, 2], F32)
            gel_ = [sb(f"p3a_gel{i}", [128, 2, 256], BF16) for i in range(2)]
            og_ = [sb(f"p3a_og{i}", [128, 256], BF16) for i in range(2)]
            pc = st.enter_context(nc.psum_tensor("p3a_pc", [128, 2], F32))
            ph_ = [st.enter_context(nc.psum_tensor(f"p3a_ph{i}", [128, 256], F32)) for i in range(2)]
            po_ = [st.enter_context(nc.psum_tensor(f"p3a_po{i}", [128, 256], F32)) for i in range(2)]
            S = Sched(nc, st, "p3a")
            block = st.enter_context(nc.Block())
            Rxc = Ring(xc_, S, True)
            Rgel = Ring(gel_)
            Rog = Ring(og_, S, True)
            Rph = Ring(ph_, excl=True)
            Rpo = Ring(po_, excl=True)
            r_w, r_cb, r_pc = Res(), Res(), Res(True)
            d_w = S.dsem()
            for g_ in gel_:
                S.op("dve", lambda e, g_=g_: e.memset(g_[:], 0.0), writes=[Rgel.res[gel_.index(g_)]])
            for kv in ("k", "v"):
                S.dma("pool", lambda e, kv=kv: e.dma_start(
                    out=w1b[:], in_=cmp_w1[kv].rearrange("(l d) j -> d l j", d=128)), d_w, writes=[r_w])
                S.dma("pool", lambda e, kv=kv: e.dma_start(out=peT[:], in_=cmp_peT[kv]), d_w, writes=[r_w])
                S.dma("pool", lambda e, kv=kv: e.dma_start(
                    out=w2b[:], in_=cmp_w2[kv].rearrange("(c p) d -> p c d", p=128)), d_w, writes=[r_w])
                for jc in range(2):
                    for l in range(32):
                        S.op("pe", lambda e, jc=jc, l=l: e.matmul(
                            pc[:, jc:jc + 1], lhsT=w1b[:, l, jc * 128:(jc + 1) * 128], rhs=peT[:, l:l + 1],
                            start=(l == 0), stop=(l == 31)), reads=[r_w], writes=[r_pc])
                S.op("act", lambda e: e.activation(out=cb[:], in_=pc[:], func=AF.Copy), reads=[r_pc], writes=[r_cb])
                src = KCT if kv == "k" else VCT
                for g in range(2):
                    xc, rxc, dxc = Rxc.next()
                    S.dma("sp", lambda e, xc=xc, g=g, src=src: e.dma_start(out=xc[:], in_=src[g]), dxc, writes=[rxc])
                    gel, rgel = Rgel.next()
                    for jc in range(2):
                        ph, rph = Rph.next()
                        for l in range(32):
                            S.op("pe", lambda e, ph=ph, xc=xc, jc=jc, l=l: e.matmul(
                                ph[:, 0:255], lhsT=w1b[:, l, jc * 128:(jc + 1) * 128],
                                rhs=xc[:, l:l + 16 * 254 + 1:16], start=(l == 0), stop=(l == 31)),
                                reads=[r_w, rxc], writes=[rph])
                        S.op("act", lambda e, ph=ph, gel=gel, jc=jc: e.activation(
                            out=gel[:, jc, 0:255], in_=ph[:, 0:255], func=AF.Gelu_apprx_tanh, bias=cb[:, jc:jc + 1]),
                            reads=[rph, r_cb], writes=[rgel])
                    if kv == "k":
                        po, rpo = Rpo.next()
                        for jc in range(2):
                            S.op("pe", lambda e, po=po, gel=gel, jc=jc: e.matmul(
                                po[:, :], lhsT=w2b[:, jc, :], rhs=gel[:, jc, :], start=(jc == 0), stop=(jc == 1)),
                                reads=[r_w, rgel], writes=[rpo])
                        og, rog, dog = Rog.next()
                        S.op("dve", lambda e, og=og, po=po: e.tensor_copy(out=og[:], in_=po[:]), reads=[rpo], writes=[rog])
                        S.dma("sp", lambda e, og=og, g=g: e.dma_start(out=KCMPT[g], in_=og[:]), dog, reads=[rog])
                    else:
                        for nk in range(2):
                            po, rpo = Rpo.next()
                            for jc in range(2):
                                S.op("pe", lambda e, po=po, gel=gel, jc=jc, nk=nk: e.matmul(
                                    po[:, 0:128], lhsT=gel[:, jc, nk * 128:(nk + 1) * 128], rhs=w2b[:, jc, :],
                                    start=(jc == 0), stop=(jc == 1)), reads=[r_w, rgel], writes=[rpo])
                            og, rog, dog = Rog.next()
                            S.op("dve", lambda e, og=og, po=po: e.tensor_copy(out=og[:, 0:128], in_=po[:, 0:128]),
                                 reads=[rpo], writes=[rog])
                            S.dma("sp", lambda e, og=og, g=g, nk=nk: e.dma_start(
                                out=VCMP[g, nk * 128:(nk + 1) * 128, :], in_=og[:, 0:128]), dog, reads=[rog])
            S.emit(block)

    def phase3b(sfx=""):
        with ExitStack() as st:
            sb = lambda name, shape, dt: st.enter_context(nc.sbuf_tensor(name + sfx, list(shape), dt))
            identb = sb("p3_idb", [128, 128], BF16)
            identf = sb("p3_idf", [128, 128], F32)
            onesb = sb("p3_ones", [128, 128], BF16)
            causb = sb("p3_caus", [128, 4, 128], BF16)
            acausb = sb("p3_acaus", [128, 4, 128], BF16)
            expt = sb("p3_expt", [64, LT, 128], BF16)
            ovb = sb("p3_ov", [128, 2, 65], BF16)
            padb = sb("p3_padb", [128, LT], F32)
            cmpb = sb("p3_cmpb", [128, 2], F32)
            kcm = sb("p3_kcm", [128, 2, 256], BF16)
            vcm = sb("p3_vcm", [128, 2, 2, 128], BF16)
            kst = sb("p3_kst", [128, 2, LT * 128], BF16)
            kwt = sb("p3_kwt", [128, 2, LT * 128], BF16)
            vs = sb("p3_vs", [128, LT, 256], BF16)
            vw = sb("p3_vw", [128, LT, 256], BF16)
            qT_ = [sb(f"p3_qT{i}", [128, 16, 128], BF16) for i in range(2)]
            gbc_ = [sb(f"p3_gbc{i}", [128, 24, 128], F32) for i in range(2)]
            keep_ = [sb(f"p3_keep{i}", [128, 64], F32) for i in range(2)]
            addt_ = [sb(f"p3_add{i}", [128, 64], F32) for i in range(2)]
            cm_ = [sb(f"p3_cm{i}", [128, 2, 128], F32) for i in range(2)]
            Ec_ = [sb(f"p3_Ec{i}", [128, 1024], BF16) for i in range(2)]
            E_ = [sb(f"p3_E{i}", [128, 1024], BF16) for i in range(3)]
            Sm_ = [sb(f"p3_Sm{i}", [128, 1024], F32) for i in range(2)]
            fac_ = [sb(f"p3_fac{i}", [128, 1024], F32) for i in range(2)]
            tmp_ = [sb(f"p3_tmp{i}", [128, 1024], F32) for i in range(2)]
            acc_ = [sb(f"p3_acc{i}", [128, 1024], F32) for i in range(2)]
            accb_ = [sb(f"p3_accb{i}", [128, 1024], BF16) for i in range(2)]
            rs8 = sb("p3_rs8", [128, 8], F32)
            imp = sb("p3_imp", [128, 64], F32)
            imp2 = sb("p3_imp2", [128, 64], F32)
            m8 = sb("p3_m8", [128, 8], F32)
            selb = sb("p3_selb", [128, 64], F32)
            selbT_ = [sb(f"p3_selbT{i}", [64, 4, 128], BF16) for i in range(2)]
            pS_ = [st.enter_context(nc.psum_tensor(f"p3_pS{i}" + sfx, [128, 1024], F32)) for i in range(2)]
            pO = st.enter_context(nc.psum_tensor("p3_pO" + sfx, [128, 1024], F32))
            pSum = st.enter_context(nc.psum_tensor("p3_pSum" + sfx, [128, 1024], F32))
            S = Sched(nc, st, "p3" + sfx)
            block = st.enter_context(nc.Block())
            r_c = Res()
            r_k = Res()
            d_cp, d_cs = S.dsem(), S.dsem()
            for dst, src in ((identb[:], t_ident), (expt[:], t_exp), (ovb[:], t_ov)):
                S.dma("pool", lambda e, dst=dst, src=src: e.dma_start(out=dst, in_=src), d_cp, writes=[r_c])
            for r4 in range(4):
                S.dma("pool", lambda e, r4=r4: e.dma_start(out=causb[:, r4, :], in_=t_caus), d_cp, writes=[r_c])
                S.dma("pool", lambda e, r4=r4: e.dma_start(out=acausb[:, r4, :], in_=t_acaus), d_cp, writes=[r_c])
            S.op("pool", lambda e: e.memset(onesb[:], 1.0), reads=[], writes=[r_c])
            for dst, src in ((identf[:], t_ident), (padb[:], t_padb), (cmpb[:], t_cmpb),
                             (kcm[:], KCMPT.rearrange("g d n -> d g n")),
                             (vcm[:], VCMP.rearrange("g (k p) d -> p g k d", p=128))):
                S.dma("sp", lambda e, dst=dst, src=src: e.dma_start(out=dst, in_=src), d_cs, writes=[r_k])
            r_big = {}
            for nm, dst, src in (("kwt", kwt[:], KWT.rearrange("g d t -> d g t")),
                                 ("vw", vw[:], VW.rearrange("(t p) c -> p t c", p=128)),
                                 ("kst", kst[:], KST.rearrange("g d t -> d g t")),
                                 ("vs", vs[:], VS.rearrange("(t p) c -> p t c", p=128))):
                r_big[nm] = Res()
                S.dma("sp", lambda e, dst=dst, src=src: e.dma_start(out=dst, in_=src), S.dsem(), writes=[r_big[nm]])
            RqT, Rgbc = Ring(qT_, S, True), Ring(gbc_, S, True)
            Rkeep, Radd, Rcm = Ring(keep_, S, True), Ring(addt_, S, True), Ring(cm_, S, True)
            REc, RE, RSm, Rfac, Rtmp, Racc = Ring(Ec_), Ring(E_), Ring(Sm_), Ring(fac_), Ring(tmp_), Ring(acc_)
            Raccb = Ring(accb_, S, True)
            RselbT = Ring(selbT_)
            RpS = Ring(pS_, excl=True)
            r_pO, r_pSum = Res(True), Res(True)
            r_small = Res()
            d_dbg = S.dsem()
            dbg_t = {}
            if "DBGSEL" in debug:
                dbg_t = {"DBGSEL": dscr("DBGSEL", [NQ, 2, 128, 64], F32), "DBGIMP": dscr("DBGIMP", [NQ, 2, 128, 64], F32),
                         "DBGM8": dscr("DBGM8", [NQ, 2, 128, 8], F32)}

            def finalize(gbc, rgbc, br, acc, racc, first):
                fac, rfac = Rfac.next()
                S.op("dve", lambda e: e.tensor_scalar_max(out=fac[:], in0=pSum[:], scalar1=1e-30),
                     reads=[r_pSum], writes=[rfac])
                S.op("dve", lambda e: e.reciprocal(out=fac[:], in_=fac[:]), reads=[rfac], writes=[rfac])
                S.op("pool", lambda e: e.tensor_tensor(
                    out=fac[:].rearrange("p (h q) -> p h q", h=8), in0=fac[:].rearrange("p (h q) -> p h q", h=8),
                    in1=gbc[:, br::3, :], op=ALU.mult), reads=[rfac, rgbc], writes=[rfac])
                if first:
                    S.op("dve", lambda e: e.tensor_tensor(out=acc[:], in0=pO[:], in1=fac[:], op=ALU.mult),
                         reads=[r_pO, rfac], writes=[racc])
                else:
                    tmp, rtmp = Rtmp.next()
                    S.op("dve", lambda e: e.tensor_tensor(out=tmp[:], in0=pO[:], in1=fac[:], op=ALU.mult),
                         reads=[r_pO, rfac], writes=[rtmp])
                    S.op("pool", lambda e: e.tensor_tensor(out=acc[:], in0=acc[:], in1=tmp[:], op=ALU.add),
                         reads=[rtmp, racc], writes=[racc])

            def attend(kt_list, ksrc, vsrc, g, qTg, rq, selbT, rselbT, i):
                n = len(kt_list)

                def scores(kt):
                    pSx, rpS = RpS.next()
                    for hf in range(2):
                        extra = []
                        if selbT is not None:
                            extra.append((expt[:, kt, :], selbT[:].rearrange("s r q -> s (r q)"), [r_c, rselbT]))
                        if kt == i:
                            extra.append((identb[:], causb[:].rearrange("k r q -> k (r q)"), [r_c]))
                        if selbT is None and kt == i - 4:
                            extra.append((identb[:], acausb[:].rearrange("k r q -> k (r q)"), [r_c]))
                        S.op("pe", lambda e, hf=hf, ne=len(extra): e.matmul(
                            pSx[:, hf * 512:(hf + 1) * 512], lhsT=ksrc[:, g, kt * 128:(kt + 1) * 128],
                            rhs=qTg[:, hf * 512:(hf + 1) * 512], start=True, stop=(ne == 0)),
                            reads=[r_big["kst" if selbT is not None else "kwt"], rq], writes=[rpS])
                        for xi, (l_, r_, deps) in enumerate(extra):
                            S.op("pe", lambda e, hf=hf, l_=l_, r_=r_, last=(xi == len(extra) - 1): e.matmul(
                                pSx[:, hf * 512:(hf + 1) * 512], lhsT=l_, rhs=r_, start=False, stop=last),
                                reads=deps, writes=[rpS])
                    E, rE = RE.next()
                    S.op("act", lambda e: e.activation(
                        out=E[:], in_=pSx[:], func=AF.Exp, scale=SCALE, bias=padb[:, kt:kt + 1]),
                        reads=[rpS, r_k], writes=[rE])
                    return E, rE

                def values(idx, kt, E, rE):
                    for hf in range(2):
                        S.op("pe", lambda e, hf=hf: e.matmul(
                            pO[:, hf * 512:(hf + 1) * 512], lhsT=vsrc[:, kt, g * 128:(g + 1) * 128],
                            rhs=E[:, hf * 512:(hf + 1) * 512], start=(idx == 0), stop=(idx == n - 1)),
                            reads=[r_big["vs" if selbT is not None else "vw"], rE], writes=[r_pO])
                        S.op("pe", lambda e, hf=hf: e.matmul(
                            pSum[:, hf * 512:(hf + 1) * 512], lhsT=onesb[:],
                            rhs=E[:, hf * 512:(hf + 1) * 512], start=(idx == 0), stop=(idx == n - 1)),
                            reads=[r_c, rE], writes=[r_pSum])

                pend = None
                for idx, kt in enumerate(kt_list):
                    cur = (idx, kt) + scores(kt)
                    if pend is not None:
                        values(*pend)
                    pend = cur
                values(*pend)

            for i in range(QT0, LT):
                qi = i - QT0
                qT, rq, dq = RqT.next()
                S.dma("sp", lambda e, qT=qT, qi=qi: e.dma_start(
                    out=qT[:], in_=NQT_[:, :, qi * 128:(qi + 1) * 128].rearrange("h p q -> p h q")), dq, writes=[rq])
                keep, rkeep, dkeep = Rkeep.next()
                addt, radd, dadd = Radd.next()
                cm, rcm, dcm = Rcm.next()
                S.dma("sp", lambda e, keep=keep, qi=qi: e.dma_start(out=keep[:], in_=t_keep[qi]), dkeep, writes=[rkeep])
                S.dma("sp", lambda e, addt=addt, qi=qi: e.dma_start(out=addt[:], in_=t_add[qi]), dadd, writes=[radd])
                for ck in range(2):
                    r0 = 128 * ck - 8 * i + 250
                    S.dma("sp", lambda e, cm=cm, ck=ck, r0=r0: e.dma_start(out=cm[:, ck, :], in_=t_cm[r0:r0 + 128, :]),
                          dcm, writes=[rcm])
                def do_group(i, qi, g, qT, rq, keep, rkeep, addt, radd, cm, rcm):
                    gbc, rgbc, dgbc = Rgbc.next()
                    S.dma("sp", lambda e, gbc=gbc, g=g, qi=qi: e.dma_start(
                        out=gbc[:], in_=NGT[g * 24:(g + 1) * 24, qi * 128:(qi + 1) * 128].unsqueeze(0).to_broadcast(
                            [128, 24, 128])), dgbc, writes=[rgbc])
                    qTg = qT[:, g * 8:(g + 1) * 8, :].rearrange("p h q -> p (h q)")
                    acc, racc = Racc.next()
                    Ecs = []
                    for ck in range(2):
                        pSx, rpS = RpS.next()
                        for hf in range(2):
                            S.op("pe", lambda e, pSx=pSx, hf=hf, ck=ck: e.matmul(
                                pSx[:, hf * 512:(hf + 1) * 512], lhsT=kcm[:, g, ck * 128:(ck + 1) * 128],
                                rhs=qTg[:, hf * 512:(hf + 1) * 512], start=True, stop=True),
                                reads=[r_k, rq], writes=[rpS])
                        Sm, rSm = RSm.next()
                        S.op("dve", lambda e, Sm=Sm, pSx=pSx, ck=ck: e.tensor_tensor(
                            out=Sm[:].rearrange("p (h q) -> p h q", h=8), in0=pSx[:].rearrange("p (h q) -> p h q", h=8),
                            in1=cm[:, ck, :][:, None, :].to_broadcast([128, 8, 128]), op=ALU.add),
                            reads=[rpS, rcm], writes=[rSm])
                        Ec, rEc = REc.next()
                        S.op("act", lambda e, Ec=Ec, Sm=Sm, ck=ck: e.activation(
                            out=Ec[:], in_=Sm[:], func=AF.Exp, scale=SCALE, bias=cmpb[:, ck:ck + 1]),
                            reads=[rSm, r_k], writes=[rEc])
                        Ecs.append((Ec, rEc))
                        for hf in range(2):
                            S.op("pe", lambda e, Ec=Ec, hf=hf, ck=ck: e.matmul(
                                pO[:, hf * 512:(hf + 1) * 512], lhsT=vcm[:, g, ck, :],
                                rhs=Ec[:, hf * 512:(hf + 1) * 512], start=(ck == 0), stop=(ck == 1)),
                                reads=[r_k, rEc], writes=[r_pO])
                            S.op("pe", lambda e, Ec=Ec, hf=hf, ck=ck: e.matmul(
                                pSum[:, hf * 512:(hf + 1) * 512], lhsT=onesb[:],
                                rhs=Ec[:, hf * 512:(hf + 1) * 512], start=(ck == 0), stop=(ck == 1)),
                                reads=[r_c, rEc], writes=[r_pSum])
                    pU, rpU = RpS.next()
                    for h in range(8):
                        for ck in range(2):
                            Ec, rEc = Ecs[ck]
                            o0 = (h // 4) * 512 + (h % 4) * 65
                            S.op("pe", lambda e, pU=pU, Ec=Ec, h=h, ck=ck, o0=o0: e.matmul(
                                pU[:, o0:o0 + 65], lhsT=Ec[:, h * 128:(h + 1) * 128], rhs=ovb[:, ck, :],
                                start=(ck == 0), stop=(ck == 1)), reads=[r_c, rEc], writes=[rpU])
                    finalize(gbc, rgbc, 0, acc, racc, True)
                    for hh in range(2):
                        S.op("dve", lambda e, pU=pU, hh=hh: e.tensor_scalar_max(
                            out=rs8[:, hh * 4:(hh + 1) * 4],
                            in0=pU[:, hh * 512:hh * 512 + 260].rearrange("p (h s) -> p h s", s=65)[:, :, 64],
                            scalar1=1e-30), reads=[rpU], writes=[r_small])
                    S.op("dve", lambda e: e.reciprocal(out=rs8[:], in_=rs8[:]), reads=[r_small], writes=[r_small])
                    for h in range(8):
                        o0 = (h // 4) * 512 + (h % 4) * 65
                        if h == 0:
                            S.op("dve", lambda e, pU=pU, o0=o0: e.tensor_scalar(
                                out=imp[:], in0=pU[:, o0:o0 + 64], scalar1=rs8[:, 0:1], scalar2=None, op0=ALU.mult),
                                reads=[rpU, r_small], writes=[r_small])
                        else:
                            S.op("dve", lambda e, pU=pU, o0=o0, h=h: e.scalar_tensor_tensor(
                                out=imp[:], in0=pU[:, o0:o0 + 64], scalar=rs8[:, h:h + 1], in1=imp[:],
                                op0=ALU.mult, op1=ALU.add), reads=[rpU, r_small], writes=[r_small])
                    S.op("dve", lambda e, keep=keep: e.tensor_tensor(out=imp[:], in0=imp[:], in1=keep[:], op=ALU.mult),
                         reads=[r_small, rkeep], writes=[r_small])
                    S.op("dve", lambda e, addt=addt: e.tensor_tensor(out=imp[:], in0=imp[:], in1=addt[:], op=ALU.add),
                         reads=[r_small, radd], writes=[r_small])
                    S.op("dve", lambda e: e.max(out=m8[:], in_=imp[:]), reads=[r_small], writes=[r_small])
                    S.op("dve", lambda e: e.match_replace(out=imp2[:], in_to_replace=m8[:], in_values=imp[:],
                                                          imm_value=-1e30), reads=[r_small], writes=[r_small])
                    S.op("dve", lambda e: e.max(out=m8[:], in_=imp2[:]), reads=[r_small], writes=[r_small])
                    S.op("dve", lambda e: e.tensor_scalar(out=selb[:], in0=imp[:], scalar1=m8[:, 7:8], scalar2=NEG,
                                                          op0=ALU.is_lt, op1=ALU.mult), reads=[r_small], writes=[r_small])
                    if "DBGSEL" in debug:
                        for nm_, src_ in (("DBGSEL", selb), ("DBGIMP", imp), ("DBGM8", m8)):
                            S.dma("sp", lambda e, nm_=nm_, src_=src_, qi=qi, g=g: e.dma_start(
                                out=dbg_t[nm_][qi, g], in_=src_[:]), d_dbg, reads=[r_small])
                    attend(list(range(i - 4, i + 1)), kwt, vw, g, qTg, rq, None, None, i)
                    finalize(gbc, rgbc, 2, acc, racc, False)
                    pT, rpT = RpS.next()
                    S.op("pe", lambda e, pT=pT: e.transpose(out=pT[0:64, 0:128], in_=selb[:, 0:64], identity=identf[:]),
                         reads=[r_small, r_k], writes=[rpT])
                    selbT, rselbT = RselbT.next()
                    S.op("act", lambda e, pT=pT, selbT=selbT: e.activation(
                        out=selbT[:], in_=pT[0:64, 0:128][:, None, :].to_broadcast([64, 4, 128]), func=AF.Copy),
                        reads=[rpT], writes=[rselbT])
                    attend(list(range(0, i + 1)), kst, vs, g, qTg, rq, selbT, rselbT, i)
                    finalize(gbc, rgbc, 1, acc, racc, False)
                    accb, raccb, daccb = Raccb.next()
                    S.op("act", lambda e, accb=accb, acc=acc: e.activation(out=accb[:], in_=acc[:], func=AF.Copy),
                         reads=[racc], writes=[raccb])
                    S.dma("sp", lambda e, accb=accb, g=g, qi=qi: e.dma_start(
                        out=NSAT[g * 8:(g + 1) * 8, :, qi * 128:(qi + 1) * 128].rearrange("h p q -> p h q"),
                        in_=accb[:].rearrange("p (h q) -> p h q", h=8)), daccb, reads=[raccb])

                for g in range(2):
                    do_group(i, qi, g, qT, rq, keep, rkeep, addt, radd, cm, rcm)
            S.emit(block)

    if 3 in phases:
        phase3a()
        phase3b()
    if 33 in phases:
        phase3b("x")


    QGROUPS = [(0, 128)] + [(128 + 512 * g, 512) for g in range(4)]

    def upproj(tag, srcT, w, gate0, addsrc, dst):
        with ExitStack() as st:
            sb = lambda name, shape, dt: st.enter_context(nc.sbuf_tensor(name, list(shape), dt))
            src = sb(f"{tag}_src", [128, 16, NQT], BF16)
            wb = [sb(f"{tag}_w{i}", [128, 16, 512], BF16) for i in range(2)]
            mg_ = [sb(f"{tag}_mg{i}", [128, NQT], BF16) for i in range(2)]
            m1_ = [sb(f"{tag}_m1{i}", [128, NQT], BF16) for i in range(2)]
            tf_ = [sb(f"{tag}_tf{i}", [128, 512], F32) for i in range(2)]
            o_ = [sb(f"{tag}_o{i}", [128, NQT], BF16) for i in range(2)]
            ps = [st.enter_context(nc.psum_tensor(f"{tag}_ps{i}", [128, 512], F32)) for i in range(4)]
            S = Sched(nc, st, tag)
            block = st.enter_context(nc.Block())
            r_srcq = [Res() for _ in range(4)]
            for q4 in range(4):
                S.dma("sp", lambda e, q4=q4: e.dma_start(out=src[:, q4 * 4:(q4 + 1) * 4, :],
                                                       in_=srcT[q4 * 4:(q4 + 1) * 4].rearrange("c p q -> p c q")),
                      S.dsem(), writes=[r_srcq[q4]])
            Rw, Rmg, Rm1, Ro = Ring(wb, S, True), Ring(mg_, S, True), Ring(m1_, S, True), Ring(o_, S, True)
            Rtf = Ring(tf_)
            Rps = Ring(ps, excl=True)
            wv = w.rearrange("(c p) n -> p c n", p=128)

            def chunk(wt, rw, ch, dc):
                mg, rmg, dmg = Rmg.next()
                S.dma("sp", lambda e: e.dma_start(out=mg[:], in_=MGT[gate0 + dc]), dmg, writes=[rmg])
                if addsrc is not None:
                    m1, rm1, dm1 = Rm1.next()
                    S.dma("sp", lambda e: e.dma_start(out=m1[:], in_=addsrc[dc]), dm1, writes=[rm1])
                o, ro, do = Ro.next()
                for (q0, nt) in QGROUPS:
                    p, rp = Rps.next()
                    for c in (range(16) if not os.environ.get("SKIPMM") else range(1)):
                        S.op("pe", lambda e, p=p, c=c, q0=q0, nt=nt: e.matmul(
                            p[:, 0:nt], lhsT=wt[:, c, ch * 128:(ch + 1) * 128], rhs=src[:, c, q0:q0 + nt],
                            start=(c == 0), stop=(c == 15)), reads=[rw, r_srcq[c // 4]], writes=[rp])
                    if addsrc is None:
                        S.op("dve", lambda e, p=p, q0=q0, nt=nt: e.tensor_tensor(
                            out=o[:, q0:q0 + nt], in0=p[:, 0:nt], in1=mg[:, q0:q0 + nt], op=ALU.mult),
                            reads=[rp, rmg], writes=[ro])
                    else:
                        tf, rtf = Rtf.next()
                        S.op("dve", lambda e, p=p, q0=q0, nt=nt, tf=tf: e.tensor_tensor(
                            out=tf[:, 0:nt], in0=p[:, 0:nt], in1=mg[:, q0:q0 + nt], op=ALU.mult),
                            reads=[rp, rmg], writes=[rtf])
                        S.op("pool", lambda e, q0=q0, nt=nt, tf=tf: e.tensor_tensor(
                            out=o[:, q0:q0 + nt], in0=tf[:, 0:nt], in1=m1[:, q0:q0 + nt], op=ALU.add),
                            reads=[rtf, rm1], writes=[ro])
                S.dma("sp", lambda e: e.dma_start(out=dst[dc], in_=o[:]), do, reads=[ro])

            def wblock(blk):
                wt, rw, dw = Rw.next()
                S.dma("pool", lambda e: e.dma_start(out=wt[:], in_=wv[:, :, blk * 512:(blk + 1) * 512]), dw, writes=[rw])
                for ch in range(4):
                    chunk(wt, rw, ch, blk * 4 + ch)

            for blk in range(4):
                wblock(blk)
            S.emit(block)

    def phase4b():
        with ExitStack() as st:
            sb = lambda name, shape, dt: st.enter_context(nc.sbuf_tensor(name, list(shape), dt))
            identb = sb("p4_id", [128, 128], BF16)
            wo = sb("p4_wo", [128, 16, D], BF16)
            w2bc = sb("p4_w2", [128, D], F32)
            mT_ = [sb(f"p4_mT{i}", [128, 16, 128], BF16) for i in range(2)]
            x_ = [sb(f"p4_x{i}", [128, D], F32) for i in range(2)]
            h_ = [sb(f"p4_h{i}", [128, D], F32) for i in range(2)]
            xn_ = [sb(f"p4_xn{i}", [128, D], BF16) for i in range(2)]
            xT_ = [sb(f"p4_xT{i}", [128, 16, 128], BF16) for i in range(2)]
            junk = sb("p4_junk", [128, D], BF16)
            ss_ = [sb(f"p4_ss{i}", [128, 1], F32) for i in range(2)]
            ps = [st.enter_context(nc.psum_tensor(f"p4_ps{i}", [128, 512], F32)) for i in range(4)]
            pt = [st.enter_context(nc.psum_tensor(f"p4_pt{i}", [128, 1024], BF16)) for i in range(2)]
            S = Sched(nc, st, "p4b")
            block = st.enter_context(nc.Block())
            r_id, r_wo, r_w2, r_junk = Res(), Res(), Res(), Res()
            d_p, d_s = S.dsem(), S.dsem()
            S.dma("pool", lambda e: e.dma_start(out=identb[:], in_=t_ident), d_p, writes=[r_id])
            wov = w_out.rearrange("(c p) n -> p c n", p=128)
            r_wob = [Res() for _ in range(4)]
            for blk in range(4):
                S.dma("pool", lambda e, blk=blk: e.dma_start(out=wo[:, :, blk * 512:(blk + 1) * 512],
                                                           in_=wov[:, :, blk * 512:(blk + 1) * 512]), S.dsem(),
                      writes=[r_wob[blk]])
            S.dma("sp", lambda e: e.dma_start(out=w2bc[:], in_=norm2_w.partition_broadcast(128)), d_s, writes=[r_w2])
            RmT, Rx, Rh, RxT = Ring(mT_, S, True), Ring(x_, S, True), Ring(h_, S, True), Ring(xT_, S, True)
            Rxn, Rss = Ring(xn_), Ring(ss_)
            Rps = Ring(ps, excl=True)
            r_pt = [Res(True), Res(True)]

            def tile(qi):
                mT, rmT, dmT = RmT.next()
                S.dma("sp", lambda e: e.dma_start(out=mT[:], in_=MRGT[:, :, qi * 128:(qi + 1) * 128].rearrange("c p q -> p c q")),
                      dmT, writes=[rmT])
                xt, rx, dx = Rx.next()
                S.dma("sp", lambda e: e.dma_start(out=xt[:], in_=x_loc[(QT0 + qi) * 128:(QT0 + qi + 1) * 128, :]), dx, writes=[rx])
                h, rh, dh = Rh.next()
                for cb in range(4):
                    p, rp = Rps.next()
                    for c in (range(16) if not os.environ.get("SKIPMM") else range(1)):
                        S.op("pe", lambda e, p=p, c=c, cb=cb: e.matmul(
                            p[:], lhsT=mT[:, c, :], rhs=wo[:, c, cb * 512:(cb + 1) * 512],
                            start=(c == 0), stop=(c == 15)), reads=[rmT, r_wob[cb]], writes=[rp])
                    S.op("dve", lambda e, p=p, cb=cb: e.tensor_tensor(
                        out=h[:, cb * 512:(cb + 1) * 512], in0=p[:], in1=xt[:, cb * 512:(cb + 1) * 512], op=ALU.add),
                        reads=[rp, rx], writes=[rh])
                S.dma("sp", lambda e: e.dma_start(out=H1[qi * 128:(qi + 1) * 128, :], in_=h[:]), dh, reads=[rh])
                ss, rss = Rss.next()
                S.op("act", lambda e: e.activation(out=junk[:], in_=h[:], func=AF.Square, accum_out=ss[:]),
                     reads=[rh], writes=[r_junk, rss])
                S.op("act", lambda e: e.activation(out=ss[:], in_=ss[:], func=AF.Sqrt, scale=1.0 / D, bias=EPS),
                     reads=[rss], writes=[rss])
                S.op("dve", lambda e: e.reciprocal(out=ss[:], in_=ss[:]), reads=[rss], writes=[rss])
                xn, rxn = Rxn.next()
                S.op("dve", lambda e: e.scalar_tensor_tensor(out=xn[:], in0=h[:], scalar=ss[:], in1=w2bc[:],
                                                             op0=ALU.mult, op1=ALU.mult),
                     reads=[rh, rss, r_w2], writes=[rxn])
                xT, rxT, dxT = RxT.next()
                for hh in range(2):
                    for cc in range(8):
                        c = hh * 8 + cc
                        S.op("pe", lambda e, c=c, cc=cc, hh=hh: e.transpose(
                            out=pt[hh][:, cc * 128:(cc + 1) * 128], in_=xn[:, c * 128:(c + 1) * 128], identity=identb[:]),
                            reads=[rxn, r_id], writes=[r_pt[hh]])
                    if hh == 0:
                        S.op("act", lambda e, hh=hh: e.activation(
                            out=xT[:, 0:8, :], in_=pt[0][:].rearrange("p (c q) -> p c q", c=8), func=AF.Copy),
                            reads=[r_pt[0]], writes=[rxT])
                    else:
                        S.op("dve", lambda e, hh=hh: e.tensor_copy(
                            out=xT[:, 8:16, :], in_=pt[1][:].rearrange("p (c q) -> p c q", c=8)),
                            reads=[r_pt[1]], writes=[rxT])
                S.dma("sp", lambda e: e.dma_start(
                    out=XN2T[:, :, qi * 128:(qi + 1) * 128].rearrange("c p q -> p c q"), in_=xT[:]), dxT, reads=[rxT])

            for qi in range(NQ):
                tile(qi)
            S.emit(block)

    if 4 in phases:
        upproj("p4a", RETGT, w_ret_up, 0, None, M1T)
        upproj("p4n", NSAT, w_nsa_up, 16, M1T, MRGT)
        phase4b()

    def phase5a():
        with ExitStack() as st:
            sb = lambda name, shape, dt: st.enter_context(nc.sbuf_tensor(name, list(shape), dt))
            xs = sb("p5_xs", [128, 16, NQT], BF16)
            wa_ = [sb(f"p5_wa{i}", [128, 16, 512], BF16) for i in range(2)]
            wb_ = [sb(f"p5_wb{i}", [128, 16, 512], BF16) for i in range(2)]
            cw = sb("p5_cw", [128, 88, 3], F32)
            cbias = sb("p5_cb", [128, 88], F32)
            halo = sb("p5_halo", [128, 1], F32)
            u_ = [sb(f"p5_u{i}", [128, 2050], F32) for i in range(4)]
            y_ = [sb(f"p5_y{i}", [128, 2048], F32) for i in range(3)]
            o_ = [sb(f"p5_o{i}", [128, 2048], BF16) for i in range(2)]
            ps = [st.enter_context(nc.psum_tensor(f"p5_ps{i}", [128, 512], F32)) for i in range(6)]
            ph = [st.enter_context(nc.psum_tensor(f"p5_ph{i}", [128, 2], F32)) for i in range(2)]
            S = Sched(nc, st, "p5a")
            block = st.enter_context(nc.Block())
            r_c = Res()
            d_c = S.dsem()
            S.dma("sp", lambda e: e.dma_start(out=xs[:], in_=XN2T.rearrange("c p q -> p c q")), d_c, writes=[r_c])
            S.dma("sp", lambda e: e.dma_start(out=cw[:], in_=conv_wT), d_c, writes=[r_c])
            S.dma("sp", lambda e: e.dma_start(out=cbias[:], in_=conv_bT), d_c, writes=[r_c])
            S.dma("sp", lambda e: e.dma_start(out=halo[:], in_=t_halo), d_c, writes=[r_c])
            Rwa, Rwb = Ring(wa_, S, True), Ring(wb_, S, True)
            Ru, Ry = Ring(u_), Ring(y_)
            Ro = Ring(o_, S, True)
            Rps, Rph = Ring(ps, excl=True), Ring(ph, excl=True)
            wv = w_ffn_up.rearrange("(c p) n -> p c n", p=128)

            def half(wt, rw, ch, cidx, ceng):
                u, ru = Ru.next()
                p, rp = Rph.next()
                for c in range(16):
                    S.op("pe", lambda e, c=c: e.matmul(p[:, 0:2], lhsT=wt[:, c, ch * 128:(ch + 1) * 128],
                                                       rhs=xs[:, c, 126:128], start=(c == 0), stop=(c == 15)),
                         reads=[rw, r_c], writes=[rp])
                S.op("act", lambda e: e.activation(out=u[:, 0:2], in_=p[:, 0:2], func=AF.Copy, scale=halo[:]),
                     reads=[rp, r_c], writes=[ru])
                for g in range(4):
                    pp, rpp = Rps.next()
                    for c in range(16):
                        S.op("pe", lambda e, c=c, pp=pp, g=g: e.matmul(
                            pp[:], lhsT=wt[:, c, ch * 128:(ch + 1) * 128],
                            rhs=xs[:, c, 128 + g * 512:128 + (g + 1) * 512], start=(c == 0), stop=(c == 15)),
                            reads=[rw, r_c], writes=[rpp])
                    S.op("act", lambda e, pp=pp, g=g: e.activation(out=u[:, 2 + g * 512:2 + (g + 1) * 512], in_=pp[:],
                                                                   func=AF.Copy), reads=[rpp], writes=[ru])
                y, ry = Ry.next()
                S.op(ceng, lambda e: e.tensor_scalar(out=y[:], in0=u[:, 2:2050], scalar1=cw[:, cidx, 2:3],
                                                     scalar2=cbias[:, cidx:cidx + 1], op0=ALU.mult, op1=ALU.add),
                     reads=[ru, r_c], writes=[ry])
                S.op(ceng, lambda e: e.scalar_tensor_tensor(out=y[:], in0=u[:, 1:2049], scalar=cw[:, cidx, 1:2], in1=y[:],
                                                            op0=ALU.mult, op1=ALU.add), reads=[ru, r_c, ry], writes=[ry])
                S.op(ceng, lambda e: e.scalar_tensor_tensor(out=y[:], in0=u[:, 0:2048], scalar=cw[:, cidx, 0:1], in1=y[:],
                                                            op0=ALU.mult, op1=ALU.add), reads=[ru, r_c, ry], writes=[ry])
                return y, ry

            def chunk(wa, rwa, wb, rwb, ch, j):
                ya, rya = half(wa, rwa, ch, j, "dve")
                yb, ryb = half(wb, rwb, ch, 44 + j, "dve")
                S.op("act", lambda e: e.activation(out=ya[:], in_=ya[:], func=AF.Silu), reads=[rya], writes=[rya])
                o, ro, do = Ro.next()
                S.op("pool", lambda e: e.tensor_tensor(out=o[:], in0=ya[:], in1=yb[:], op=ALU.mult),
                     reads=[rya, ryb], writes=[ro])
                S.dma("sp", lambda e: e.dma_start(out=ACTT[:, :, j, :].rearrange("t p q -> p t q"),
                                                  in_=o[:].rearrange("p (t q) -> p t q", q=128)), do, reads=[ro])

            def wblock(jb):
                wa, rwa, dwa = Rwa.next()
                wb, rwb, dwb = Rwb.next()
                S.dma("pool", lambda e: e.dma_start(out=wa[:], in_=wv[:, :, jb * 512:(jb + 1) * 512]), dwa, writes=[rwa])
                S.dma("pool", lambda e: e.dma_start(out=wb[:], in_=wv[:, :, DFF + jb * 512:DFF + (jb + 1) * 512]), dwb,
                      writes=[rwb])
                for ch in range(4):
                    chunk(wa, rwa, wb, rwb, ch, jb * 4 + ch)

            for jb in range(11):
                wblock(jb)
            S.emit(block)

    def phase5b():
        with ExitStack() as st:
            sb = lambda name, shape, dt: st.enter_context(nc.sbuf_tensor(name, list(shape), dt))
            wd_ = [sb(f"p5b_w{i}", [128, 44, 512], BF16) for i in range(2)]
            a_ = [sb(f"p5b_a{i}", [128, 44, 128], BF16) for i in range(3)]
            h_ = [sb(f"p5b_h{i}", [128, 512], F32) for i in range(3)]
            ps = [st.enter_context(nc.psum_tensor(f"p5b_ps{i}", [128, 512], F32)) for i in range(4)]
            S = Sched(nc, st, "p5b")
            block = st.enter_context(nc.Block())
            Rw, Ra, Rh = Ring(wd_, S, True), Ring(a_, S, True), Ring(h_, S, True)
            Rwq = Ring(list(range(8)), S, True)
            r_hs = [Res() for _ in h_]
            d_hs = [S.dsem() for _ in h_]
            Rps = Ring(ps, excl=True)
            wv = w_ffn_down.rearrange("(j p) n -> p j n", p=128)

            def tile(wt, rw, cb, t):
                a, ra, da = Ra.next()
                S.dma("sp", lambda e: e.dma_start(out=a[:], in_=ACTT[t]), da, writes=[ra])
                h, rh, dh = Rh.next()
                k = Rh.i
                S.dma("sp", lambda e: e.dma_start(out=h[:], in_=H1[(t + 1) * 128:(t + 2) * 128, cb * 512:(cb + 1) * 512]),
                      dh, writes=[rh])
                p, rp = Rps.next()
                for j in (range(44) if not os.environ.get("SKIPMM") else range(1)):
                    S.op("pe", lambda e, j=j: e.matmul(p[:], lhsT=a[:, j, :], rhs=wt[:, j, :], start=(j == 0), stop=(j == 43)),
                         reads=[ra, rw[j // 11]], writes=[rp])
                S.op("dve", lambda e: e.tensor_tensor(out=h[:], in0=p[:], in1=h[:], op=ALU.add), reads=[rp, rh], writes=[rh])
                S.dma("sp", lambda e: e.dma_start(out=H2[t * 128:(t + 1) * 128, cb * 512:(cb + 1) * 512], in_=h[:]),
                      d_hs[k], reads=[rh], writes=[r_hs[k]])

            def wblock(cb):
                wt, rw, dw = Rw.next()
                rws = Rwq.res[4 * Rw.i:4 * Rw.i + 4]
                dws = Rwq.ds[4 * Rw.i:4 * Rw.i + 4]
                for q4 in range(4):
                    S.dma("pool", lambda e, q4=q4: e.dma_start(out=wt[:, q4 * 11:(q4 + 1) * 11, :],
                                                             in_=wv[:, q4 * 11:(q4 + 1) * 11, cb * 512:(cb + 1) * 512]),
                          dws[q4], writes=[rws[q4]])
                for t in range(16):
                    tile(wt, rws, cb, t)

            for cb in range(4):
                wblock(cb)
            S.emit(block)

    def phase5c():
        with ExitStack() as st:
            sb = lambda name, shape, dt: st.enter_context(nc.sbuf_tensor(name, list(shape), dt))
            wf = sb("p5c_wf", [128, D], F32)
            h_ = [sb(f"p5c_h{i}", [128, D], F32) for i in range(2)]
            o_ = [sb(f"p5c_o{i}", [128, D], F32) for i in range(2)]
            junk = sb("p5c_junk", [128, D], BF16)
            ss_ = [sb(f"p5c_ss{i}", [128, 1], F32) for i in range(2)]
            S = Sched(nc, st, "p5c")
            block = st.enter_context(nc.Block())
            r_w, r_junk = Res(), Res()
            S.dma("sp", lambda e: e.dma_start(out=wf[:], in_=final_norm_w.partition_broadcast(128)), S.dsem(), writes=[r_w])
            Rh, Ro, Rss = Ring(h_, S, True), Ring(o_, S, True), Ring(ss_)

            def tile(t):
                h, rh, dh = Rh.next()
                S.dma("sp", lambda e: e.dma_start(out=h[:], in_=H2[t * 128:(t + 1) * 128, :]), dh, writes=[rh])
                ss, rss = Rss.next()
                S.op("act", lambda e: e.activation(out=junk[:], in_=h[:], func=AF.Square, accum_out=ss[:]),
                     reads=[rh], writes=[r_junk, rss])
                S.op("act", lambda e: e.activation(out=ss[:], in_=ss[:], func=AF.Sqrt, scale=1.0 / D, bias=EPS),
                     reads=[rss], writes=[rss])
                S.op("dve", lambda e: e.reciprocal(out=ss[:], in_=ss[:]), reads=[rss], writes=[rss])
                o, ro, do = Ro.next()
                S.op("dve", lambda e: e.scalar_tensor_tensor(out=o[:], in0=h[:], scalar=ss[:], in1=wf[:],
                                                             op0=ALU.mult, op1=ALU.mult), reads=[rh, rss, r_w], writes=[ro])
                S.dma("sp", lambda e: e.dma_start(out=out[t * 128:(t + 1) * 128, :], in_=o[:]), do, reads=[ro])

            for t in range(16):
                tile(t)
            S.emit(block)

    if 5 in phases:
        phase5a()
        phase5b()
        phase5c()

    return nc


def _tables(s):
    pad = 2048 if s == 0 else 0
    tl = np.arange(LT * 128)
    act = np.maximum(tl - pad, 0).astype(np.float64)
    is_pad = tl < pad
    half = 64
    freq = 10000.0 ** (-np.arange(half, dtype=np.float64) / half)
    ang = act[:, None] * freq[None, :]
    cs = np.concatenate([np.cos(ang), np.sin(ang)], axis=1)
    t_cs = cs.reshape(LT, 128, 128).transpose(1, 0, 2).astype(np.float32)
    t_csq = (cs[QT0 * 128:] * (128 ** -0.5)).reshape(NQ, 128, 128).transpose(1, 0, 2).astype(np.float32)
    gam = 1.0 - 2.0 ** (-5.0 - np.arange(8, dtype=np.float64))
    j = np.arange(128, dtype=np.float64)
    rel = j[None, :] - j[:, None]
    decT = np.where(rel[:, None, :] >= 0, gam[None, :, None] ** np.maximum(rel[:, None, :], 0), 0.0)
    qdec = np.broadcast_to((gam[:, None] ** (j[None, :] + 1.0))[None], (128, 8, 128))
    kdec = gam[None, :] ** (127.0 - j[:, None])
    padb = np.where(is_pad, NEG, 0.0).reshape(LT, 128).T
    n = np.arange(256)
    cmp_invalid = (n * 16 < pad) | (n >= 255)
    cmpb = np.where(cmp_invalid, NEG, 0.0).reshape(2, 128).T
    r = np.arange(512)[:, None]
    q = np.arange(128)[None, :]
    cm = np.where(16 * (r - 250) + 31 <= q, 0.0, NEG)
    keep = np.ones((NQ, 128, 64))
    add = np.zeros((NQ, 128, 64))
    blk = np.arange(64)[None, :]
    b0 = pad // 64
    for qi in range(NQ):
        t = (QT0 + qi) * 128 + np.arange(128)
        cur = (t // 64)[:, None]
        forced = (blk == b0) | (blk == cur) | (blk == cur - 1)
        neg = (blk > cur) | (blk < b0)
        keep[qi] = np.where(forced | neg, 0.0, 1.0)
        add[qi] = np.where(neg, -1e4, np.where(forced, 1e4, 0.0))
    sidx = np.arange(64)[:, None, None]
    kt = np.arange(LT)[None, :, None]
    kk = np.arange(128)[None, None, :]
    texp = (sidx == 2 * kt + (kk >= 64)).astype(np.float32)
    k_ = np.arange(128)[:, None]
    caus = np.where(k_ > q, NEG, 0.0)
    acaus = np.where(k_ <= q, NEG, 0.0)
    nn = np.arange(256)[:, None]
    ss = np.arange(64)[None, :]
    ov = ((nn * 16 < ss * 64 + 64) & (nn * 16 + 32 > ss * 64)).astype(np.float64)
    ov = np.concatenate([ov, np.ones((256, 1))], axis=1)
    ov[255] = 0.0
    t_ov = ov.reshape(2, 128, 65).transpose(1, 0, 2)
    f = lambda a: np.ascontiguousarray(a, dtype=np.float32)
    return {
        "t_cs": f(t_cs), "t_csq": f(t_csq), "t_decT": f(decT), "t_qdec": f(qdec), "t_kdec": f(kdec),
        "t_padb": f(padb), "t_cmpb": f(cmpb), "t_cm": f(cm), "t_keep": f(keep), "t_add": f(add),
        "t_exp": f(texp), "t_caus": f(caus), "t_acaus": f(acaus), "t_ident": f(np.eye(128)),
        "t_ov": f(t_ov), "t_halo": f(np.full((128, 1), float(s))),
    }


def make_in_maps(inputs):
    g = lambda k: np.asarray(inputs[k], dtype=np.float32)
    x = g("x")
    shared = {
        "norm1_w": g("norm1_w")[0][None], "w_in": g("w_in")[0], "ret_norm_w": g("ret_norm_w")[0][None],
        "w_ret_up": g("w_ret_up")[0],
        "cmp_peT_k": np.ascontiguousarray(g("cmp_pe_k")[0].T), "cmp_peT_v": np.ascontiguousarray(g("cmp_pe_v")[0].T),
        "cmp_w1_k": g("cmp_w1_k")[0], "cmp_w1_v": g("cmp_w1_v")[0],
        "cmp_w2_k": g("cmp_w2_k")[0], "cmp_w2_v": g("cmp_w2_v")[0],
        "w_nsa_up": g("w_nsa_up")[0], "w_out": g("w_out")[0], "norm2_w": g("norm2_w")[0][None],
        "w_ffn_up": g("w_ffn_up")[0],
        "conv_wT": np.ascontiguousarray(g("conv_w")[0].reshape(3, 88, 128).transpose(2, 1, 0)),
        "conv_bT": np.ascontiguousarray(g("conv_b")[0].reshape(88, 128).T),
        "w_ffn_down": g("w_ffn_down")[0], "final_norm_w": g("final_norm_w")[None],
    }
    tabs = [_tables(0), _tables(1)]
    maps = []
    for c in range(8):
        b, s = c // 2, c % 2
        if s == 1:
            xl = np.ascontiguousarray(x[b])
        else:
            xl = np.concatenate([np.zeros((2048, D), np.float32), x[b, :2048]], axis=0)
        m = dict(shared)
        m.update(tabs[s])
        m["x_loc"] = xl
        maps.append(m)
    return maps


_NC = None


def kernel(**inputs):
    global _NC
    if _NC is None:
        _NC = build_program()
    maps = make_in_maps(inputs)
    res = run_bass_kernel_spmd(_NC, maps, core_ids=list(range(8)))
    outp = np.zeros((4, 4096, D), np.float32)
    for c in range(8):
        b, s = c // 2, c % 2
        outp[b, s * 2048:(s + 1) * 2048] = res.results[c]["out"]
    return outp
```

```python
import math
import os
from contextlib import ExitStack
import numpy as np
import concourse.bass as bass
import concourse.mybir as mybir
from concourse.bass_utils import run_bass_kernel_spmd

F32 = mybir.dt.float32
BF16 = mybir.dt.bfloat16
AF = mybir.ActivationFunctionType
ALU = mybir.AluOpType
AX = mybir.AxisListType

D = 2048
LT = 32
QT0 = 15
NQ = LT - QT0
NQT = NQ * 128
IN_COLS = 13872
DFF = 5632
EPS = 1e-6
NEG = -30000.0
SCALE = 128 ** -0.5


class Tok:
    __slots__ = ("sem", "val")

    def __init__(self, sem, val):
        self.sem = sem
        self.val = val


class Res:
    __slots__ = ("w", "r", "excl")

    def __init__(self, excl=False):
        self.w = None
        self.r = []
        self.excl = excl


class DSem:
    __slots__ = ("sem", "val", "eng")

    def __init__(self, sem):
        self.sem = sem
        self.val = 0
        self.eng = None


ENGS = ("pe", "act", "dve", "pool", "sp")


class Sched:
    def __init__(self, nc, stack, tag):
        self.nc = nc
        self.q = {e: [] for e in ENGS}
        self.cnt = {e: 0 for e in ENGS}
        self.allsems = []
        self.sem = {e: self._alloc(f"{tag}_s_{e}") for e in ENGS}
        self.waited = {e: {} for e in ENGS}
        self.dsems = []
        self.tag = tag
        stack.callback(self._cleanup)

    def _alloc(self, name):
        h = self.nc.alloc_semaphore(name=name)
        self.allsems.append(h)
        return h

    def _cleanup(self):
        self.nc.clear_and_free_semaphores(self.allsems)
        self.nc.all_engine_barrier()

    def dsem(self):
        d = DSem(self._alloc(f"{self.tag}_d{len(self.dsems)}"))
        self.dsems.append(d)
        return d

    def _deps(self, eng, reads, writes):
        need = {}

        def add(t):
            k = id(t.sem)
            if k not in need or need[k].val < t.val:
                need[k] = t

        for r in reads:
            if r.w is not None:
                add(r.w)
        for w in writes:
            if w.w is not None:
                add(w.w)
            for t in w.r:
                add(t)
        waits = []
        wd = self.waited[eng]
        own = id(self.sem[eng])
        for k, t in need.items():
            if eng == "pe" and k == own:
                continue
            if wd.get(k, 0) < t.val:
                wd[k] = t.val
                waits.append((t.sem, t.val))
        return waits

    def _fin(self, tok, reads, writes):
        for r in reads:
            r.r.append(tok)
        for w in writes:
            w.w = tok
            w.r = []
        return tok

    def op(self, eng, fn, reads=(), writes=()):
        ex = [r for r in reads if r.excl]
        if ex:
            reads = [r for r in reads if not r.excl]
            writes = list(writes) + ex
        waits = self._deps(eng, reads, writes)
        self.cnt[eng] += 1
        self.q[eng].append((waits, fn, (self.sem[eng], 1)))
        return self._fin(Tok(self.sem[eng], self.cnt[eng]), reads, writes)

    def dma(self, eng, fn, ds, reads=(), writes=()):
        assert ds.eng in (None, eng), "one issuing engine per DMA semaphore"
        ds.eng = eng
        waits = self._deps(eng, reads, writes)
        ds.val += 16
        self.q[eng].append((waits, fn, (ds.sem, 16)))
        return self._fin(Tok(ds.sem, ds.val), reads, writes)

    def emit(self, block):
        finals = [(d.sem, d.val) for d in self.dsems if d.val > 0]

        def run(engobj, name, tail=False):
            for waits, fn, inc in self.q[name]:
                for s, v in waits:
                    engobj.wait_ge(s, v)
                fn(engobj).then_inc(inc[0], inc[1])
            if tail:
                for s, v in finals:
                    engobj.wait_ge(s, v)
                for e in ENGS:
                    if e != name and self.cnt[e] > 0:
                        engobj.wait_ge(self.sem[e], self.cnt[e])

        @block.sync
        def _(eng):
            run(eng, "sp", tail=True)

        @block.tensor
        def _(eng):
            run(eng, "pe")

        @block.scalar
        def _(eng):
            run(eng, "act")

        @block.vector
        def _(eng):
            run(eng, "dve")

        @block.gpsimd
        def _(eng):
            run(eng, "pool")


class Ring:
    def __init__(self, bufs, S=None, with_dsem=False, excl=False):
        self.bufs = bufs
        self.res = [Res(excl) for _ in bufs]
        self.ds = [S.dsem() for _ in bufs] if with_dsem else None
        self.i = -1

    def next(self):
        self.i = (self.i + 1) % len(self.bufs)
        if self.ds is not None:
            return self.bufs[self.i], self.res[self.i], self.ds[self.i]
        return self.bufs[self.i], self.res[self.i]


ALL_PHASES = (1, 2, 3, 4, 5)


def build_program(debug=(), phases=ALL_PHASES):
    nc = bass.Bass("TRN2", target_bir_lowering=False)

    def din(name, shape, dt=F32):
        return nc.dram_tensor(name, list(shape), dt, kind="ExternalInput").ap()

    def dscr(name, shape, dt=BF16):
        kind = "ExternalOutput" if name in debug else "Internal"
        return nc.dram_tensor(name, list(shape), dt, kind=kind).ap()

    x_loc = din("x_loc", [LT * 128, D])
    norm1_w = din("norm1_w", [1, D])
    w_in = din("w_in", [D, IN_COLS])
    ret_norm_w = din("ret_norm_w", [1, D])
    w_ret_up = din("w_ret_up", [D, D])
    cmp_peT = {"k": din("cmp_peT_k", [128, 32]), "v": din("cmp_peT_v", [128, 32])}
    cmp_w1 = {"k": din("cmp_w1_k", [4096, 256]), "v": din("cmp_w1_v", [4096, 256])}
    cmp_w2 = {"k": din("cmp_w2_k", [256, 128]), "v": din("cmp_w2_v", [256, 128])}
    w_nsa_up = din("w_nsa_up", [D, D])
    w_out = din("w_out", [D, D])
    norm2_w = din("norm2_w", [1, D])
    w_ffn_up = din("w_ffn_up", [D, 2 * DFF])
    conv_wT = din("conv_wT", [128, 88, 3])
    conv_bT = din("conv_bT", [128, 88])
    w_ffn_down = din("w_ffn_down", [DFF, D])
    final_norm_w = din("final_norm_w", [1, D])
    t_cs = din("t_cs", [128, LT, 128])
    t_csq = din("t_csq", [128, NQ, 128])
    t_decT = din("t_decT", [128, 8, 128])
    t_qdec = din("t_qdec", [128, 8, 128])
    t_kdec = din("t_kdec", [128, 8])
    t_padb = din("t_padb", [128, LT])
    t_cmpb = din("t_cmpb", [128, 2])
    t_cm = din("t_cm", [512, 128])
    t_keep = din("t_keep", [NQ, 128, 64])
    t_add = din("t_add", [NQ, 128, 64])
    t_exp = din("t_exp", [64, LT, 128])
    t_caus = din("t_caus", [128, 128])
    t_acaus = din("t_acaus", [128, 128])
    t_ident = din("t_ident", [128, 128])
    t_ov = din("t_ov", [128, 2, 65])
    t_halo = din("t_halo", [128, 1])

    out = nc.dram_tensor("out", [16 * 128, D], F32, kind="ExternalOutput").ap()

    RK = dscr("RK", [LT * 128, 1024])
    RV = dscr("RV", [LT * 128, 2048])
    RQ = dscr("RQ", [NQT, 1024])
    RG = dscr("RG", [NQT, 2048])
    NQT_ = dscr("NQT", [16, 128, NQT])
    KCT = dscr("KCT", [2, 128, LT * 128])
    VCT = dscr("VCT", [2, 128, LT * 128])
    KST = dscr("KST", [2, 128, LT * 128])
    KWT = dscr("KWT", [2, 128, LT * 128])
    VS = dscr("VS", [LT * 128, 256])
    VW = dscr("VW", [LT * 128, 256])
    NGT = dscr("NGT", [48, NQT], F32)
    MGT = dscr("MGT", [32, 128, NQT])
    RETGT = dscr("RETGT", [16, 128, NQT])
    NSAT = dscr("NSAT", [16, 128, NQT])
    M1T = dscr("M1T", [16, 128, NQT])
    MRGT = dscr("MRGT", [16, 128, NQT])
    H1 = dscr("H1", [NQT, D], F32)
    XN2T = dscr("XN2T", [16, 128, NQT])
    ACTT = dscr("ACTT", [16, 128, 44, 128])
    H2 = dscr("H2", [2048, D], F32)

    w_in_v = w_in.rearrange("(c p) n -> p c n", p=128)

    def phase01():
        with ExitStack() as st01:
            xnT = st01.enter_context(nc.sbuf_tensor("xnT", [128, 16, LT * 128], BF16))
            identb = st01.enter_context(nc.sbuf_tensor("identb", [128, 128], BF16))

            with ExitStack() as st:
                xin = [st.enter_context(nc.sbuf_tensor(f"p0_x{i}", [128, D], F32)) for i in range(2)]
                sq = st.enter_context(nc.sbuf_tensor("p0_sq", [128, D], BF16))
                xnb = [st.enter_context(nc.sbuf_tensor(f"p0_xn{i}", [128, D], BF16)) for i in range(2)]
                wbc = st.enter_context(nc.sbuf_tensor("p0_wbc", [128, D], F32))
                ss = [st.enter_context(nc.sbuf_tensor(f"p0_ss{i}", [128, 1], F32)) for i in range(2)]
                rs = [st.enter_context(nc.sbuf_tensor(f"p0_rs{i}", [128, 1], F32)) for i in range(2)]
                pt = [st.enter_context(nc.psum_tensor(f"p0_pt{i}", [128, D], BF16)) for i in range(2)]
                S = Sched(nc, st, "p0")
                block = st.enter_context(nc.Block())
                r_c = Res()
                d_c = S.dsem()
                S.dma("sp", lambda e: e.dma_start(out=wbc[:], in_=norm1_w.partition_broadcast(128)), d_c, writes=[r_c])
                r_id = Res()
                S.dma("pool", lambda e: e.dma_start(out=identb[:], in_=t_ident), S.dsem(), writes=[r_id])
                Rx = Ring(xin, S, True)
                Rxn = Ring(xnb)
                Rss = Ring(ss)
                Rrs = Ring(rs)
                Rpt = Ring(pt, excl=True)
                r_sq = Res()
                r_xnT = Res()
                for t in range(LT):
                    xb, rx, dx = Rx.next()
                    S.dma("sp", lambda e, xb=xb, t=t: e.dma_start(out=xb[:], in_=x_loc[t * 128:(t + 1) * 128, :]),
                          dx, writes=[rx])
                    sb, rss = Rss.next()
                    S.op("act", lambda e, xb=xb, sb=sb: e.activation(out=sq[:], in_=xb[:], func=AF.Square,
                                                                      accum_out=sb[:]),
                         reads=[rx], writes=[r_sq, rss])
                    rb, rrs = Rrs.next()
                    S.op("act", lambda e, sb=sb, rb=rb: e.activation(out=rb[:], in_=sb[:], func=AF.Sqrt,
                                                                     scale=1.0 / D, bias=EPS),
                         reads=[rss], writes=[rrs])
                    S.op("dve", lambda e, rb=rb: e.reciprocal(out=rb[:], in_=rb[:]),
                         reads=[rrs], writes=[rrs])
                    xn, rxn = Rxn.next()
                    S.op("dve", lambda e, xn=xn, xb=xb, rb=rb: e.scalar_tensor_tensor(
                        out=xn[:], in0=xb[:], scalar=rb[:], in1=wbc[:], op0=ALU.mult, op1=ALU.mult),
                        reads=[rx, rrs, r_c], writes=[rxn])
                    pb, rp = Rpt.next()
                    for c in range(16):
                        S.op("pe", lambda e, pb=pb, xn=xn, c=c: e.transpose(
                            out=pb[:, c * 128:(c + 1) * 128], in_=xn[:, c * 128:(c + 1) * 128], identity=identb[:]),
                            reads=[rxn, r_id], writes=[rp])
                    for hh in range(2):
                        eng = "act" if hh == 0 else "dve"
                        if eng == "act":
                            S.op("act", lambda e, pb=pb, t=t, hh=hh: e.activation(
                                out=xnT[:, hh * 8:(hh + 1) * 8, t * 128:(t + 1) * 128],
                                in_=pb[:, hh * 1024:(hh + 1) * 1024].rearrange("p (c q) -> p c q", c=8),
                                func=AF.Copy), reads=[rp], writes=[r_xnT])
                        else:
                            S.op("dve", lambda e, pb=pb, t=t, hh=hh: e.tensor_copy(
                                out=xnT[:, hh * 8:(hh + 1) * 8, t * 128:(t + 1) * 128],
                                in_=pb[:, hh * 1024:(hh + 1) * 1024].rearrange("p (c q) -> p c q", c=8)),
                                reads=[rp], writes=[r_xnT])
                S.emit(block)

            if 0 in phases and 1 not in phases:
                return

            with ExitStack() as st:
                wb = [st.enter_context(nc.sbuf_tensor(f"p1_w{i}", [128, 16, 512], BF16)) for i in range(2)]
                cs = st.enter_context(nc.sbuf_tensor("p1_cs", [128, LT, 128], F32))
                csq = st.enter_context(nc.sbuf_tensor("p1_csq", [128, NQ, 128], F32))
                xf = [st.enter_context(nc.sbuf_tensor(f"p1_xf{i}", [128, 512], F32)) for i in range(2)]
                tmp = [st.enter_context(nc.sbuf_tensor(f"p1_t{i}", [128, 4, 256], F32)) for i in range(2)]
                ob = [st.enter_context(nc.sbuf_tensor(f"p1_o{i}", [128, 512], BF16)) for i in range(3)]
                of = [st.enter_context(nc.sbuf_tensor(f"p1_of{i}", [48, 512], F32)) for i in range(2)]
                ps = [st.enter_context(nc.psum_tensor(f"p1_ps{i}", [128, 512], F32)) for i in range(4)]
                S = Sched(nc, st, "p1")
                block = st.enter_context(nc.Block())
                r_tab = Res()
                d_tab = S.dsem()
                S.dma("sp", lambda e: e.dma_start(out=cs[:], in_=t_cs), d_tab, writes=[r_tab])
                S.dma("sp", lambda e: e.dma_start(out=csq[:], in_=t_csq), d_tab, writes=[r_tab])
                Rw = Ring(wb, S, True)
                Rps = Ring(ps, excl=True)
                Rxf = Ring(xf)
                Rtmp = Ring(tmp)
                Rob = Ring(ob, S, True)
                Rof = Ring(of, S, True)
                r_x = Res()
                alt = [0]

                def load_w(col0, ncols):
                    w, rw, dw = Rw.next()
                    S.dma("pool", lambda e, w=w: e.dma_start(out=w[:, :, 0:ncols], in_=w_in_v[:, :, col0:col0 + ncols]),
                          dw, writes=[rw])
                    return w, rw

                def mm_tm(w, rw, t, c0, n):
                    p, rp = Rps.next()
                    for c in range(16):
                        S.op("pe", lambda e, p=p, c=c: e.matmul(
                            p[:, 0:n], lhsT=xnT[:, c, t * 128:(t + 1) * 128], rhs=w[:, c, c0:c0 + n],
                            start=(c == 0), stop=(c == 15)), reads=[rw, r_x], writes=[rp])
                    return p, rp

                def mm_fm(w, rw, tok0, ntok, c0, m):
                    p, rp = Rps.next()
                    for c in range(16):
                        S.op("pe", lambda e, p=p, c=c: e.matmul(
                            p[0:m, 0:ntok], lhsT=w[:, c, c0:c0 + m], rhs=xnT[:, c, tok0:tok0 + ntok],
                            start=(c == 0), stop=(c == 15)), reads=[rw, r_x], writes=[rp])
                    return p, rp

                def evac_store(p, rp, m, n, dst, func=None):
                    o, ro, do = Rob.next()
                    if func is not None:
                        S.op("act", lambda e: e.activation(out=o[0:m, 0:n], in_=p[0:m, 0:n], func=func),
                             reads=[rp], writes=[ro])
                    else:
                        alt[0] ^= 1
                        if alt[0]:
                            S.op("act", lambda e: e.activation(out=o[0:m, 0:n], in_=p[0:m, 0:n], func=AF.Copy),
                                 reads=[rp], writes=[ro])
                        else:
                            S.op("dve", lambda e: e.tensor_copy(out=o[0:m, 0:n], in_=p[0:m, 0:n]),
                                 reads=[rp], writes=[ro])
                    S.dma("sp", lambda e: e.dma_start(out=dst, in_=o[0:m, 0:n]), do, reads=[ro])

                def rotary(p, rp, ctab, ti, dst):
                    x_, rxf = Rxf.next()
                    S.op("act", lambda e: e.activation(out=x_[:], in_=p[:], func=AF.Copy), reads=[rp], writes=[rxf])
                    xv = x_[:].rearrange("p (h t d) -> p h t d", h=4, t=2)
                    cosb = ctab[:, ti, 0:64][:, None, :].to_broadcast([128, 4, 64])
                    sinb = ctab[:, ti, 64:128][:, None, :].to_broadcast([128, 4, 64])
                    tm_, rt = Rtmp.next()
                    o, ro, do = Rob.next()
                    ov = o[:].rearrange("p (h t d) -> p h t d", h=4, t=2)
                    x1 = xv[:, :, 0, :]
                    x2 = xv[:, :, 1, :]
                    S.op("pool", lambda e: e.tensor_tensor(out=tm_[:, :, 0:64], in0=x1, in1=cosb, op=ALU.mult),
                         reads=[rxf, r_tab], writes=[rt])
                    S.op("pool", lambda e: e.tensor_tensor(out=tm_[:, :, 64:128], in0=x2, in1=sinb, op=ALU.mult),
                         reads=[rxf, r_tab], writes=[rt])
                    S.op("dve", lambda e: e.tensor_tensor(out=tm_[:, :, 128:192], in0=x1, in1=sinb, op=ALU.mult),
                         reads=[rxf, r_tab], writes=[rt])
                    S.op("dve", lambda e: e.tensor_tensor(out=tm_[:, :, 192:256], in0=x2, in1=cosb, op=ALU.mult),
                         reads=[rxf, r_tab], writes=[rt])
                    S.op("pool", lambda e: e.tensor_tensor(out=ov[:, :, 0, :], in0=tm_[:, :, 0:64],
                                                           in1=tm_[:, :, 64:128], op=ALU.subtract),
                         reads=[rt], writes=[ro])
                    S.op("dve", lambda e: e.tensor_tensor(out=ov[:, :, 1, :], in0=tm_[:, :, 128:192],
                                                          in1=tm_[:, :, 192:256], op=ALU.add),
                         reads=[rt], writes=[ro])
                    S.dma("sp", lambda e: e.dma_start(out=dst, in_=o[:]), do, reads=[ro])

                qtiles = list(range(QT0, LT))
                qgroups = [(QT0 * 128, 128)] + [((16 + 4 * g) * 128, 512) for g in range(4)]
                agroups = [(g * 512, 512) for g in range(8)]

                for blk in range(2):
                    w, rw = load_w(1024 + blk * 512, 512)
                    for t in range(LT):
                        p, rp = mm_tm(w, rw, t, 0, 512)
                        rotary(p, rp, cs, t, RK[t * 128:(t + 1) * 128, blk * 512:(blk + 1) * 512])
                for blk in range(2):
                    w, rw = load_w(blk * 512, 512)
                    for t in qtiles:
                        p, rp = mm_tm(w, rw, t, 0, 512)
                        qi = t - QT0
                        rotary(p, rp, csq, qi, RQ[qi * 128:(qi + 1) * 128, blk * 512:(blk + 1) * 512])
                for blk in range(4):
                    w, rw = load_w(2048 + blk * 512, 512)
                    for t in range(LT):
                        p, rp = mm_tm(w, rw, t, 0, 512)
                        evac_store(p, rp, 128, 512, RV[t * 128:(t + 1) * 128, blk * 512:(blk + 1) * 512])
                w, rw = load_w(8192, 512)
                for (dst, c0) in ((KCT, 0), (VCT, 256)):
                    for g in range(2):
                        for tok0, ntok in agroups:
                            p, rp = mm_fm(w, rw, tok0, ntok, c0 + g * 128, 128)
                            evac_store(p, rp, 128, ntok, dst[g, :, tok0:tok0 + ntok])
                for (col0, dfm, dtm) in ((8704, KST, VS), (9216, KWT, VW)):
                    w, rw = load_w(col0, 512)
                    for g in range(2):
                        for tok0, ntok in agroups:
                            p, rp = mm_fm(w, rw, tok0, ntok, g * 128, 128)
                            evac_store(p, rp, 128, ntok, dfm[g, :, tok0:tok0 + ntok])
                    for t in range(LT):
                        p, rp = mm_tm(w, rw, t, 256, 256)
                        evac_store(p, rp, 128, 256, dtm[t * 128:(t + 1) * 128, :])
                for blk in range(4):
                    w, rw = load_w(6144 + blk * 512, 512)
                    for ch in range(4):
                        for tok0, ntok in qgroups:
                            p, rp = mm_fm(w, rw, tok0, ntok, ch * 128, 128)
                            q0 = tok0 - QT0 * 128
                            evac_store(p, rp, 128, ntok, NQT_[blk * 4 + ch, :, q0:q0 + ntok])
                for blk in range(4):
                    w, rw = load_w(4096 + blk * 512, 512)
                    for t in qtiles:
                        p, rp = mm_tm(w, rw, t, 0, 512)
                        qi = t - QT0
                        evac_store(p, rp, 128, 512, RG[qi * 128:(qi + 1) * 128, blk * 512:(blk + 1) * 512],
                                   func=AF.Silu)
                for blk in range(8):
                    w, rw = load_w(9776 + blk * 512, 512)
                    for ch in range(4):
                        for tok0, ntok in qgroups:
                            p, rp = mm_fm(w, rw, tok0, ntok, ch * 128, 128)
                            q0 = tok0 - QT0 * 128
                            evac_store(p, rp, 128, ntok, MGT[blk * 4 + ch, :, q0:q0 + ntok], func=AF.Sigmoid)
                w, rw = load_w(9728, 48)
                for tok0, ntok in qgroups:
                    p, rp = mm_fm(w, rw, tok0, ntok, 0, 48)
                    q0 = tok0 - QT0 * 128
                    o, ro, do = Rof.next()
                    S.op("act", lambda e, o=o, p=p, ntok=ntok: e.activation(out=o[0:48, 0:ntok], in_=p[0:48, 0:ntok],
                                                                           func=AF.Sigmoid), reads=[rp], writes=[ro])
                    S.dma("sp", lambda e, o=o, q0=q0, ntok=ntok: e.dma_start(out=NGT[:, q0:q0 + ntok],
                                                                            in_=o[0:48, 0:ntok]), do, reads=[ro])
                S.emit(block)

    if 1 in phases:
        phase01()

    def phase2():
        gam = [1.0 - 2.0 ** (-5.0 - h) for h in range(8)]
        with ExitStack() as st:
            sb = lambda name, shape, dt: st.enter_context(nc.sbuf_tensor(name, list(shape), dt))
            identb = sb("p2_id", [128, 128], BF16)
            decT = sb("p2_decT", [128, 8, 128], F32)
            qdec = sb("p2_qdec", [128, 8, 128], F32)
            kdec = sb("p2_kdec", [128, 8], F32)
            retw = sb("p2_retw", [128, D], F32)
            kc_ = [sb(f"p2_k{i}", [128, 1024], BF16) for i in range(2)]
            vc_ = [sb(f"p2_v{i}", [128, 2048], BF16) for i in range(2)]
            qc_ = [sb(f"p2_q{i}", [128, 1024], BF16) for i in range(2)]
            gc_ = [sb(f"p2_g{i}", [128, 2048], BF16) for i in range(2)]
            kd_ = [sb(f"p2_kd{i}", [128, 1024], BF16) for i in range(2)]
            kT_ = [sb(f"p2_kT{i}", [128, 1024], BF16) for i in range(2)]
            qT_ = [sb(f"p2_qT{i}", [128, 1024], BF16) for i in range(2)]
            qTd_ = [sb(f"p2_qTd{i}", [128, 1024], BF16) for i in range(2)]
            ST_ = [sb(f"p2_ST{i}", [128, 512], BF16) for i in range(2)]
            stf = sb("p2_stf", [128, 2048], F32)
            stb = [sb(f"p2_stb{i}", [128, 2048], BF16) for i in range(2)]
            junk = sb("p2_junk", [128, 256], BF16)
            ssq_ = [sb(f"p2_ssq{i}", [128, 4], F32) for i in range(2)]
            tmpf_ = [sb(f"p2_tmpf{i}", [128, 1024], F32) for i in range(2)]
            gat_ = [sb(f"p2_gat{i}", [128, 2048], BF16) for i in range(2)]
            gT_ = [sb(f"p2_gT{i}", [128, 16, 128], BF16) for i in range(2)]
            pkT = st.enter_context(nc.psum_tensor("p2_pkT", [128, 1024], BF16))
            pqT = st.enter_context(nc.psum_tensor("p2_pqT", [128, 1024], BF16))
            pS = st.enter_context(nc.psum_tensor("p2_pS", [128, 512], F32))
            pO = st.enter_context(nc.psum_tensor("p2_pO", [128, 1024], F32))
            pKV = st.enter_context(nc.psum_tensor("p2_pKV", [128, 1024], F32))
            pGT = st.enter_context(nc.psum_tensor("p2_pGT", [128, 1024], BF16))
            S = Sched(nc, st, "p2")
            block = st.enter_context(nc.Block())
            r_tab = Res()
            d_tab = S.dsem()
            r_id = Res()
            S.dma("pool", lambda e: e.dma_start(out=identb[:], in_=t_ident), S.dsem(), writes=[r_id])
            S.dma("sp", lambda e: e.dma_start(out=decT[:], in_=t_decT), d_tab, writes=[r_tab])
            S.dma("sp", lambda e: e.dma_start(out=qdec[:], in_=t_qdec), d_tab, writes=[r_tab])
            S.dma("sp", lambda e: e.dma_start(out=kdec[:], in_=t_kdec), d_tab, writes=[r_tab])
            S.dma("sp", lambda e: e.dma_start(out=retw[:], in_=ret_norm_w.partition_broadcast(128)), d_tab,
                  writes=[r_tab])
            Rk, Rv, Rq, Rg = Ring(kc_, S, True), Ring(vc_, S, True), Ring(qc_, S, True), Ring(gc_, S, True)
            Rkd, RkT, RqT, RqTd, RST = Ring(kd_), Ring(kT_), Ring(qT_), Ring(qTd_), Ring(ST_)
            Rssq, Rtmpf, Rgat = Ring(ssq_), Ring(tmpf_), Ring(gat_)
            RgT = Ring(gT_, S, True)
            r_stf = Res()
            r_stb = [Res(), Res()]
            r_junk = Res()
            r_pkT, r_pqT, r_pS, r_pO, r_pKV, r_pGT = (Res(True) for _ in range(6))
            S.op("dve", lambda e: e.memset(stf[:], 0.0), writes=[r_stf])
            S.op("pool", lambda e: e.memset(stb[0][:], 0.0), writes=[r_stb[0]])
            for c in range(LT):
                k_, rk, dk = Rk.next()
                v_, rv, dv = Rv.next()
                S.dma("sp", lambda e, k_=k_, c=c: e.dma_start(out=k_[:], in_=RK[c * 128:(c + 1) * 128, :]), dk, writes=[rk])
                S.dma("sp", lambda e, v_=v_, c=c: e.dma_start(out=v_[:], in_=RV[c * 128:(c + 1) * 128, :]), dv, writes=[rv])
                sbc, rsb = stb[c % 2], r_stb[c % 2]
                sbn, rsn = stb[(c + 1) % 2], r_stb[(c + 1) % 2]
                if c >= QT0 and os.environ.get('P2_OUT', '1') == '1':
                    qi = c - QT0
                    q_, rq, dq = Rq.next()
                    g_, rg, dg = Rg.next()
                    S.dma("sp", lambda e, q_=q_, qi=qi: e.dma_start(out=q_[:], in_=RQ[qi * 128:(qi + 1) * 128, :]), dq, writes=[rq])
                    S.dma("sp", lambda e, g_=g_, qi=qi: e.dma_start(out=g_[:], in_=RG[qi * 128:(qi + 1) * 128, :]), dg, writes=[rg])
                    for h in range(8):
                        S.op("pe", lambda e, k_=k_, h=h: e.transpose(out=pkT[:, h * 128:(h + 1) * 128],
                                                                     in_=k_[:, h * 128:(h + 1) * 128], identity=identb[:]),
                             reads=[rk, r_id], writes=[r_pkT])
                    kT, rkT = RkT.next()
                    S.op("act", lambda e, kT=kT: e.activation(out=kT[:], in_=pkT[:], func=AF.Copy), reads=[r_pkT], writes=[rkT])
                    for h in range(8):
                        S.op("pe", lambda e, q_=q_, h=h: e.transpose(out=pqT[:, h * 128:(h + 1) * 128],
                                                                     in_=q_[:, h * 128:(h + 1) * 128], identity=identb[:]),
                             reads=[rq, r_id], writes=[r_pqT])
                    qT, rqT = RqT.next()
                    qTd, rqTd = RqTd.next()
                    S.op("act", lambda e, qT=qT: e.activation(out=qT[:], in_=pqT[:], func=AF.Copy), reads=[r_pqT], writes=[rqT])
                    S.op("dve", lambda e, qTd=qTd: e.tensor_tensor(
                        out=qTd[:].rearrange("p (h n) -> p h n", h=8), in0=pqT[:].rearrange("p (h n) -> p h n", h=8),
                        in1=qdec[:], op=ALU.mult), reads=[r_pqT, r_tab], writes=[rqTd])
                    gat, rgat = Rgat.next()
                    lvl = int(os.environ.get('P2_LVL', '9'))
                    for hg in (range(2) if lvl >= 2 else []):
                        for j in range(4):
                            h = hg * 4 + j
                            S.op("pe", lambda e, kT=kT, qT=qT, h=h, j=j: e.matmul(
                                pS[:, j * 128:(j + 1) * 128], lhsT=kT[:, h * 128:(h + 1) * 128],
                                rhs=qT[:, h * 128:(h + 1) * 128], start=True, stop=True),
                                reads=[rkT, rqT], writes=[r_pS])
                        ST, rST = RST.next()
                        S.op("dve", lambda e, ST=ST, hg=hg: e.tensor_tensor(
                            out=ST[:].rearrange("p (h n) -> p h n", h=4), in0=pS[:].rearrange("p (h n) -> p h n", h=4),
                            in1=decT[:, hg * 4:(hg + 1) * 4, :], op=ALU.mult), reads=[r_pS, r_tab], writes=[rST])
                        for j in range(4):
                            h = hg * 4 + j
                            S.op("pe", lambda e, ST=ST, v_=v_, h=h, j=j: e.matmul(
                                pO[:, j * 256:(j + 1) * 256], lhsT=ST[:, j * 128:(j + 1) * 128],
                                rhs=v_[:, h * 256:(h + 1) * 256], start=True, stop=False),
                                reads=[rST, rv], writes=[r_pO])
                            S.op("pe", lambda e, qTd=qTd, sbc=sbc, h=h, j=j: e.matmul(
                                pO[:, j * 256:(j + 1) * 256], lhsT=qTd[:, h * 128:(h + 1) * 128],
                                rhs=sbc[:, h * 256:(h + 1) * 256], start=False, stop=True),
                                reads=[rqTd, rsb], writes=[r_pO])
                        if lvl < 3:
                            continue
                        ssq, rssq = Rssq.next()
                        for j in range(4):
                            S.op("act", lambda e, ssq=ssq, j=j: e.activation(
                                out=junk[:], in_=pO[:, j * 256:(j + 1) * 256], func=AF.Square, accum_out=ssq[:, j:j + 1]),
                                reads=[r_pO], writes=[r_junk, rssq])
                        S.op("act", lambda e, ssq=ssq: e.activation(out=ssq[:], in_=ssq[:], func=AF.Sqrt,
                                                                    scale=1.0 / 256, bias=EPS), reads=[rssq], writes=[rssq])
                        S.op("dve", lambda e, ssq=ssq: e.reciprocal(out=ssq[:], in_=ssq[:]), reads=[rssq], writes=[rssq])
                        tmpf, rtf = Rtmpf.next()
                        for j in range(4):
                            h = hg * 4 + j
                            S.op("dve", lambda e, tmpf=tmpf, ssq=ssq, h=h, j=j: e.scalar_tensor_tensor(
                                out=tmpf[:, j * 256:(j + 1) * 256], in0=pO[:, j * 256:(j + 1) * 256],
                                scalar=ssq[:, j:j + 1], in1=retw[:, h * 256:(h + 1) * 256], op0=ALU.mult, op1=ALU.mult),
                                reads=[r_pO, rssq, r_tab], writes=[rtf])
                        if lvl < 4:
                            continue
                        S.op("pool", lambda e, gat=gat, tmpf=tmpf, g_=g_, hg=hg: e.tensor_tensor(
                            out=gat[:, hg * 1024:(hg + 1) * 1024], in0=tmpf[:], in1=g_[:, hg * 1024:(hg + 1) * 1024],
                            op=ALU.mult), reads=[rtf, rg], writes=[rgat])
                    if lvl < 5:
                        continue
                    gT, rgT, dgT = RgT.next()
                    for hh in range(2):
                        for cc in range(8):
                            ch = hh * 8 + cc
                            S.op("pe", lambda e, gat=gat, ch=ch, cc=cc: e.transpose(
                                out=pGT[:, cc * 128:(cc + 1) * 128], in_=gat[:, ch * 128:(ch + 1) * 128],
                                identity=identb[:]), reads=[rgat, r_id], writes=[r_pGT])
                        if hh == 0:
                            S.op("act", lambda e, gT=gT: e.activation(
                                out=gT[:, 0:8, :], in_=pGT[:].rearrange("p (c q) -> p c q", c=8), func=AF.Copy),
                                reads=[r_pGT], writes=[rgT])
                        else:
                            S.op("dve", lambda e, gT=gT: e.tensor_copy(
                                out=gT[:, 8:16, :], in_=pGT[:].rearrange("p (c q) -> p c q", c=8)),
                                reads=[r_pGT], writes=[rgT])
                    S.dma("sp", lambda e, gT=gT, qi=qi: e.dma_start(
                        out=RETGT[:, :, qi * 128:(qi + 1) * 128].rearrange("c p q -> p c q"), in_=gT[:]),
                        dgT, reads=[rgT])
                if c < LT - 1 and os.environ.get('P2_STATE', '1') == '1':
                    kd, rkd = Rkd.next()
                    S.op("pool", lambda e, kd=kd, k_=k_: e.tensor_tensor(
                        out=kd[:].rearrange("p (h d) -> p h d", h=8), in0=k_[:].rearrange("p (h d) -> p h d", h=8),
                        in1=kdec[:, :, None].to_broadcast([128, 8, 128]), op=ALU.mult),
                        reads=[rk, r_tab], writes=[rkd])
                    for hg in range(2):
                        for j in range(4):
                            h = hg * 4 + j
                            S.op("pe", lambda e, kd=kd, v_=v_, h=h, j=j: e.matmul(
                                pKV[:, j * 256:(j + 1) * 256], lhsT=kd[:, h * 128:(h + 1) * 128],
                                rhs=v_[:, h * 256:(h + 1) * 256], start=True, stop=True),
                                reads=[rkd, rv], writes=[r_pKV])
                        for j in range(4):
                            h = hg * 4 + j
                            S.op("dve", lambda e, h=h, j=j: e.scalar_tensor_tensor(
                                out=stf[:, h * 256:(h + 1) * 256], in0=stf[:, h * 256:(h + 1) * 256],
                                scalar=float(gam[h] ** 128), in1=pKV[:, j * 256:(j + 1) * 256],
                                op0=ALU.mult, op1=ALU.add), reads=[r_pKV, r_stf], writes=[r_stf])
                    S.op("act", lambda e, sbn=sbn: e.activation(out=sbn[:], in_=stf[:], func=AF.Copy),
                         reads=[r_stf], writes=[rsn])
            S.emit(block)

    if 2 in phases:
        phase2()

    KCMPT = dscr("KCMPT", [2, 128, 256])
    VCMP = dscr("VCMP", [2, 256, 128])

    def phase3a():
        with ExitStack() as st:
            sb = lambda name, shape, dt: st.enter_context(nc.sbuf_tensor(name, list(shape), dt))
            xc_ = [sb(f"p3a_xc{i}", [128, LT * 128], BF16) for i in range(2)]
            w1b = sb("p3a_w1", [128, 32, 256], BF16)
            peT = sb("p3a_pe", [128, 32], BF16)
            w2b = sb("p3a_w2", [128, 2, 128], BF16)
            cb = sb("p3a_cb", [128, 2], F32)
            gel_ = [sb(f"p3a_gel{i}", [128, 2, 256], BF16) for i in range(2)]
            og_ = [sb(f"p3a_og{i}", [128, 256], BF16) for i in range(2)]
            pc = st.enter_context(nc.psum_tensor("p3a_pc", [128, 2], F32))
            ph_ = [st.enter_context(nc.psum_tensor(f"p3a_ph{i}", [128, 256], F32)) for i in range(2)]
            po_ = [st.enter_context(nc.psum_tensor(f"p3a_po{i}", [128, 256], F32)) for i in range(2)]
            S = Sched(nc, st, "p3a")
            block = st.enter_context(nc.Block())
            Rxc = Ring(xc_, S, True)
            Rgel = Ring(gel_)
            Rog = Ring(og_, S, True)
            Rph = Ring(ph_, excl=True)
            Rpo = Ring(po_, excl=True)
            r_w, r_cb, r_pc = Res(), Res(), Res(True)
            d_w = S.dsem()
            for g_ in gel_:
                S.op("dve", lambda e, g_=g_: e.memset(g_[:], 0.0), writes=[Rgel.res[gel_.index(g_)]])
            for kv in ("k", "v"):
                S.dma("pool", lambda e, kv=kv: e.dma_start(
                    out=w1b[:], in_=cmp_w1[kv].rearrange("(l d) j -> d l j", d=128)), d_w, writes=[r_w])
                S.dma("pool", lambda e, kv=kv: e.dma_start(out=peT[:], in_=cmp_peT[kv]), d_w, writes=[r_w])
                S.dma("pool", lambda e, kv=kv: e.dma_start(
                    out=w2b[:], in_=cmp_w2[kv].rearrange("(c p) d -> p c d", p=128)), d_w, writes=[r_w])
                for jc in range(2):
                    for l in range(32):
                        S.op("pe", lambda e, jc=jc, l=l: e.matmul(
                            pc[:, jc:jc + 1], lhsT=w1b[:, l, jc * 128:(jc + 1) * 128], rhs=peT[:, l:l + 1],
                            start=(l == 0), stop=(l == 31)), reads=[r_w], writes=[r_pc])
                S.op("act", lambda e: e.activation(out=cb[:], in_=pc[:], func=AF.Copy), reads=[r_pc], writes=[r_cb])
                src = KCT if kv == "k" else VCT
                for g in range(2):
                    xc, rxc, dxc = Rxc.next()
                    S.dma("sp", lambda e, xc=xc, g=g, src=src: e.dma_start(out=xc[:], in_=src[g]), dxc, writes=[rxc])
                    gel, rgel = Rgel.next()
                    for jc in range(2):
                        ph, rph = Rph.next()
                        for l in range(32):
                            S.op("pe", lambda e, ph=ph, xc=xc, jc=jc, l=l: e.matmul(
                                ph[:, 0:255], lhsT=w1b[:, l, jc * 128:(jc + 1) * 128],
                                rhs=xc[:, l:l + 16 * 254 + 1:16], start=(l == 0), stop=(l == 31)),
                                reads=[r_w, rxc], writes=[rph])
                        S.op("act", lambda e, ph=ph, gel=gel, jc=jc: e.activation(
                            out=gel[:, jc, 0:255], in_=ph[:, 0:255], func=AF.Gelu_apprx_tanh, bias=cb[:, jc:jc + 1]),
                            reads=[rph, r_cb], writes=[rgel])
                    if kv == "k":
                        po, rpo = Rpo.next()
                        for jc in range(2):
                            S.op("pe", lambda e, po=po, gel=gel, jc=jc: e.matmul(
                                po[:, :], lhsT=w2b[:, jc, :], rhs=gel[:, jc, :], start=(jc == 0), stop=(jc == 1)),
                                reads=[r_w, rgel], writes=[rpo])
                        og, rog, dog = Rog.next()
                        S.op("dve", lambda e, og=og, po=po: e.tensor_copy(out=og[:], in_=po[:]), reads=[rpo], writes=[rog])
                        S.dma("sp", lambda e, og=og, g=g: e.dma_start(out=KCMPT[g], in_=og[:]), dog, reads=[rog])
                    else:
                        for nk in range(2):
                            po, rpo = Rpo.next()
                            for jc in range(2):
                                S.op("pe", lambda e, po=po, gel=gel, jc=jc, nk=nk: e.matmul(
                                    po[:, 0:128], lhsT=gel[:, jc, nk * 128:(nk + 1) * 128], rhs=w2b[:, jc, :],
                                    start=(jc == 0), stop=(jc == 1)), reads=[r_w, rgel], writes=[rpo])
                            og, rog, dog = Rog.next()
                            S.op("dve", lambda e, og=og, po=po: e.tensor_copy(out=og[:, 0:128], in_=po[:, 0:128]),
                                 reads=[rpo], writes=[rog])
                            S.dma("sp", lambda e, og=og, g=g, nk=nk: e.dma_start(
                                out=VCMP[g, nk * 128:(nk + 1) * 128, :], in_=og[:, 0:128]), dog, reads=[rog])
            S.emit(block)

    def phase3b(sfx=""):
        with ExitStack() as st:
            sb = lambda name, shape, dt: st.enter_context(nc.sbuf_tensor(name + sfx, list(shape), dt))
            identb = sb("p3_idb", [128, 128], BF16)
            identf = sb("p3_idf", [128, 128], F32)
            onesb = sb("p3_ones", [128, 128], BF16)
            causb = sb("p3_caus", [128, 4, 128], BF16)
            acausb = sb("p3_acaus", [128, 4, 128], BF16)
            expt = sb("p3_expt", [64, LT, 128], BF16)
            ovb = sb("p3_ov", [128, 2, 65], BF16)
            padb = sb("p3_padb", [128, LT], F32)
            cmpb = sb("p3_cmpb", [128, 2], F32)
            kcm = sb("p3_kcm", [128, 2, 256], BF16)
            vcm = sb("p3_vcm", [128, 2, 2, 128], BF16)
            kst = sb("p3_kst", [128, 2, LT * 128], BF16)
            kwt = sb("p3_kwt", [128, 2, LT * 128], BF16)
            vs = sb("p3_vs", [128, LT, 256], BF16)
            vw = sb("p3_vw", [128, LT, 256], BF16)
            qT_ = [sb(f"p3_qT{i}", [128, 16, 128], BF16) for i in range(2)]
            gbc_ = [sb(f"p3_gbc{i}", [128, 24, 128], F32) for i in range(2)]
            keep_ = [sb(f"p3_keep{i}", [128, 64], F32) for i in range(2)]
            addt_ = [sb(f"p3_add{i}", [128, 64], F32) for i in range(2)]
            cm_ = [sb(f"p3_cm{i}", [128, 2, 128], F32) for i in range(2)]
            Ec_ = [sb(f"p3_Ec{i}", [128, 1024], BF16) for i in range(2)]
            E_ = [sb(f"p3_E{i}", [128, 1024], BF16) for i in range(3)]
            Sm_ = [sb(f"p3_Sm{i}", [128, 1024], F32) for i in range(2)]
            fac_ = [sb(f"p3_fac{i}", [128, 1024], F32) for i in range(2)]
            tmp_ = [sb(f"p3_tmp{i}", [128, 1024], F32) for i in range(2)]
            acc_ = [sb(f"p3_acc{i}", [128, 1024], F32) for i in range(2)]
            accb_ = [sb(f"p3_accb{i}", [128, 1024], BF16) for i in range(2)]
            oc_ = [sb(f"p3_oc{i}", [128, 1024], F32) for i in range(2)]
            Usb_ = [sb(f"p3_Usb{i}", [128, 2, 260], F32) for i in range(2)]
            rs8 = sb("p3_rs8", [128, 8], F32)
            imp = sb("p3_imp", [128, 64], F32)
            imp2 = sb("p3_imp2", [128, 64], F32)
            m8 = sb("p3_m8", [128, 8], F32)
            selb = sb("p3_selb", [128, 64], F32)
            selbT_ = [sb(f"p3_selbT{i}", [64, 4, 128], BF16) for i in range(2)]
            pS_ = [st.enter_context(nc.psum_tensor(f"p3_pS{i}" + sfx, [128, 1024], F32)) for i in range(2)]
            pO = st.enter_context(nc.psum_tensor("p3_pO" + sfx, [128, 1024], F32))
            pSum = st.enter_context(nc.psum_tensor("p3_pSum" + sfx, [128, 1024], F32))
            S = Sched(nc, st, "p3" + sfx)
            block = st.enter_context(nc.Block())
            r_c = Res()
            r_k = Res()
            d_cp, d_cs = S.dsem(), S.dsem()
            for dst, src in ((identb[:], t_ident), (expt[:], t_exp), (ovb[:], t_ov)):
                S.dma("pool", lambda e, dst=dst, src=src: e.dma_start(out=dst, in_=src), d_cp, writes=[r_c])
            for r4 in range(4):
                S.dma("pool", lambda e, r4=r4: e.dma_start(out=causb[:, r4, :], in_=t_caus), d_cp, writes=[r_c])
                S.dma("pool", lambda e, r4=r4: e.dma_start(out=acausb[:, r4, :], in_=t_acaus), d_cp, writes=[r_c])
            S.op("pool", lambda e: e.memset(onesb[:], 1.0), reads=[], writes=[r_c])
            for dst, src in ((identf[:], t_ident), (padb[:], t_padb), (cmpb[:], t_cmpb),
                             (kcm[:], KCMPT.rearrange("g d n -> d g n")),
                             (vcm[:], VCMP.rearrange("g (k p) d -> p g k d", p=128))):
                S.dma("sp", lambda e, dst=dst, src=src: e.dma_start(out=dst, in_=src), d_cs, writes=[r_k])
            r_big = {}
            for nm, dst, src in (("kwt", kwt[:], KWT.rearrange("g d t -> d g t")),
                                 ("vw", vw[:], VW.rearrange("(t p) c -> p t c", p=128)),
                                 ("kst", kst[:], KST.rearrange("g d t -> d g t")),
                                 ("vs", vs[:], VS.rearrange("(t p) c -> p t c", p=128))):
                r_big[nm] = Res()
                S.dma("sp", lambda e, dst=dst, src=src: e.dma_start(out=dst, in_=src), S.dsem(), writes=[r_big[nm]])
            RqT, Rgbc = Ring(qT_, S, True), Ring(gbc_, S, True)
            Rkeep, Radd, Rcm = Ring(keep_, S, True), Ring(addt_, S, True), Ring(cm_, S, True)
            REc, RE, RSm, Rfac, Rtmp, Racc = Ring(Ec_), Ring(E_), Ring(Sm_), Ring(fac_), Ring(tmp_), Ring(acc_)
            Raccb = Ring(accb_, S, True)
            RselbT = Ring(selbT_)
            Roc, RUsb = Ring(oc_), Ring(Usb_)
            RpS = Ring(pS_, excl=True)
            r_pO, r_pSum = Res(True), Res(True)
            r_small = Res()
            d_dbg = S.dsem()
            dbg_t = {}
            if "DBGSEL" in debug:
                dbg_t = {"DBGSEL": dscr("DBGSEL", [NQ, 2, 128, 64], F32), "DBGIMP": dscr("DBGIMP", [NQ, 2, 128, 64], F32),
                         "DBGM8": dscr("DBGM8", [NQ, 2, 128, 8], F32)}

            def finalize(gbc, rgbc, br, acc, racc, first):
                oc, roc = Roc.next()
                S.op("act", lambda e: e.activation(out=oc[:], in_=pO[:], func=AF.Copy), reads=[r_pO], writes=[roc])
                fac, rfac = Rfac.next()
                S.op("dve", lambda e: e.tensor_scalar_max(out=fac[:], in0=pSum[:], scalar1=1e-30),
                     reads=[r_pSum], writes=[rfac])
                S.op("dve", lambda e: e.reciprocal(out=fac[:], in_=fac[:]), reads=[rfac], writes=[rfac])
                S.op("pool", lambda e: e.tensor_tensor(
                    out=fac[:].rearrange("p (h q) -> p h q", h=8), in0=fac[:].rearrange("p (h q) -> p h q", h=8),
                    in1=gbc[:, br::3, :], op=ALU.mult), reads=[rfac, rgbc], writes=[rfac])
                if first:
                    S.op("dve", lambda e: e.tensor_tensor(out=acc[:], in0=oc[:], in1=fac[:], op=ALU.mult),
                         reads=[roc, rfac], writes=[racc])
                else:
                    tmp, rtmp = Rtmp.next()
                    S.op("dve", lambda e: e.tensor_tensor(out=tmp[:], in0=oc[:], in1=fac[:], op=ALU.mult),
                         reads=[roc, rfac], writes=[rtmp])
                    S.op("pool", lambda e: e.tensor_tensor(out=acc[:], in0=acc[:], in1=tmp[:], op=ALU.add),
                         reads=[rtmp, racc], writes=[racc])

            def attend(kt_list, ksrc, vsrc, g, qTg, rq, selbT, rselbT, i):
                n = len(kt_list)

                def scores(kt):
                    pSx, rpS = RpS.next()
                    for hf in range(2):
                        extra = []
                        if selbT is not None:
                            extra.append((expt[:, kt, :], selbT[:].rearrange("s r q -> s (r q)"), [r_c, rselbT]))
                        if kt == i:
                            extra.append((identb[:], causb[:].rearrange("k r q -> k (r q)"), [r_c]))
                        if selbT is None and kt == i - 4:
                            extra.append((identb[:], acausb[:].rearrange("k r q -> k (r q)"), [r_c]))
                        S.op("pe", lambda e, hf=hf, ne=len(extra): e.matmul(
                            pSx[:, hf * 512:(hf + 1) * 512], lhsT=ksrc[:, g, kt * 128:(kt + 1) * 128],
                            rhs=qTg[:, hf * 512:(hf + 1) * 512], start=True, stop=(ne == 0)),
                            reads=[r_big["kst" if selbT is not None else "kwt"], rq], writes=[rpS])
                        for xi, (l_, r_, deps) in enumerate(extra):
                            S.op("pe", lambda e, hf=hf, l_=l_, r_=r_, last=(xi == len(extra) - 1): e.matmul(
                                pSx[:, hf * 512:(hf + 1) * 512], lhsT=l_, rhs=r_, start=False, stop=last),
                                reads=deps, writes=[rpS])
                    E, rE = RE.next()
                    S.op("act", lambda e: e.activation(
                        out=E[:], in_=pSx[:], func=AF.Exp, scale=SCALE, bias=padb[:, kt:kt + 1]),
                        reads=[rpS, r_k], writes=[rE])
                    return E, rE

                def values(idx, kt, E, rE):
                    for hf in range(2):
                        S.op("pe", lambda e, hf=hf: e.matmul(
                            pO[:, hf * 512:(hf + 1) * 512], lhsT=vsrc[:, kt, g * 128:(g + 1) * 128],
                            rhs=E[:, hf * 512:(hf + 1) * 512], start=(idx == 0), stop=(idx == n - 1)),
                            reads=[r_big["vs" if selbT is not None else "vw"], rE], writes=[r_pO])
                        S.op("pe", lambda e, hf=hf: e.matmul(
                            pSum[:, hf * 512:(hf + 1) * 512], lhsT=onesb[:],
                            rhs=E[:, hf * 512:(hf + 1) * 512], start=(idx == 0), stop=(idx == n - 1)),
                            reads=[r_c, rE], writes=[r_pSum])

                pend = None
                for idx, kt in enumerate(kt_list):
                    cur = (idx, kt) + scores(kt)
                    if pend is not None:
                        values(*pend)
                    pend = cur
                values(*pend)

            for i in range(QT0, LT):
                qi = i - QT0
                qT, rq, dq = RqT.next()
                S.dma("sp", lambda e, qT=qT, qi=qi: e.dma_start(
                    out=qT[:], in_=NQT_[:, :, qi * 128:(qi + 1) * 128].rearrange("h p q -> p h q")), dq, writes=[rq])
                keep, rkeep, dkeep = Rkeep.next()
                addt, radd, dadd = Radd.next()
                cm, rcm, dcm = Rcm.next()
                S.dma("sp", lambda e, keep=keep, qi=qi: e.dma_start(out=keep[:], in_=t_keep[qi]), dkeep, writes=[rkeep])
                S.dma("sp", lambda e, addt=addt, qi=qi: e.dma_start(out=addt[:], in_=t_add[qi]), dadd, writes=[radd])
                for ck in range(2):
                    r0 = 128 * ck - 8 * i + 250
                    S.dma("sp", lambda e, cm=cm, ck=ck, r0=r0: e.dma_start(out=cm[:, ck, :], in_=t_cm[r0:r0 + 128, :]),
                          dcm, writes=[rcm])
                def do_group(i, qi, g, qT, rq, keep, rkeep, addt, radd, cm, rcm):
                    gbc, rgbc, dgbc = Rgbc.next()
                    S.dma("sp", lambda e, gbc=gbc, g=g, qi=qi: e.dma_start(
                        out=gbc[:], in_=NGT[g * 24:(g + 1) * 24, qi * 128:(qi + 1) * 128].unsqueeze(0).to_broadcast(
                            [128, 24, 128])), dgbc, writes=[rgbc])
                    qTg = qT[:, g * 8:(g + 1) * 8, :].rearrange("p h q -> p (h q)")
                    acc, racc = Racc.next()
                    Ecs = []
                    for ck in range(2):
                        pSx, rpS = RpS.next()
                        for hf in range(2):
                            S.op("pe", lambda e, pSx=pSx, hf=hf, ck=ck: e.matmul(
                                pSx[:, hf * 512:(hf + 1) * 512], lhsT=kcm[:, g, ck * 128:(ck + 1) * 128],
                                rhs=qTg[:, hf * 512:(hf + 1) * 512], start=True, stop=True),
                                reads=[r_k, rq], writes=[rpS])
                        Sm, rSm = RSm.next()
                        S.op("dve", lambda e, Sm=Sm, pSx=pSx, ck=ck: e.tensor_tensor(
                            out=Sm[:].rearrange("p (h q) -> p h q", h=8), in0=pSx[:].rearrange("p (h q) -> p h q", h=8),
                            in1=cm[:, ck, :][:, None, :].to_broadcast([128, 8, 128]), op=ALU.add),
                            reads=[rpS, rcm], writes=[rSm])
                        Ec, rEc = REc.next()
                        S.op("act", lambda e, Ec=Ec, Sm=Sm, ck=ck: e.activation(
                            out=Ec[:], in_=Sm[:], func=AF.Exp, scale=SCALE, bias=cmpb[:, ck:ck + 1]),
                            reads=[rSm, r_k], writes=[rEc])
                        Ecs.append((Ec, rEc))
                        for hf in range(2):
                            S.op("pe", lambda e, Ec=Ec, hf=hf, ck=ck: e.matmul(
                                pO[:, hf * 512:(hf + 1) * 512], lhsT=vcm[:, g, ck, :],
                                rhs=Ec[:, hf * 512:(hf + 1) * 512], start=(ck == 0), stop=(ck == 1)),
                                reads=[r_k, rEc], writes=[r_pO])
                            S.op("pe", lambda e, Ec=Ec, hf=hf, ck=ck: e.matmul(
                                pSum[:, hf * 512:(hf + 1) * 512], lhsT=onesb[:],
                                rhs=Ec[:, hf * 512:(hf + 1) * 512], start=(ck == 0), stop=(ck == 1)),
                                reads=[r_c, rEc], writes=[r_pSum])
                    pU, rpU = RpS.next()
                    for h in range(8):
                        for ck in range(2):
                            Ec, rEc = Ecs[ck]
                            o0 = (h // 4) * 512 + (h % 4) * 65
                            S.op("pe", lambda e, pU=pU, Ec=Ec, h=h, ck=ck, o0=o0: e.matmul(
                                pU[:, o0:o0 + 65], lhsT=Ec[:, h * 128:(h + 1) * 128], rhs=ovb[:, ck, :],
                                start=(ck == 0), stop=(ck == 1)), reads=[r_c, rEc], writes=[rpU])
                    finalize(gbc, rgbc, 0, acc, racc, True)
                    Usb, rUsb = RUsb.next()
                    S.op("act", lambda e, pU=pU, Usb=Usb: e.activation(
                        out=Usb[:], in_=pU[:].rearrange("p (b x) -> p b x", b=2)[:, :, 0:260], func=AF.Copy),
                        reads=[rpU], writes=[rUsb])
                    for hh in range(2):
                        S.op("dve", lambda e, Usb=Usb, hh=hh: e.tensor_scalar_max(
                            out=rs8[:, hh * 4:(hh + 1) * 4],
                            in0=Usb[:, hh, :].rearrange("p (h s) -> p h s", s=65)[:, :, 64],
                            scalar1=1e-30), reads=[rUsb], writes=[r_small])
                    S.op("dve", lambda e: e.reciprocal(out=rs8[:], in_=rs8[:]), reads=[r_small], writes=[r_small])
                    for h in range(8):
                        o0 = (h % 4) * 65
                        if h == 0:
                            S.op("dve", lambda e, Usb=Usb, o0=o0: e.tensor_scalar(
                                out=imp[:], in0=Usb[:, 0, o0:o0 + 64], scalar1=rs8[:, 0:1], scalar2=None, op0=ALU.mult),
                                reads=[rUsb, r_small], writes=[r_small])
                        else:
                            S.op("dve", lambda e, Usb=Usb, o0=o0, h=h: e.scalar_tensor_tensor(
                                out=imp[:], in0=Usb[:, h // 4, o0:o0 + 64], scalar=rs8[:, h:h + 1], in1=imp[:],
                                op0=ALU.mult, op1=ALU.add), reads=[rUsb, r_small], writes=[r_small])
                    S.op("dve", lambda e, keep=keep: e.tensor_tensor(out=imp[:], in0=imp[:], in1=keep[:], op=ALU.mult),
                         reads=[r_small, rkeep], writes=[r_small])
                    S.op("dve", lambda e, addt=addt: e.tensor_tensor(out=imp[:], in0=imp[:], in1=addt[:], op=ALU.add),
                         reads=[r_small, radd], writes=[r_small])
                    S.op("dve", lambda e: e.max(out=m8[:], in_=imp[:]), reads=[r_small], writes=[r_small])
                    S.op("dve", lambda e: e.match_replace(out=imp2[:], in_to_replace=m8[:], in_values=imp[:],
                                                          imm_value=-1e30), reads=[r_small], writes=[r_small])
                    S.op("dve", lambda e: e.max(out=m8[:], in_=imp2[:]), reads=[r_small], writes=[r_small])
                    S.op("dve", lambda e: e.tensor_scalar(out=selb[:], in0=imp[:], scalar1=m8[:, 7:8], scalar2=NEG,
                                                          op0=ALU.is_lt, op1=ALU.mult), reads=[r_small], writes=[r_small])
                    if "DBGSEL" in debug:
                        for nm_, src_ in (("DBGSEL", selb), ("DBGIMP", imp), ("DBGM8", m8)):
                            S.dma("sp", lambda e, nm_=nm_, src_=src_, qi=qi, g=g: e.dma_start(
                                out=dbg_t[nm_][qi, g], in_=src_[:]), d_dbg, reads=[r_small])
                    attend(list(range(i - 4, i + 1)), kwt, vw, g, qTg, rq, None, None, i)
                    finalize(gbc, rgbc, 2, acc, racc, False)
                    pT, rpT = RpS.next()
                    S.op("pe", lambda e, pT=pT: e.transpose(out=pT[0:64, 0:128], in_=selb[:, 0:64], identity=identf[:]),
                         reads=[r_small, r_k], writes=[rpT])
                    selbT, rselbT = RselbT.next()
                    S.op("act", lambda e, pT=pT, selbT=selbT: e.activation(
                        out=selbT[:], in_=pT[0:64, 0:128][:, None, :].to_broadcast([64, 4, 128]), func=AF.Copy),
                        reads=[rpT], writes=[rselbT])
                    attend(list(range(0, i + 1)), kst, vs, g, qTg, rq, selbT, rselbT, i)
                    finalize(gbc, rgbc, 1, acc, racc, False)
                    accb, raccb, daccb = Raccb.next()
                    S.op("act", lambda e, accb=accb, acc=acc: e.activation(out=accb[:], in_=acc[:], func=AF.Copy),
                         reads=[racc], writes=[raccb])
                    S.dma("sp", lambda e, accb=accb, g=g, qi=qi: e.dma_start(
                        out=NSAT[g * 8:(g + 1) * 8, :, qi * 128:(qi + 1) * 128].rearrange("h p q -> p h q"),
                        in_=accb[:].rearrange("p (h q) -> p h q", h=8)), daccb, reads=[raccb])

                for g in range(2):
                    do_group(i, qi, g, qT, rq, keep, rkeep, addt, radd, cm, rcm)
            S.emit(block)

    if 3 in phases:
        phase3a()
        phase3b()
    if 33 in phases:
        phase3b("x")


    QGROUPS = [(0, 128)] + [(128 + 512 * g, 512) for g in range(4)]

    def upproj(tag, srcT, w, gate0, addsrc, dst):
        with ExitStack() as st:
            sb = lambda name, shape, dt: st.enter_context(nc.sbuf_tensor(name, list(shape), dt))
            src = sb(f"{tag}_src", [128, 16, NQT], BF16)
            wb = [sb(f"{tag}_w{i}", [128, 16, 512], BF16) for i in range(2)]
            mg_ = [sb(f"{tag}_mg{i}", [128, NQT], BF16) for i in range(2)]
            m1_ = [sb(f"{tag}_m1{i}", [128, NQT], BF16) for i in range(2)]
            tf_ = [sb(f"{tag}_tf{i}", [128, 512], F32) for i in range(2)]
            o_ = [sb(f"{tag}_o{i}", [128, NQT], BF16) for i in range(2)]
            ps = [st.enter_context(nc.psum_tensor(f"{tag}_ps{i}", [128, 512], F32)) for i in range(4)]
            S = Sched(nc, st, tag)
            block = st.enter_context(nc.Block())
            r_srcq = [Res() for _ in range(4)]
            for q4 in range(4):
                S.dma("sp", lambda e, q4=q4: e.dma_start(out=src[:, q4 * 4:(q4 + 1) * 4, :],
                                                       in_=srcT[q4 * 4:(q4 + 1) * 4].rearrange("c p q -> p c q")),
                      S.dsem(), writes=[r_srcq[q4]])
            Rw, Rmg, Rm1, Ro = Ring(wb, S, True), Ring(mg_, S, True), Ring(m1_, S, True), Ring(o_, S, True)
            Rtf = Ring(tf_)
            Rps = Ring(ps, excl=True)
            wv = w.rearrange("(c p) n -> p c n", p=128)

            def chunk(wt, rw, ch, dc):
                mg, rmg, dmg = Rmg.next()
                S.dma("sp", lambda e: e.dma_start(out=mg[:], in_=MGT[gate0 + dc]), dmg, writes=[rmg])
                if addsrc is not None:
                    m1, rm1, dm1 = Rm1.next()
                    S.dma("sp", lambda e: e.dma_start(out=m1[:], in_=addsrc[dc]), dm1, writes=[rm1])
                o, ro, do = Ro.next()
                for (q0, nt) in QGROUPS:
                    p, rp = Rps.next()
                    for c in (range(16) if not os.environ.get("SKIPMM") else range(1)):
                        S.op("pe", lambda e, p=p, c=c, q0=q0, nt=nt: e.matmul(
                            p[:, 0:nt], lhsT=wt[:, c, ch * 128:(ch + 1) * 128], rhs=src[:, c, q0:q0 + nt],
                            start=(c == 0), stop=(c == 15)), reads=[rw, r_srcq[c // 4]], writes=[rp])
                    if addsrc is None:
                        S.op("dve", lambda e, p=p, q0=q0, nt=nt: e.tensor_tensor(
                            out=o[:, q0:q0 + nt], in0=p[:, 0:nt], in1=mg[:, q0:q0 + nt], op=ALU.mult),
                            reads=[rp, rmg], writes=[ro])
                    else:
                        tf, rtf = Rtf.next()
                        S.op("dve", lambda e, p=p, q0=q0, nt=nt, tf=tf: e.tensor_tensor(
                            out=tf[:, 0:nt], in0=p[:, 0:nt], in1=mg[:, q0:q0 + nt], op=ALU.mult),
                            reads=[rp, rmg], writes=[rtf])
                        S.op("pool", lambda e, q0=q0, nt=nt, tf=tf: e.tensor_tensor(
                            out=o[:, q0:q0 + nt], in0=tf[:, 0:nt], in1=m1[:, q0:q0 + nt], op=ALU.add),
                            reads=[rtf, rm1], writes=[ro])
                S.dma("sp", lambda e: e.dma_start(out=dst[dc], in_=o[:]), do, reads=[ro])

            def wblock(blk):
                wt, rw, dw = Rw.next()
                S.dma("pool", lambda e: e.dma_start(out=wt[:], in_=wv[:, :, blk * 512:(blk + 1) * 512]), dw, writes=[rw])
                for ch in range(4):
                    chunk(wt, rw, ch, blk * 4 + ch)

            for blk in range(4):
                wblock(blk)
            S.emit(block)

    def phase4b():
        with ExitStack() as st:
            sb = lambda name, shape, dt: st.enter_context(nc.sbuf_tensor(name, list(shape), dt))
            identb = sb("p4_id", [128, 128], BF16)
            wo = sb("p4_wo", [128, 16, D], BF16)
            w2bc = sb("p4_w2", [128, D], F32)
            mT_ = [sb(f"p4_mT{i}", [128, 16, 128], BF16) for i in range(2)]
            x_ = [sb(f"p4_x{i}", [128, D], F32) for i in range(2)]
            h_ = [sb(f"p4_h{i}", [128, D], F32) for i in range(2)]
            xn_ = [sb(f"p4_xn{i}", [128, D], BF16) for i in range(2)]
            xT_ = [sb(f"p4_xT{i}", [128, 16, 128], BF16) for i in range(2)]
            junk = sb("p4_junk", [128, D], BF16)
            ss_ = [sb(f"p4_ss{i}", [128, 1], F32) for i in range(2)]
            ps = [st.enter_context(nc.psum_tensor(f"p4_ps{i}", [128, 512], F32)) for i in range(4)]
            pt = [st.enter_context(nc.psum_tensor(f"p4_pt{i}", [128, 1024], BF16)) for i in range(2)]
            S = Sched(nc, st, "p4b")
            block = st.enter_context(nc.Block())
            r_id, r_wo, r_w2, r_junk = Res(), Res(), Res(), Res()
            d_p, d_s = S.dsem(), S.dsem()
            S.dma("pool", lambda e: e.dma_start(out=identb[:], in_=t_ident), d_p, writes=[r_id])
            wov = w_out.rearrange("(c p) n -> p c n", p=128)
            r_wob = [Res() for _ in range(4)]
            for blk in range(4):
                S.dma("pool", lambda e, blk=blk: e.dma_start(out=wo[:, :, blk * 512:(blk + 1) * 512],
                                                           in_=wov[:, :, blk * 512:(blk + 1) * 512]), S.dsem(),
                      writes=[r_wob[blk]])
            S.dma("sp", lambda e: e.dma_start(out=w2bc[:], in_=norm2_w.partition_broadcast(128)), d_s, writes=[r_w2])
            RmT, Rx, Rh, RxT = Ring(mT_, S, True), Ring(x_, S, True), Ring(h_, S, True), Ring(xT_, S, True)
            Rxn, Rss = Ring(xn_), Ring(ss_)
            Rps = Ring(ps, excl=True)
            r_pt = [Res(True), Res(True)]

            def tile(qi):
                mT, rmT, dmT = RmT.next()
                S.dma("sp", lambda e: e.dma_start(out=mT[:], in_=MRGT[:, :, qi * 128:(qi + 1) * 128].rearrange("c p q -> p c q")),
                      dmT, writes=[rmT])
                xt, rx, dx = Rx.next()
                S.dma("sp", lambda e: e.dma_start(out=xt[:], in_=x_loc[(QT0 + qi) * 128:(QT0 + qi + 1) * 128, :]), dx, writes=[rx])
                h, rh, dh = Rh.next()
                for cb in range(4):
                    p, rp = Rps.next()
                    for c in (range(16) if not os.environ.get("SKIPMM") else range(1)):
                        S.op("pe", lambda e, p=p, c=c, cb=cb: e.matmul(
                            p[:], lhsT=mT[:, c, :], rhs=wo[:, c, cb * 512:(cb + 1) * 512],
                            start=(c == 0), stop=(c == 15)), reads=[rmT, r_wob[cb]], writes=[rp])
                    S.op("dve", lambda e, p=p, cb=cb: e.tensor_tensor(
                        out=h[:, cb * 512:(cb + 1) * 512], in0=p[:], in1=xt[:, cb * 512:(cb + 1) * 512], op=ALU.add),
                        reads=[rp, rx], writes=[rh])
                S.dma("sp", lambda e: e.dma_start(out=H1[qi * 128:(qi + 1) * 128, :], in_=h[:]), dh, reads=[rh])
                ss, rss = Rss.next()
                S.op("act", lambda e: e.activation(out=junk[:], in_=h[:], func=AF.Square, accum_out=ss[:]),
                     reads=[rh], writes=[r_junk, rss])
                S.op("act", lambda e: e.activation(out=ss[:], in_=ss[:], func=AF.Sqrt, scale=1.0 / D, bias=EPS),
                     reads=[rss], writes=[rss])
                S.op("dve", lambda e: e.reciprocal(out=ss[:], in_=ss[:]), reads=[rss], writes=[rss])
                xn, rxn = Rxn.next()
                S.op("dve", lambda e: e.scalar_tensor_tensor(out=xn[:], in0=h[:], scalar=ss[:], in1=w2bc[:],
                                                             op0=ALU.mult, op1=ALU.mult),
                     reads=[rh, rss, r_w2], writes=[rxn])
                xT, rxT, dxT = RxT.next()
                for hh in range(2):
                    for cc in range(8):
                        c = hh * 8 + cc
                        S.op("pe", lambda e, c=c, cc=cc, hh=hh: e.transpose(
                            out=pt[hh][:, cc * 128:(cc + 1) * 128], in_=xn[:, c * 128:(c + 1) * 128], identity=identb[:]),
                            reads=[rxn, r_id], writes=[r_pt[hh]])
                    if hh == 0:
                        S.op("act", lambda e, hh=hh: e.activation(
                            out=xT[:, 0:8, :], in_=pt[0][:].rearrange("p (c q) -> p c q", c=8), func=AF.Copy),
                            reads=[r_pt[0]], writes=[rxT])
                    else:
                        S.op("dve", lambda e, hh=hh: e.tensor_copy(
                            out=xT[:, 8:16, :], in_=pt[1][:].rearrange("p (c q) -> p c q", c=8)),
                            reads=[r_pt[1]], writes=[rxT])
                S.dma("sp", lambda e: e.dma_start(
                    out=XN2T[:, :, qi * 128:(qi + 1) * 128].rearrange("c p q -> p c q"), in_=xT[:]), dxT, reads=[rxT])

            for qi in range(NQ):
                tile(qi)
            S.emit(block)

    if 4 in phases:
        upproj("p4a", RETGT, w_ret_up, 0, None, M1T)
        upproj("p4n", NSAT, w_nsa_up, 16, M1T, MRGT)
        phase4b()

    def phase5a():
        with ExitStack() as st:
            sb = lambda name, shape, dt: st.enter_context(nc.sbuf_tensor(name, list(shape), dt))
            xs = sb("p5_xs", [128, 16, NQT], BF16)
            wa_ = [sb(f"p5_wa{i}", [128, 16, 512], BF16) for i in range(2)]
            wb_ = [sb(f"p5_wb{i}", [128, 16, 512], BF16) for i in range(2)]
            cw = sb("p5_cw", [128, 88, 3], F32)
            cbias = sb("p5_cb", [128, 88], F32)
            halo = sb("p5_halo", [128, 1], F32)
            u_ = [sb(f"p5_u{i}", [128, 2050], F32) for i in range(4)]
            y_ = [sb(f"p5_y{i}", [128, 2048], F32) for i in range(3)]
            o_ = [sb(f"p5_o{i}", [128, 2048], BF16) for i in range(2)]
            ps = [st.enter_context(nc.psum_tensor(f"p5_ps{i}", [128, 512], F32)) for i in range(6)]
            ph = [st.enter_context(nc.psum_tensor(f"p5_ph{i}", [128, 2], F32)) for i in range(2)]
            S = Sched(nc, st, "p5a")
            block = st.enter_context(nc.Block())
            r_c = Res()
            d_c = S.dsem()
            S.dma("sp", lambda e: e.dma_start(out=xs[:], in_=XN2T.rearrange("c p q -> p c q")), d_c, writes=[r_c])
            S.dma("sp", lambda e: e.dma_start(out=cw[:], in_=conv_wT), d_c, writes=[r_c])
            S.dma("sp", lambda e: e.dma_start(out=cbias[:], in_=conv_bT), d_c, writes=[r_c])
            S.dma("sp", lambda e: e.dma_start(out=halo[:], in_=t_halo), d_c, writes=[r_c])
            Rwa, Rwb = Ring(wa_, S, True), Ring(wb_, S, True)
            Ru, Ry = Ring(u_), Ring(y_)
            Ro = Ring(o_, S, True)
            Rps, Rph = Ring(ps, excl=True), Ring(ph, excl=True)
            wv = w_ffn_up.rearrange("(c p) n -> p c n", p=128)

            def half(wt, rw, ch, cidx, ceng):
                u, ru = Ru.next()
                p, rp = Rph.next()
                for c in range(16):
                    S.op("pe", lambda e, c=c: e.matmul(p[:, 0:2], lhsT=wt[:, c, ch * 128:(ch + 1) * 128],
                                                       rhs=xs[:, c, 126:128], start=(c == 0), stop=(c == 15)),
                         reads=[rw, r_c], writes=[rp])
                S.op("act", lambda e: e.activation(out=u[:, 0:2], in_=p[:, 0:2], func=AF.Copy, scale=halo[:]),
                     reads=[rp, r_c], writes=[ru])
                for g in range(4):
                    pp, rpp = Rps.next()
                    for c in range(16):
                        S.op("pe", lambda e, c=c, pp=pp, g=g: e.matmul(
                            pp[:], lhsT=wt[:, c, ch * 128:(ch + 1) * 128],
                            rhs=xs[:, c, 128 + g * 512:128 + (g + 1) * 512], start=(c == 0), stop=(c == 15)),
                            reads=[rw, r_c], writes=[rpp])
                    S.op("act", lambda e, pp=pp, g=g: e.activation(out=u[:, 2 + g * 512:2 + (g + 1) * 512], in_=pp[:],
                                                                   func=AF.Copy), reads=[rpp], writes=[ru])
                y, ry = Ry.next()
                S.op(ceng, lambda e: e.tensor_scalar(out=y[:], in0=u[:, 2:2050], scalar1=cw[:, cidx, 2:3],
                                                     scalar2=cbias[:, cidx:cidx + 1], op0=ALU.mult, op1=ALU.add),
                     reads=[ru, r_c], writes=[ry])
                S.op(ceng, lambda e: e.scalar_tensor_tensor(out=y[:], in0=u[:, 1:2049], scalar=cw[:, cidx, 1:2], in1=y[:],
                                                            op0=ALU.mult, op1=ALU.add), reads=[ru, r_c, ry], writes=[ry])
                S.op(ceng, lambda e: e.scalar_tensor_tensor(out=y[:], in0=u[:, 0:2048], scalar=cw[:, cidx, 0:1], in1=y[:],
                                                            op0=ALU.mult, op1=ALU.add), reads=[ru, r_c, ry], writes=[ry])
                return y, ry

            def chunk(wa, rwa, wb, rwb, ch, j):
                ya, rya = half(wa, rwa, ch, j, "dve")
                yb, ryb = half(wb, rwb, ch, 44 + j, "dve")
                S.op("act", lambda e: e.activation(out=ya[:], in_=ya[:], func=AF.Silu), reads=[rya], writes=[rya])
                o, ro, do = Ro.next()
                S.op("pool", lambda e: e.tensor_tensor(out=o[:], in0=ya[:], in1=yb[:], op=ALU.mult),
                     reads=[rya, ryb], writes=[ro])
                S.dma("sp", lambda e: e.dma_start(out=ACTT[:, :, j, :].rearrange("t p q -> p t q"),
                                                  in_=o[:].rearrange("p (t q) -> p t q", q=128)), do, reads=[ro])

            def wblock(jb):
                wa, rwa, dwa = Rwa.next()
                wb, rwb, dwb = Rwb.next()
                S.dma("pool", lambda e: e.dma_start(out=wa[:], in_=wv[:, :, jb * 512:(jb + 1) * 512]), dwa, writes=[rwa])
                S.dma("pool", lambda e: e.dma_start(out=wb[:], in_=wv[:, :, DFF + jb * 512:DFF + (jb + 1) * 512]), dwb,
                      writes=[rwb])
                for ch in range(4):
                    chunk(wa, rwa, wb, rwb, ch, jb * 4 + ch)

            for jb in range(11):
                wblock(jb)
            S.emit(block)

    def phase5b():
        with ExitStack() as st:
            sb = lambda name, shape, dt: st.enter_context(nc.sbuf_tensor(name, list(shape), dt))
            wd_ = [sb(f"p5b_w{i}", [128, 44, 512], BF16) for i in range(2)]
            a_ = [sb(f"p5b_a{i}", [128, 44, 128], BF16) for i in range(3)]
            h_ = [sb(f"p5b_h{i}", [128, 512], F32) for i in range(3)]
            ps = [st.enter_context(nc.psum_tensor(f"p5b_ps{i}", [128, 512], F32)) for i in range(4)]
            S = Sched(nc, st, "p5b")
            block = st.enter_context(nc.Block())
            Rw, Ra, Rh = Ring(wd_, S, True), Ring(a_, S, True), Ring(h_, S, True)
            Rwq = Ring(list(range(8)), S, True)
            r_hs = [Res() for _ in h_]
            d_hs = [S.dsem() for _ in h_]
            Rps = Ring(ps, excl=True)
            wv = w_ffn_down.rearrange("(j p) n -> p j n", p=128)

            def tile(wt, rw, cb, t):
                a, ra, da = Ra.next()
                S.dma("sp", lambda e: e.dma_start(out=a[:], in_=ACTT[t]), da, writes=[ra])
                h, rh, dh = Rh.next()
                k = Rh.i
                S.dma("sp", lambda e: e.dma_start(out=h[:], in_=H1[(t + 1) * 128:(t + 2) * 128, cb * 512:(cb + 1) * 512]),
                      dh, writes=[rh])
                p, rp = Rps.next()
                for j in (range(44) if not os.environ.get("SKIPMM") else range(1)):
                    S.op("pe", lambda e, j=j: e.matmul(p[:], lhsT=a[:, j, :], rhs=wt[:, j, :], start=(j == 0), stop=(j == 43)),
                         reads=[ra, rw[j // 11]], writes=[rp])
                S.op("dve", lambda e: e.tensor_tensor(out=h[:], in0=p[:], in1=h[:], op=ALU.add), reads=[rp, rh], writes=[rh])
                S.dma("sp", lambda e: e.dma_start(out=H2[t * 128:(t + 1) * 128, cb * 512:(cb + 1) * 512], in_=h[:]),
                      d_hs[k], reads=[rh], writes=[r_hs[k]])

            def wblock(cb):
                wt, rw, dw = Rw.next()
                rws = Rwq.res[4 * Rw.i:4 * Rw.i + 4]
                dws = Rwq.ds[4 * Rw.i:4 * Rw.i + 4]
                for q4 in range(4):
                    S.dma("pool", lambda e, q4=q4: e.dma_start(out=wt[:, q4 * 11:(q4 + 1) * 11, :],
                                                             in_=wv[:, q4 * 11:(q4 + 1) * 11, cb * 512:(cb + 1) * 512]),
                          dws[q4], writes=[rws[q4]])
                for t in range(16):
                    tile(wt, rws, cb, t)

            for cb in range(4):
                wblock(cb)
            S.emit(block)

    def phase5c():
        with ExitStack() as st:
            sb = lambda name, shape, dt: st.enter_context(nc.sbuf_tensor(name, list(shape), dt))
            wf = sb("p5c_wf", [128, D], F32)
            h_ = [sb(f"p5c_h{i}", [128, D], F32) for i in range(2)]
            o_ = [sb(f"p5c_o{i}", [128, D], F32) for i in range(2)]
            junk = sb("p5c_junk", [128, D], BF16)
            ss_ = [sb(f"p5c_ss{i}", [128, 1], F32) for i in range(2)]
            S = Sched(nc, st, "p5c")
            block = st.enter_context(nc.Block())
            r_w, r_junk = Res(), Res()
            S.dma("sp", lambda e: e.dma_start(out=wf[:], in_=final_norm_w.partition_broadcast(128)), S.dsem(), writes=[r_w])
            Rh, Ro, Rss = Ring(h_, S, True), Ring(o_, S, True), Ring(ss_)

            def tile(t):
                h, rh, dh = Rh.next()
                S.dma("sp", lambda e: e.dma_start(out=h[:], in_=H2[t * 128:(t + 1) * 128, :]), dh, writes=[rh])
                ss, rss = Rss.next()
                S.op("act", lambda e: e.activation(out=junk[:], in_=h[:], func=AF.Square, accum_out=ss[:]),
                     reads=[rh], writes=[r_junk, rss])
                S.op("act", lambda e: e.activation(out=ss[:], in_=ss[:], func=AF.Sqrt, scale=1.0 / D, bias=EPS),
                     reads=[rss], writes=[rss])
                S.op("dve", lambda e: e.reciprocal(out=ss[:], in_=ss[:]), reads=[rss], writes=[rss])
                o, ro, do = Ro.next()
                S.op("dve", lambda e: e.scalar_tensor_tensor(out=o[:], in0=h[:], scalar=ss[:], in1=wf[:],
                                                             op0=ALU.mult, op1=ALU.mult), reads=[rh, rss, r_w], writes=[ro])
                S.dma("sp", lambda e: e.dma_start(out=out[t * 128:(t + 1) * 128, :], in_=o[:]), do, reads=[ro])

            for t in range(16):
                tile(t)
            S.emit(block)

    if 5 in phases:
        phase5a()
        phase5b()
        phase5c()

    return nc


def _tables(s):
    pad = 2048 if s == 0 else 0
    tl = np.arange(LT * 128)
    act = np.maximum(tl - pad, 0).astype(np.float64)
    is_pad = tl < pad
    half = 64
    freq = 10000.0 ** (-np.arange(half, dtype=np.float64) / half)
    ang = act[:, None] * freq[None, :]
    cs = np.concatenate([np.cos(ang), np.sin(ang)], axis=1)
    t_cs = cs.reshape(LT, 128, 128).transpose(1, 0, 2).astype(np.float32)
    t_csq = (cs[QT0 * 128:] * (128 ** -0.5)).reshape(NQ, 128, 128).transpose(1, 0, 2).astype(np.float32)
    gam = 1.0 - 2.0 ** (-5.0 - np.arange(8, dtype=np.float64))
    j = np.arange(128, dtype=np.float64)
    rel = j[None, :] - j[:, None]
    decT = np.where(rel[:, None, :] >= 0, gam[None, :, None] ** np.maximum(rel[:, None, :], 0), 0.0)
    qdec = np.broadcast_to((gam[:, None] ** (j[None, :] + 1.0))[None], (128, 8, 128))
    kdec = gam[None, :] ** (127.0 - j[:, None])
    padb = np.where(is_pad, NEG, 0.0).reshape(LT, 128).T
    n = np.arange(256)
    cmp_invalid = (n * 16 < pad) | (n >= 255)
    cmpb = np.where(cmp_invalid, NEG, 0.0).reshape(2, 128).T
    r = np.arange(512)[:, None]
    q = np.arange(128)[None, :]
    cm = np.where(16 * (r - 250) + 31 <= q, 0.0, NEG)
    keep = np.ones((NQ, 128, 64))
    add = np.zeros((NQ, 128, 64))
    blk = np.arange(64)[None, :]
    b0 = pad // 64
    for qi in range(NQ):
        t = (QT0 + qi) * 128 + np.arange(128)
        cur = (t // 64)[:, None]
        forced = (blk == b0) | (blk == cur) | (blk == cur - 1)
        neg = (blk > cur) | (blk < b0)
        keep[qi] = np.where(forced | neg, 0.0, 1.0)
        add[qi] = np.where(neg, -1e4, np.where(forced, 1e4, 0.0))
    sidx = np.arange(64)[:, None, None]
    kt = np.arange(LT)[None, :, None]
    kk = np.arange(128)[None, None, :]
    texp = (sidx == 2 * kt + (kk >= 64)).astype(np.float32)
    k_ = np.arange(128)[:, None]
    caus = np.where(k_ > q, NEG, 0.0)
    acaus = np.where(k_ <= q, NEG, 0.0)
    nn = np.arange(256)[:, None]
    ss = np.arange(64)[None, :]
    ov = ((nn * 16 < ss * 64 + 64) & (nn * 16 + 32 > ss * 64)).astype(np.float64)
    ov = np.concatenate([ov, np.ones((256, 1))], axis=1)
    ov[255] = 0.0
    t_ov = ov.reshape(2, 128, 65).transpose(1, 0, 2)
    f = lambda a: np.ascontiguousarray(a, dtype=np.float32)
    return {
        "t_cs": f(t_cs), "t_csq": f(t_csq), "t_decT": f(decT), "t_qdec": f(qdec), "t_kdec": f(kdec),
        "t_padb": f(padb), "t_cmpb": f(cmpb), "t_cm": f(cm), "t_keep": f(keep), "t_add": f(add),
        "t_exp": f(texp), "t_caus": f(caus), "t_acaus": f(acaus), "t_ident": f(np.eye(128)),
        "t_ov": f(t_ov), "t_halo": f(np.full((128, 1), float(s))),
    }


def make_in_maps(inputs):
    g = lambda k: np.asarray(inputs[k], dtype=np.float32)
    x = g("x")
    shared = {
        "norm1_w": g("norm1_w")[0][None], "w_in": g("w_in")[0], "ret_norm_w": g("ret_norm_w")[0][None],
        "w_ret_up": g("w_ret_up")[0],
        "cmp_peT_k": np.ascontiguousarray(g("cmp_pe_k")[0].T), "cmp_peT_v": np.ascontiguousarray(g("cmp_pe_v")[0].T),
        "cmp_w1_k": g("cmp_w1_k")[0], "cmp_w1_v": g("cmp_w1_v")[0],
        "cmp_w2_k": g("cmp_w2_k")[0], "cmp_w2_v": g("cmp_w2_v")[0],
        "w_nsa_up": g("w_nsa_up")[0], "w_out": g("w_out")[0], "norm2_w": g("norm2_w")[0][None],
        "w_ffn_up": g("w_ffn_up")[0],
        "conv_wT": np.ascontiguousarray(g("conv_w")[0].reshape(3, 88, 128).transpose(2, 1, 0)),
        "conv_bT": np.ascontiguousarray(g("conv_b")[0].reshape(88, 128).T),
        "w_ffn_down": g("w_ffn_down")[0], "final_norm_w": g("final_norm_w")[None],
    }
    tabs = [_tables(0), _tables(1)]
    maps = []
    for c in range(8):
        b, s = c // 2, c % 2
        if s == 1:
            xl = np.ascontiguousarray(x[b])
        else:
            xl = np.concatenate([np.zeros((2048, D), np.float32), x[b, :2048]], axis=0)
        m = dict(shared)
        m.update(tabs[s])
        m["x_loc"] = xl
        maps.append(m)
    return maps


_NC = None


def kernel(**inputs):
    global _NC
    if _NC is None:
        _NC = build_program()
    maps = make_in_maps(inputs)
    res = run_bass_kernel_spmd(_NC, maps, core_ids=list(range(8)))
    outp = np.zeros((4, 4096, D), np.float32)
    for c in range(8):
        b, s = c // 2, c % 2
        outp[b, s * 2048:(s + 1) * 2048] = res.results[c]["out"]
    return outp
```

```python
import math
import os
from contextlib import ExitStack
import numpy as np
import concourse.bass as bass
import concourse.mybir as mybir
from concourse.bass_utils import run_bass_kernel_spmd

F32 = mybir.dt.float32
BF16 = mybir.dt.bfloat16
AF = mybir.ActivationFunctionType
ALU = mybir.AluOpType
AX = mybir.AxisListType

D = 2048
LT = 32
QT0 = 15
NQ = LT - QT0
NQT = NQ * 128
IN_COLS = 13872
DFF = 5632
EPS = 1e-6
NEG = -30000.0
SCALE = 128 ** -0.5


class Tok:
    __slots__ = ("sem", "val")

    def __init__(self, sem, val):
        self.sem = sem
        self.val = val


class Res:
    __slots__ = ("w", "r", "excl")

    def __init__(self, excl=False):
        self.w = None
        self.r = []
        self.excl = excl


class DSem:
    __slots__ = ("sem", "val", "eng")

    def __init__(self, sem):
        self.sem = sem
        self.val = 0
        self.eng = None


ENGS = ("pe", "act", "dve", "pool", "sp")


class Sched:
    def __init__(self, nc, stack, tag):
        self.nc = nc
        self.q = {e: [] for e in ENGS}
        self.cnt = {e: 0 for e in ENGS}
        self.allsems = []
        self.sem = {e: self._alloc(f"{tag}_s_{e}") for e in ENGS}
        self.waited = {e: {} for e in ENGS}
        self.dsems = []
        self.tag = tag
        stack.callback(self._cleanup)

    def _alloc(self, name):
        h = self.nc.alloc_semaphore(name=name)
        self.allsems.append(h)
        return h

    def _cleanup(self):
        self.nc.clear_and_free_semaphores(self.allsems)
        self.nc.all_engine_barrier()

    def dsem(self):
        d = DSem(self._alloc(f"{self.tag}_d{len(self.dsems)}"))
        self.dsems.append(d)
        return d

    def _deps(self, eng, reads, writes):
        need = {}

        def add(t):
            k = id(t.sem)
            if k not in need or need[k].val < t.val:
                need[k] = t

        for r in reads:
            if r.w is not None:
                add(r.w)
        for w in writes:
            if w.w is not None:
                add(w.w)
            for t in w.r:
                add(t)
        waits = []
        wd = self.waited[eng]
        own = id(self.sem[eng])
        for k, t in need.items():
            if eng == "pe" and k == own:
                continue
            if wd.get(k, 0) < t.val:
                wd[k] = t.val
                waits.append((t.sem, t.val))
        return waits

    def _fin(self, tok, reads, writes):
        for r in reads:
            r.r.append(tok)
        for w in writes:
            w.w = tok
            w.r = []
        return tok

    def op(self, eng, fn, reads=(), writes=()):
        ex = [r for r in reads if r.excl]
        if ex:
            reads = [r for r in reads if not r.excl]
            writes = list(writes) + ex
        waits = self._deps(eng, reads, writes)
        self.cnt[eng] += 1
        self.q[eng].append((waits, fn, (self.sem[eng], 1)))
        return self._fin(Tok(self.sem[eng], self.cnt[eng]), reads, writes)

    def dma(self, eng, fn, ds, reads=(), writes=()):
        assert ds.eng in (None, eng), "one issuing engine per DMA semaphore"
        ds.eng = eng
        waits = self._deps(eng, reads, writes)
        ds.val += 16
        self.q[eng].append((waits, fn, (ds.sem, 16)))
        return self._fin(Tok(ds.sem, ds.val), reads, writes)

    def emit(self, block):
        finals = [(d.sem, d.val) for d in self.dsems if d.val > 0]

        def run(engobj, name, tail=False):
            for waits, fn, inc in self.q[name]:
                for s, v in waits:
                    engobj.wait_ge(s, v)
                fn(engobj).then_inc(inc[0], inc[1])
            if tail:
                for s, v in finals:
                    engobj.wait_ge(s, v)
                for e in ENGS:
                    if e != name and self.cnt[e] > 0:
                        engobj.wait_ge(self.sem[e], self.cnt[e])

        @block.sync
        def _(eng):
            run(eng, "sp", tail=True)

        @block.tensor
        def _(eng):
            run(eng, "pe")

        @block.scalar
        def _(eng):
            run(eng, "act")

        @block.vector
        def _(eng):
            run(eng, "dve")

        @block.gpsimd
        def _(eng):
            run(eng, "pool")


class Ring:
    def __init__(self, bufs, S=None, with_dsem=False, excl=False):
        self.bufs = bufs
        self.res = [Res(excl) for _ in bufs]
        self.ds = [S.dsem() for _ in bufs] if with_dsem else None
        self.i = -1

    def next(self):
        self.i = (self.i + 1) % len(self.bufs)
        if self.ds is not None:
            return self.bufs[self.i], self.res[self.i], self.ds[self.i]
        return self.bufs[self.i], self.res[self.i]


ALL_PHASES = (1, 2, 3, 4, 5)


def build_program(debug=(), phases=ALL_PHASES):
    nc = bass.Bass("TRN2", target_bir_lowering=False)

    def din(name, shape, dt=F32):
        return nc.dram_tensor(name, list(shape), dt, kind="ExternalInput").ap()

    def dscr(name, shape, dt=BF16):
        kind = "ExternalOutput" if name in debug else "Internal"
        return nc.dram_tensor(name, list(shape), dt, kind=kind).ap()

    x_loc = din("x_loc", [LT * 128, D])
    norm1_w = din("norm1_w", [1, D])
    w_in = din("w_in", [D, IN_COLS])
    ret_norm_w = din("ret_norm_w", [1, D])
    w_ret_up = din("w_ret_up", [D, D])
    cmp_peT = {"k": din("cmp_peT_k", [128, 32]), "v": din("cmp_peT_v", [128, 32])}
    cmp_w1 = {"k": din("cmp_w1_k", [4096, 256]), "v": din("cmp_w1_v", [4096, 256])}
    cmp_w2 = {"k": din("cmp_w2_k", [256, 128]), "v": din("cmp_w2_v", [256, 128])}
    w_nsa_up = din("w_nsa_up", [D, D])
    w_out = din("w_out", [D, D])
    norm2_w = din("norm2_w", [1, D])
    w_ffn_up = din("w_ffn_up", [D, 2 * DFF])
    conv_wT = din("conv_wT", [128, 88, 3])
    conv_bT = din("conv_bT", [128, 88])
    w_ffn_down = din("w_ffn_down", [DFF, D])
    final_norm_w = din("final_norm_w", [1, D])
    t_cs = din("t_cs", [128, LT, 128])
    t_csq = din("t_csq", [128, NQ, 128])
    t_decT = din("t_decT", [128, 8, 128])
    t_qdec = din("t_qdec", [128, 8, 128])
    t_kdec = din("t_kdec", [128, 8])
    t_padb = din("t_padb", [128, LT])
    t_cmpb = din("t_cmpb", [128, 2])
    t_cm = din("t_cm", [512, 128])
    t_keep = din("t_keep", [NQ, 128, 64])
    t_add = din("t_add", [NQ, 128, 64])
    t_exp = din("t_exp", [64, LT, 128])
    t_caus = din("t_caus", [128, 128])
    t_acaus = din("t_acaus", [128, 128])
    t_ident = din("t_ident", [128, 128])
    t_ov = din("t_ov", [128, 2, 65])
    t_halo = din("t_halo", [128, 1])

    out = nc.dram_tensor("out", [16 * 128, D], F32, kind="ExternalOutput").ap()

    RK = dscr("RK", [LT * 128, 1024])
    RV = dscr("RV", [LT * 128, 2048])
    RQ = dscr("RQ", [NQT, 1024])
    RG = dscr("RG", [NQT, 2048])
    NQT_ = dscr("NQT", [16, 128, NQT])
    KCT = dscr("KCT", [2, 128, LT * 128])
    VCT = dscr("VCT", [2, 128, LT * 128])
    KST = dscr("KST", [2, 128, LT * 128])
    KWT = dscr("KWT", [2, 128, LT * 128])
    VS = dscr("VS", [LT * 128, 256])
    VW = dscr("VW", [LT * 128, 256])
    NGT = dscr("NGT", [48, NQT], F32)
    MGT = dscr("MGT", [32, 128, NQT])
    RETGT = dscr("RETGT", [16, 128, NQT])
    NSAT = dscr("NSAT", [16, 128, NQT])
    M1T = dscr("M1T", [16, 128, NQT])
    MRGT = dscr("MRGT", [16, 128, NQT])
    H1 = dscr("H1", [NQT, D], F32)
    XN2T = dscr("XN2T", [16, 128, NQT])
    ACTT = dscr("ACTT", [16, 128, 44, 128])
    H2 = dscr("H2", [2048, D], F32)

    w_in_v = w_in.rearrange("(c p) n -> p c n", p=128)

    def phase01():
        with ExitStack() as st01:
            xnT = st01.enter_context(nc.sbuf_tensor("xnT", [128, 16, LT * 128], BF16))
            identb = st01.enter_context(nc.sbuf_tensor("identb", [128, 128], BF16))

            with ExitStack() as st:
                xin = [st.enter_context(nc.sbuf_tensor(f"p0_x{i}", [128, D], F32)) for i in range(2)]
                sq = st.enter_context(nc.sbuf_tensor("p0_sq", [128, D], BF16))
                xnb = [st.enter_context(nc.sbuf_tensor(f"p0_xn{i}", [128, D], BF16)) for i in range(2)]
                wbc = st.enter_context(nc.sbuf_tensor("p0_wbc", [128, D], F32))
                ss = [st.enter_context(nc.sbuf_tensor(f"p0_ss{i}", [128, 1], F32)) for i in range(2)]
                rs = [st.enter_context(nc.sbuf_tensor(f"p0_rs{i}", [128, 1], F32)) for i in range(2)]
                pt = [st.enter_context(nc.psum_tensor(f"p0_pt{i}", [128, D], BF16)) for i in range(2)]
                S = Sched(nc, st, "p0")
                block = st.enter_context(nc.Block())
                r_c = Res()
                d_c = S.dsem()
                S.dma("sp", lambda e: e.dma_start(out=wbc[:], in_=norm1_w.partition_broadcast(128)), d_c, writes=[r_c])
                r_id = Res()
                S.dma("pool", lambda e: e.dma_start(out=identb[:], in_=t_ident), S.dsem(), writes=[r_id])
                Rx = Ring(xin, S, True)
                Rxn = Ring(xnb)
                Rss = Ring(ss)
                Rrs = Ring(rs)
                Rpt = Ring(pt, excl=True)
                r_sq = Res()
                r_xnT = Res()
                for t in range(LT):
                    xb, rx, dx = Rx.next()
                    S.dma("sp", lambda e, xb=xb, t=t: e.dma_start(out=xb[:], in_=x_loc[t * 128:(t + 1) * 128, :]),
                          dx, writes=[rx])
                    sb, rss = Rss.next()
                    S.op("act", lambda e, xb=xb, sb=sb: e.activation(out=sq[:], in_=xb[:], func=AF.Square,
                                                                      accum_out=sb[:]),
                         reads=[rx], writes=[r_sq, rss])
                    rb, rrs = Rrs.next()
                    S.op("act", lambda e, sb=sb, rb=rb: e.activation(out=rb[:], in_=sb[:], func=AF.Sqrt,
                                                                     scale=1.0 / D, bias=EPS),
                         reads=[rss], writes=[rrs])
                    S.op("dve", lambda e, rb=rb: e.reciprocal(out=rb[:], in_=rb[:]),
                         reads=[rrs], writes=[rrs])
                    xn, rxn = Rxn.next()
                    S.op("dve", lambda e, xn=xn, xb=xb, rb=rb: e.scalar_tensor_tensor(
                        out=xn[:], in0=xb[:], scalar=rb[:], in1=wbc[:], op0=ALU.mult, op1=ALU.mult),
                        reads=[rx, rrs, r_c], writes=[rxn])
                    pb, rp = Rpt.next()
                    for c in range(16):
                        S.op("pe", lambda e, pb=pb, xn=xn, c=c: e.transpose(
                            out=pb[:, c * 128:(c + 1) * 128], in_=xn[:, c * 128:(c + 1) * 128], identity=identb[:]),
                            reads=[rxn, r_id], writes=[rp])
                    for hh in range(2):
                        eng = "act" if hh == 0 else "dve"
                        if eng == "act":
                            S.op("act", lambda e, pb=pb, t=t, hh=hh: e.activation(
                                out=xnT[:, hh * 8:(hh + 1) * 8, t * 128:(t + 1) * 128],
                                in_=pb[:, hh * 1024:(hh + 1) * 1024].rearrange("p (c q) -> p c q", c=8),
                                func=AF.Copy), reads=[rp], writes=[r_xnT])
                        else:
                            S.op("dve", lambda e, pb=pb, t=t, hh=hh: e.tensor_copy(
                                out=xnT[:, hh * 8:(hh + 1) * 8, t * 128:(t + 1) * 128],
                                in_=pb[:, hh * 1024:(hh + 1) * 1024].rearrange("p (c q) -> p c q", c=8)),
                                reads=[rp], writes=[r_xnT])
                S.emit(block)

            if 0 in phases and 1 not in phases:
                return

            with ExitStack() as st:
                wb = [st.enter_context(nc.sbuf_tensor(f"p1_w{i}", [128, 16, 512], BF16)) for i in range(2)]
                cs = st.enter_context(nc.sbuf_tensor("p1_cs", [128, LT, 128], F32))
                csq = st.enter_context(nc.sbuf_tensor("p1_csq", [128, NQ, 128], F32))
                xf = [st.enter_context(nc.sbuf_tensor(f"p1_xf{i}", [128, 512], F32)) for i in range(2)]
                tmp = [st.enter_context(nc.sbuf_tensor(f"p1_t{i}", [128, 4, 256], F32)) for i in range(2)]
                ob = [st.enter_context(nc.sbuf_tensor(f"p1_o{i}", [128, 512], BF16)) for i in range(3)]
                of = [st.enter_context(nc.sbuf_tensor(f"p1_of{i}", [48, 512], F32)) for i in range(2)]
                ps = [st.enter_context(nc.psum_tensor(f"p1_ps{i}", [128, 512], F32)) for i in range(4)]
                S = Sched(nc, st, "p1")
                block = st.enter_context(nc.Block())
                r_tab = Res()
                d_tab = S.dsem()
                S.dma("sp", lambda e: e.dma_start(out=cs[:], in_=t_cs), d_tab, writes=[r_tab])
                S.dma("sp", lambda e: e.dma_start(out=csq[:], in_=t_csq), d_tab, writes=[r_tab])
                Rw = Ring(wb, S, True)
                Rps = Ring(ps, excl=True)
                Rxf = Ring(xf)
                Rtmp = Ring(tmp)
                Rob = Ring(ob, S, True)
                Rof = Ring(of, S, True)
                r_x = Res()
                alt = [0]

                def load_w(col0, ncols):
                    w, rw, dw = Rw.next()
                    S.dma("pool", lambda e, w=w: e.dma_start(out=w[:, :, 0:ncols], in_=w_in_v[:, :, col0:col0 + ncols]),
                          dw, writes=[rw])
                    return w, rw

                def mm_tm(w, rw, t, c0, n):
                    p, rp = Rps.next()
                    for c in range(16):
                        S.op("pe", lambda e, p=p, c=c: e.matmul(
                            p[:, 0:n], lhsT=xnT[:, c, t * 128:(t + 1) * 128], rhs=w[:, c, c0:c0 + n],
                            start=(c == 0), stop=(c == 15)), reads=[rw, r_x], writes=[rp])
                    return p, rp

                def mm_fm(w, rw, tok0, ntok, c0, m):
                    p, rp = Rps.next()
                    for c in range(16):
                        S.op("pe", lambda e, p=p, c=c: e.matmul(
                            p[0:m, 0:ntok], lhsT=w[:, c, c0:c0 + m], rhs=xnT[:, c, tok0:tok0 + ntok],
                            start=(c == 0), stop=(c == 15)), reads=[rw, r_x], writes=[rp])
                    return p, rp

                def evac_store(p, rp, m, n, dst, func=None):
                    o, ro, do = Rob.next()
                    if func is not None:
                        S.op("act", lambda e: e.activation(out=o[0:m, 0:n], in_=p[0:m, 0:n], func=func),
                             reads=[rp], writes=[ro])
                    else:
                        alt[0] ^= 1
                        if alt[0]:
                            S.op("act", lambda e: e.activation(out=o[0:m, 0:n], in_=p[0:m, 0:n], func=AF.Copy),
                                 reads=[rp], writes=[ro])
                        else:
                            S.op("dve", lambda e: e.tensor_copy(out=o[0:m, 0:n], in_=p[0:m, 0:n]),
                                 reads=[rp], writes=[ro])
                    S.dma("sp", lambda e: e.dma_start(out=dst, in_=o[0:m, 0:n]), do, reads=[ro])

                def rotary(p, rp, ctab, ti, dst):
                    x_, rxf = Rxf.next()
                    S.op("act", lambda e: e.activation(out=x_[:], in_=p[:], func=AF.Copy), reads=[rp], writes=[rxf])
                    xv = x_[:].rearrange("p (h t d) -> p h t d", h=4, t=2)
                    cosb = ctab[:, ti, 0:64][:, None, :].to_broadcast([128, 4, 64])
                    sinb = ctab[:, ti, 64:128][:, None, :].to_broadcast([128, 4, 64])
                    tm_, rt = Rtmp.next()
                    o, ro, do = Rob.next()
                    ov = o[:].rearrange("p (h t d) -> p h t d", h=4, t=2)
                    x1 = xv[:, :, 0, :]
                    x2 = xv[:, :, 1, :]
                    S.op("pool", lambda e: e.tensor_tensor(out=tm_[:, :, 0:64], in0=x1, in1=cosb, op=ALU.mult),
                         reads=[rxf, r_tab], writes=[rt])
                    S.op("pool", lambda e: e.tensor_tensor(out=tm_[:, :, 64:128], in0=x2, in1=sinb, op=ALU.mult),
                         reads=[rxf, r_tab], writes=[rt])
                    S.op("dve", lambda e: e.tensor_tensor(out=tm_[:, :, 128:192], in0=x1, in1=sinb, op=ALU.mult),
                         reads=[rxf, r_tab], writes=[rt])
                    S.op("dve", lambda e: e.tensor_tensor(out=tm_[:, :, 192:256], in0=x2, in1=cosb, op=ALU.mult),
                         reads=[rxf, r_tab], writes=[rt])
                    S.op("pool", lambda e: e.tensor_tensor(out=ov[:, :, 0, :], in0=tm_[:, :, 0:64],
                                                           in1=tm_[:, :, 64:128], op=ALU.subtract),
                         reads=[rt], writes=[ro])
                    S.op("dve", lambda e: e.tensor_tensor(out=ov[:, :, 1, :], in0=tm_[:, :, 128:192],
                                                          in1=tm_[:, :, 192:256], op=ALU.add),
                         reads=[rt], writes=[ro])
                    S.dma("sp", lambda e: e.dma_start(out=dst, in_=o[:]), do, reads=[ro])

                qtiles = list(range(QT0, LT))
                qgroups = [(QT0 * 128, 128)] + [((16 + 4 * g) * 128, 512) for g in range(4)]
                agroups = [(g * 512, 512) for g in range(8)]

                for blk in range(2):
                    w, rw = load_w(1024 + blk * 512, 512)
                    for t in range(LT):
                        p, rp = mm_tm(w, rw, t, 0, 512)
                        rotary(p, rp, cs, t, RK[t * 128:(t + 1) * 128, blk * 512:(blk + 1) * 512])
                for blk in range(2):
                    w, rw = load_w(blk * 512, 512)
                    for t in qtiles:
                        p, rp = mm_tm(w, rw, t, 0, 512)
                        qi = t - QT0
                        rotary(p, rp, csq, qi, RQ[qi * 128:(qi + 1) * 128, blk * 512:(blk + 1) * 512])
                for blk in range(4):
                    w, rw = load_w(2048 + blk * 512, 512)
                    for t in range(LT):
                        p, rp = mm_tm(w, rw, t, 0, 512)
                        evac_store(p, rp, 128, 512, RV[t * 128:(t + 1) * 128, blk * 512:(blk + 1) * 512])
                w, rw = load_w(8192, 512)
                for (dst, c0) in ((KCT, 0), (VCT, 256)):
                    for g in range(2):
                        for tok0, ntok in agroups:
                            p, rp = mm_fm(w, rw, tok0, ntok, c0 + g * 128, 128)
                            evac_store(p, rp, 128, ntok, dst[g, :, tok0:tok0 + ntok])
                for (col0, dfm, dtm) in ((8704, KST, VS), (9216, KWT, VW)):
                    w, rw = load_w(col0, 512)
                    for g in range(2):
                        for tok0, ntok in agroups:
                            p, rp = mm_fm(w, rw, tok0, ntok, g * 128, 128)
                            evac_store(p, rp, 128, ntok, dfm[g, :, tok0:tok0 + ntok])
                    for t in range(LT):
                        p, rp = mm_tm(w, rw, t, 256, 256)
                        evac_store(p, rp, 128, 256, dtm[t * 128:(t + 1) * 128, :])
                for blk in range(4):
                    w, rw = load_w(6144 + blk * 512, 512)
                    for ch in range(4):
                        for tok0, ntok in qgroups:
                            p, rp = mm_fm(w, rw, tok0, ntok, ch * 128, 128)
                            q0 = tok0 - QT0 * 128
                            evac_store(p, rp, 128, ntok, NQT_[blk * 4 + ch, :, q0:q0 + ntok])
                for blk in range(4):
                    w, rw = load_w(4096 + blk * 512, 512)
                    for t in qtiles:
                        p, rp = mm_tm(w, rw, t, 0, 512)
                        qi = t - QT0
                        evac_store(p, rp, 128, 512, RG[qi * 128:(qi + 1) * 128, blk * 512:(blk + 1) * 512],
                                   func=AF.Silu)
                for blk in range(8):
                    w, rw = load_w(9776 + blk * 512, 512)
                    for ch in range(4):
                        for tok0, ntok in qgroups:
                            p, rp = mm_fm(w, rw, tok0, ntok, ch * 128, 128)
                            q0 = tok0 - QT0 * 128
                            evac_store(p, rp, 128, ntok, MGT[blk * 4 + ch, :, q0:q0 + ntok], func=AF.Sigmoid)
                w, rw = load_w(9728, 48)
                for tok0, ntok in qgroups:
                    p, rp = mm_fm(w, rw, tok0, ntok, 0, 48)
                    q0 = tok0 - QT0 * 128
                    o, ro, do = Rof.next()
                    S.op("act", lambda e, o=o, p=p, ntok=ntok: e.activation(out=o[0:48, 0:ntok], in_=p[0:48, 0:ntok],
                                                                           func=AF.Sigmoid), reads=[rp], writes=[ro])
                    S.dma("sp", lambda e, o=o, q0=q0, ntok=ntok: e.dma_start(out=NGT[:, q0:q0 + ntok],
                                                                            in_=o[0:48, 0:ntok]), do, reads=[ro])
                S.emit(block)

    if 1 in phases:
        phase01()

    def phase2():
        gam = [1.0 - 2.0 ** (-5.0 - h) for h in range(8)]
        with ExitStack() as st:
            sb = lambda name, shape, dt: st.enter_context(nc.sbuf_tensor(name, list(shape), dt))
            identb = sb("p2_id", [128, 128], BF16)
            decT = sb("p2_decT", [128, 8, 128], F32)
            qdec = sb("p2_qdec", [128, 8, 128], F32)
            kdec = sb("p2_kdec", [128, 8], F32)
            retw = sb("p2_retw", [128, D], F32)
            kc_ = [sb(f"p2_k{i}", [128, 1024], BF16) for i in range(2)]
            vc_ = [sb(f"p2_v{i}", [128, 2048], BF16) for i in range(2)]
            qc_ = [sb(f"p2_q{i}", [128, 1024], BF16) for i in range(2)]
            gc_ = [sb(f"p2_g{i}", [128, 2048], BF16) for i in range(2)]
            kd_ = [sb(f"p2_kd{i}", [128, 1024], BF16) for i in range(2)]
            kT_ = [sb(f"p2_kT{i}", [128, 1024], BF16) for i in range(2)]
            qT_ = [sb(f"p2_qT{i}", [128, 1024], BF16) for i in range(2)]
            qTd_ = [sb(f"p2_qTd{i}", [128, 1024], BF16) for i in range(2)]
            ST_ = [sb(f"p2_ST{i}", [128, 512], BF16) for i in range(2)]
            stf = sb("p2_stf", [128, 2048], F32)
            stb = [sb(f"p2_stb{i}", [128, 2048], BF16) for i in range(2)]
            junk = sb("p2_junk", [128, 256], BF16)
            ssq_ = [sb(f"p2_ssq{i}", [128, 4], F32) for i in range(2)]
            tmpf_ = [sb(f"p2_tmpf{i}", [128, 1024], F32) for i in range(2)]
            gat_ = [sb(f"p2_gat{i}", [128, 2048], BF16) for i in range(2)]
            gT_ = [sb(f"p2_gT{i}", [128, 16, 128], BF16) for i in range(2)]
            pkT = st.enter_context(nc.psum_tensor("p2_pkT", [128, 1024], BF16))
            pqT = st.enter_context(nc.psum_tensor("p2_pqT", [128, 1024], BF16))
            pS = st.enter_context(nc.psum_tensor("p2_pS", [128, 512], F32))
            pO = st.enter_context(nc.psum_tensor("p2_pO", [128, 1024], F32))
            pKV = st.enter_context(nc.psum_tensor("p2_pKV", [128, 1024], F32))
            pGT = st.enter_context(nc.psum_tensor("p2_pGT", [128, 1024], BF16))
            S = Sched(nc, st, "p2")
            block = st.enter_context(nc.Block())
            r_tab = Res()
            d_tab = S.dsem()
            r_id = Res()
            S.dma("pool", lambda e: e.dma_start(out=identb[:], in_=t_ident), S.dsem(), writes=[r_id])
            S.dma("sp", lambda e: e.dma_start(out=decT[:], in_=t_decT), d_tab, writes=[r_tab])
            S.dma("sp", lambda e: e.dma_start(out=qdec[:], in_=t_qdec), d_tab, writes=[r_tab])
            S.dma("sp", lambda e: e.dma_start(out=kdec[:], in_=t_kdec), d_tab, writes=[r_tab])
            S.dma("sp", lambda e: e.dma_start(out=retw[:], in_=ret_norm_w.partition_broadcast(128)), d_tab,
                  writes=[r_tab])
            Rk, Rv, Rq, Rg = Ring(kc_, S, True), Ring(vc_, S, True), Ring(qc_, S, True), Ring(gc_, S, True)
            Rkd, RkT, RqT, RqTd, RST = Ring(kd_), Ring(kT_), Ring(qT_), Ring(qTd_), Ring(ST_)
            Rssq, Rtmpf, Rgat = Ring(ssq_), Ring(tmpf_), Ring(gat_)
            RgT = Ring(gT_, S, True)
            r_stf = Res()
            r_stb = [Res(), Res()]
            r_junk = Res()
            r_pkT, r_pqT, r_pS, r_pO, r_pKV, r_pGT = (Res(True) for _ in range(6))
            S.op("dve", lambda e: e.memset(stf[:], 0.0), writes=[r_stf])
            S.op("pool", lambda e: e.memset(stb[0][:], 0.0), writes=[r_stb[0]])
            for c in range(LT):
                k_, rk, dk = Rk.next()
                v_, rv, dv = Rv.next()
                S.dma("sp", lambda e, k_=k_, c=c: e.dma_start(out=k_[:], in_=RK[c * 128:(c + 1) * 128, :]), dk, writes=[rk])
                S.dma("sp", lambda e, v_=v_, c=c: e.dma_start(out=v_[:], in_=RV[c * 128:(c + 1) * 128, :]), dv, writes=[rv])
                sbc, rsb = stb[c % 2], r_stb[c % 2]
                sbn, rsn = stb[(c + 1) % 2], r_stb[(c + 1) % 2]
                if c >= QT0 and os.environ.get('P2_OUT', '1') == '1':
                    qi = c - QT0
                    q_, rq, dq = Rq.next()
                    g_, rg, dg = Rg.next()
                    S.dma("sp", lambda e, q_=q_, qi=qi: e.dma_start(out=q_[:], in_=RQ[qi * 128:(qi + 1) * 128, :]), dq, writes=[rq])
                    S.dma("sp", lambda e, g_=g_, qi=qi: e.dma_start(out=g_[:], in_=RG[qi * 128:(qi + 1) * 128, :]), dg, writes=[rg])
                    for h in range(8):
                        S.op("pe", lambda e, k_=k_, h=h: e.transpose(out=pkT[:, h * 128:(h + 1) * 128],
                                                                     in_=k_[:, h * 128:(h + 1) * 128], identity=identb[:]),
                             reads=[rk, r_id], writes=[r_pkT])
                    kT, rkT = RkT.next()
                    S.op("act", lambda e, kT=kT: e.activation(out=kT[:], in_=pkT[:], func=AF.Copy), reads=[r_pkT], writes=[rkT])
                    for h in range(8):
                        S.op("pe", lambda e, q_=q_, h=h: e.transpose(out=pqT[:, h * 128:(h + 1) * 128],
                                                                     in_=q_[:, h * 128:(h + 1) * 128], identity=identb[:]),
                             reads=[rq, r_id], writes=[r_pqT])
                    qT, rqT = RqT.next()
                    qTd, rqTd = RqTd.next()
                    S.op("act", lambda e, qT=qT: e.activation(out=qT[:], in_=pqT[:], func=AF.Copy), reads=[r_pqT], writes=[rqT])
                    S.op("dve", lambda e, qTd=qTd: e.tensor_tensor(
                        out=qTd[:].rearrange("p (h n) -> p h n", h=8), in0=pqT[:].rearrange("p (h n) -> p h n", h=8),
                        in1=qdec[:], op=ALU.mult), reads=[r_pqT, r_tab], writes=[rqTd])
                    gat, rgat = Rgat.next()
                    lvl = int(os.environ.get('P2_LVL', '9'))
                    for hg in (range(2) if lvl >= 2 else []):
                        for j in range(4):
                            h = hg * 4 + j
                            S.op("pe", lambda e, kT=kT, qT=qT, h=h, j=j: e.matmul(
                                pS[:, j * 128:(j + 1) * 128], lhsT=kT[:, h * 128:(h + 1) * 128],
                                rhs=qT[:, h * 128:(h + 1) * 128], start=True, stop=True),
                                reads=[rkT, rqT], writes=[r_pS])
                        ST, rST = RST.next()
                        S.op("dve", lambda e, ST=ST, hg=hg: e.tensor_tensor(
                            out=ST[:].rearrange("p (h n) -> p h n", h=4), in0=pS[:].rearrange("p (h n) -> p h n", h=4),
                            in1=decT[:, hg * 4:(hg + 1) * 4, :], op=ALU.mult), reads=[r_pS, r_tab], writes=[rST])
                        for j in range(4):
                            h = hg * 4 + j
                            S.op("pe", lambda e, ST=ST, v_=v_, h=h, j=j: e.matmul(
                                pO[:, j * 256:(j + 1) * 256], lhsT=ST[:, j * 128:(j + 1) * 128],
                                rhs=v_[:, h * 256:(h + 1) * 256], start=True, stop=False),
                                reads=[rST, rv], writes=[r_pO])
                            S.op("pe", lambda e, qTd=qTd, sbc=sbc, h=h, j=j: e.matmul(
                                pO[:, j * 256:(j + 1) * 256], lhsT=qTd[:, h * 128:(h + 1) * 128],
                                rhs=sbc[:, h * 256:(h + 1) * 256], start=False, stop=True),
                                reads=[rqTd, rsb], writes=[r_pO])
                        if lvl < 3:
                            continue
                        ssq, rssq = Rssq.next()
                        for j in range(4):
                            S.op("act", lambda e, ssq=ssq, j=j: e.activation(
                                out=junk[:], in_=pO[:, j * 256:(j + 1) * 256], func=AF.Square, accum_out=ssq[:, j:j + 1]),
                                reads=[r_pO], writes=[r_junk, rssq])
                        S.op("act", lambda e, ssq=ssq: e.activation(out=ssq[:], in_=ssq[:], func=AF.Sqrt,
                                                                    scale=1.0 / 256, bias=EPS), reads=[rssq], writes=[rssq])
                        S.op("dve", lambda e, ssq=ssq: e.reciprocal(out=ssq[:], in_=ssq[:]), reads=[rssq], writes=[rssq])
                        tmpf, rtf = Rtmpf.next()
                        for j in range(4):
                            h = hg * 4 + j
                            S.op("dve", lambda e, tmpf=tmpf, ssq=ssq, h=h, j=j: e.scalar_tensor_tensor(
                                out=tmpf[:, j * 256:(j + 1) * 256], in0=pO[:, j * 256:(j + 1) * 256],
                                scalar=ssq[:, j:j + 1], in1=retw[:, h * 256:(h + 1) * 256], op0=ALU.mult, op1=ALU.mult),
                                reads=[r_pO, rssq, r_tab], writes=[rtf])
                        if lvl < 4:
                            continue
                        S.op("pool", lambda e, gat=gat, tmpf=tmpf, g_=g_, hg=hg: e.tensor_tensor(
                            out=gat[:, hg * 1024:(hg + 1) * 1024], in0=tmpf[:], in1=g_[:, hg * 1024:(hg + 1) * 1024],
                            op=ALU.mult), reads=[rtf, rg], writes=[rgat])
                    if lvl < 5:
                        continue
                    gT, rgT, dgT = RgT.next()
                    for hh in range(2):
                        for cc in range(8):
                            ch = hh * 8 + cc
                            S.op("pe", lambda e, gat=gat, ch=ch, cc=cc: e.transpose(
                                out=pGT[:, cc * 128:(cc + 1) * 128], in_=gat[:, ch * 128:(ch + 1) * 128],
                                identity=identb[:]), reads=[rgat, r_id], writes=[r_pGT])
                        if hh == 0:
                            S.op("act", lambda e, gT=gT: e.activation(
                                out=gT[:, 0:8, :], in_=pGT[:].rearrange("p (c q) -> p c q", c=8), func=AF.Copy),
                                reads=[r_pGT], writes=[rgT])
                        else:
                            S.op("dve", lambda e, gT=gT: e.tensor_copy(
                                out=gT[:, 8:16, :], in_=pGT[:].rearrange("p (c q) -> p c q", c=8)),
                                reads=[r_pGT], writes=[rgT])
                    S.dma("pool", lambda e, gT=gT, qi=qi: e.dma_start(
                        out=RETGT[:, :, qi * 128:(qi + 1) * 128].rearrange("c p q -> p c q"), in_=gT[:]),
                        dgT, reads=[rgT])
                if c < LT - 1 and os.environ.get('P2_STATE', '1') == '1':
                    kd, rkd = Rkd.next()
                    S.op("pool", lambda e, kd=kd, k_=k_: e.tensor_tensor(
                        out=kd[:].rearrange("p (h d) -> p h d", h=8), in0=k_[:].rearrange("p (h d) -> p h d", h=8),
                        in1=kdec[:, :, None].to_broadcast([128, 8, 128]), op=ALU.mult),
                        reads=[rk, r_tab], writes=[rkd])
                    for hg in range(2):
                        for j in range(4):
                            h = hg * 4 + j
                            S.op("pe", lambda e, kd=kd, v_=v_, h=h, j=j: e.matmul(
                                pKV[:, j * 256:(j + 1) * 256], lhsT=kd[:, h * 128:(h + 1) * 128],
                                rhs=v_[:, h * 256:(h + 1) * 256], start=True, stop=True),
                                reads=[rkd, rv], writes=[r_pKV])
                        for j in range(4):
                            h = hg * 4 + j
                            S.op("dve", lambda e, h=h, j=j: e.scalar_tensor_tensor(
                                out=stf[:, h * 256:(h + 1) * 256], in0=stf[:, h * 256:(h + 1) * 256],
                                scalar=float(gam[h] ** 128), in1=pKV[:, j * 256:(j + 1) * 256],
                                op0=ALU.mult, op1=ALU.add), reads=[r_pKV, r_stf], writes=[r_stf])
                    S.op("act", lambda e, sbn=sbn: e.activation(out=sbn[:], in_=stf[:], func=AF.Copy),
                         reads=[r_stf], writes=[rsn])
            S.emit(block)

    if 2 in phases:
        phase2()

    KCMPT = dscr("KCMPT", [2, 128, 256])
    VCMP = dscr("VCMP", [2, 256, 128])

    def phase3a():
        with ExitStack() as st:
            sb = lambda name, shape, dt: st.enter_context(nc.sbuf_tensor(name, list(shape), dt))
            xc_ = [sb(f"p3a_xc{i}", [128, LT * 128], BF16) for i in range(2)]
            w1b = sb("p3a_w1", [128, 32, 256], BF16)
            peT = sb("p3a_pe", [128, 32], BF16)
            w2b = sb("p3a_w2", [128, 2, 128], BF16)
            cb = sb("p3a_cb", [128, 2], F32)
            gel_ = [sb(f"p3a_gel{i}", [128, 2, 256], BF16) for i in range(2)]
            og_ = [sb(f"p3a_og{i}", [128, 256], BF16) for i in range(2)]
            pc = st.enter_context(nc.psum_tensor("p3a_pc", [128, 2], F32))
            ph_ = [st.enter_context(nc.psum_tensor(f"p3a_ph{i}", [128, 256], F32)) for i in range(2)]
            po_ = [st.enter_context(nc.psum_tensor(f"p3a_po{i}", [128, 256], F32)) for i in range(2)]
            S = Sched(nc, st, "p3a")
            block = st.enter_context(nc.Block())
            Rxc = Ring(xc_, S, True)
            Rgel = Ring(gel_)
            Rog = Ring(og_, S, True)
            Rph = Ring(ph_, excl=True)
            Rpo = Ring(po_, excl=True)
            r_w, r_cb, r_pc = Res(), Res(), Res(True)
            d_w = S.dsem()
            for g_ in gel_:
                S.op("dve", lambda e, g_=g_: e.memset(g_[:], 0.0), writes=[Rgel.res[gel_.index(g_)]])
            for kv in ("k", "v"):
                S.dma("pool", lambda e, kv=kv: e.dma_start(
                    out=w1b[:], in_=cmp_w1[kv].rearrange("(l d) j -> d l j", d=128)), d_w, writes=[r_w])
                S.dma("pool", lambda e, kv=kv: e.dma_start(out=peT[:], in_=cmp_peT[kv]), d_w, writes=[r_w])
                S.dma("pool", lambda e, kv=kv: e.dma_start(
                    out=w2b[:], in_=cmp_w2[kv].rearrange("(c p) d -> p c d", p=128)), d_w, writes=[r_w])
                for jc in range(2):
                    for l in range(32):
                        S.op("pe", lambda e, jc=jc, l=l: e.matmul(
                            pc[:, jc:jc + 1], lhsT=w1b[:, l, jc * 128:(jc + 1) * 128], rhs=peT[:, l:l + 1],
                            start=(l == 0), stop=(l == 31)), reads=[r_w], writes=[r_pc])
                S.op("act", lambda e: e.activation(out=cb[:], in_=pc[:], func=AF.Copy), reads=[r_pc], writes=[r_cb])
                src = KCT if kv == "k" else VCT
                for g in range(2):
                    xc, rxc, dxc = Rxc.next()
                    S.dma("sp", lambda e, xc=xc, g=g, src=src: e.dma_start(out=xc[:], in_=src[g]), dxc, writes=[rxc])
                    gel, rgel = Rgel.next()
                    for jc in range(2):
                        ph, rph = Rph.next()
                        for l in range(32):
                            S.op("pe", lambda e, ph=ph, xc=xc, jc=jc, l=l: e.matmul(
                                ph[:, 0:255], lhsT=w1b[:, l, jc * 128:(jc + 1) * 128],
                                rhs=xc[:, l:l + 16 * 254 + 1:16], start=(l == 0), stop=(l == 31)),
                                reads=[r_w, rxc], writes=[rph])
                        S.op("act", lambda e, ph=ph, gel=gel, jc=jc: e.activation(
                            out=gel[:, jc, 0:255], in_=ph[:, 0:255], func=AF.Gelu_apprx_tanh, bias=cb[:, jc:jc + 1]),
                            reads=[rph, r_cb], writes=[rgel])
                    if kv == "k":
                        po, rpo = Rpo.next()
                        for jc in range(2):
                            S.op("pe", lambda e, po=po, gel=gel, jc=jc: e.matmul(
                                po[:, :], lhsT=w2b[:, jc, :], rhs=gel[:, jc, :], start=(jc == 0), stop=(jc == 1)),
                                reads=[r_w, rgel], writes=[rpo])
                        og, rog, dog = Rog.next()
                        S.op("dve", lambda e, og=og, po=po: e.tensor_copy(out=og[:], in_=po[:]), reads=[rpo], writes=[rog])
                        S.dma("sp", lambda e, og=og, g=g: e.dma_start(out=KCMPT[g], in_=og[:]), dog, reads=[rog])
                    else:
                        for nk in range(2):
                            po, rpo = Rpo.next()
                            for jc in range(2):
                                S.op("pe", lambda e, po=po, gel=gel, jc=jc, nk=nk: e.matmul(
                                    po[:, 0:128], lhsT=gel[:, jc, nk * 128:(nk + 1) * 128], rhs=w2b[:, jc, :],
                                    start=(jc == 0), stop=(jc == 1)), reads=[r_w, rgel], writes=[rpo])
                            og, rog, dog = Rog.next()
                            S.op("dve", lambda e, og=og, po=po: e.tensor_copy(out=og[:, 0:128], in_=po[:, 0:128]),
                                 reads=[rpo], writes=[rog])
                            S.dma("sp", lambda e, og=og, g=g, nk=nk: e.dma_start(
                                out=VCMP[g, nk * 128:(nk + 1) * 128, :], in_=og[:, 0:128]), dog, reads=[rog])
            S.emit(block)

    def phase3b(sfx=""):
        with ExitStack() as st:
            sb = lambda name, shape, dt: st.enter_context(nc.sbuf_tensor(name + sfx, list(shape), dt))
            identb = sb("p3_idb", [128, 128], BF16)
            identf = sb("p3_idf", [128, 128], F32)
            onesb = sb("p3_ones", [128, 128], BF16)
            causb = sb("p3_caus", [128, 4, 128], BF16)
            acausb = sb("p3_acaus", [128, 4, 128], BF16)
            expt = sb("p3_expt", [64, LT, 128], BF16)
            ovb = sb("p3_ov", [128, 2, 65], BF16)
            padb = sb("p3_padb", [128, LT], F32)
            cmpb = sb("p3_cmpb", [128, 2], F32)
            kcm = sb("p3_kcm", [128, 2, 256], BF16)
            vcm = sb("p3_vcm", [128, 2, 2, 128], BF16)
            kst = sb("p3_kst", [128, 2, LT * 128], BF16)
            kwt = sb("p3_kwt", [128, 2, LT * 128], BF16)
            vs = sb("p3_vs", [128, LT, 256], BF16)
            vw = sb("p3_vw", [128, LT, 256], BF16)
            qT_ = [sb(f"p3_qT{i}", [128, 16, 128], BF16) for i in range(2)]
            gbc_ = [sb(f"p3_gbc{i}", [128, 24, 128], F32) for i in range(2)]
            keep_ = [sb(f"p3_keep{i}", [128, 64], F32) for i in range(2)]
            addt_ = [sb(f"p3_add{i}", [128, 64], F32) for i in range(2)]
            cm_ = [sb(f"p3_cm{i}", [128, 2, 128], F32) for i in range(2)]
            Ec_ = [sb(f"p3_Ec{i}", [128, 1024], BF16) for i in range(2)]
            E_ = [sb(f"p3_E{i}", [128, 1024], BF16) for i in range(3)]
            Sm_ = [sb(f"p3_Sm{i}", [128, 1024], F32) for i in range(2)]
            fac_ = [sb(f"p3_fac{i}", [128, 1024], F32) for i in range(2)]
            tmp_ = [sb(f"p3_tmp{i}", [128, 1024], F32) for i in range(2)]
            acc_ = [sb(f"p3_acc{i}", [128, 1024], F32) for i in range(2)]
            accb_ = [sb(f"p3_accb{i}", [128, 1024], BF16) for i in range(2)]
            oc_ = [sb(f"p3_oc{i}", [128, 1024], F32) for i in range(2)]
            Usb_ = [sb(f"p3_Usb{i}", [128, 2, 260], F32) for i in range(2)]
            rs8 = sb("p3_rs8", [128, 8], F32)
            imp = sb("p3_imp", [128, 64], F32)
            imp2 = sb("p3_imp2", [128, 64], F32)
            m8 = sb("p3_m8", [128, 8], F32)
            selb = sb("p3_selb", [128, 64], F32)
            selbT_ = [sb(f"p3_selbT{i}", [64, 4, 128], BF16) for i in range(2)]
            pS_ = [st.enter_context(nc.psum_tensor(f"p3_pS{i}" + sfx, [128, 1024], F32)) for i in range(2)]
            pO = st.enter_context(nc.psum_tensor("p3_pO" + sfx, [128, 1024], F32))
            pSum = st.enter_context(nc.psum_tensor("p3_pSum" + sfx, [128, 1024], F32))
            S = Sched(nc, st, "p3" + sfx)
            block = st.enter_context(nc.Block())
            r_c = Res()
            r_k = Res()
            d_cp, d_cs = S.dsem(), S.dsem()
            for dst, src in ((identb[:], t_ident), (expt[:], t_exp), (ovb[:], t_ov)):
                S.dma("pool", lambda e, dst=dst, src=src: e.dma_start(out=dst, in_=src), d_cp, writes=[r_c])
            for r4 in range(4):
                S.dma("pool", lambda e, r4=r4: e.dma_start(out=causb[:, r4, :], in_=t_caus), d_cp, writes=[r_c])
                S.dma("pool", lambda e, r4=r4: e.dma_start(out=acausb[:, r4, :], in_=t_acaus), d_cp, writes=[r_c])
            S.op("pool", lambda e: e.memset(onesb[:], 1.0), reads=[], writes=[r_c])
            for dst, src in ((identf[:], t_ident), (padb[:], t_padb), (cmpb[:], t_cmpb),
                             (kcm[:], KCMPT.rearrange("g d n -> d g n")),
                             (vcm[:], VCMP.rearrange("g (k p) d -> p g k d", p=128))):
                S.dma("sp", lambda e, dst=dst, src=src: e.dma_start(out=dst, in_=src), d_cs, writes=[r_k])
            r_big = {}
            for nm, dst, src in (("kwt", kwt[:], KWT.rearrange("g d t -> d g t")),
                                 ("vw", vw[:], VW.rearrange("(t p) c -> p t c", p=128)),
                                 ("kst", kst[:], KST.rearrange("g d t -> d g t")),
                                 ("vs", vs[:], VS.rearrange("(t p) c -> p t c", p=128))):
                r_big[nm] = Res()
                S.dma("sp", lambda e, dst=dst, src=src: e.dma_start(out=dst, in_=src), S.dsem(), writes=[r_big[nm]])
            RqT, Rgbc = Ring(qT_, S, True), Ring(gbc_, S, True)
            Rkeep, Radd, Rcm = Ring(keep_, S, True), Ring(addt_, S, True), Ring(cm_, S, True)
            REc, RE, RSm, Rfac, Rtmp, Racc = Ring(Ec_), Ring(E_), Ring(Sm_), Ring(fac_), Ring(tmp_), Ring(acc_)
            Raccb = Ring(accb_, S, True)
            RselbT = Ring(selbT_)
            Roc, RUsb = Ring(oc_), Ring(Usb_)
            RpS = Ring(pS_, excl=True)
            r_pO, r_pSum = Res(True), Res(True)
            r_small = Res()
            d_dbg = S.dsem()
            dbg_t = {}
            if "DBGSEL" in debug:
                dbg_t = {"DBGSEL": dscr("DBGSEL", [NQ, 2, 128, 64], F32), "DBGIMP": dscr("DBGIMP", [NQ, 2, 128, 64], F32),
                         "DBGM8": dscr("DBGM8", [NQ, 2, 128, 8], F32)}

            def finalize(gbc, rgbc, br, acc, racc, first):
                oc, roc = Roc.next()
                S.op("act", lambda e: e.activation(out=oc[:], in_=pO[:], func=AF.Copy), reads=[r_pO], writes=[roc])
                fac, rfac = Rfac.next()
                S.op("dve", lambda e: e.tensor_scalar_max(out=fac[:], in0=pSum[:], scalar1=1e-30),
                     reads=[r_pSum], writes=[rfac])
                S.op("dve", lambda e: e.reciprocal(out=fac[:], in_=fac[:]), reads=[rfac], writes=[rfac])
                S.op("pool", lambda e: e.tensor_tensor(
                    out=fac[:].rearrange("p (h q) -> p h q", h=8), in0=fac[:].rearrange("p (h q) -> p h q", h=8),
                    in1=gbc[:, br::3, :], op=ALU.mult), reads=[rfac, rgbc], writes=[rfac])
                if first:
                    S.op("dve", lambda e: e.tensor_tensor(out=acc[:], in0=oc[:], in1=fac[:], op=ALU.mult),
                         reads=[roc, rfac], writes=[racc])
                else:
                    tmp, rtmp = Rtmp.next()
                    S.op("dve", lambda e: e.tensor_tensor(out=tmp[:], in0=oc[:], in1=fac[:], op=ALU.mult),
                         reads=[roc, rfac], writes=[rtmp])
                    S.op("pool", lambda e: e.tensor_tensor(out=acc[:], in0=acc[:], in1=tmp[:], op=ALU.add),
                         reads=[rtmp, racc], writes=[racc])

            def attend(kt_list, ksrc, vsrc, g, qTg, rq, selbT, rselbT, i):
                n = len(kt_list)

                def scores(kt):
                    pSx, rpS = RpS.next()
                    for hf in range(2):
                        extra = []
                        if selbT is not None:
                            extra.append((expt[:, kt, :], selbT[:].rearrange("s r q -> s (r q)"), [r_c, rselbT]))
                        if kt == i:
                            extra.append((identb[:], causb[:].rearrange("k r q -> k (r q)"), [r_c]))
                        if selbT is None and kt == i - 4:
                            extra.append((identb[:], acausb[:].rearrange("k r q -> k (r q)"), [r_c]))
                        S.op("pe", lambda e, hf=hf, ne=len(extra): e.matmul(
                            pSx[:, hf * 512:(hf + 1) * 512], lhsT=ksrc[:, g, kt * 128:(kt + 1) * 128],
                            rhs=qTg[:, hf * 512:(hf + 1) * 512], start=True, stop=(ne == 0)),
                            reads=[r_big["kst" if selbT is not None else "kwt"], rq], writes=[rpS])
                        for xi, (l_, r_, deps) in enumerate(extra):
                            S.op("pe", lambda e, hf=hf, l_=l_, r_=r_, last=(xi == len(extra) - 1): e.matmul(
                                pSx[:, hf * 512:(hf + 1) * 512], lhsT=l_, rhs=r_, start=False, stop=last),
                                reads=deps, writes=[rpS])
                    E, rE = RE.next()
                    S.op("act", lambda e: e.activation(
                        out=E[:], in_=pSx[:], func=AF.Exp, scale=SCALE, bias=padb[:, kt:kt + 1]),
                        reads=[rpS, r_k], writes=[rE])
                    return E, rE

                def values(idx, kt, E, rE):
                    for hf in range(2):
                        S.op("pe", lambda e, hf=hf: e.matmul(
                            pO[:, hf * 512:(hf + 1) * 512], lhsT=vsrc[:, kt, g * 128:(g + 1) * 128],
                            rhs=E[:, hf * 512:(hf + 1) * 512], start=(idx == 0), stop=(idx == n - 1)),
                            reads=[r_big["vs" if selbT is not None else "vw"], rE], writes=[r_pO])
                        S.op("pe", lambda e, hf=hf: e.matmul(
                            pSum[:, hf * 512:(hf + 1) * 512], lhsT=onesb[:],
                            rhs=E[:, hf * 512:(hf + 1) * 512], start=(idx == 0), stop=(idx == n - 1)),
                            reads=[r_c, rE], writes=[r_pSum])

                pend = None
                for idx, kt in enumerate(kt_list):
                    cur = (idx, kt) + scores(kt)
                    if pend is not None:
                        values(*pend)
                    pend = cur
                values(*pend)

            for i in range(QT0, LT):
                qi = i - QT0
                qT, rq, dq = RqT.next()
                S.dma("sp", lambda e, qT=qT, qi=qi: e.dma_start(
                    out=qT[:], in_=NQT_[:, :, qi * 128:(qi + 1) * 128].rearrange("h p q -> p h q")), dq, writes=[rq])
                keep, rkeep, dkeep = Rkeep.next()
                addt, radd, dadd = Radd.next()
                cm, rcm, dcm = Rcm.next()
                S.dma("sp", lambda e, keep=keep, qi=qi: e.dma_start(out=keep[:], in_=t_keep[qi]), dkeep, writes=[rkeep])
                S.dma("sp", lambda e, addt=addt, qi=qi: e.dma_start(out=addt[:], in_=t_add[qi]), dadd, writes=[radd])
                for ck in range(2):
                    r0 = 128 * ck - 8 * i + 250
                    S.dma("sp", lambda e, cm=cm, ck=ck, r0=r0: e.dma_start(out=cm[:, ck, :], in_=t_cm[r0:r0 + 128, :]),
                          dcm, writes=[rcm])
                def do_group(i, qi, g, qT, rq, keep, rkeep, addt, radd, cm, rcm):
                    gbc, rgbc, dgbc = Rgbc.next()
                    S.dma("sp", lambda e, gbc=gbc, g=g, qi=qi: e.dma_start(
                        out=gbc[:], in_=NGT[g * 24:(g + 1) * 24, qi * 128:(qi + 1) * 128].unsqueeze(0).to_broadcast(
                            [128, 24, 128])), dgbc, writes=[rgbc])
                    qTg = qT[:, g * 8:(g + 1) * 8, :].rearrange("p h q -> p (h q)")
                    acc, racc = Racc.next()
                    Ecs = []
                    for ck in range(2):
                        pSx, rpS = RpS.next()
                        for hf in range(2):
                            S.op("pe", lambda e, pSx=pSx, hf=hf, ck=ck: e.matmul(
                                pSx[:, hf * 512:(hf + 1) * 512], lhsT=kcm[:, g, ck * 128:(ck + 1) * 128],
                                rhs=qTg[:, hf * 512:(hf + 1) * 512], start=True, stop=True),
                                reads=[r_k, rq], writes=[rpS])
                        Sm, rSm = RSm.next()
                        S.op("dve", lambda e, Sm=Sm, pSx=pSx, ck=ck: e.tensor_tensor(
                            out=Sm[:].rearrange("p (h q) -> p h q", h=8), in0=pSx[:].rearrange("p (h q) -> p h q", h=8),
                            in1=cm[:, ck, :][:, None, :].to_broadcast([128, 8, 128]), op=ALU.add),
                            reads=[rpS, rcm], writes=[rSm])
                        Ec, rEc = REc.next()
                        S.op("act", lambda e, Ec=Ec, Sm=Sm, ck=ck: e.activation(
                            out=Ec[:], in_=Sm[:], func=AF.Exp, scale=SCALE, bias=cmpb[:, ck:ck + 1]),
                            reads=[rSm, r_k], writes=[rEc])
                        Ecs.append((Ec, rEc))
                        for hf in range(2):
                            S.op("pe", lambda e, Ec=Ec, hf=hf, ck=ck: e.matmul(
                                pO[:, hf * 512:(hf + 1) * 512], lhsT=vcm[:, g, ck, :],
                                rhs=Ec[:, hf * 512:(hf + 1) * 512], start=(ck == 0), stop=(ck == 1)),
                                reads=[r_k, rEc], writes=[r_pO])
                            S.op("pe", lambda e, Ec=Ec, hf=hf, ck=ck: e.matmul(
                                pSum[:, hf * 512:(hf + 1) * 512], lhsT=onesb[:],
                                rhs=Ec[:, hf * 512:(hf + 1) * 512], start=(ck == 0), stop=(ck == 1)),
                                reads=[r_c, rEc], writes=[r_pSum])
                    pU, rpU = RpS.next()
                    for h in range(8):
                        for ck in range(2):
                            Ec, rEc = Ecs[ck]
                            o0 = (h // 4) * 512 + (h % 4) * 65
                            S.op("pe", lambda e, pU=pU, Ec=Ec, h=h, ck=ck, o0=o0: e.matmul(
                                pU[:, o0:o0 + 65], lhsT=Ec[:, h * 128:(h + 1) * 128], rhs=ovb[:, ck, :],
                                start=(ck == 0), stop=(ck == 1)), reads=[r_c, rEc], writes=[rpU])
                    finalize(gbc, rgbc, 0, acc, racc, True)
                    Usb, rUsb = RUsb.next()
                    S.op("act", lambda e, pU=pU, Usb=Usb: e.activation(
                        out=Usb[:], in_=pU[:].rearrange("p (b x) -> p b x", b=2)[:, :, 0:260], func=AF.Copy),
                        reads=[rpU], writes=[rUsb])
                    for hh in range(2):
                        S.op("dve", lambda e, Usb=Usb, hh=hh: e.tensor_scalar_max(
                            out=rs8[:, hh * 4:(hh + 1) * 4],
                            in0=Usb[:, hh, :].rearrange("p (h s) -> p h s", s=65)[:, :, 64],
                            scalar1=1e-30), reads=[rUsb], writes=[r_small])
                    S.op("dve", lambda e: e.reciprocal(out=rs8[:], in_=rs8[:]), reads=[r_small], writes=[r_small])
                    for h in range(8):
                        o0 = (h % 4) * 65
                        if h == 0:
                            S.op("dve", lambda e, Usb=Usb, o0=o0: e.tensor_scalar(
                                out=imp[:], in0=Usb[:, 0, o0:o0 + 64], scalar1=rs8[:, 0:1], scalar2=None, op0=ALU.mult),
                                reads=[rUsb, r_small], writes=[r_small])
                        else:
                            S.op("dve", lambda e, Usb=Usb, o0=o0, h=h: e.scalar_tensor_tensor(
                                out=imp[:], in0=Usb[:, h // 4, o0:o0 + 64], scalar=rs8[:, h:h + 1], in1=imp[:],
                                op0=ALU.mult, op1=ALU.add), reads=[rUsb, r_small], writes=[r_small])
                    S.op("dve", lambda e, keep=keep: e.tensor_tensor(out=imp[:], in0=imp[:], in1=keep[:], op=ALU.mult),
                         reads=[r_small, rkeep], writes=[r_small])
                    S.op("dve", lambda e, addt=addt: e.tensor_tensor(out=imp[:], in0=imp[:], in1=addt[:], op=ALU.add),
                         reads=[r_small, radd], writes=[r_small])
                    S.op("dve", lambda e: e.max(out=m8[:], in_=imp[:]), reads=[r_small], writes=[r_small])
                    S.op("dve", lambda e: e.match_replace(out=imp2[:], in_to_replace=m8[:], in_values=imp[:],
                                                          imm_value=-1e30), reads=[r_small], writes=[r_small])
                    S.op("dve", lambda e: e.max(out=m8[:], in_=imp2[:]), reads=[r_small], writes=[r_small])
                    S.op("dve", lambda e: e.tensor_scalar(out=selb[:], in0=imp[:], scalar1=m8[:, 7:8], scalar2=NEG,
                                                          op0=ALU.is_lt, op1=ALU.mult), reads=[r_small], writes=[r_small])
                    if "DBGSEL" in debug:
                        for nm_, src_ in (("DBGSEL", selb), ("DBGIMP", imp), ("DBGM8", m8)):
                            S.dma("sp", lambda e, nm_=nm_, src_=src_, qi=qi, g=g: e.dma_start(
                                out=dbg_t[nm_][qi, g], in_=src_[:]), d_dbg, reads=[r_small])
                    attend(list(range(i - 4, i + 1)), kwt, vw, g, qTg, rq, None, None, i)
                    finalize(gbc, rgbc, 2, acc, racc, False)
                    pT, rpT = RpS.next()
                    S.op("pe", lambda e, pT=pT: e.transpose(out=pT[0:64, 0:128], in_=selb[:, 0:64], identity=identf[:]),
                         reads=[r_small, r_k], writes=[rpT])
                    selbT, rselbT = RselbT.next()
                    S.op("act", lambda e, pT=pT, selbT=selbT: e.activation(
                        out=selbT[:], in_=pT[0:64, 0:128][:, None, :].to_broadcast([64, 4, 128]), func=AF.Copy),
                        reads=[rpT], writes=[rselbT])
                    attend(list(range(0, i + 1)), kst, vs, g, qTg, rq, selbT, rselbT, i)
                    finalize(gbc, rgbc, 1, acc, racc, False)
                    accb, raccb, daccb = Raccb.next()
                    S.op("pool", lambda e, accb=accb, acc=acc: e.tensor_copy(out=accb[:], in_=acc[:]),
                         reads=[racc], writes=[raccb])
                    S.dma("pool", lambda e, accb=accb, g=g, qi=qi: e.dma_start(
                        out=NSAT[g * 8:(g + 1) * 8, :, qi * 128:(qi + 1) * 128].rearrange("h p q -> p h q"),
                        in_=accb[:].rearrange("p (h q) -> p h q", h=8)), daccb, reads=[raccb])

                for g in range(2):
                    do_group(i, qi, g, qT, rq, keep, rkeep, addt, radd, cm, rcm)
            S.emit(block)

    if 3 in phases:
        phase3a()
        phase3b()
    if 33 in phases:
        phase3b("x")


    QGROUPS = [(0, 128)] + [(128 + 512 * g, 512) for g in range(4)]

    def upproj(tag, srcT, w, gate0, addsrc, dst):
        with ExitStack() as st:
            sb = lambda name, shape, dt: st.enter_context(nc.sbuf_tensor(name, list(shape), dt))
            src = sb(f"{tag}_src", [128, 16, NQT], BF16)
            wb = [sb(f"{tag}_w{i}", [128, 16, 512], BF16) for i in range(2)]
            mg_ = [sb(f"{tag}_mg{i}", [128, NQT], BF16) for i in range(2)]
            m1_ = [sb(f"{tag}_m1{i}", [128, NQT], BF16) for i in range(2)]
            tf_ = [sb(f"{tag}_tf{i}", [128, 512], F32) for i in range(2)]
            o_ = [sb(f"{tag}_o{i}", [128, NQT], BF16) for i in range(2)]
            ps = [st.enter_context(nc.psum_tensor(f"{tag}_ps{i}", [128, 512], F32)) for i in range(4)]
            S = Sched(nc, st, tag)
            block = st.enter_context(nc.Block())
            r_srcq = [Res() for _ in range(4)]
            for q4 in range(4):
                S.dma("sp", lambda e, q4=q4: e.dma_start(out=src[:, q4 * 4:(q4 + 1) * 4, :],
                                                       in_=srcT[q4 * 4:(q4 + 1) * 4].rearrange("c p q -> p c q")),
                      S.dsem(), writes=[r_srcq[q4]])
            Rw, Rmg, Rm1, Ro = Ring(wb, S, True), Ring(mg_, S, True), Ring(m1_, S, True), Ring(o_, S, True)
            Rtf = Ring(tf_)
            Rps = Ring(ps, excl=True)
            wv = w.rearrange("(c p) n -> p c n", p=128)

            def chunk(wt, rw, ch, dc):
                mg, rmg, dmg = Rmg.next()
                S.dma("act", lambda e: e.dma_start(out=mg[:], in_=MGT[gate0 + dc]), dmg, writes=[rmg])
                if addsrc is not None:
                    m1, rm1, dm1 = Rm1.next()
                    S.dma("act", lambda e: e.dma_start(out=m1[:], in_=addsrc[dc]), dm1, writes=[rm1])
                o, ro, do = Ro.next()
                for (q0, nt) in QGROUPS:
                    p, rp = Rps.next()
                    for c in (range(16) if not os.environ.get("SKIPMM") else range(1)):
                        S.op("pe", lambda e, p=p, c=c, q0=q0, nt=nt: e.matmul(
                            p[:, 0:nt], lhsT=wt[:, c, ch * 128:(ch + 1) * 128], rhs=src[:, c, q0:q0 + nt],
                            start=(c == 0), stop=(c == 15)), reads=[rw, r_srcq[c // 4]], writes=[rp])
                    if addsrc is None:
                        S.op("dve", lambda e, p=p, q0=q0, nt=nt: e.tensor_tensor(
                            out=o[:, q0:q0 + nt], in0=p[:, 0:nt], in1=mg[:, q0:q0 + nt], op=ALU.mult),
                            reads=[rp, rmg], writes=[ro])
                    else:
                        tf, rtf = Rtf.next()
                        S.op("dve", lambda e, p=p, q0=q0, nt=nt, tf=tf: e.tensor_tensor(
                            out=tf[:, 0:nt], in0=p[:, 0:nt], in1=mg[:, q0:q0 + nt], op=ALU.mult),
                            reads=[rp, rmg], writes=[rtf])
                        S.op("pool", lambda e, q0=q0, nt=nt, tf=tf: e.tensor_tensor(
                            out=o[:, q0:q0 + nt], in0=tf[:, 0:nt], in1=m1[:, q0:q0 + nt], op=ALU.add),
                            reads=[rtf, rm1], writes=[ro])
                S.dma("sp", lambda e: e.dma_start(out=dst[dc], in_=o[:]), do, reads=[ro])

            def wblock(blk):
                wt, rw, dw = Rw.next()
                S.dma("pool", lambda e: e.dma_start(out=wt[:], in_=wv[:, :, blk * 512:(blk + 1) * 512]), dw, writes=[rw])
                for ch in range(4):
                    chunk(wt, rw, ch, blk * 4 + ch)

            for blk in range(4):
                wblock(blk)
            S.emit(block)

    def phase4b():
        with ExitStack() as st:
            sb = lambda name, shape, dt: st.enter_context(nc.sbuf_tensor(name, list(shape), dt))
            identb = sb("p4_id", [128, 128], BF16)
            wo = sb("p4_wo", [128, 16, D], BF16)
            w2bc = sb("p4_w2", [128, D], F32)
            mT_ = [sb(f"p4_mT{i}", [128, 16, 128], BF16) for i in range(2)]
            x_ = [sb(f"p4_x{i}", [128, D], F32) for i in range(2)]
            h_ = [sb(f"p4_h{i}", [128, D], F32) for i in range(2)]
            xn_ = [sb(f"p4_xn{i}", [128, D], BF16) for i in range(2)]
            xT_ = [sb(f"p4_xT{i}", [128, 16, 128], BF16) for i in range(2)]
            junk = sb("p4_junk", [128, D], BF16)
            ss_ = [sb(f"p4_ss{i}", [128, 1], F32) for i in range(2)]
            ps = [st.enter_context(nc.psum_tensor(f"p4_ps{i}", [128, 512], F32)) for i in range(4)]
            pt = [st.enter_context(nc.psum_tensor(f"p4_pt{i}", [128, 1024], BF16)) for i in range(2)]
            S = Sched(nc, st, "p4b")
            block = st.enter_context(nc.Block())
            r_id, r_wo, r_w2, r_junk = Res(), Res(), Res(), Res()
            d_p, d_s = S.dsem(), S.dsem()
            S.dma("pool", lambda e: e.dma_start(out=identb[:], in_=t_ident), d_p, writes=[r_id])
            wov = w_out.rearrange("(c p) n -> p c n", p=128)
            r_wob = [Res() for _ in range(4)]
            for blk in range(4):
                S.dma("pool", lambda e, blk=blk: e.dma_start(out=wo[:, :, blk * 512:(blk + 1) * 512],
                                                           in_=wov[:, :, blk * 512:(blk + 1) * 512]), S.dsem(),
                      writes=[r_wob[blk]])
            S.dma("sp", lambda e: e.dma_start(out=w2bc[:], in_=norm2_w.partition_broadcast(128)), d_s, writes=[r_w2])
            RmT, Rx, Rh, RxT = Ring(mT_, S, True), Ring(x_, S, True), Ring(h_, S, True), Ring(xT_, S, True)
            Rxn, Rss = Ring(xn_), Ring(ss_)
            Rps = Ring(ps, excl=True)
            r_pt = [Res(True), Res(True)]

            def tile(qi):
                mT, rmT, dmT = RmT.next()
                S.dma("sp", lambda e: e.dma_start(out=mT[:], in_=MRGT[:, :, qi * 128:(qi + 1) * 128].rearrange("c p q -> p c q")),
                      dmT, writes=[rmT])
                xt, rx, dx = Rx.next()
                S.dma("sp", lambda e: e.dma_start(out=xt[:], in_=x_loc[(QT0 + qi) * 128:(QT0 + qi + 1) * 128, :]), dx, writes=[rx])
                h, rh, dh = Rh.next()
                for cb in range(4):
                    p, rp = Rps.next()
                    for c in (range(16) if not os.environ.get("SKIPMM") else range(1)):
                        S.op("pe", lambda e, p=p, c=c, cb=cb: e.matmul(
                            p[:], lhsT=mT[:, c, :], rhs=wo[:, c, cb * 512:(cb + 1) * 512],
                            start=(c == 0), stop=(c == 15)), reads=[rmT, r_wob[cb]], writes=[rp])
                    S.op("dve", lambda e, p=p, cb=cb: e.tensor_tensor(
                        out=h[:, cb * 512:(cb + 1) * 512], in0=p[:], in1=xt[:, cb * 512:(cb + 1) * 512], op=ALU.add),
                        reads=[rp, rx], writes=[rh])
                S.dma("pool", lambda e: e.dma_start(out=H1[qi * 128:(qi + 1) * 128, :], in_=h[:]), dh, reads=[rh])
                ss, rss = Rss.next()
                S.op("act", lambda e: e.activation(out=junk[:], in_=h[:], func=AF.Square, accum_out=ss[:]),
                     reads=[rh], writes=[r_junk, rss])
                S.op("act", lambda e: e.activation(out=ss[:], in_=ss[:], func=AF.Sqrt, scale=1.0 / D, bias=EPS),
                     reads=[rss], writes=[rss])
                S.op("dve", lambda e: e.reciprocal(out=ss[:], in_=ss[:]), reads=[rss], writes=[rss])
                xn, rxn = Rxn.next()
                S.op("dve", lambda e: e.scalar_tensor_tensor(out=xn[:], in0=h[:], scalar=ss[:], in1=w2bc[:],
                                                             op0=ALU.mult, op1=ALU.mult),
                     reads=[rh, rss, r_w2], writes=[rxn])
                xT, rxT, dxT = RxT.next()
                for hh in range(2):
                    for cc in range(8):
                        c = hh * 8 + cc
                        S.op("pe", lambda e, c=c, cc=cc, hh=hh: e.transpose(
                            out=pt[hh][:, cc * 128:(cc + 1) * 128], in_=xn[:, c * 128:(c + 1) * 128], identity=identb[:]),
                            reads=[rxn, r_id], writes=[r_pt[hh]])
                    if hh == 0:
                        S.op("act", lambda e, hh=hh: e.activation(
                            out=xT[:, 0:8, :], in_=pt[0][:].rearrange("p (c q) -> p c q", c=8), func=AF.Copy),
                            reads=[r_pt[0]], writes=[rxT])
                    else:
                        S.op("dve", lambda e, hh=hh: e.tensor_copy(
                            out=xT[:, 8:16, :], in_=pt[1][:].rearrange("p (c q) -> p c q", c=8)),
                            reads=[r_pt[1]], writes=[rxT])
                S.dma("pool", lambda e: e.dma_start(
                    out=XN2T[:, :, qi * 128:(qi + 1) * 128].rearrange("c p q -> p c q"), in_=xT[:]), dxT, reads=[rxT])

            for qi in range(NQ):
                tile(qi)
            S.emit(block)

    if 4 in phases:
        upproj("p4a", RETGT, w_ret_up, 0, None, M1T)
        upproj("p4n", NSAT, w_nsa_up, 16, M1T, MRGT)
        phase4b()

    def phase5a():
        with ExitStack() as st:
            sb = lambda name, shape, dt: st.enter_context(nc.sbuf_tensor(name, list(shape), dt))
            xs = sb("p5_xs", [128, 16, NQT], BF16)
            wa_ = [sb(f"p5_wa{i}", [128, 16, 512], BF16) for i in range(2)]
            wb_ = [sb(f"p5_wb{i}", [128, 16, 512], BF16) for i in range(2)]
            cw = sb("p5_cw", [128, 88, 3], F32)
            cbias = sb("p5_cb", [128, 88], F32)
            halo = sb("p5_halo", [128, 1], F32)
            u_ = [sb(f"p5_u{i}", [128, 2050], F32) for i in range(4)]
            y_ = [sb(f"p5_y{i}", [128, 2048], F32) for i in range(3)]
            o_ = [sb(f"p5_o{i}", [128, 2048], BF16) for i in range(2)]
            ps = [st.enter_context(nc.psum_tensor(f"p5_ps{i}", [128, 512], F32)) for i in range(6)]
            ph = [st.enter_context(nc.psum_tensor(f"p5_ph{i}", [128, 2], F32)) for i in range(2)]
            S = Sched(nc, st, "p5a")
            block = st.enter_context(nc.Block())
            r_c = Res()
            d_c = S.dsem()
            S.dma("sp", lambda e: e.dma_start(out=xs[:], in_=XN2T.rearrange("c p q -> p c q")), d_c, writes=[r_c])
            S.dma("sp", lambda e: e.dma_start(out=cw[:], in_=conv_wT), d_c, writes=[r_c])
            S.dma("sp", lambda e: e.dma_start(out=cbias[:], in_=conv_bT), d_c, writes=[r_c])
            S.dma("sp", lambda e: e.dma_start(out=halo[:], in_=t_halo), d_c, writes=[r_c])
            Rwa, Rwb = Ring(wa_, S, True), Ring(wb_, S, True)
            Ru, Ry = Ring(u_), Ring(y_)
            Ro = Ring(o_, S, True)
            Rps, Rph = Ring(ps, excl=True), Ring(ph, excl=True)
            wv = w_ffn_up.rearrange("(c p) n -> p c n", p=128)

            def half(wt, rw, ch, cidx, ceng):
                u, ru = Ru.next()
                p, rp = Rph.next()
                for c in range(16):
                    S.op("pe", lambda e, c=c: e.matmul(p[:, 0:2], lhsT=wt[:, c, ch * 128:(ch + 1) * 128],
                                                       rhs=xs[:, c, 126:128], start=(c == 0), stop=(c == 15)),
                         reads=[rw, r_c], writes=[rp])
                S.op("act", lambda e: e.activation(out=u[:, 0:2], in_=p[:, 0:2], func=AF.Copy, scale=halo[:]),
                     reads=[rp, r_c], writes=[ru])
                for g in range(4):
                    pp, rpp = Rps.next()
                    for c in range(16):
                        S.op("pe", lambda e, c=c, pp=pp, g=g: e.matmul(
                            pp[:], lhsT=wt[:, c, ch * 128:(ch + 1) * 128],
                            rhs=xs[:, c, 128 + g * 512:128 + (g + 1) * 512], start=(c == 0), stop=(c == 15)),
                            reads=[rw, r_c], writes=[rpp])
                    S.op("act", lambda e, pp=pp, g=g: e.activation(out=u[:, 2 + g * 512:2 + (g + 1) * 512], in_=pp[:],
                                                                   func=AF.Copy), reads=[rpp], writes=[ru])
                y, ry = Ry.next()
                S.op(ceng, lambda e: e.tensor_scalar(out=y[:], in0=u[:, 2:2050], scalar1=cw[:, cidx, 2:3],
                                                     scalar2=cbias[:, cidx:cidx + 1], op0=ALU.mult, op1=ALU.add),
                     reads=[ru, r_c], writes=[ry])
                S.op(ceng, lambda e: e.scalar_tensor_tensor(out=y[:], in0=u[:, 1:2049], scalar=cw[:, cidx, 1:2], in1=y[:],
                                                            op0=ALU.mult, op1=ALU.add), reads=[ru, r_c, ry], writes=[ry])
                S.op(ceng, lambda e: e.scalar_tensor_tensor(out=y[:], in0=u[:, 0:2048], scalar=cw[:, cidx, 0:1], in1=y[:],
                                                            op0=ALU.mult, op1=ALU.add), reads=[ru, r_c, ry], writes=[ry])
                return y, ry

            def chunk(wa, rwa, wb, rwb, ch, j):
                ya, rya = half(wa, rwa, ch, j, "dve")
                yb, ryb = half(wb, rwb, ch, 44 + j, "dve")
                S.op("act", lambda e: e.activation(out=ya[:], in_=ya[:], func=AF.Silu), reads=[rya], writes=[rya])
                o, ro, do = Ro.next()
                S.op("pool", lambda e: e.tensor_tensor(out=o[:], in0=ya[:], in1=yb[:], op=ALU.mult),
                     reads=[rya, ryb], writes=[ro])
                S.dma("sp", lambda e: e.dma_start(out=ACTT[:, :, j, :].rearrange("t p q -> p t q"),
                                                  in_=o[:].rearrange("p (t q) -> p t q", q=128)), do, reads=[ro])

            def wblock(jb):
                wa, rwa, dwa = Rwa.next()
                wb, rwb, dwb = Rwb.next()
                S.dma("pool", lambda e: e.dma_start(out=wa[:], in_=wv[:, :, jb * 512:(jb + 1) * 512]), dwa, writes=[rwa])
                S.dma("pool", lambda e: e.dma_start(out=wb[:], in_=wv[:, :, DFF + jb * 512:DFF + (jb + 1) * 512]), dwb,
                      writes=[rwb])
                for ch in range(4):
                    chunk(wa, rwa, wb, rwb, ch, jb * 4 + ch)

            for jb in range(11):
                wblock(jb)
            S.emit(block)

    def phase5b():
        with ExitStack() as st:
            sb = lambda name, shape, dt: st.enter_context(nc.sbuf_tensor(name, list(shape), dt))
            wd_ = [sb(f"p5b_w{i}", [128, 44, 512], BF16) for i in range(2)]
            a_ = [sb(f"p5b_a{i}", [128, 44, 128], BF16) for i in range(3)]
            h_ = [sb(f"p5b_h{i}", [128, 512], F32) for i in range(3)]
            ps = [st.enter_context(nc.psum_tensor(f"p5b_ps{i}", [128, 512], F32)) for i in range(4)]
            S = Sched(nc, st, "p5b")
            block = st.enter_context(nc.Block())
            Rw, Ra, Rh = Ring(wd_, S, True), Ring(a_, S, True), Ring(h_, S, True)
            Rwq = Ring(list(range(8)), S, True)
            r_hs = [Res() for _ in h_]
            d_hs = [S.dsem() for _ in h_]
            Rps = Ring(ps, excl=True)
            wv = w_ffn_down.rearrange("(j p) n -> p j n", p=128)

            def tile(wt, rw, cb, t):
                a, ra, da = Ra.next()
                S.dma("act", lambda e: e.dma_start(out=a[:], in_=ACTT[t]), da, writes=[ra])
                h, rh, dh = Rh.next()
                k = Rh.i
                S.dma("act", lambda e: e.dma_start(out=h[:], in_=H1[(t + 1) * 128:(t + 2) * 128, cb * 512:(cb + 1) * 512]),
                      dh, writes=[rh])
                p, rp = Rps.next()
                for j in (range(44) if not os.environ.get("SKIPMM") else range(1)):
                    S.op("pe", lambda e, j=j: e.matmul(p[:], lhsT=a[:, j, :], rhs=wt[:, j, :], start=(j == 0), stop=(j == 43)),
                         reads=[ra, rw[j // 11]], writes=[rp])
                S.op("dve", lambda e: e.tensor_tensor(out=h[:], in0=p[:], in1=h[:], op=ALU.add), reads=[rp, rh], writes=[rh])
                S.dma("sp", lambda e: e.dma_start(out=H2[t * 128:(t + 1) * 128, cb * 512:(cb + 1) * 512], in_=h[:]),
                      d_hs[k], reads=[rh], writes=[r_hs[k]])

            def wblock(cb):
                wt, rw, dw = Rw.next()
                rws = Rwq.res[4 * Rw.i:4 * Rw.i + 4]
                dws = Rwq.ds[4 * Rw.i:4 * Rw.i + 4]
                for q4 in range(4):
                    S.dma("pool", lambda e, q4=q4: e.dma_start(out=wt[:, q4 * 11:(q4 + 1) * 11, :],
                                                             in_=wv[:, q4 * 11:(q4 + 1) * 11, cb * 512:(cb + 1) * 512]),
                          dws[q4], writes=[rws[q4]])
                for t in range(16):
                    tile(wt, rws, cb, t)

            for cb in range(4):
                wblock(cb)
            S.emit(block)

    def phase5c():
        with ExitStack() as st:
            sb = lambda name, shape, dt: st.enter_context(nc.sbuf_tensor(name, list(shape), dt))
            wf = sb("p5c_wf", [128, D], F32)
            h_ = [sb(f"p5c_h{i}", [128, D], F32) for i in range(2)]
            o_ = [sb(f"p5c_o{i}", [128, D], F32) for i in range(2)]
            junk = sb("p5c_junk", [128, D], BF16)
            ss_ = [sb(f"p5c_ss{i}", [128, 1], F32) for i in range(2)]
            S = Sched(nc, st, "p5c")
            block = st.enter_context(nc.Block())
            r_w, r_junk = Res(), Res()
            S.dma("sp", lambda e: e.dma_start(out=wf[:], in_=final_norm_w.partition_broadcast(128)), S.dsem(), writes=[r_w])
            Rh, Ro, Rss = Ring(h_, S, True), Ring(o_, S, True), Ring(ss_)

            def tile(t):
                h, rh, dh = Rh.next()
                S.dma("sp", lambda e: e.dma_start(out=h[:], in_=H2[t * 128:(t + 1) * 128, :]), dh, writes=[rh])
                ss, rss = Rss.next()
                S.op("act", lambda e: e.activation(out=junk[:], in_=h[:], func=AF.Square, accum_out=ss[:]),
                     reads=[rh], writes=[r_junk, rss])
                S.op("act", lambda e: e.activation(out=ss[:], in_=ss[:], func=AF.Sqrt, scale=1.0 / D, bias=EPS),
                     reads=[rss], writes=[rss])
                S.op("dve", lambda e: e.reciprocal(out=ss[:], in_=ss[:]), reads=[rss], writes=[rss])
                o, ro, do = Ro.next()
                S.op("dve", lambda e: e.scalar_tensor_tensor(out=o[:], in0=h[:], scalar=ss[:], in1=wf[:],
                                                             op0=ALU.mult, op1=ALU.mult), reads=[rh, rss, r_w], writes=[ro])
                S.dma("pool", lambda e: e.dma_start(out=out[t * 128:(t + 1) * 128, :], in_=o[:]), do, reads=[ro])

            for t in range(16):
                tile(t)
            S.emit(block)

    if 5 in phases:
        phase5a()
        phase5b()
        phase5c()

    return nc


def _tables(s):
    pad = 2048 if s == 0 else 0
    tl = np.arange(LT * 128)
    act = np.maximum(tl - pad, 0).astype(np.float64)
    is_pad = tl < pad
    half = 64
    freq = 10000.0 ** (-np.arange(half, dtype=np.float64) / half)
    ang = act[:, None] * freq[None, :]
    cs = np.concatenate([np.cos(ang), np.sin(ang)], axis=1)
    t_cs = cs.reshape(LT, 128, 128).transpose(1, 0, 2).astype(np.float32)
    t_csq = (cs[QT0 * 128:] * (128 ** -0.5)).reshape(NQ, 128, 128).transpose(1, 0, 2).astype(np.float32)
    gam = 1.0 - 2.0 ** (-5.0 - np.arange(8, dtype=np.float64))
    j = np.arange(128, dtype=np.float64)
    rel = j[None, :] - j[:, None]
    decT = np.where(rel[:, None, :] >= 0, gam[None, :, None] ** np.maximum(rel[:, None, :], 0), 0.0)
    qdec = np.broadcast_to((gam[:, None] ** (j[None, :] + 1.0))[None], (128, 8, 128))
    kdec = gam[None, :] ** (127.0 - j[:, None])
    padb = np.where(is_pad, NEG, 0.0).reshape(LT, 128).T
    n = np.arange(256)
    cmp_invalid = (n * 16 < pad) | (n >= 255)
    cmpb = np.where(cmp_invalid, NEG, 0.0).reshape(2, 128).T
    r = np.arange(512)[:, None]
    q = np.arange(128)[None, :]
    cm = np.where(16 * (r - 250) + 31 <= q, 0.0, NEG)
    keep = np.ones((NQ, 128, 64))
    add = np.zeros((NQ, 128, 64))
    blk = np.arange(64)[None, :]
    b0 = pad // 64
    for qi in range(NQ):
        t = (QT0 + qi) * 128 + np.arange(128)
        cur = (t // 64)[:, None]
        forced = (blk == b0) | (blk == cur) | (blk == cur - 1)
        neg = (blk > cur) | (blk < b0)
        keep[qi] = np.where(forced | neg, 0.0, 1.0)
        add[qi] = np.where(neg, -1e4, np.where(forced, 1e4, 0.0))
    sidx = np.arange(64)[:, None, None]
    kt = np.arange(LT)[None, :, None]
    kk = np.arange(128)[None, None, :]
    texp = (sidx == 2 * kt + (kk >= 64)).astype(np.float32)
    k_ = np.arange(128)[:, None]
    caus = np.where(k_ > q, NEG, 0.0)
    acaus = np.where(k_ <= q, NEG, 0.0)
    nn = np.arange(256)[:, None]
    ss = np.arange(64)[None, :]
    ov = ((nn * 16 < ss * 64 + 64) & (nn * 16 + 32 > ss * 64)).astype(np.float64)
    ov = np.concatenate([ov, np.ones((256, 1))], axis=1)
    ov[255] = 0.0
    t_ov = ov.reshape(2, 128, 65).transpose(1, 0, 2)
    f = lambda a: np.ascontiguousarray(a, dtype=np.float32)
    return {
        "t_cs": f(t_cs), "t_csq": f(t_csq), "t_decT": f(decT), "t_qdec": f(qdec), "t_kdec": f(kdec),
        "t_padb": f(padb), "t_cmpb": f(cmpb), "t_cm": f(cm), "t_keep": f(keep), "t_add": f(add),
        "t_exp": f(texp), "t_caus": f(caus), "t_acaus": f(acaus), "t_ident": f(np.eye(128)),
        "t_ov": f(t_ov), "t_halo": f(np.full((128, 1), float(s))),
    }


def make_in_maps(inputs):
    g = lambda k: np.asarray(inputs[k], dtype=np.float32)
    x = g("x")
    shared = {
        "norm1_w": g("norm1_w")[0][None], "w_in": g("w_in")[0], "ret_norm_w": g("ret_norm_w")[0][None],
        "w_ret_up": g("w_ret_up")[0],
        "cmp_peT_k": np.ascontiguousarray(g("cmp_pe_k")[0].T), "cmp_peT_v": np.ascontiguousarray(g("cmp_pe_v")[0].T),
        "cmp_w1_k": g("cmp_w1_k")[0], "cmp_w1_v": g("cmp_w1_v")[0],
        "cmp_w2_k": g("cmp_w2_k")[0], "cmp_w2_v": g("cmp_w2_v")[0],
        "w_nsa_up": g("w_nsa_up")[0], "w_out": g("w_out")[0], "norm2_w": g("norm2_w")[0][None],
        "w_ffn_up": g("w_ffn_up")[0],
        "conv_wT": np.ascontiguousarray(g("conv_w")[0].reshape(3, 88, 128).transpose(2, 1, 0)),
        "conv_bT": np.ascontiguousarray(g("conv_b")[0].reshape(88, 128).T),
        "w_ffn_down": g("w_ffn_down")[0], "final_norm_w": g("final_norm_w")[None],
    }
    tabs = [_tables(0), _tables(1)]
    maps = []
    for c in range(8):
        b, s = c // 2, c % 2
        if s == 1:
            xl = np.ascontiguousarray(x[b])
        else:
            xl = np.concatenate([np.zeros((2048, D), np.float32), x[b, :2048]], axis=0)
        m = dict(shared)
        m.update(tabs[s])
        m["x_loc"] = xl
        maps.append(m)
    return maps


_NC = None


def kernel(**inputs):
    global _NC
    if _NC is None:
        _NC = build_program()
    maps = make_in_maps(inputs)
    res = run_bass_kernel_spmd(_NC, maps, core_ids=list(range(8)))
    outp = np.zeros((4, 4096, D), np.float32)
    for c in range(8):
        b, s = c // 2, c % 2
        outp[b, s * 2048:(s + 1) * 2048] = res.results[c]["out"]
    return outp
```

```python
import math
import os
from contextlib import ExitStack
import numpy as np
import concourse.bass as bass
import concourse.mybir as mybir
from concourse.bass_utils import run_bass_kernel_spmd

F32 = mybir.dt.float32
BF16 = mybir.dt.bfloat16
AF = mybir.ActivationFunctionType
ALU = mybir.AluOpType
AX = mybir.AxisListType

D = 2048
LT = 32
QT0 = 15
NQ = LT - QT0
NQT = NQ * 128
IN_COLS = 13872
DFF = 5632
EPS = 1e-6
NEG = -30000.0
SCALE = 128 ** -0.5


class Tok:
    __slots__ = ("sem", "val")

    def __init__(self, sem, val):
        self.sem = sem
        self.val = val


class Res:
    __slots__ = ("w", "r", "excl")

    def __init__(self, excl=False):
        self.w = None
        self.r = []
        self.excl = excl


class DSem:
    __slots__ = ("sem", "val", "eng")

    def __init__(self, sem):
        self.sem = sem
        self.val = 0
        self.eng = None


ENGS = ("pe", "act", "dve", "pool", "sp")


class Sched:
    def __init__(self, nc, stack, tag):
        self.nc = nc
        self.q = {e: [] for e in ENGS}
        self.cnt = {e: 0 for e in ENGS}
        self.allsems = []
        self.sem = {e: self._alloc(f"{tag}_s_{e}") for e in ENGS}
        self.waited = {e: {} for e in ENGS}
        self.dsems = []
        self.tag = tag
        stack.callback(self._cleanup)

    def _alloc(self, name):
        h = self.nc.alloc_semaphore(name=name)
        self.allsems.append(h)
        return h

    def _cleanup(self):
        self.nc.clear_and_free_semaphores(self.allsems)
        self.nc.all_engine_barrier()

    def dsem(self):
        d = DSem(self._alloc(f"{self.tag}_d{len(self.dsems)}"))
        self.dsems.append(d)
        return d

    def _deps(self, eng, reads, writes):
        need = {}

        def add(t):
            k = id(t.sem)
            if k not in need or need[k].val < t.val:
                need[k] = t

        for r in reads:
            if r.w is not None:
                add(r.w)
        for w in writes:
            if w.w is not None:
                add(w.w)
            for t in w.r:
                add(t)
        waits = []
        wd = self.waited[eng]
        own = id(self.sem[eng])
        for k, t in need.items():
            if eng == "pe" and k == own:
                continue
            if wd.get(k, 0) < t.val:
                wd[k] = t.val
                waits.append((t.sem, t.val))
        return waits

    def _fin(self, tok, reads, writes):
        for r in reads:
            r.r.append(tok)
        for w in writes:
            w.w = tok
            w.r = []
        return tok

    def op(self, eng, fn, reads=(), writes=()):
        ex = [r for r in reads if r.excl]
        if ex:
            reads = [r for r in reads if not r.excl]
            writes = list(writes) + ex
        waits = self._deps(eng, reads, writes)
        self.cnt[eng] += 1
        self.q[eng].append((waits, fn, (self.sem[eng], 1)))
        return self._fin(Tok(self.sem[eng], self.cnt[eng]), reads, writes)

    def dma(self, eng, fn, ds, reads=(), writes=()):
        assert ds.eng in (None, eng), "one issuing engine per DMA semaphore"
        ds.eng = eng
        waits = self._deps(eng, reads, writes)
        ds.val += 16
        self.q[eng].append((waits, fn, (ds.sem, 16)))
        return self._fin(Tok(ds.sem, ds.val), reads, writes)

    def emit(self, block):
        finals = [(d.sem, d.val) for d in self.dsems if d.val > 0]

        def run(engobj, name, tail=False):
            for waits, fn, inc in self.q[name]:
                for s, v in waits:
                    engobj.wait_ge(s, v)
                fn(engobj).then_inc(inc[0], inc[1])
            if tail:
                for s, v in finals:
                    engobj.wait_ge(s, v)
                for e in ENGS:
                    if e != name and self.cnt[e] > 0:
                        engobj.wait_ge(self.sem[e], self.cnt[e])

        @block.sync
        def _(eng):
            run(eng, "sp", tail=True)

        @block.tensor
        def _(eng):
            run(eng, "pe")

        @block.scalar
        def _(eng):
            run(eng, "act")

        @block.vector
        def _(eng):
            run(eng, "dve")

        @block.gpsimd
        def _(eng):
            run(eng, "pool")


class Ring:
    def __init__(self, bufs, S=None, with_dsem=False, excl=False):
        self.bufs = bufs
        self.res = [Res(excl) for _ in bufs]
        self.ds = [S.dsem() for _ in bufs] if with_dsem else None
        self.i = -1

    def next(self):
        self.i = (self.i + 1) % len(self.bufs)
        if self.ds is not None:
            return self.bufs[self.i], self.res[self.i], self.ds[self.i]
        return self.bufs[self.i], self.res[self.i]


ALL_PHASES = (1, 2, 3, 4, 5)


def build_program(debug=(), phases=ALL_PHASES):
    nc = bass.Bass("TRN2", target_bir_lowering=False)

    def din(name, shape, dt=F32):
        return nc.dram_tensor(name, list(shape), dt, kind="ExternalInput").ap()

    def dscr(name, shape, dt=BF16):
        kind = "ExternalOutput" if name in debug else "Internal"
        return nc.dram_tensor(name, list(shape), dt, kind=kind).ap()

    x_loc = din("x_loc", [LT * 128, D])
    norm1_w = din("norm1_w", [1, D])
    w_in = din("w_in", [D, IN_COLS])
    ret_norm_w = din("ret_norm_w", [1, D])
    w_ret_up = din("w_ret_up", [D, D])
    cmp_peT = {"k": din("cmp_peT_k", [128, 32]), "v": din("cmp_peT_v", [128, 32])}
    cmp_w1 = {"k": din("cmp_w1_k", [4096, 256]), "v": din("cmp_w1_v", [4096, 256])}
    cmp_w2 = {"k": din("cmp_w2_k", [256, 128]), "v": din("cmp_w2_v", [256, 128])}
    w_nsa_up = din("w_nsa_up", [D, D])
    w_out = din("w_out", [D, D])
    norm2_w = din("norm2_w", [1, D])
    w_ffn_up = din("w_ffn_up", [D, 2 * DFF])
    conv_wT = din("conv_wT", [128, 88, 3])
    conv_bT = din("conv_bT", [128, 88])
    w_ffn_down = din("w_ffn_down", [DFF, D])
    final_norm_w = din("final_norm_w", [1, D])
    t_cs = din("t_cs", [128, LT, 128])
    t_csq = din("t_csq", [128, NQ, 128])
    t_decT = din("t_decT", [128, 8, 128])
    t_qdec = din("t_qdec", [128, 8, 128])
    t_kdec = din("t_kdec", [128, 8])
    t_padb = din("t_padb", [128, LT])
    t_cmpb = din("t_cmpb", [128, 2])
    t_cm = din("t_cm", [512, 128])
    t_keep = din("t_keep", [NQ, 128, 64])
    t_add = din("t_add", [NQ, 128, 64])
    t_exp = din("t_exp", [64, LT, 128])
    t_caus = din("t_caus", [128, 128])
    t_acaus = din("t_acaus", [128, 128])
    t_ident = din("t_ident", [128, 128])
    t_ov = din("t_ov", [128, 2, 65])
    t_halo = din("t_halo", [128, 1])

    out = nc.dram_tensor("out", [16 * 128, D], F32, kind="ExternalOutput").ap()

    RK = dscr("RK", [LT * 128, 1024])
    RV = dscr("RV", [LT * 128, 2048])
    RQ = dscr("RQ", [NQT, 1024])
    RG = dscr("RG", [NQT, 2048])
    NQT_ = dscr("NQT", [16, 128, NQT])
    KCT = dscr("KCT", [2, 128, LT * 128])
    VCT = dscr("VCT", [2, 128, LT * 128])
    KST = dscr("KST", [2, 128, LT * 128])
    KWT = dscr("KWT", [2, 128, LT * 128])
    VS = dscr("VS", [LT * 128, 256])
    VW = dscr("VW", [LT * 128, 256])
    NGT = dscr("NGT", [48, NQT], F32)
    MGT = dscr("MGT", [32, 128, NQT])
    RETGT = dscr("RETGT", [16, 128, NQT])
    NSAT = dscr("NSAT", [16, 128, NQT])
    M1T = dscr("M1T", [16, 128, NQT])
    MRGT = dscr("MRGT", [16, 128, NQT])
    H1 = dscr("H1", [NQT, D], F32)
    XN2T = dscr("XN2T", [16, 128, NQT])
    ACTT = dscr("ACTT", [16, 128, 44, 128])
    H2 = dscr("H2", [2048, D], F32)

    w_in_v = w_in.rearrange("(c p) n -> p c n", p=128)

    def phase01():
        with ExitStack() as st01:
            xnT = st01.enter_context(nc.sbuf_tensor("xnT", [128, 16, LT * 128], BF16))
            identb = st01.enter_context(nc.sbuf_tensor("identb", [128, 128], BF16))

            with ExitStack() as st:
                xin = [st.enter_context(nc.sbuf_tensor(f"p0_x{i}", [128, D], F32)) for i in range(2)]
                sq = st.enter_context(nc.sbuf_tensor("p0_sq", [128, D], BF16))
                xnb = [st.enter_context(nc.sbuf_tensor(f"p0_xn{i}", [128, D], BF16)) for i in range(2)]
                wbc = st.enter_context(nc.sbuf_tensor("p0_wbc", [128, D], F32))
                ss = [st.enter_context(nc.sbuf_tensor(f"p0_ss{i}", [128, 1], F32)) for i in range(2)]
                rs = [st.enter_context(nc.sbuf_tensor(f"p0_rs{i}", [128, 1], F32)) for i in range(2)]
                pt = [st.enter_context(nc.psum_tensor(f"p0_pt{i}", [128, D], BF16)) for i in range(2)]
                S = Sched(nc, st, "p0")
                block = st.enter_context(nc.Block())
                r_c = Res()
                d_c = S.dsem()
                S.dma("sp", lambda e: e.dma_start(out=wbc[:], in_=norm1_w.partition_broadcast(128)), d_c, writes=[r_c])
                r_id = Res()
                S.dma("pool", lambda e: e.dma_start(out=identb[:], in_=t_ident), S.dsem(), writes=[r_id])
                Rx = Ring(xin, S, True)
                Rxn = Ring(xnb)
                Rss = Ring(ss)
                Rrs = Ring(rs)
                Rpt = Ring(pt, excl=True)
                r_sq = Res()
                r_xnT = Res()
                def p0_front(t):
                    xb, rx, dx = Rx.next()
                    S.dma("sp", lambda e, xb=xb, t=t: e.dma_start(out=xb[:], in_=x_loc[t * 128:(t + 1) * 128, :]),
                          dx, writes=[rx])
                    sb, rss = Rss.next()
                    S.op("act", lambda e, xb=xb, sb=sb: e.activation(out=sq[:], in_=xb[:], func=AF.Square,
                                                                      accum_out=sb[:]),
                         reads=[rx], writes=[r_sq, rss])
                    rb, rrs = Rrs.next()
                    S.op("act", lambda e, sb=sb, rb=rb: e.activation(out=rb[:], in_=sb[:], func=AF.Sqrt,
                                                                     scale=1.0 / D, bias=EPS),
                         reads=[rss], writes=[rrs])
                    S.op("dve", lambda e, rb=rb: e.reciprocal(out=rb[:], in_=rb[:]),
                         reads=[rrs], writes=[rrs])
                    xn, rxn = Rxn.next()
                    S.op("dve", lambda e, xn=xn, xb=xb, rb=rb: e.scalar_tensor_tensor(
                        out=xn[:], in0=xb[:], scalar=rb[:], in1=wbc[:], op0=ALU.mult, op1=ALU.mult),
                        reads=[rx, rrs, r_c], writes=[rxn])
                    return xn, rxn

                def p0_back(t, xn, rxn):
                    pb, rp = Rpt.next()
                    for c in range(16):
                        S.op("pe", lambda e, pb=pb, xn=xn, c=c: e.transpose(
                            out=pb[:, c * 128:(c + 1) * 128], in_=xn[:, c * 128:(c + 1) * 128], identity=identb[:]),
                            reads=[rxn, r_id], writes=[rp])
                    for hh in range(2):
                        eng = "act" if hh == 0 else "dve"
                        if eng == "act":
                            S.op("act", lambda e, pb=pb, t=t, hh=hh: e.activation(
                                out=xnT[:, hh * 8:(hh + 1) * 8, t * 128:(t + 1) * 128],
                                in_=pb[:, hh * 1024:(hh + 1) * 1024].rearrange("p (c q) -> p c q", c=8),
                                func=AF.Copy), reads=[rp], writes=[r_xnT])
                        else:
                            S.op("dve", lambda e, pb=pb, t=t, hh=hh: e.tensor_copy(
                                out=xnT[:, hh * 8:(hh + 1) * 8, t * 128:(t + 1) * 128],
                                in_=pb[:, hh * 1024:(hh + 1) * 1024].rearrange("p (c q) -> p c q", c=8)),
                                reads=[rp], writes=[r_xnT])

                pend0 = None
                for t in range(LT):
                    cur0 = (t,) + p0_front(t)
                    if pend0 is not None:
                        p0_back(*pend0)
                    pend0 = cur0
                p0_back(*pend0)
                S.emit(block)

            if 0 in phases and 1 not in phases:
                return

            with ExitStack() as st:
                wb = [st.enter_context(nc.sbuf_tensor(f"p1_w{i}", [128, 16, 512], BF16)) for i in range(2)]
                cs = st.enter_context(nc.sbuf_tensor("p1_cs", [128, LT, 128], F32))
                csq = st.enter_context(nc.sbuf_tensor("p1_csq", [128, NQ, 128], F32))
                xf = [st.enter_context(nc.sbuf_tensor(f"p1_xf{i}", [128, 512], F32)) for i in range(2)]
                tmp = [st.enter_context(nc.sbuf_tensor(f"p1_t{i}", [128, 4, 256], F32)) for i in range(2)]
                ob = [st.enter_context(nc.sbuf_tensor(f"p1_o{i}", [128, 512], BF16)) for i in range(3)]
                of = [st.enter_context(nc.sbuf_tensor(f"p1_of{i}", [48, 512], F32)) for i in range(2)]
                ps = [st.enter_context(nc.psum_tensor(f"p1_ps{i}", [128, 512], F32)) for i in range(4)]
                S = Sched(nc, st, "p1")
                block = st.enter_context(nc.Block())
                r_tab = Res()
                d_tab = S.dsem()
                S.dma("sp", lambda e: e.dma_start(out=cs[:], in_=t_cs), d_tab, writes=[r_tab])
                S.dma("sp", lambda e: e.dma_start(out=csq[:], in_=t_csq), d_tab, writes=[r_tab])
                Rw = Ring(wb, S, True)
                Rps = Ring(ps, excl=True)
                Rxf = Ring(xf)
                Rtmp = Ring(tmp)
                Rob = Ring(ob, S, True)
                Rof = Ring(of, S, True)
                r_x = Res()
                alt = [0]

                def load_w(col0, ncols):
                    w, rw, dw = Rw.next()
                    S.dma("pool", lambda e, w=w: e.dma_start(out=w[:, :, 0:ncols], in_=w_in_v[:, :, col0:col0 + ncols]),
                          dw, writes=[rw])
                    return w, rw

                def mm_tm(w, rw, t, c0, n):
                    p, rp = Rps.next()
                    for c in range(16):
                        S.op("pe", lambda e, p=p, c=c: e.matmul(
                            p[:, 0:n], lhsT=xnT[:, c, t * 128:(t + 1) * 128], rhs=w[:, c, c0:c0 + n],
                            start=(c == 0), stop=(c == 15)), reads=[rw, r_x], writes=[rp])
                    return p, rp

                def mm_fm(w, rw, tok0, ntok, c0, m):
                    p, rp = Rps.next()
                    for c in range(16):
                        S.op("pe", lambda e, p=p, c=c: e.matmul(
                            p[0:m, 0:ntok], lhsT=w[:, c, c0:c0 + m], rhs=xnT[:, c, tok0:tok0 + ntok],
                            start=(c == 0), stop=(c == 15)), reads=[rw, r_x], writes=[rp])
                    return p, rp

                def evac_store(p, rp, m, n, dst, func=None):
                    o, ro, do = Rob.next()
                    if func is not None:
                        S.op("act", lambda e: e.activation(out=o[0:m, 0:n], in_=p[0:m, 0:n], func=func),
                             reads=[rp], writes=[ro])
                    else:
                        alt[0] ^= 1
                        if alt[0]:
                            S.op("act", lambda e: e.activation(out=o[0:m, 0:n], in_=p[0:m, 0:n], func=AF.Copy),
                                 reads=[rp], writes=[ro])
                        else:
                            S.op("dve", lambda e: e.tensor_copy(out=o[0:m, 0:n], in_=p[0:m, 0:n]),
                                 reads=[rp], writes=[ro])
                    S.dma("sp", lambda e: e.dma_start(out=dst, in_=o[0:m, 0:n]), do, reads=[ro])

                def rotary(p, rp, ctab, ti, dst):
                    x_, rxf = Rxf.next()
                    S.op("act", lambda e: e.activation(out=x_[:], in_=p[:], func=AF.Copy), reads=[rp], writes=[rxf])
                    xv = x_[:].rearrange("p (h t d) -> p h t d", h=4, t=2)
                    cosb = ctab[:, ti, 0:64][:, None, :].to_broadcast([128, 4, 64])
                    sinb = ctab[:, ti, 64:128][:, None, :].to_broadcast([128, 4, 64])
                    tm_, rt = Rtmp.next()
                    o, ro, do = Rob.next()
                    ov = o[:].rearrange("p (h t d) -> p h t d", h=4, t=2)
                    x1 = xv[:, :, 0, :]
                    x2 = xv[:, :, 1, :]
                    S.op("pool", lambda e: e.tensor_tensor(out=tm_[:, :, 0:64], in0=x1, in1=cosb, op=ALU.mult),
                         reads=[rxf, r_tab], writes=[rt])
                    S.op("pool", lambda e: e.tensor_tensor(out=tm_[:, :, 64:128], in0=x2, in1=sinb, op=ALU.mult),
                         reads=[rxf, r_tab], writes=[rt])
                    S.op("dve", lambda e: e.tensor_tensor(out=tm_[:, :, 128:192], in0=x1, in1=sinb, op=ALU.mult),
                         reads=[rxf, r_tab], writes=[rt])
                    S.op("dve", lambda e: e.tensor_tensor(out=tm_[:, :, 192:256], in0=x2, in1=cosb, op=ALU.mult),
                         reads=[rxf, r_tab], writes=[rt])
                    S.op("pool", lambda e: e.tensor_tensor(out=ov[:, :, 0, :], in0=tm_[:, :, 0:64],
                                                           in1=tm_[:, :, 64:128], op=ALU.subtract),
                         reads=[rt], writes=[ro])
                    S.op("dve", lambda e: e.tensor_tensor(out=ov[:, :, 1, :], in0=tm_[:, :, 128:192],
                                                          in1=tm_[:, :, 192:256], op=ALU.add),
                         reads=[rt], writes=[ro])
                    S.dma("sp", lambda e: e.dma_start(out=dst, in_=o[:]), do, reads=[ro])

                qtiles = list(range(QT0, LT))
                qgroups = [(QT0 * 128, 128)] + [((16 + 4 * g) * 128, 512) for g in range(4)]
                agroups = [(g * 512, 512) for g in range(8)]

                for blk in range(2):
                    w, rw = load_w(1024 + blk * 512, 512)
                    for t in range(LT):
                        p, rp = mm_tm(w, rw, t, 0, 512)
                        rotary(p, rp, cs, t, RK[t * 128:(t + 1) * 128, blk * 512:(blk + 1) * 512])
                for blk in range(2):
                    w, rw = load_w(blk * 512, 512)
                    for t in qtiles:
                        p, rp = mm_tm(w, rw, t, 0, 512)
                        qi = t - QT0
                        rotary(p, rp, csq, qi, RQ[qi * 128:(qi + 1) * 128, blk * 512:(blk + 1) * 512])
                for blk in range(4):
                    w, rw = load_w(2048 + blk * 512, 512)
                    for t in range(LT):
                        p, rp = mm_tm(w, rw, t, 0, 512)
                        evac_store(p, rp, 128, 512, RV[t * 128:(t + 1) * 128, blk * 512:(blk + 1) * 512])
                w, rw = load_w(8192, 512)
                for (dst, c0) in ((KCT, 0), (VCT, 256)):
                    for g in range(2):
                        for tok0, ntok in agroups:
                            p, rp = mm_fm(w, rw, tok0, ntok, c0 + g * 128, 128)
                            evac_store(p, rp, 128, ntok, dst[g, :, tok0:tok0 + ntok])
                for (col0, dfm, dtm) in ((8704, KST, VS), (9216, KWT, VW)):
                    w, rw = load_w(col0, 512)
                    for g in range(2):
                        for tok0, ntok in agroups:
                            p, rp = mm_fm(w, rw, tok0, ntok, g * 128, 128)
                            evac_store(p, rp, 128, ntok, dfm[g, :, tok0:tok0 + ntok])
                    for t in range(LT):
                        p, rp = mm_tm(w, rw, t, 256, 256)
                        evac_store(p, rp, 128, 256, dtm[t * 128:(t + 1) * 128, :])
                for blk in range(4):
                    w, rw = load_w(6144 + blk * 512, 512)
                    for ch in range(4):
                        for tok0, ntok in qgroups:
                            p, rp = mm_fm(w, rw, tok0, ntok, ch * 128, 128)
                            q0 = tok0 - QT0 * 128
                            evac_store(p, rp, 128, ntok, NQT_[blk * 4 + ch, :, q0:q0 + ntok])
                for blk in range(4):
                    w, rw = load_w(4096 + blk * 512, 512)
                    for t in qtiles:
                        p, rp = mm_tm(w, rw, t, 0, 512)
                        qi = t - QT0
                        evac_store(p, rp, 128, 512, RG[qi * 128:(qi + 1) * 128, blk * 512:(blk + 1) * 512],
                                   func=AF.Silu)
                for blk in range(8):
                    w, rw = load_w(9776 + blk * 512, 512)
                    for ch in range(4):
                        for tok0, ntok in qgroups:
                            p, rp = mm_fm(w, rw, tok0, ntok, ch * 128, 128)
                            q0 = tok0 - QT0 * 128
                            evac_store(p, rp, 128, ntok, MGT[blk * 4 + ch, :, q0:q0 + ntok], func=AF.Sigmoid)
                w, rw = load_w(9728, 48)
                for tok0, ntok in qgroups:
                    p, rp = mm_fm(w, rw, tok0, ntok, 0, 48)
                    q0 = tok0 - QT0 * 128
                    o, ro, do = Rof.next()
                    S.op("act", lambda e, o=o, p=p, ntok=ntok: e.activation(out=o[0:48, 0:ntok], in_=p[0:48, 0:ntok],
                                                                           func=AF.Sigmoid), reads=[rp], writes=[ro])
                    S.dma("sp", lambda e, o=o, q0=q0, ntok=ntok: e.dma_start(out=NGT[:, q0:q0 + ntok],
                                                                            in_=o[0:48, 0:ntok]), do, reads=[ro])
                S.emit(block)

    if 1 in phases:
        phase01()

    def phase2():
        gam = [1.0 - 2.0 ** (-5.0 - h) for h in range(8)]
        with ExitStack() as st:
            sb = lambda name, shape, dt: st.enter_context(nc.sbuf_tensor(name, list(shape), dt))
            identb = sb("p2_id", [128, 128], BF16)
            decT = sb("p2_decT", [128, 8, 128], F32)
            qdec = sb("p2_qdec", [128, 8, 128], F32)
            kdec = sb("p2_kdec", [128, 8], F32)
            retw = sb("p2_retw", [128, D], F32)
            kc_ = [sb(f"p2_k{i}", [128, 1024], BF16) for i in range(2)]
            vc_ = [sb(f"p2_v{i}", [128, 2048], BF16) for i in range(2)]
            qc_ = [sb(f"p2_q{i}", [128, 1024], BF16) for i in range(2)]
            gc_ = [sb(f"p2_g{i}", [128, 2048], BF16) for i in range(2)]
            kd_ = [sb(f"p2_kd{i}", [128, 1024], BF16) for i in range(2)]
            kT_ = [sb(f"p2_kT{i}", [128, 1024], BF16) for i in range(2)]
            qT_ = [sb(f"p2_qT{i}", [128, 1024], BF16) for i in range(2)]
            qTd_ = [sb(f"p2_qTd{i}", [128, 1024], BF16) for i in range(2)]
            ST_ = [sb(f"p2_ST{i}", [128, 512], BF16) for i in range(2)]
            stf = sb("p2_stf", [128, 2048], F32)
            stb = [sb(f"p2_stb{i}", [128, 2048], BF16) for i in range(2)]
            junk = sb("p2_junk", [128, 256], BF16)
            ssq_ = [sb(f"p2_ssq{i}", [128, 4], F32) for i in range(2)]
            tmpf_ = [sb(f"p2_tmpf{i}", [128, 1024], F32) for i in range(2)]
            gat_ = [sb(f"p2_gat{i}", [128, 2048], BF16) for i in range(2)]
            gT_ = [sb(f"p2_gT{i}", [128, 16, 128], BF16) for i in range(2)]
            pkT = st.enter_context(nc.psum_tensor("p2_pkT", [128, 1024], BF16))
            pqT = st.enter_context(nc.psum_tensor("p2_pqT", [128, 1024], BF16))
            pS = st.enter_context(nc.psum_tensor("p2_pS", [128, 512], F32))
            pO = st.enter_context(nc.psum_tensor("p2_pO", [128, 1024], F32))
            pKV = st.enter_context(nc.psum_tensor("p2_pKV", [128, 1024], F32))
            pGT = st.enter_context(nc.psum_tensor("p2_pGT", [128, 1024], BF16))
            S = Sched(nc, st, "p2")
            block = st.enter_context(nc.Block())
            r_tab = Res()
            d_tab = S.dsem()
            r_id = Res()
            S.dma("pool", lambda e: e.dma_start(out=identb[:], in_=t_ident), S.dsem(), writes=[r_id])
            S.dma("sp", lambda e: e.dma_start(out=decT[:], in_=t_decT), d_tab, writes=[r_tab])
            S.dma("sp", lambda e: e.dma_start(out=qdec[:], in_=t_qdec), d_tab, writes=[r_tab])
            S.dma("sp", lambda e: e.dma_start(out=kdec[:], in_=t_kdec), d_tab, writes=[r_tab])
            S.dma("sp", lambda e: e.dma_start(out=retw[:], in_=ret_norm_w.partition_broadcast(128)), d_tab,
                  writes=[r_tab])
            Rk, Rv, Rq, Rg = Ring(kc_, S, True), Ring(vc_, S, True), Ring(qc_, S, True), Ring(gc_, S, True)
            Rkd, RkT, RqT, RqTd, RST = Ring(kd_), Ring(kT_), Ring(qT_), Ring(qTd_), Ring(ST_)
            Rssq, Rtmpf, Rgat = Ring(ssq_), Ring(tmpf_), Ring(gat_)
            RgT = Ring(gT_, S, True)
            r_stf = Res()
            r_stb = [Res(), Res()]
            r_junk = Res()
            r_pkT, r_pqT, r_pS, r_pO, r_pKV, r_pGT = (Res(True) for _ in range(6))
            S.op("dve", lambda e: e.memset(stf[:], 0.0), writes=[r_stf])
            S.op("pool", lambda e: e.memset(stb[0][:], 0.0), writes=[r_stb[0]])
            for c in range(LT):
                k_, rk, dk = Rk.next()
                v_, rv, dv = Rv.next()
                S.dma("sp", lambda e, k_=k_, c=c: e.dma_start(out=k_[:], in_=RK[c * 128:(c + 1) * 128, :]), dk, writes=[rk])
                S.dma("sp", lambda e, v_=v_, c=c: e.dma_start(out=v_[:], in_=RV[c * 128:(c + 1) * 128, :]), dv, writes=[rv])
                sbc, rsb = stb[c % 2], r_stb[c % 2]
                sbn, rsn = stb[(c + 1) % 2], r_stb[(c + 1) % 2]
                if c >= QT0 and os.environ.get('P2_OUT', '1') == '1':
                    qi = c - QT0
                    q_, rq, dq = Rq.next()
                    g_, rg, dg = Rg.next()
                    S.dma("sp", lambda e, q_=q_, qi=qi: e.dma_start(out=q_[:], in_=RQ[qi * 128:(qi + 1) * 128, :]), dq, writes=[rq])
                    S.dma("sp", lambda e, g_=g_, qi=qi: e.dma_start(out=g_[:], in_=RG[qi * 128:(qi + 1) * 128, :]), dg, writes=[rg])
                    for h in range(8):
                        S.op("pe", lambda e, k_=k_, h=h: e.transpose(out=pkT[:, h * 128:(h + 1) * 128],
                                                                     in_=k_[:, h * 128:(h + 1) * 128], identity=identb[:]),
                             reads=[rk, r_id], writes=[r_pkT])
                    kT, rkT = RkT.next()
                    S.op("act", lambda e, kT=kT: e.activation(out=kT[:], in_=pkT[:], func=AF.Copy), reads=[r_pkT], writes=[rkT])
                    for h in range(8):
                        S.op("pe", lambda e, q_=q_, h=h: e.transpose(out=pqT[:, h * 128:(h + 1) * 128],
                                                                     in_=q_[:, h * 128:(h + 1) * 128], identity=identb[:]),
                             reads=[rq, r_id], writes=[r_pqT])
                    qT, rqT = RqT.next()
                    qTd, rqTd = RqTd.next()
                    S.op("act", lambda e, qT=qT: e.activation(out=qT[:], in_=pqT[:], func=AF.Copy), reads=[r_pqT], writes=[rqT])
                    S.op("dve", lambda e, qTd=qTd: e.tensor_tensor(
                        out=qTd[:].rearrange("p (h n) -> p h n", h=8), in0=pqT[:].rearrange("p (h n) -> p h n", h=8),
                        in1=qdec[:], op=ALU.mult), reads=[r_pqT, r_tab], writes=[rqTd])
                    gat, rgat = Rgat.next()
                    lvl = int(os.environ.get('P2_LVL', '9'))
                    for hg in (range(2) if lvl >= 2 else []):
                        for j in range(4):
                            h = hg * 4 + j
                            S.op("pe", lambda e, kT=kT, qT=qT, h=h, j=j: e.matmul(
                                pS[:, j * 128:(j + 1) * 128], lhsT=kT[:, h * 128:(h + 1) * 128],
                                rhs=qT[:, h * 128:(h + 1) * 128], start=True, stop=True),
                                reads=[rkT, rqT], writes=[r_pS])
                        ST, rST = RST.next()
                        S.op("dve", lambda e, ST=ST, hg=hg: e.tensor_tensor(
                            out=ST[:].rearrange("p (h n) -> p h n", h=4), in0=pS[:].rearrange("p (h n) -> p h n", h=4),
                            in1=decT[:, hg * 4:(hg + 1) * 4, :], op=ALU.mult), reads=[r_pS, r_tab], writes=[rST])
                        for j in range(4):
                            h = hg * 4 + j
                            S.op("pe", lambda e, ST=ST, v_=v_, h=h, j=j: e.matmul(
                                pO[:, j * 256:(j + 1) * 256], lhsT=ST[:, j * 128:(j + 1) * 128],
                                rhs=v_[:, h * 256:(h + 1) * 256], start=True, stop=False),
                                reads=[rST, rv], writes=[r_pO])
                            S.op("pe", lambda e, qTd=qTd, sbc=sbc, h=h, j=j: e.matmul(
                                pO[:, j * 256:(j + 1) * 256], lhsT=qTd[:, h * 128:(h + 1) * 128],
                                rhs=sbc[:, h * 256:(h + 1) * 256], start=False, stop=True),
                                reads=[rqTd, rsb], writes=[r_pO])
                        if lvl < 3:
                            continue
                        ssq, rssq = Rssq.next()
                        for j in range(4):
                            S.op("act", lambda e, ssq=ssq, j=j: e.activation(
                                out=junk[:], in_=pO[:, j * 256:(j + 1) * 256], func=AF.Square, accum_out=ssq[:, j:j + 1]),
                                reads=[r_pO], writes=[r_junk, rssq])
                        S.op("act", lambda e, ssq=ssq: e.activation(out=ssq[:], in_=ssq[:], func=AF.Sqrt,
                                                                    scale=1.0 / 256, bias=EPS), reads=[rssq], writes=[rssq])
                        S.op("dve", lambda e, ssq=ssq: e.reciprocal(out=ssq[:], in_=ssq[:]), reads=[rssq], writes=[rssq])
                        tmpf, rtf = Rtmpf.next()
                        for j in range(4):
                            h = hg * 4 + j
                            S.op("dve", lambda e, tmpf=tmpf, ssq=ssq, h=h, j=j: e.scalar_tensor_tensor(
                                out=tmpf[:, j * 256:(j + 1) * 256], in0=pO[:, j * 256:(j + 1) * 256],
                                scalar=ssq[:, j:j + 1], in1=retw[:, h * 256:(h + 1) * 256], op0=ALU.mult, op1=ALU.mult),
                                reads=[r_pO, rssq, r_tab], writes=[rtf])
                        if lvl < 4:
                            continue
                        S.op("pool", lambda e, gat=gat, tmpf=tmpf, g_=g_, hg=hg: e.tensor_tensor(
                            out=gat[:, hg * 1024:(hg + 1) * 1024], in0=tmpf[:], in1=g_[:, hg * 1024:(hg + 1) * 1024],
                            op=ALU.mult), reads=[rtf, rg], writes=[rgat])
                    if lvl < 5:
                        continue
                    gT, rgT, dgT = RgT.next()
                    for hh in range(2):
                        for cc in range(8):
                            ch = hh * 8 + cc
                            S.op("pe", lambda e, gat=gat, ch=ch, cc=cc: e.transpose(
                                out=pGT[:, cc * 128:(cc + 1) * 128], in_=gat[:, ch * 128:(ch + 1) * 128],
                                identity=identb[:]), reads=[rgat, r_id], writes=[r_pGT])
                        if hh == 0:
                            S.op("act", lambda e, gT=gT: e.activation(
                                out=gT[:, 0:8, :], in_=pGT[:].rearrange("p (c q) -> p c q", c=8), func=AF.Copy),
                                reads=[r_pGT], writes=[rgT])
                        else:
                            S.op("dve", lambda e, gT=gT: e.tensor_copy(
                                out=gT[:, 8:16, :], in_=pGT[:].rearrange("p (c q) -> p c q", c=8)),
                                reads=[r_pGT], writes=[rgT])
                    S.dma("pool", lambda e, gT=gT, qi=qi: e.dma_start(
                        out=RETGT[:, :, qi * 128:(qi + 1) * 128].rearrange("c p q -> p c q"), in_=gT[:]),
                        dgT, reads=[rgT])
                if c < LT - 1 and os.environ.get('P2_STATE', '1') == '1':
                    kd, rkd = Rkd.next()
                    S.op("pool", lambda e, kd=kd, k_=k_: e.tensor_tensor(
                        out=kd[:].rearrange("p (h d) -> p h d", h=8), in0=k_[:].rearrange("p (h d) -> p h d", h=8),
                        in1=kdec[:, :, None].to_broadcast([128, 8, 128]), op=ALU.mult),
                        reads=[rk, r_tab], writes=[rkd])
                    for hg in range(2):
                        for j in range(4):
                            h = hg * 4 + j
                            S.op("pe", lambda e, kd=kd, v_=v_, h=h, j=j: e.matmul(
                                pKV[:, j * 256:(j + 1) * 256], lhsT=kd[:, h * 128:(h + 1) * 128],
                                rhs=v_[:, h * 256:(h + 1) * 256], start=True, stop=True),
                                reads=[rkd, rv], writes=[r_pKV])
                        for j in range(4):
                            h = hg * 4 + j
                            S.op("dve", lambda e, h=h, j=j: e.scalar_tensor_tensor(
                                out=stf[:, h * 256:(h + 1) * 256], in0=stf[:, h * 256:(h + 1) * 256],
                                scalar=float(gam[h] ** 128), in1=pKV[:, j * 256:(j + 1) * 256],
                                op0=ALU.mult, op1=ALU.add), reads=[r_pKV, r_stf], writes=[r_stf])
                    S.op("act", lambda e, sbn=sbn: e.activation(out=sbn[:], in_=stf[:], func=AF.Copy),
                         reads=[r_stf], writes=[rsn])
            S.emit(block)

    if 2 in phases:
        phase2()

    KCMPT = dscr("KCMPT", [2, 128, 256])
    VCMP = dscr("VCMP", [2, 256, 128])

    def phase3a():
        with ExitStack() as st:
            sb = lambda name, shape, dt: st.enter_context(nc.sbuf_tensor(name, list(shape), dt))
            xc_ = [sb(f"p3a_xc{i}", [128, LT * 128], BF16) for i in range(2)]
            w1b = sb("p3a_w1", [128, 32, 256], BF16)
            peT = sb("p3a_pe", [128, 32], BF16)
            w2b = sb("p3a_w2", [128, 2, 128], BF16)
            cb = sb("p3a_cb", [128, 2], F32)
            gel_ = [sb(f"p3a_gel{i}", [128, 2, 256], BF16) for i in range(2)]
            og_ = [sb(f"p3a_og{i}", [128, 256], BF16) for i in range(2)]
            pc = st.enter_context(nc.psum_tensor("p3a_pc", [128, 2], F32))
            ph_ = [st.enter_context(nc.psum_tensor(f"p3a_ph{i}", [128, 256], F32)) for i in range(2)]
            po_ = [st.enter_context(nc.psum_tensor(f"p3a_po{i}", [128, 256], F32)) for i in range(2)]
            S = Sched(nc, st, "p3a")
            block = st.enter_context(nc.Block())
            Rxc = Ring(xc_, S, True)
            Rgel = Ring(gel_)
            Rog = Ring(og_, S, True)
            Rph = Ring(ph_, excl=True)
            Rpo = Ring(po_, excl=True)
            r_w, r_cb, r_pc = Res(), Res(), Res(True)
            d_w = S.dsem()
            for g_ in gel_:
                S.op("dve", lambda e, g_=g_: e.memset(g_[:], 0.0), writes=[Rgel.res[gel_.index(g_)]])
            for kv in ("k", "v"):
                S.dma("pool", lambda e, kv=kv: e.dma_start(
                    out=w1b[:], in_=cmp_w1[kv].rearrange("(l d) j -> d l j", d=128)), d_w, writes=[r_w])
                S.dma("pool", lambda e, kv=kv: e.dma_start(out=peT[:], in_=cmp_peT[kv]), d_w, writes=[r_w])
                S.dma("pool", lambda e, kv=kv: e.dma_start(
                    out=w2b[:], in_=cmp_w2[kv].rearrange("(c p) d -> p c d", p=128)), d_w, writes=[r_w])
                for jc in range(2):
                    for l in range(32):
                        S.op("pe", lambda e, jc=jc, l=l: e.matmul(
                            pc[:, jc:jc + 1], lhsT=w1b[:, l, jc * 128:(jc + 1) * 128], rhs=peT[:, l:l + 1],
                            start=(l == 0), stop=(l == 31)), reads=[r_w], writes=[r_pc])
                S.op("act", lambda e: e.activation(out=cb[:], in_=pc[:], func=AF.Copy), reads=[r_pc], writes=[r_cb])
                src = KCT if kv == "k" else VCT
                for g in range(2):
                    xc, rxc, dxc = Rxc.next()
                    S.dma("sp", lambda e, xc=xc, g=g, src=src: e.dma_start(out=xc[:], in_=src[g]), dxc, writes=[rxc])
                    gel, rgel = Rgel.next()
                    for jc in range(2):
                        ph, rph = Rph.next()
                        for l in range(32):
                            S.op("pe", lambda e, ph=ph, xc=xc, jc=jc, l=l: e.matmul(
                                ph[:, 0:255], lhsT=w1b[:, l, jc * 128:(jc + 1) * 128],
                                rhs=xc[:, l:l + 16 * 254 + 1:16], start=(l == 0), stop=(l == 31)),
                                reads=[r_w, rxc], writes=[rph])
                        S.op("act", lambda e, ph=ph, gel=gel, jc=jc: e.activation(
                            out=gel[:, jc, 0:255], in_=ph[:, 0:255], func=AF.Gelu_apprx_tanh, bias=cb[:, jc:jc + 1]),
                            reads=[rph, r_cb], writes=[rgel])
                    if kv == "k":
                        po, rpo = Rpo.next()
                        for jc in range(2):
                            S.op("pe", lambda e, po=po, gel=gel, jc=jc: e.matmul(
                                po[:, :], lhsT=w2b[:, jc, :], rhs=gel[:, jc, :], start=(jc == 0), stop=(jc == 1)),
                                reads=[r_w, rgel], writes=[rpo])
                        og, rog, dog = Rog.next()
                        S.op("dve", lambda e, og=og, po=po: e.tensor_copy(out=og[:], in_=po[:]), reads=[rpo], writes=[rog])
                        S.dma("sp", lambda e, og=og, g=g: e.dma_start(out=KCMPT[g], in_=og[:]), dog, reads=[rog])
                    else:
                        for nk in range(2):
                            po, rpo = Rpo.next()
                            for jc in range(2):
                                S.op("pe", lambda e, po=po, gel=gel, jc=jc, nk=nk: e.matmul(
                                    po[:, 0:128], lhsT=gel[:, jc, nk * 128:(nk + 1) * 128], rhs=w2b[:, jc, :],
                                    start=(jc == 0), stop=(jc == 1)), reads=[r_w, rgel], writes=[rpo])
                            og, rog, dog = Rog.next()
                            S.op("dve", lambda e, og=og, po=po: e.tensor_copy(out=og[:, 0:128], in_=po[:, 0:128]),
                                 reads=[rpo], writes=[rog])
                            S.dma("sp", lambda e, og=og, g=g, nk=nk: e.dma_start(
                                out=VCMP[g, nk * 128:(nk + 1) * 128, :], in_=og[:, 0:128]), dog, reads=[rog])
            S.emit(block)

    def phase3b(sfx=""):
        with ExitStack() as st:
            sb = lambda name, shape, dt: st.enter_context(nc.sbuf_tensor(name + sfx, list(shape), dt))
            identb = sb("p3_idb", [128, 128], BF16)
            identf = sb("p3_idf", [128, 128], F32)
            onesb = sb("p3_ones", [128, 128], BF16)
            causb = sb("p3_caus", [128, 4, 128], BF16)
            acausb = sb("p3_acaus", [128, 4, 128], BF16)
            expt = sb("p3_expt", [64, LT, 128], BF16)
            ovb = sb("p3_ov", [128, 2, 65], BF16)
            padb = sb("p3_padb", [128, LT], F32)
            cmpb = sb("p3_cmpb", [128, 2], F32)
            kcm = sb("p3_kcm", [128, 2, 256], BF16)
            vcm = sb("p3_vcm", [128, 2, 2, 128], BF16)
            kst = sb("p3_kst", [128, 2, LT * 128], BF16)
            kwt = sb("p3_kwt", [128, 2, LT * 128], BF16)
            vs = sb("p3_vs", [128, LT, 256], BF16)
            vw = sb("p3_vw", [128, LT, 256], BF16)
            qT_ = [sb(f"p3_qT{i}", [128, 16, 128], BF16) for i in range(2)]
            gbc_ = [sb(f"p3_gbc{i}", [128, 24, 128], F32) for i in range(2)]
            keep_ = [sb(f"p3_keep{i}", [128, 64], F32) for i in range(2)]
            addt_ = [sb(f"p3_add{i}", [128, 64], F32) for i in range(2)]
            cm_ = [sb(f"p3_cm{i}", [128, 2, 128], F32) for i in range(2)]
            Ec_ = [sb(f"p3_Ec{i}", [128, 1024], BF16) for i in range(2)]
            E_ = [sb(f"p3_E{i}", [128, 1024], BF16) for i in range(3)]
            Sm_ = [sb(f"p3_Sm{i}", [128, 1024], F32) for i in range(2)]
            fac_ = [sb(f"p3_fac{i}", [128, 1024], F32) for i in range(2)]
            tmp_ = [sb(f"p3_tmp{i}", [128, 1024], F32) for i in range(2)]
            acc_ = [sb(f"p3_acc{i}", [128, 1024], F32) for i in range(2)]
            accb_ = [sb(f"p3_accb{i}", [128, 1024], BF16) for i in range(2)]
            oc_ = [sb(f"p3_oc{i}", [128, 1024], F32) for i in range(2)]
            Usb_ = [sb(f"p3_Usb{i}", [128, 2, 260], F32) for i in range(2)]
            rs8 = sb("p3_rs8", [128, 8], F32)
            imp = sb("p3_imp", [128, 64], F32)
            imp2 = sb("p3_imp2", [128, 64], F32)
            m8 = sb("p3_m8", [128, 8], F32)
            selb = sb("p3_selb", [128, 64], F32)
            selbT_ = [sb(f"p3_selbT{i}", [64, 4, 128], BF16) for i in range(2)]
            pS_ = [st.enter_context(nc.psum_tensor(f"p3_pS{i}" + sfx, [128, 1024], F32)) for i in range(2)]
            pO = st.enter_context(nc.psum_tensor("p3_pO" + sfx, [128, 1024], F32))
            pSum = st.enter_context(nc.psum_tensor("p3_pSum" + sfx, [128, 1024], F32))
            S = Sched(nc, st, "p3" + sfx)
            block = st.enter_context(nc.Block())
            r_c = Res()
            r_k = Res()
            d_cp, d_cs = S.dsem(), S.dsem()
            for dst, src in ((identb[:], t_ident), (expt[:], t_exp), (ovb[:], t_ov)):
                S.dma("pool", lambda e, dst=dst, src=src: e.dma_start(out=dst, in_=src), d_cp, writes=[r_c])
            for r4 in range(4):
                S.dma("pool", lambda e, r4=r4: e.dma_start(out=causb[:, r4, :], in_=t_caus), d_cp, writes=[r_c])
                S.dma("pool", lambda e, r4=r4: e.dma_start(out=acausb[:, r4, :], in_=t_acaus), d_cp, writes=[r_c])
            S.op("pool", lambda e: e.memset(onesb[:], 1.0), reads=[], writes=[r_c])
            for dst, src in ((identf[:], t_ident), (padb[:], t_padb), (cmpb[:], t_cmpb),
                             (kcm[:], KCMPT.rearrange("g d n -> d g n")),
                             (vcm[:], VCMP.rearrange("g (k p) d -> p g k d", p=128))):
                S.dma("sp", lambda e, dst=dst, src=src: e.dma_start(out=dst, in_=src), d_cs, writes=[r_k])
            r_big = {}
            for nm, dst, src in (("kwt", kwt[:], KWT.rearrange("g d t -> d g t")),
                                 ("vw", vw[:], VW.rearrange("(t p) c -> p t c", p=128)),
                                 ("kst", kst[:], KST.rearrange("g d t -> d g t")),
                                 ("vs", vs[:], VS.rearrange("(t p) c -> p t c", p=128))):
                r_big[nm] = Res()
                S.dma("sp", lambda e, dst=dst, src=src: e.dma_start(out=dst, in_=src), S.dsem(), writes=[r_big[nm]])
            RqT, Rgbc = Ring(qT_, S, True), Ring(gbc_, S, True)
            Rkeep, Radd, Rcm = Ring(keep_, S, True), Ring(addt_, S, True), Ring(cm_, S, True)
            REc, RE, RSm, Rfac, Rtmp, Racc = Ring(Ec_), Ring(E_), Ring(Sm_), Ring(fac_), Ring(tmp_), Ring(acc_)
            Raccb = Ring(accb_, S, True)
            RselbT = Ring(selbT_)
            Roc, RUsb = Ring(oc_), Ring(Usb_)
            RpS = Ring(pS_, excl=True)
            r_pO, r_pSum = Res(True), Res(True)
            r_small = Res()
            d_dbg = S.dsem()
            dbg_t = {}
            if "DBGSEL" in debug:
                dbg_t = {"DBGSEL": dscr("DBGSEL", [NQ, 2, 128, 64], F32), "DBGIMP": dscr("DBGIMP", [NQ, 2, 128, 64], F32),
                         "DBGM8": dscr("DBGM8", [NQ, 2, 128, 8], F32)}

            def finalize(gbc, rgbc, br, acc, racc, first):
                oc, roc = Roc.next()
                S.op("act", lambda e: e.activation(out=oc[:], in_=pO[:], func=AF.Copy), reads=[r_pO], writes=[roc])
                fac, rfac = Rfac.next()
                S.op("dve", lambda e: e.tensor_scalar_max(out=fac[:], in0=pSum[:], scalar1=1e-30),
                     reads=[r_pSum], writes=[rfac])
                S.op("dve", lambda e: e.reciprocal(out=fac[:], in_=fac[:]), reads=[rfac], writes=[rfac])
                S.op("pool", lambda e: e.tensor_tensor(
                    out=fac[:].rearrange("p (h q) -> p h q", h=8), in0=fac[:].rearrange("p (h q) -> p h q", h=8),
                    in1=gbc[:, br::3, :], op=ALU.mult), reads=[rfac, rgbc], writes=[rfac])
                if first:
                    S.op("dve", lambda e: e.tensor_tensor(out=acc[:], in0=oc[:], in1=fac[:], op=ALU.mult),
                         reads=[roc, rfac], writes=[racc])
                else:
                    tmp, rtmp = Rtmp.next()
                    S.op("dve", lambda e: e.tensor_tensor(out=tmp[:], in0=oc[:], in1=fac[:], op=ALU.mult),
                         reads=[roc, rfac], writes=[rtmp])
                    S.op("pool", lambda e: e.tensor_tensor(out=acc[:], in0=acc[:], in1=tmp[:], op=ALU.add),
                         reads=[rtmp, racc], writes=[racc])

            def attend(kt_list, ksrc, vsrc, g, qTg, rq, selbT, rselbT, i):
                n = len(kt_list)

                def scores(kt):
                    pSx, rpS = RpS.next()
                    for hf in range(2):
                        extra = []
                        if selbT is not None:
                            extra.append((expt[:, kt, :], selbT[:].rearrange("s r q -> s (r q)"), [r_c, rselbT]))
                        if kt == i:
                            extra.append((identb[:], causb[:].rearrange("k r q -> k (r q)"), [r_c]))
                        if selbT is None and kt == i - 4:
                            extra.append((identb[:], acausb[:].rearrange("k r q -> k (r q)"), [r_c]))
                        S.op("pe", lambda e, hf=hf, ne=len(extra): e.matmul(
                            pSx[:, hf * 512:(hf + 1) * 512], lhsT=ksrc[:, g, kt * 128:(kt + 1) * 128],
                            rhs=qTg[:, hf * 512:(hf + 1) * 512], start=True, stop=(ne == 0)),
                            reads=[r_big["kst" if selbT is not None else "kwt"], rq], writes=[rpS])
                        for xi, (l_, r_, deps) in enumerate(extra):
                            S.op("pe", lambda e, hf=hf, l_=l_, r_=r_, last=(xi == len(extra) - 1): e.matmul(
                                pSx[:, hf * 512:(hf + 1) * 512], lhsT=l_, rhs=r_, start=False, stop=last),
                                reads=deps, writes=[rpS])
                    E, rE = RE.next()
                    S.op("act", lambda e: e.activation(
                        out=E[:], in_=pSx[:], func=AF.Exp, scale=SCALE, bias=padb[:, kt:kt + 1]),
                        reads=[rpS, r_k], writes=[rE])
                    return E, rE

                def values(idx, kt, E, rE):
                    for hf in range(2):
                        S.op("pe", lambda e, hf=hf: e.matmul(
                            pO[:, hf * 512:(hf + 1) * 512], lhsT=vsrc[:, kt, g * 128:(g + 1) * 128],
                            rhs=E[:, hf * 512:(hf + 1) * 512], start=(idx == 0), stop=(idx == n - 1)),
                            reads=[r_big["vs" if selbT is not None else "vw"], rE], writes=[r_pO])
                        S.op("pe", lambda e, hf=hf: e.matmul(
                            pSum[:, hf * 512:(hf + 1) * 512], lhsT=onesb[:],
                            rhs=E[:, hf * 512:(hf + 1) * 512], start=(idx == 0), stop=(idx == n - 1)),
                            reads=[r_c, rE], writes=[r_pSum])

                pend = None
                for idx, kt in enumerate(kt_list):
                    cur = (idx, kt) + scores(kt)
                    if pend is not None:
                        values(*pend)
                    pend = cur
                values(*pend)

            for i in range(QT0, LT):
                qi = i - QT0
                qT, rq, dq = RqT.next()
                S.dma("sp", lambda e, qT=qT, qi=qi: e.dma_start(
                    out=qT[:], in_=NQT_[:, :, qi * 128:(qi + 1) * 128].rearrange("h p q -> p h q")), dq, writes=[rq])
                keep, rkeep, dkeep = Rkeep.next()
                addt, radd, dadd = Radd.next()
                cm, rcm, dcm = Rcm.next()
                S.dma("sp", lambda e, keep=keep, qi=qi: e.dma_start(out=keep[:], in_=t_keep[qi]), dkeep, writes=[rkeep])
                S.dma("sp", lambda e, addt=addt, qi=qi: e.dma_start(out=addt[:], in_=t_add[qi]), dadd, writes=[radd])
                for ck in range(2):
                    r0 = 128 * ck - 8 * i + 250
                    S.dma("sp", lambda e, cm=cm, ck=ck, r0=r0: e.dma_start(out=cm[:, ck, :], in_=t_cm[r0:r0 + 128, :]),
                          dcm, writes=[rcm])
                def do_group(i, qi, g, qT, rq, keep, rkeep, addt, radd, cm, rcm):
                    gbc, rgbc, dgbc = Rgbc.next()
                    S.dma("sp", lambda e, gbc=gbc, g=g, qi=qi: e.dma_start(
                        out=gbc[:], in_=NGT[g * 24:(g + 1) * 24, qi * 128:(qi + 1) * 128].unsqueeze(0).to_broadcast(
                            [128, 24, 128])), dgbc, writes=[rgbc])
                    qTg = qT[:, g * 8:(g + 1) * 8, :].rearrange("p h q -> p (h q)")
                    acc, racc = Racc.next()
                    Ecs = []
                    for ck in range(2):
                        pSx, rpS = RpS.next()
                        for hf in range(2):
                            S.op("pe", lambda e, pSx=pSx, hf=hf, ck=ck: e.matmul(
                                pSx[:, hf * 512:(hf + 1) * 512], lhsT=kcm[:, g, ck * 128:(ck + 1) * 128],
                                rhs=qTg[:, hf * 512:(hf + 1) * 512], start=True, stop=True),
                                reads=[r_k, rq], writes=[rpS])
                        Sm, rSm = RSm.next()
                        S.op("dve", lambda e, Sm=Sm, pSx=pSx, ck=ck: e.tensor_tensor(
                            out=Sm[:].rearrange("p (h q) -> p h q", h=8), in0=pSx[:].rearrange("p (h q) -> p h q", h=8),
                            in1=cm[:, ck, :][:, None, :].to_broadcast([128, 8, 128]), op=ALU.add),
                            reads=[rpS, rcm], writes=[rSm])
                        Ec, rEc = REc.next()
                        S.op("act", lambda e, Ec=Ec, Sm=Sm, ck=ck: e.activation(
                            out=Ec[:], in_=Sm[:], func=AF.Exp, scale=SCALE, bias=cmpb[:, ck:ck + 1]),
                            reads=[rSm, r_k], writes=[rEc])
                        Ecs.append((Ec, rEc))
                        for hf in range(2):
                            S.op("pe", lambda e, Ec=Ec, hf=hf, ck=ck: e.matmul(
                                pO[:, hf * 512:(hf + 1) * 512], lhsT=vcm[:, g, ck, :],
                                rhs=Ec[:, hf * 512:(hf + 1) * 512], start=(ck == 0), stop=(ck == 1)),
                                reads=[r_k, rEc], writes=[r_pO])
                            S.op("pe", lambda e, Ec=Ec, hf=hf, ck=ck: e.matmul(
                                pSum[:, hf * 512:(hf + 1) * 512], lhsT=onesb[:],
                                rhs=Ec[:, hf * 512:(hf + 1) * 512], start=(ck == 0), stop=(ck == 1)),
                                reads=[r_c, rEc], writes=[r_pSum])
                    pU, rpU = RpS.next()
                    for h in range(8):
                        for ck in range(2):
                            Ec, rEc = Ecs[ck]
                            o0 = (h // 4) * 512 + (h % 4) * 65
                            S.op("pe", lambda e, pU=pU, Ec=Ec, h=h, ck=ck, o0=o0: e.matmul(
                                pU[:, o0:o0 + 65], lhsT=Ec[:, h * 128:(h + 1) * 128], rhs=ovb[:, ck, :],
                                start=(ck == 0), stop=(ck == 1)), reads=[r_c, rEc], writes=[rpU])
                    finalize(gbc, rgbc, 0, acc, racc, True)
                    Usb, rUsb = RUsb.next()
                    S.op("act", lambda e, pU=pU, Usb=Usb: e.activation(
                        out=Usb[:], in_=pU[:].rearrange("p (b x) -> p b x", b=2)[:, :, 0:260], func=AF.Copy),
                        reads=[rpU], writes=[rUsb])
                    for hh in range(2):
                        S.op("dve", lambda e, Usb=Usb, hh=hh: e.tensor_scalar_max(
                            out=rs8[:, hh * 4:(hh + 1) * 4],
                            in0=Usb[:, hh, :].rearrange("p (h s) -> p h s", s=65)[:, :, 64],
                            scalar1=1e-30), reads=[rUsb], writes=[r_small])
                    S.op("dve", lambda e: e.reciprocal(out=rs8[:], in_=rs8[:]), reads=[r_small], writes=[r_small])
                    for h in range(8):
                        o0 = (h % 4) * 65
                        if h == 0:
                            S.op("dve", lambda e, Usb=Usb, o0=o0: e.tensor_scalar(
                                out=imp[:], in0=Usb[:, 0, o0:o0 + 64], scalar1=rs8[:, 0:1], scalar2=None, op0=ALU.mult),
                                reads=[rUsb, r_small], writes=[r_small])
                        else:
                            S.op("dve", lambda e, Usb=Usb, o0=o0, h=h: e.scalar_tensor_tensor(
                                out=imp[:], in0=Usb[:, h // 4, o0:o0 + 64], scalar=rs8[:, h:h + 1], in1=imp[:],
                                op0=ALU.mult, op1=ALU.add), reads=[rUsb, r_small], writes=[r_small])
                    S.op("dve", lambda e, keep=keep: e.tensor_tensor(out=imp[:], in0=imp[:], in1=keep[:], op=ALU.mult),
                         reads=[r_small, rkeep], writes=[r_small])
                    S.op("dve", lambda e, addt=addt: e.tensor_tensor(out=imp[:], in0=imp[:], in1=addt[:], op=ALU.add),
                         reads=[r_small, radd], writes=[r_small])
                    S.op("dve", lambda e: e.max(out=m8[:], in_=imp[:]), reads=[r_small], writes=[r_small])
                    S.op("dve", lambda e: e.match_replace(out=imp2[:], in_to_replace=m8[:], in_values=imp[:],
                                                          imm_value=-1e30), reads=[r_small], writes=[r_small])
                    S.op("dve", lambda e: e.max(out=m8[:], in_=imp2[:]), reads=[r_small], writes=[r_small])
                    S.op("dve", lambda e: e.tensor_scalar(out=selb[:], in0=imp[:], scalar1=m8[:, 7:8], scalar2=NEG,
                                                          op0=ALU.is_lt, op1=ALU.mult), reads=[r_small], writes=[r_small])
                    if "DBGSEL" in debug:
                        for nm_, src_ in (("DBGSEL", selb), ("DBGIMP", imp), ("DBGM8", m8)):
                            S.dma("sp", lambda e, nm_=nm_, src_=src_, qi=qi, g=g: e.dma_start(
                                out=dbg_t[nm_][qi, g], in_=src_[:]), d_dbg, reads=[r_small])
                    attend(list(range(i - 4, i + 1)), kwt, vw, g, qTg, rq, None, None, i)
                    finalize(gbc, rgbc, 2, acc, racc, False)
                    pT, rpT = RpS.next()
                    S.op("pe", lambda e, pT=pT: e.transpose(out=pT[0:64, 0:128], in_=selb[:, 0:64], identity=identf[:]),
                         reads=[r_small, r_k], writes=[rpT])
                    selbT, rselbT = RselbT.next()
                    S.op("act", lambda e, pT=pT, selbT=selbT: e.activation(
                        out=selbT[:], in_=pT[0:64, 0:128][:, None, :].to_broadcast([64, 4, 128]), func=AF.Copy),
                        reads=[rpT], writes=[rselbT])
                    attend(list(range(0, i + 1)), kst, vs, g, qTg, rq, selbT, rselbT, i)
                    finalize(gbc, rgbc, 1, acc, racc, False)
                    accb, raccb, daccb = Raccb.next()
                    S.op("pool", lambda e, accb=accb, acc=acc: e.tensor_copy(out=accb[:], in_=acc[:]),
                         reads=[racc], writes=[raccb])
                    S.dma("pool", lambda e, accb=accb, g=g, qi=qi: e.dma_start(
                        out=NSAT[g * 8:(g + 1) * 8, :, qi * 128:(qi + 1) * 128].rearrange("h p q -> p h q"),
                        in_=accb[:].rearrange("p (h q) -> p h q", h=8)), daccb, reads=[raccb])

                for g in range(2):
                    do_group(i, qi, g, qT, rq, keep, rkeep, addt, radd, cm, rcm)
            S.emit(block)

    if 3 in phases:
        phase3a()
        phase3b()
    if 33 in phases:
        phase3b("x")


    QGROUPS = [(0, 128)] + [(128 + 512 * g, 512) for g in range(4)]

    def upproj(tag, srcT, w, gate0, addsrc, dst):
        with ExitStack() as st:
            sb = lambda name, shape, dt: st.enter_context(nc.sbuf_tensor(name, list(shape), dt))
            src = sb(f"{tag}_src", [128, 16, NQT], BF16)
            wb = [sb(f"{tag}_w{i}", [128, 16, 512], BF16) for i in range(2)]
            mg_ = [sb(f"{tag}_mg{i}", [128, NQT], BF16) for i in range(2)]
            m1_ = [sb(f"{tag}_m1{i}", [128, NQT], BF16) for i in range(2)]
            tf_ = [sb(f"{tag}_tf{i}", [128, 512], F32) for i in range(2)]
            o_ = [sb(f"{tag}_o{i}", [128, NQT], BF16) for i in range(2)]
            ps = [st.enter_context(nc.psum_tensor(f"{tag}_ps{i}", [128, 512], F32)) for i in range(4)]
            S = Sched(nc, st, tag)
            block = st.enter_context(nc.Block())
            r_srcq = [Res() for _ in range(4)]
            for q4 in range(4):
                S.dma("sp", lambda e, q4=q4: e.dma_start(out=src[:, q4 * 4:(q4 + 1) * 4, :],
                                                       in_=srcT[q4 * 4:(q4 + 1) * 4].rearrange("c p q -> p c q")),
                      S.dsem(), writes=[r_srcq[q4]])
            Rw, Rmg, Rm1, Ro = Ring(wb, S, True), Ring(mg_, S, True), Ring(m1_, S, True), Ring(o_, S, True)
            Rtf = Ring(tf_)
            Rps = Ring(ps, excl=True)
            wv = w.rearrange("(c p) n -> p c n", p=128)

            def chunk(wt, rw, ch, dc):
                mg, rmg, dmg = Rmg.next()
                S.dma("act", lambda e: e.dma_start(out=mg[:], in_=MGT[gate0 + dc]), dmg, writes=[rmg])
                if addsrc is not None:
                    m1, rm1, dm1 = Rm1.next()
                    S.dma("act", lambda e: e.dma_start(out=m1[:], in_=addsrc[dc]), dm1, writes=[rm1])
                o, ro, do = Ro.next()
                for (q0, nt) in QGROUPS:
                    p, rp = Rps.next()
                    for c in (range(16) if not os.environ.get("SKIPMM") else range(1)):
                        S.op("pe", lambda e, p=p, c=c, q0=q0, nt=nt: e.matmul(
                            p[:, 0:nt], lhsT=wt[:, c, ch * 128:(ch + 1) * 128], rhs=src[:, c, q0:q0 + nt],
                            start=(c == 0), stop=(c == 15)), reads=[rw, r_srcq[c // 4]], writes=[rp])
                    if addsrc is None:
                        S.op("dve", lambda e, p=p, q0=q0, nt=nt: e.tensor_tensor(
                            out=o[:, q0:q0 + nt], in0=p[:, 0:nt], in1=mg[:, q0:q0 + nt], op=ALU.mult),
                            reads=[rp, rmg], writes=[ro])
                    else:
                        tf, rtf = Rtf.next()
                        S.op("dve", lambda e, p=p, q0=q0, nt=nt, tf=tf: e.tensor_tensor(
                            out=tf[:, 0:nt], in0=p[:, 0:nt], in1=mg[:, q0:q0 + nt], op=ALU.mult),
                            reads=[rp, rmg], writes=[rtf])
                        S.op("pool", lambda e, q0=q0, nt=nt, tf=tf: e.tensor_tensor(
                            out=o[:, q0:q0 + nt], in0=tf[:, 0:nt], in1=m1[:, q0:q0 + nt], op=ALU.add),
                            reads=[rtf, rm1], writes=[ro])
                S.dma("sp", lambda e: e.dma_start(out=dst[dc], in_=o[:]), do, reads=[ro])

            def wblock(blk):
                wt, rw, dw = Rw.next()
                S.dma("pool", lambda e: e.dma_start(out=wt[:], in_=wv[:, :, blk * 512:(blk + 1) * 512]), dw, writes=[rw])
                for ch in range(4):
                    chunk(wt, rw, ch, blk * 4 + ch)

            for blk in range(4):
                wblock(blk)
            S.emit(block)

    def phase4b():
        with ExitStack() as st:
            sb = lambda name, shape, dt: st.enter_context(nc.sbuf_tensor(name, list(shape), dt))
            identb = sb("p4_id", [128, 128], BF16)
            wo = sb("p4_wo", [128, 16, D], BF16)
            w2bc = sb("p4_w2", [128, D], F32)
            mT_ = [sb(f"p4_mT{i}", [128, 16, 128], BF16) for i in range(2)]
            x_ = [sb(f"p4_x{i}", [128, D], F32) for i in range(2)]
            h_ = [sb(f"p4_h{i}", [128, D], F32) for i in range(2)]
            xn_ = [sb(f"p4_xn{i}", [128, D], BF16) for i in range(2)]
            xT_ = [sb(f"p4_xT{i}", [128, 16, 128], BF16) for i in range(2)]
            junk = sb("p4_junk", [128, D], BF16)
            ss_ = [sb(f"p4_ss{i}", [128, 1], F32) for i in range(2)]
            ps = [st.enter_context(nc.psum_tensor(f"p4_ps{i}", [128, 512], F32)) for i in range(4)]
            pt = [st.enter_context(nc.psum_tensor(f"p4_pt{i}", [128, 1024], BF16)) for i in range(2)]
            S = Sched(nc, st, "p4b")
            block = st.enter_context(nc.Block())
            r_id, r_wo, r_w2, r_junk = Res(), Res(), Res(), Res()
            d_p, d_s = S.dsem(), S.dsem()
            S.dma("pool", lambda e: e.dma_start(out=identb[:], in_=t_ident), d_p, writes=[r_id])
            wov = w_out.rearrange("(c p) n -> p c n", p=128)
            r_wob = [Res() for _ in range(4)]
            for blk in range(4):
                S.dma("pool", lambda e, blk=blk: e.dma_start(out=wo[:, :, blk * 512:(blk + 1) * 512],
                                                           in_=wov[:, :, blk * 512:(blk + 1) * 512]), S.dsem(),
                      writes=[r_wob[blk]])
            S.dma("sp", lambda e: e.dma_start(out=w2bc[:], in_=norm2_w.partition_broadcast(128)), d_s, writes=[r_w2])
            RmT, Rx, Rh, RxT = Ring(mT_, S, True), Ring(x_, S, True), Ring(h_, S, True), Ring(xT_, S, True)
            Rxn, Rss = Ring(xn_), Ring(ss_)
            Rps = Ring(ps, excl=True)
            r_pt = [Res(True), Res(True)]

            def tile(qi):
                mT, rmT, dmT = RmT.next()
                S.dma("sp", lambda e: e.dma_start(out=mT[:], in_=MRGT[:, :, qi * 128:(qi + 1) * 128].rearrange("c p q -> p c q")),
                      dmT, writes=[rmT])
                xt, rx, dx = Rx.next()
                S.dma("sp", lambda e: e.dma_start(out=xt[:], in_=x_loc[(QT0 + qi) * 128:(QT0 + qi + 1) * 128, :]), dx, writes=[rx])
                h, rh, dh = Rh.next()
                for cb in range(4):
                    p, rp = Rps.next()
                    for c in (range(16) if not os.environ.get("SKIPMM") else range(1)):
                        S.op("pe", lambda e, p=p, c=c, cb=cb: e.matmul(
                            p[:], lhsT=mT[:, c, :], rhs=wo[:, c, cb * 512:(cb + 1) * 512],
                            start=(c == 0), stop=(c == 15)), reads=[rmT, r_wob[cb]], writes=[rp])
                    S.op("dve", lambda e, p=p, cb=cb: e.tensor_tensor(
                        out=h[:, cb * 512:(cb + 1) * 512], in0=p[:], in1=xt[:, cb * 512:(cb + 1) * 512], op=ALU.add),
                        reads=[rp, rx], writes=[rh])
                S.dma("pool", lambda e: e.dma_start(out=H1[qi * 128:(qi + 1) * 128, :], in_=h[:]), dh, reads=[rh])
                ss, rss = Rss.next()
                S.op("act", lambda e: e.activation(out=junk[:], in_=h[:], func=AF.Square, accum_out=ss[:]),
                     reads=[rh], writes=[r_junk, rss])
                S.op("act", lambda e: e.activation(out=ss[:], in_=ss[:], func=AF.Sqrt, scale=1.0 / D, bias=EPS),
                     reads=[rss], writes=[rss])
                S.op("dve", lambda e: e.reciprocal(out=ss[:], in_=ss[:]), reads=[rss], writes=[rss])
                xn, rxn = Rxn.next()
                S.op("dve", lambda e: e.scalar_tensor_tensor(out=xn[:], in0=h[:], scalar=ss[:], in1=w2bc[:],
                                                             op0=ALU.mult, op1=ALU.mult),
                     reads=[rh, rss, r_w2], writes=[rxn])
                return xn, rxn

            def tile_back(qi, xn, rxn):
                xT, rxT, dxT = RxT.next()
                for hh in range(2):
                    for cc in range(8):
                        c = hh * 8 + cc
                        S.op("pe", lambda e, c=c, cc=cc, hh=hh: e.transpose(
                            out=pt[hh][:, cc * 128:(cc + 1) * 128], in_=xn[:, c * 128:(c + 1) * 128], identity=identb[:]),
                            reads=[rxn, r_id], writes=[r_pt[hh]])
                    if hh == 0:
                        S.op("act", lambda e, hh=hh: e.activation(
                            out=xT[:, 0:8, :], in_=pt[0][:].rearrange("p (c q) -> p c q", c=8), func=AF.Copy),
                            reads=[r_pt[0]], writes=[rxT])
                    else:
                        S.op("dve", lambda e, hh=hh: e.tensor_copy(
                            out=xT[:, 8:16, :], in_=pt[1][:].rearrange("p (c q) -> p c q", c=8)),
                            reads=[r_pt[1]], writes=[rxT])
                S.dma("pool", lambda e: e.dma_start(
                    out=XN2T[:, :, qi * 128:(qi + 1) * 128].rearrange("c p q -> p c q"), in_=xT[:]), dxT, reads=[rxT])

            pend4 = None
            for qi in range(NQ):
                cur4 = (qi,) + tile(qi)
                if pend4 is not None:
                    tile_back(*pend4)
                pend4 = cur4
            tile_back(*pend4)
            S.emit(block)

    if 4 in phases:
        upproj("p4a", RETGT, w_ret_up, 0, None, M1T)
        upproj("p4n", NSAT, w_nsa_up, 16, M1T, MRGT)
        phase4b()

    def phase5a():
        with ExitStack() as st:
            sb = lambda name, shape, dt: st.enter_context(nc.sbuf_tensor(name, list(shape), dt))
            xs = sb("p5_xs", [128, 16, NQT], BF16)
            wa_ = [sb(f"p5_wa{i}", [128, 16, 512], BF16) for i in range(2)]
            wb_ = [sb(f"p5_wb{i}", [128, 16, 512], BF16) for i in range(2)]
            cw = sb("p5_cw", [128, 88, 3], F32)
            cbias = sb("p5_cb", [128, 88], F32)
            halo = sb("p5_halo", [128, 1], F32)
            u_ = [sb(f"p5_u{i}", [128, 2050], F32) for i in range(4)]
            y_ = [sb(f"p5_y{i}", [128, 2048], F32) for i in range(3)]
            o_ = [sb(f"p5_o{i}", [128, 2048], BF16) for i in range(2)]
            ps = [st.enter_context(nc.psum_tensor(f"p5_ps{i}", [128, 512], F32)) for i in range(6)]
            ph = [st.enter_context(nc.psum_tensor(f"p5_ph{i}", [128, 2], F32)) for i in range(2)]
            S = Sched(nc, st, "p5a")
            block = st.enter_context(nc.Block())
            r_c = Res()
            d_c = S.dsem()
            S.dma("sp", lambda e: e.dma_start(out=xs[:], in_=XN2T.rearrange("c p q -> p c q")), d_c, writes=[r_c])
            S.dma("sp", lambda e: e.dma_start(out=cw[:], in_=conv_wT), d_c, writes=[r_c])
            S.dma("sp", lambda e: e.dma_start(out=cbias[:], in_=conv_bT), d_c, writes=[r_c])
            S.dma("sp", lambda e: e.dma_start(out=halo[:], in_=t_halo), d_c, writes=[r_c])
            Rwa, Rwb = Ring(wa_, S, True), Ring(wb_, S, True)
            Ru, Ry = Ring(u_), Ring(y_)
            Ro = Ring(o_, S, True)
            Rps, Rph = Ring(ps, excl=True), Ring(ph, excl=True)
            wv = w_ffn_up.rearrange("(c p) n -> p c n", p=128)

            def half(wt, rw, ch, cidx, ceng):
                u, ru = Ru.next()
                p, rp = Rph.next()
                for c in range(16):
                    S.op("pe", lambda e, c=c: e.matmul(p[:, 0:2], lhsT=wt[:, c, ch * 128:(ch + 1) * 128],
                                                       rhs=xs[:, c, 126:128], start=(c == 0), stop=(c == 15)),
                         reads=[rw, r_c], writes=[rp])
                S.op("act", lambda e: e.activation(out=u[:, 0:2], in_=p[:, 0:2], func=AF.Copy, scale=halo[:]),
                     reads=[rp, r_c], writes=[ru])
                for g in range(4):
                    pp, rpp = Rps.next()
                    for c in range(16):
                        S.op("pe", lambda e, c=c, pp=pp, g=g: e.matmul(
                            pp[:], lhsT=wt[:, c, ch * 128:(ch + 1) * 128],
                            rhs=xs[:, c, 128 + g * 512:128 + (g + 1) * 512], start=(c == 0), stop=(c == 15)),
                            reads=[rw, r_c], writes=[rpp])
                    S.op("act", lambda e, pp=pp, g=g: e.activation(out=u[:, 2 + g * 512:2 + (g + 1) * 512], in_=pp[:],
                                                                   func=AF.Copy), reads=[rpp], writes=[ru])
                y, ry = Ry.next()
                S.op(ceng, lambda e: e.tensor_scalar(out=y[:], in0=u[:, 2:2050], scalar1=cw[:, cidx, 2:3],
                                                     scalar2=cbias[:, cidx:cidx + 1], op0=ALU.mult, op1=ALU.add),
                     reads=[ru, r_c], writes=[ry])
                S.op(ceng, lambda e: e.scalar_tensor_tensor(out=y[:], in0=u[:, 1:2049], scalar=cw[:, cidx, 1:2], in1=y[:],
                                                            op0=ALU.mult, op1=ALU.add), reads=[ru, r_c, ry], writes=[ry])
                S.op(ceng, lambda e: e.scalar_tensor_tensor(out=y[:], in0=u[:, 0:2048], scalar=cw[:, cidx, 0:1], in1=y[:],
                                                            op0=ALU.mult, op1=ALU.add), reads=[ru, r_c, ry], writes=[ry])
                return y, ry

            def chunk(wa, rwa, wb, rwb, ch, j):
                ya, rya = half(wa, rwa, ch, j, "dve")
                yb, ryb = half(wb, rwb, ch, 44 + j, "dve")
                S.op("act", lambda e: e.activation(out=ya[:], in_=ya[:], func=AF.Silu), reads=[rya], writes=[rya])
                o, ro, do = Ro.next()
                S.op("pool", lambda e: e.tensor_tensor(out=o[:], in0=ya[:], in1=yb[:], op=ALU.mult),
                     reads=[rya, ryb], writes=[ro])
                S.dma("sp", lambda e: e.dma_start(out=ACTT[:, :, j, :].rearrange("t p q -> p t q"),
                                                  in_=o[:].rearrange("p (t q) -> p t q", q=128)), do, reads=[ro])

            def wblock(jb):
                wa, rwa, dwa = Rwa.next()
                wb, rwb, dwb = Rwb.next()
                S.dma("pool", lambda e: e.dma_start(out=wa[:], in_=wv[:, :, jb * 512:(jb + 1) * 512]), dwa, writes=[rwa])
                S.dma("pool", lambda e: e.dma_start(out=wb[:], in_=wv[:, :, DFF + jb * 512:DFF + (jb + 1) * 512]), dwb,
                      writes=[rwb])
                for ch in range(4):
                    chunk(wa, rwa, wb, rwb, ch, jb * 4 + ch)

            for jb in range(11):
                wblock(jb)
            S.emit(block)

    def phase5b():
        with ExitStack() as st:
            sb = lambda name, shape, dt: st.enter_context(nc.sbuf_tensor(name, list(shape), dt))
            wd_ = [sb(f"p5b_w{i}", [128, 44, 512], BF16) for i in range(2)]
            a_ = [sb(f"p5b_a{i}", [128, 44, 128], BF16) for i in range(3)]
            h_ = [sb(f"p5b_h{i}", [128, 512], F32) for i in range(3)]
            ps = [st.enter_context(nc.psum_tensor(f"p5b_ps{i}", [128, 512], F32)) for i in range(4)]
            S = Sched(nc, st, "p5b")
            block = st.enter_context(nc.Block())
            Rw, Ra, Rh = Ring(wd_, S, True), Ring(a_, S, True), Ring(h_, S, True)
            Rwq = Ring(list(range(8)), S, True)
            r_hs = [Res() for _ in h_]
            d_hs = [S.dsem() for _ in h_]
            Rps = Ring(ps, excl=True)
            wv = w_ffn_down.rearrange("(j p) n -> p j n", p=128)

            def tile(wt, rw, cb, t):
                a, ra, da = Ra.next()
                S.dma("act", lambda e: e.dma_start(out=a[:], in_=ACTT[t]), da, writes=[ra])
                h, rh, dh = Rh.next()
                k = Rh.i
                S.dma("act", lambda e: e.dma_start(out=h[:], in_=H1[(t + 1) * 128:(t + 2) * 128, cb * 512:(cb + 1) * 512]),
                      dh, writes=[rh])
                p, rp = Rps.next()
                for j in (range(44) if not os.environ.get("SKIPMM") else range(1)):
                    S.op("pe", lambda e, j=j: e.matmul(p[:], lhsT=a[:, j, :], rhs=wt[:, j, :], start=(j == 0), stop=(j == 43)),
                         reads=[ra, rw[j // 11]], writes=[rp])
                S.op("dve", lambda e: e.tensor_tensor(out=h[:], in0=p[:], in1=h[:], op=ALU.add), reads=[rp, rh], writes=[rh])
                S.dma("sp", lambda e: e.dma_start(out=H2[t * 128:(t + 1) * 128, cb * 512:(cb + 1) * 512], in_=h[:]),
                      d_hs[k], reads=[rh], writes=[r_hs[k]])

            def wblock(cb):
                wt, rw, dw = Rw.next()
                rws = Rwq.res[4 * Rw.i:4 * Rw.i + 4]
                dws = Rwq.ds[4 * Rw.i:4 * Rw.i + 4]
                for q4 in range(4):
                    S.dma("pool", lambda e, q4=q4: e.dma_start(out=wt[:, q4 * 11:(q4 + 1) * 11, :],
                                                             in_=wv[:, q4 * 11:(q4 + 1) * 11, cb * 512:(cb + 1) * 512]),
                          dws[q4], writes=[rws[q4]])
                for t in range(16):
                    tile(wt, rws, cb, t)

            for cb in range(4):
                wblock(cb)
            S.emit(block)

    def phase5c():
        with ExitStack() as st:
            sb = lambda name, shape, dt: st.enter_context(nc.sbuf_tensor(name, list(shape), dt))
            wf = sb("p5c_wf", [128, D], F32)
            h_ = [sb(f"p5c_h{i}", [128, D], F32) for i in range(2)]
            o_ = [sb(f"p5c_o{i}", [128, D], F32) for i in range(2)]
            junk = sb("p5c_junk", [128, D], BF16)
            ss_ = [sb(f"p5c_ss{i}", [128, 1], F32) for i in range(2)]
            S = Sched(nc, st, "p5c")
            block = st.enter_context(nc.Block())
            r_w, r_junk = Res(), Res()
            S.dma("sp", lambda e: e.dma_start(out=wf[:], in_=final_norm_w.partition_broadcast(128)), S.dsem(), writes=[r_w])
            Rh, Ro, Rss = Ring(h_, S, True), Ring(o_, S, True), Ring(ss_)

            def tile(t):
                h, rh, dh = Rh.next()
                S.dma("sp", lambda e: e.dma_start(out=h[:], in_=H2[t * 128:(t + 1) * 128, :]), dh, writes=[rh])
                ss, rss = Rss.next()
                S.op("act", lambda e: e.activation(out=junk[:], in_=h[:], func=AF.Square, accum_out=ss[:]),
                     reads=[rh], writes=[r_junk, rss])
                S.op("act", lambda e: e.activation(out=ss[:], in_=ss[:], func=AF.Sqrt, scale=1.0 / D, bias=EPS),
                     reads=[rss], writes=[rss])
                S.op("dve", lambda e: e.reciprocal(out=ss[:], in_=ss[:]), reads=[rss], writes=[rss])
                o, ro, do = Ro.next()
                S.op("dve", lambda e: e.scalar_tensor_tensor(out=o[:], in0=h[:], scalar=ss[:], in1=wf[:],
                                                             op0=ALU.mult, op1=ALU.mult), reads=[rh, rss, r_w], writes=[ro])
                S.dma("pool", lambda e: e.dma_start(out=out[t * 128:(t + 1) * 128, :], in_=o[:]), do, reads=[ro])

            for t in range(16):
                tile(t)
            S.emit(block)

    if 5 in phases:
        phase5a()
        phase5b()
        phase5c()

    return nc


def _tables(s):
    pad = 2048 if s == 0 else 0
    tl = np.arange(LT * 128)
    act = np.maximum(tl - pad, 0).astype(np.float64)
    is_pad = tl < pad
    half = 64
    freq = 10000.0 ** (-np.arange(half, dtype=np.float64) / half)
    ang = act[:, None] * freq[None, :]
    cs = np.concatenate([np.cos(ang), np.sin(ang)], axis=1)
    t_cs = cs.reshape(LT, 128, 128).transpose(1, 0, 2).astype(np.float32)
    t_csq = (cs[QT0 * 128:] * (128 ** -0.5)).reshape(NQ, 128, 128).transpose(1, 0, 2).astype(np.float32)
    gam = 1.0 - 2.0 ** (-5.0 - np.arange(8, dtype=np.float64))
    j = np.arange(128, dtype=np.float64)
    rel = j[None, :] - j[:, None]
    decT = np.where(rel[:, None, :] >= 0, gam[None, :, None] ** np.maximum(rel[:, None, :], 0), 0.0)
    qdec = np.broadcast_to((gam[:, None] ** (j[None, :] + 1.0))[None], (128, 8, 128))
    kdec = gam[None, :] ** (127.0 - j[:, None])
    padb = np.where(is_pad, NEG, 0.0).reshape(LT, 128).T
    n = np.arange(256)
    cmp_invalid = (n * 16 < pad) | (n >= 255)
    cmpb = np.where(cmp_invalid, NEG, 0.0).reshape(2, 128).T
    r = np.arange(512)[:, None]
    q = np.arange(128)[None, :]
    cm = np.where(16 * (r - 250) + 31 <= q, 0.0, NEG)
    keep = np.ones((NQ, 128, 64))
    add = np.zeros((NQ, 128, 64))
    blk = np.arange(64)[None, :]
    b0 = pad // 64
    for qi in range(NQ):
        t = (QT0 + qi) * 128 + np.arange(128)
        cur = (t // 64)[:, None]
        forced = (blk == b0) | (blk == cur) | (blk == cur - 1)
        neg = (blk > cur) | (blk < b0)
        keep[qi] = np.where(forced | neg, 0.0, 1.0)
        add[qi] = np.where(neg, -1e4, np.where(forced, 1e4, 0.0))
    sidx = np.arange(64)[:, None, None]
    kt = np.arange(LT)[None, :, None]
    kk = np.arange(128)[None, None, :]
    texp = (sidx == 2 * kt + (kk >= 64)).astype(np.float32)
    k_ = np.arange(128)[:, None]
    caus = np.where(k_ > q, NEG, 0.0)
    acaus = np.where(k_ <= q, NEG, 0.0)
    nn = np.arange(256)[:, None]
    ss = np.arange(64)[None, :]
    ov = ((nn * 16 < ss * 64 + 64) & (nn * 16 + 32 > ss * 64)).astype(np.float64)
    ov = np.concatenate([ov, np.ones((256, 1))], axis=1)
    ov[255] = 0.0
    t_ov = ov.reshape(2, 128, 65).transpose(1, 0, 2)
    f = lambda a: np.ascontiguousarray(a, dtype=np.float32)
    return {
        "t_cs": f(t_cs), "t_csq": f(t_csq), "t_decT": f(decT), "t_qdec": f(qdec), "t_kdec": f(kdec),
        "t_padb": f(padb), "t_cmpb": f(cmpb), "t_cm": f(cm), "t_keep": f(keep), "t_add": f(add),
        "t_exp": f(texp), "t_caus": f(caus), "t_acaus": f(acaus), "t_ident": f(np.eye(128)),
        "t_ov": f(t_ov), "t_halo": f(np.full((128, 1), float(s))),
    }


def make_in_maps(inputs):
    g = lambda k: np.asarray(inputs[k], dtype=np.float32)
    x = g("x")
    shared = {
        "norm1_w": g("norm1_w")[0][None], "w_in": g("w_in")[0], "ret_norm_w": g("ret_norm_w")[0][None],
        "w_ret_up": g("w_ret_up")[0],
        "cmp_peT_k": np.ascontiguousarray(g("cmp_pe_k")[0].T), "cmp_peT_v": np.ascontiguousarray(g("cmp_pe_v")[0].T),
        "cmp_w1_k": g("cmp_w1_k")[0], "cmp_w1_v": g("cmp_w1_v")[0],
        "cmp_w2_k": g("cmp_w2_k")[0], "cmp_w2_v": g("cmp_w2_v")[0],
        "w_nsa_up": g("w_nsa_up")[0], "w_out": g("w_out")[0], "norm2_w": g("norm2_w")[0][None],
        "w_ffn_up": g("w_ffn_up")[0],
        "conv_wT": np.ascontiguousarray(g("conv_w")[0].reshape(3, 88, 128).transpose(2, 1, 0)),
        "conv_bT": np.ascontiguousarray(g("conv_b")[0].reshape(88, 128).T),
        "w_ffn_down": g("w_ffn_down")[0], "final_norm_w": g("final_norm_w")[None],
    }
    tabs = [_tables(0), _tables(1)]
    maps = []
    for c in range(8):
        b, s = c // 2, c % 2
        if s == 1:
            xl = np.ascontiguousarray(x[b])
        else:
            xl = np.concatenate([np.zeros((2048, D), np.float32), x[b, :2048]], axis=0)
        m = dict(shared)
        m.update(tabs[s])
        m["x_loc"] = xl
        maps.append(m)
    return maps


_NC = None


def kernel(**inputs):
    global _NC
    if _NC is None:
        _NC = build_program()
    maps = make_in_maps(inputs)
    res = run_bass_kernel_spmd(_NC, maps, core_ids=list(range(8)))
    outp = np.zeros((4, 4096, D), np.float32)
    for c in range(8):
        b, s = c // 2, c % 2
        outp[b, s * 2048:(s + 1) * 2048] = res.results[c]["out"]
    return outp
```
